# Optimizing a Trainium2 kernel written in Bass

```python
import jax, jax.numpy as jnp
from jax import lax
import numpy as np

D_MODEL = 1024
BATCH = 8
SEQ = 4096
DEPTH = 1
DEC_BATCH = 32
DEC_SEQ = 8
PAST_LEN = 16384
PAGE_SIZE = 128

HEAD_DIM = 64
MIX_W = D_MODEL
RWKV_W = MIX_W // 2
ATT_W = MIX_W - RWKV_W
H_R = RWKV_W // HEAD_DIM
H_A = ATT_W // HEAD_DIM
D_DECAY = 64
D_ICLR = 64
D_GATE = 128
RWKV_COLS = 3 * RWKV_W + D_DECAY + D_ICLR + D_GATE
D_IN = RWKV_COLS + 3 * ATT_W
DIL_PATTERN = ((128, 1), (512, 4), (2048, 16))
WINDOW_MAX = max(w for w, _ in DIL_PATTERN)
ROPE_THETA = 10000.0
D_FF = 2816
CONV_W = 3
NORM_EPS = 1e-6
LNX_EPS = 64e-5
NEG_INF = -1e30

kernel_name = 'hymba_rwkv7_dilated_convffn_step'


def _rms(x, g):
    xf = x.astype(jnp.float32)
    y = xf * lax.rsqrt(jnp.mean(xf * xf, axis=-1, keepdims=True) + NORM_EPS)
    return (y * g.astype(jnp.float32)).astype(x.dtype)


def _rope(t, pos):
    half = t.shape[-1] // 2
    inv = ROPE_THETA ** (-jnp.arange(half, dtype=jnp.float32) / half)
    ang = pos.astype(jnp.float32)[:, None] * inv[None, :]
    cos = jnp.cos(ang)[None, :, None, :]
    sin = jnp.sin(ang)[None, :, None, :]
    tf = t.astype(jnp.float32)
    t1, t2 = tf[..., :half], tf[..., half:]
    return jnp.concatenate([t1 * cos - t2 * sin, t1 * sin + t2 * cos], axis=-1).astype(t.dtype)


def _dilated_branch_prompt(q, k, v, dilation, n_steps):
    B, L, H, Dh = q.shape
    qb = n_steps
    seg = dilation * qb
    Lp = -(-L // seg) * seg
    pad = Lp - L
    M = Lp // dilation
    nb = M // qb

    def to_sub(t):
        t = jnp.pad(t, ((0, 0), (pad, 0), (0, 0), (0, 0)))
        t = t.reshape(B, M, dilation, H, Dh).transpose(0, 2, 1, 3, 4)
        return t.reshape(B, dilation, nb, qb, H, Dh)

    def with_prev(t):
        prev = jnp.pad(t, ((0, 0), (0, 0), (1, 0), (0, 0), (0, 0), (0, 0)))[:, :, :-1]
        return jnp.concatenate([prev, t], axis=3)

    qs = to_sub(q)
    kb = with_prev(to_sub(k))
    vb = with_prev(to_sub(v))
    s = jnp.einsum('brnqhd,brnkhd->brnhqk', qs, kb).astype(jnp.float32) * (Dh ** -0.5)

    qi = jnp.arange(qb)[:, None]
    ki = jnp.arange(2 * qb)[None, :]
    dist = qi + qb - ki
    band = (dist >= 0) & (dist <= n_steps)
    key_sub = jnp.arange(nb)[:, None] * qb + jnp.arange(2 * qb)[None, :] - qb
    key_pos = key_sub[None] * dilation + jnp.arange(dilation)[:, None, None] - pad
    mask = band[None, None] & (key_pos >= 0)[:, :, None, :]
    mask = mask[None, :, :, None]

    s = jnp.where(mask, s, NEG_INF)
    m = jnp.max(s, axis=-1)
    p = jnp.where(mask, jnp.exp(s - m[..., None]), 0.0)
    l = jnp.sum(p, axis=-1)
    o = jnp.einsum('brnhqk,brnkhd->brnqhd', p, vb.astype(jnp.float32))

    def from_sub(t):
        rest = t.shape[4:]
        t = t.reshape((B, dilation, M) + rest).swapaxes(1, 2).reshape((B, Lp) + rest)
        return t[:, pad:]

    return from_sub(o), from_sub(m.swapaxes(3, 4)), from_sub(l.swapaxes(3, 4))


def _dilated_branch_sample(q, kc, vc, dilation, n_steps, offset):
    T = q.shape[1]
    j = jnp.arange(n_steps + 1)
    idx = offset + jnp.arange(T)[:, None] - j[None, :] * dilation
    valid = idx >= 0
    idx = jnp.maximum(idx, 0)
    kg = kc[:, idx]
    vg = vc[:, idx]
    s = jnp.einsum('bthd,btjhd->bthj', q, kg).astype(jnp.float32) * (q.shape[-1] ** -0.5)
    mask = valid[None, :, None, :]
    s = jnp.where(mask, s, NEG_INF)
    m = jnp.max(s, axis=-1)
    p = jnp.where(mask, jnp.exp(s - m[..., None]), 0.0)
    l = jnp.sum(p, axis=-1)
    o = jnp.einsum('bthj,btjhd->bthd', p, vg.astype(jnp.float32))
    return o, m, l


def _merge_branches(branches):
    o = jnp.stack([b[0] for b in branches])
    m = jnp.stack([b[1] for b in branches])
    l = jnp.stack([b[2] for b in branches])
    wt = jnp.exp(m - jnp.max(m, axis=0))
    num = jnp.sum(wt[..., None] * o, axis=0)
    den = jnp.sum(wt * l, axis=0)
    return num / den[..., None]


def _wkv_step(S, inp):
    r, w, k, v, a, b = inp
    sa = jnp.einsum('bhij,bhj->bhi', S, a)
    S = S * w[:, :, None, :] + sa[..., None] * b[:, :, None, :] + v[..., None] * k[:, :, None, :]
    return S, jnp.einsum('bhij,bhj->bhi', S, r)


def _rwkv7(r, k, v, wl, al, gl, S0, w0, w_decay_up, a0, w_iclr_up, w_gate_up,
           k_k, k_a, r_k, lnx_w, lnx_b):
    B, T, _ = r.shape
    f32 = jnp.float32
    r, k, v = r.astype(f32), k.astype(f32), v.astype(f32)
    w = -jax.nn.softplus(-(w0 + jnp.tanh(wl.astype(f32)) @ w_decay_up)) - 0.5
    decay = jnp.exp(-jnp.exp(w))
    a = jax.nn.sigmoid(a0 + al.astype(f32) @ w_iclr_up)
    g = jax.nn.sigmoid(gl.astype(f32)) @ w_gate_up
    heads = lambda t: t.reshape(B, T, H_R, HEAD_DIM)
    kk = heads(k * k_k)
    kk = kk / jnp.maximum(jnp.sqrt(jnp.sum(kk * kk, axis=-1, keepdims=True)), 1e-12)
    k = k * (1.0 + (a - 1.0) * k_a)
    rh, kh, vh, wh = heads(r), heads(k), heads(v), heads(decay)
    aa = -kk
    bb = kk * heads(a)
    tm = lambda t: jnp.swapaxes(t, 0, 1)
    S_fin, y = lax.scan(_wkv_step, S0.astype(f32),
                        (tm(rh), tm(wh), tm(kh), tm(vh), tm(aa), tm(bb)))
    y = tm(y)
    mu = jnp.mean(y, axis=-1, keepdims=True)
    var = jnp.mean(jnp.square(y - mu), axis=-1, keepdims=True)
    y = ((y - mu) * lax.rsqrt(var + LNX_EPS)).reshape(B, T, RWKV_W) * lnx_w + lnx_b
    bonus = jnp.sum(rh * kh * r_k, axis=-1, keepdims=True) * vh
    y = (y + bonus.reshape(B, T, RWKV_W)) * g
    return y, S_fin


def _layer(x, pos, h_prev, wkv0, k_buf, v_buf, conv_prev,
           norm_mix_pre, norm_mix_post, norm_ffn_pre, norm_ffn_post, w_in, mu_shift,
           w0, w_decay_up, a0, w_iclr_up, w_gate_up, k_k, k_a, r_k, lnx_w, lnx_b,
           w_out, w_ffn_up, ffn_conv_w, ffn_conv_b, w_ffn_down):
    B, T, _ = x.shape
    h = _rms(x, norm_mix_pre)
    proj = jnp.concatenate([h_prev[:, None].astype(h.dtype), h], axis=1) @ w_in
    cur = proj[:, 1:]
    prv = proj[:, :-1, :RWKV_COLS]
    rw = cur[..., :RWKV_COLS]
    mixed = rw + (prv - rw) * mu_shift
    r, k, v, wl, al, gl = jnp.split(mixed, [RWKV_W, 2 * RWKV_W, 3 * RWKV_W,
                                            3 * RWKV_W + D_DECAY,
                                            3 * RWKV_W + D_DECAY + D_ICLR], axis=-1)
    rwkv_out, wkv_new = _rwkv7(r, k, v, wl, al, gl, wkv0, w0, w_decay_up, a0, w_iclr_up,
                               w_gate_up, k_k, k_a, r_k, lnx_w, lnx_b)

    qa, ka, va = [t.reshape(B, T, H_A, HEAD_DIM)
                  for t in jnp.split(cur[..., RWKV_COLS:], 3, axis=-1)]
    qa = _rope(qa, pos)
    ka = _rope(ka, pos)
    if k_buf is None:
        branches = [_dilated_branch_prompt(qa, ka, va, d, w // d) for w, d in DIL_PATTERN]
        keep = min(WINDOW_MAX, T)
        k_new, v_new = ka[:, -keep:], va[:, -keep:]
    else:
        W = k_buf.shape[1]
        kc = jnp.concatenate([k_buf.astype(ka.dtype), ka], axis=1)
        vc = jnp.concatenate([v_buf.astype(va.dtype), va], axis=1)
        branches = [_dilated_branch_sample(qa, kc, vc, d, w // d, W) for w, d in DIL_PATTERN]
        k_new, v_new = kc[:, -W:], vc[:, -W:]
    att_out = _merge_branches(branches).reshape(B, T, ATT_W)

    mix = jnp.concatenate([rwkv_out.astype(x.dtype), att_out.astype(x.dtype)], axis=-1) @ w_out
    x = x + _rms(mix, norm_mix_post).astype(x.dtype)

    h2 = _rms(x, norm_ffn_pre)
    u = h2 @ w_ffn_up
    ue = jnp.concatenate([conv_prev.astype(u.dtype), u], axis=1)
    c = sum(ue[:, i:i + T] * ffn_conv_w[i] for i in range(CONV_W)) + ffn_conv_b
    gate, up = jnp.split(c, 2, axis=-1)
    f = (jax.nn.silu(gate) * up) @ w_ffn_down
    x = x + _rms(f, norm_ffn_post).astype(x.dtype)
    return x, h[:, -1], wkv_new, k_new, v_new, ue[:, -(CONV_W - 1):]


def setup_inputs(seed: int = 0) -> dict:
    key = jax.random.key(seed)
    ks = jax.random.split(key, 32)
    nrm = jax.random.normal
    W_BUF = min(WINDOW_MAX, PAST_LEN)
    L = DEPTH
    return {
        'x_prompt': nrm(ks[0], (BATCH, SEQ, D_MODEL), jnp.float32),
        'x_sample': nrm(ks[1], (DEC_BATCH, DEC_SEQ, D_MODEL), jnp.float32),
        'state_rwkv_shift': nrm(ks[2], (L, DEC_BATCH, D_MODEL), jnp.float32),
        'state_rwkv_wkv': nrm(ks[3], (L, DEC_BATCH, H_R, HEAD_DIM, HEAD_DIM), jnp.float32),
        'cache_att_k': nrm(ks[4], (L, DEC_BATCH, W_BUF, H_A, HEAD_DIM), jnp.float32),
        'cache_att_v': nrm(ks[5], (L, DEC_BATCH, W_BUF, H_A, HEAD_DIM), jnp.float32),
        'state_ffn_conv': nrm(ks[6], (L, DEC_BATCH, CONV_W - 1, 2 * D_FF), jnp.float32),
        'norm_mix_pre': 1.0 + 0.05 * nrm(ks[7], (L, D_MODEL), jnp.float32),
        'norm_mix_post': 1.0 + 0.05 * nrm(ks[8], (L, D_MODEL), jnp.float32),
        'norm_ffn_pre': 1.0 + 0.05 * nrm(ks[9], (L, D_MODEL), jnp.float32),
        'norm_ffn_post': 1.0 + 0.05 * nrm(ks[10], (L, D_MODEL), jnp.float32),
        'w_in': nrm(ks[11], (L, D_MODEL, D_IN), jnp.float32) * D_MODEL ** -0.5,
        'mu_shift': jax.random.uniform(ks[12], (L, RWKV_COLS), jnp.float32),
        'w0': -0.5 - 3.5 * jax.random.uniform(ks[13], (L, RWKV_W), jnp.float32),
        'w_decay_up': nrm(ks[14], (L, D_DECAY, RWKV_W), jnp.float32) * 0.1 * D_DECAY ** -0.5,
        'a0': 0.1 * nrm(ks[15], (L, RWKV_W), jnp.float32),
        'w_iclr_up': nrm(ks[16], (L, D_ICLR, RWKV_W), jnp.float32) * D_ICLR ** -0.5,
        'w_gate_up': nrm(ks[17], (L, D_GATE, RWKV_W), jnp.float32) * D_GATE ** -0.5,
        'k_k': 0.85 + 0.05 * nrm(ks[18], (L, RWKV_W), jnp.float32),
        'k_a': 1.0 + 0.05 * nrm(ks[19], (L, RWKV_W), jnp.float32),
        'r_k': 0.1 * nrm(ks[20], (L, H_R, HEAD_DIM), jnp.float32),
        'lnx_w': 1.0 + 0.05 * nrm(ks[21], (L, RWKV_W), jnp.float32),
        'lnx_b': 0.01 * nrm(ks[22], (L, RWKV_W), jnp.float32),
        'w_out': nrm(ks[23], (L, MIX_W, D_MODEL), jnp.float32) * MIX_W ** -0.5,
        'w_ffn_up': nrm(ks[24], (L, D_MODEL, 2 * D_FF), jnp.float32) * D_MODEL ** -0.5,
        'ffn_conv_w': nrm(ks[25], (L, CONV_W, 2 * D_FF), jnp.float32) * CONV_W ** -0.5,
        'ffn_conv_b': 0.01 * nrm(ks[26], (L, 2 * D_FF), jnp.float32),
        'w_ffn_down': nrm(ks[27], (L, D_FF, D_MODEL), jnp.float32) * D_FF ** -0.5,
    }


def reference(x_prompt, x_sample, state_rwkv_shift, state_rwkv_wkv, cache_att_k, cache_att_v,
              state_ffn_conv, norm_mix_pre, norm_mix_post, norm_ffn_pre, norm_ffn_post, w_in,
              mu_shift, w0, w_decay_up, a0, w_iclr_up, w_gate_up, k_k, k_a, r_k, lnx_w, lnx_b,
              w_out, w_ffn_up, ffn_conv_w, ffn_conv_b, w_ffn_down):
    Bp, Tp, _ = x_prompt.shape
    Ts = x_sample.shape[1]
    pos_p = jnp.arange(Tp, dtype=jnp.int32)
    pos_s = PAST_LEN + jnp.arange(Ts, dtype=jnp.int32)
    yp, ys = x_prompt, x_sample
    p_shift, p_wkv, p_k, p_v, p_conv = [], [], [], [], []
    s_shift, s_wkv, s_k, s_v, s_conv = [], [], [], [], []
    for l in range(DEPTH):
        wts = (norm_mix_pre[l], norm_mix_post[l], norm_ffn_pre[l], norm_ffn_post[l], w_in[l],
               mu_shift[l], w0[l], w_decay_up[l], a0[l], w_iclr_up[l], w_gate_up[l], k_k[l],
               k_a[l], r_k[l], lnx_w[l], lnx_b[l], w_out[l], w_ffn_up[l], ffn_conv_w[l],
               ffn_conv_b[l], w_ffn_down[l])
        yp, a1, a2, a3, a4, a5 = _layer(
            yp, pos_p, jnp.zeros((Bp, D_MODEL), x_prompt.dtype),
            jnp.zeros((Bp, H_R, HEAD_DIM, HEAD_DIM), jnp.float32), None, None,
            jnp.zeros((Bp, CONV_W - 1, 2 * D_FF), x_prompt.dtype), *wts)
        p_shift.append(a1); p_wkv.append(a2); p_k.append(a3); p_v.append(a4); p_conv.append(a5)
        ys, b1, b2, b3, b4, b5 = _layer(
            ys, pos_s, state_rwkv_shift[l], state_rwkv_wkv[l], cache_att_k[l], cache_att_v[l],
            state_ffn_conv[l], *wts)
        s_shift.append(b1); s_wkv.append(b2); s_k.append(b3); s_v.append(b4); s_conv.append(b5)
    return (yp, ys,
            jnp.stack(p_shift), jnp.stack(p_wkv), jnp.stack(p_k), jnp.stack(p_v), jnp.stack(p_conv),
            jnp.stack(s_shift), jnp.stack(s_wkv), jnp.stack(s_k), jnp.stack(s_v), jnp.stack(s_conv))
```

```python
import contextlib
import os
RCUT = int(os.environ.get('RCUT', '9'))
RPRE = int(os.environ.get('RPRE', '99'))
RSUB = int(os.environ.get('RSUB', '99'))
BCUT = int(os.environ.get('BCUT', '99'))
import numpy as np
import ml_dtypes
import concourse.bass as bass
import concourse.mybir as mybir
from concourse.bass_utils import run_bass_kernel_spmd

F32 = mybir.dt.float32
BF16 = mybir.dt.bfloat16
AF = mybir.ActivationFunctionType
ALU = mybir.AluOpType
AX = mybir.AxisListType

D = 1024
HD = 64
RW = 512
NCOL_R = 1792
D_IN = 3328
DFF = 2816
PAST_LEN = 16384
WBUF = 2048
NORM_EPS = 1e-6
LNX_EPS = 64e-5
C0 = float(np.exp(-0.5))

_ESZ = {F32: 4, BF16: 2}


def _box(ap):
    name = ap.tensor.name
    dims = ap.ap
    off = int(ap.offset)
    esz = _ESZ.get(ap.dtype, 4)
    if 'DRAM' in str(ap.space).upper():
        ext = sum((c - 1) * abs(s) for s, c in dims)
        return (name, 0, 1, off * esz, (off + ext + 1) * esz)
    pstride = dims[0][0]
    pcount = dims[0][1]
    if pstride == 0:
        p0, f0 = 0, off
    else:
        p0 = off // pstride
        f0 = off - p0 * pstride
    ext = sum((c - 1) * abs(s) for s, c in dims[1:])
    b0, b1 = f0 * esz, (f0 + ext + 1) * esz
    if 'PSUM' in str(ap.space).upper():
        return (name, (p0 // 32) * 32, ((p0 + pcount + 31) // 32) * 32, (b0 // 2048) * 2048, ((b1 + 2047) // 2048) * 2048)
    return (name, p0, p0 + pcount, b0, b1)


class Prog:
    COMPUTE = ('pe', 'act', 'dve', 'pool')

    def __init__(self, nc, n_dma_sems=(20, 12, 8)):
        self.nc = nc
        self.ops = {e: [] for e in ('pe', 'act', 'dve', 'pool', 'sp')}
        self.nseq = {e: 0 for e in self.COMPUTE}
        self.waited = {}
        self.recs = {}
        self.dmaq = {'sp': [n_dma_sems[0], 0], 'pool': [n_dma_sems[1], 0], 'act': [n_dma_sems[2], 0]}
        self.sems = {}

    def _deps(self, eng, outs, ins, waiter=None):
        waiter = waiter or eng
        deps = {}
        for ap, is_w in [(a, True) for a in outs] + [(a, False) for a in ins]:
            name, p0, p1, f0, f1 = _box(ap)
            for r in self.recs.get(name, ()):
                if r[0] < p1 and p0 < r[1] and r[2] < f1 and f0 < r[3]:
                    if is_w or r[6] or (name == 'PS' and r[7] != eng):
                        if r[7] == eng and eng in self.COMPUTE:
                            if eng == 'pe':
                                continue
                            if not (r[6] and not is_w):
                                continue
                        if deps.get(r[4], 0) < r[5]:
                            deps[r[4]] = r[5]
        out = []
        for key, val in deps.items():
            if self.waited.get((waiter, key), 0) < val:
                self.waited[(waiter, key)] = val
                out.append((key, val))
        return out

    def _record(self, eng, outs, ins, key, val):
        for ap, is_w in [(a, True) for a in outs] + [(a, False) for a in ins]:
            name, p0, p1, f0, f1 = _box(ap)
            lst = self.recs.setdefault(name, [])
            if is_w:
                lst[:] = [r for r in lst if not (p0 <= r[0] and r[1] <= p1 and f0 <= r[2] and r[3] <= f1)]
            else:
                lst[:] = [r for r in lst if not ((not r[6]) and r[4] == key and p0 <= r[0] and r[1] <= p1
                                                 and f0 <= r[2] and r[3] <= f1)]
            lst.append([p0, p1, f0, f1, key, val, is_w, eng])

    def op(self, eng, fn, outs, ins):
        waits = self._deps(eng, outs, ins)
        self.nseq[eng] += 1
        key = 'c_' + eng
        self._record(eng, outs, ins, key, self.nseq[eng])
        self.ops[eng].append(('c', waits, fn, key))

    def dma(self, q, out, in_, **kw):
        ns, cnt = self.dmaq[q]
        slot = cnt % ns
        rnd = cnt // ns
        self.dmaq[q][1] += 1
        key = 'd_%s_%d' % (q, slot)
        waits = self._deps('dma_' + q, [out], [in_], waiter=q)
        if rnd > 0 and self.waited.get((q, key), 0) < 16 * rnd:
            self.waited[(q, key)] = 16 * rnd
            waits.append((key, 16 * rnd))
        self._record('dma_' + q, [out], [in_], key, 16 * (rnd + 1))
        self.ops[q].append(('d', waits, (out, in_, kw), key))

    def final_wait_all(self, eng='sp'):
        waits = []
        for q, (ns, cnt) in self.dmaq.items():
            for slot in range(min(ns, cnt)):
                nuse = (cnt - slot + ns - 1) // ns
                waits.append(('d_%s_%d' % (q, slot), 16 * nuse))
        for e in self.COMPUTE:
            if self.nseq[e] > 0:
                waits.append(('c_' + e, self.nseq[e]))
        self.ops[eng].append(('w', waits, None, None))

    def build(self):
        nc = self.nc
        keys = set()
        for e, lst in self.ops.items():
            for kind, waits, payload, key in lst:
                if key:
                    keys.add(key)
                for k, v in waits:
                    keys.add(k)
        with contextlib.ExitStack() as st:
            for k in sorted(keys):
                self.sems[k] = st.enter_context(nc.semaphore(k))
            block = st.enter_context(nc.Block())
            sems = self.sems

            def runner(lst):
                def run(e):
                    for kind, waits, payload, key in lst:
                        for k, v in waits:
                            e.wait_ge(sems[k], v)
                        if kind == 'c':
                            payload(e).then_inc(sems[key], 1)
                        elif kind == 'd':
                            out, in_, kw = payload
                            e.dma_start(out=out, in_=in_, **kw).then_inc(sems[key], 16)
                return run

            block.tensor(runner(self.ops['pe']))
            block.scalar(runner(self.ops['act']))
            block.vector(runner(self.ops['dve']))
            block.gpsimd(runner(self.ops['pool']))
            block.sync(runner(self.ops['sp']))


class Arena:
    def __init__(self, nc, name, nbytes):
        self.t = nc.alloc_sbuf_tensor(name, [128, nbytes // 4], F32)
        self.cap = nbytes
        self.off = 0
        self.peak = 0

    def alloc(self, shape, dt):
        shape = tuple(int(s) for s in shape)
        n = int(np.prod(shape))
        nb = (n * _ESZ[dt] + 63) // 64 * 64
        off = self.off
        self.off += nb
        self.peak = max(self.peak, self.off)
        assert self.off <= self.cap, ("SBUF arena overflow", self.off, self.cap)
        ap = self.t[:, off // 4:(off + nb) // 4]
        if dt != F32:
            ap = ap.bitcast(dt)
        ap = ap[:, 0:n]
        if len(shape) > 1:
            names = ['a%d' % i for i in range(len(shape))]
            pat = "p (%s) -> p %s" % (' '.join(names), ' '.join(names))
            ap = ap.rearrange(pat, **{nm: s for nm, s in zip(names, shape)})
        return ap

    def mark(self):
        return self.off

    def release(self, m):
        self.off = m


def build_program(T=4096, debug=False, stages='ABC'):
    RT = 256
    NT = T // 512
    nc = bass.Bass("TRN2", target_bir_lowering=False)
    P = Prog(nc)
    KEEP = min(WBUF, T)

    def din(name, shape, dt=F32):
        return nc.dram_tensor(name, list(shape), dt, kind="ExternalInput").ap()

    def dout(name, shape, dt=F32):
        return nc.dram_tensor(name, list(shape), dt, kind="ExternalOutput").ap()

    def dscr(name, shape, dt=BF16):
        return nc.dram_tensor(name, list(shape), dt, kind="Internal").ap()

    x_p = din("x_prompt", [T, D])
    x_s = din("x_sample", [4, 8, D])
    st_shift = din("state_rwkv_shift", [4, D])
    st_wkv = din("state_rwkv_wkv", [4, 8, 64, 64])
    ck = din("cache_att_k", [4, WBUF, 512])
    cv = din("cache_att_v", [4, WBUF, 512])
    st_conv = din("state_ffn_conv", [4, 2, 2 * DFF])
    n_mix_pre = din("norm_mix_pre", [1, D])
    n_mix_post = din("norm_mix_post", [1, D])
    n_ffn_pre = din("norm_ffn_pre", [1, D])
    n_ffn_post = din("norm_ffn_post", [1, D])
    w_in = din("w_in", [D, D_IN])
    mu_shift = din("mu_shift", [1, NCOL_R])
    w0_d = din("w0", [RW, 1])
    w_decay_up = din("w_decay_up", [64, RW])
    a0_d = din("a0", [RW, 1])
    w_iclr_up = din("w_iclr_up", [64, RW])
    w_gate_up = din("w_gate_up", [128, RW])
    k_k_d = din("k_k", [RW, 1])
    k_a_d = din("k_a", [RW, 1])
    r_k_d = din("r_k", [RW, 1])
    lnx_w_d = din("lnx_w", [RW, 1])
    lnx_b_d = din("lnx_b", [RW, 1])
    w_out = din("w_out", [D, D])
    w_up = din("w_ffn_up", [D, 2 * DFF])
    cw_d = din("ffn_conv_w", [3, 2 * DFF])
    cb_d = din("ffn_conv_b", [2 * DFF, 1])
    w_dn = din("w_ffn_down", [DFF, D])
    cos_p = din("cos_p", [128, T])
    sin_p = din("sin_p", [128, T])
    cos_s = din("cos_s", [128, 64])
    sin_s = din("sin_s", [128, 64])
    cost_p = din("cost_p", [KEEP, 32])
    sint_p = din("sint_p", [KEEP, 32])
    cost_s = din("cost_s", [128, 32])
    sint_s = din("sint_s", [128, 32])

    lb_s = din("lb_s", [8, 17 * 128])

    y_p = dout("y_prompt", [T, D])
    y_s = dout("y_sample", [4, 8, D])
    o_pshift = dout("p_shift", [1, D])
    o_pwkv = dout("p_wkv", [8, 64, 64])
    o_pk = dout("p_att_k", [KEEP, 512])
    o_pv = dout("p_att_v", [KEEP, 512])
    o_pconv = dout("p_conv", [2, 2 * DFF])
    o_sshift = dout("s_shift", [4, D])
    o_swkv = dout("s_wkv", [4, 8, 64, 64])
    o_sk = dout("s_att_k", [4, WBUF, 512])
    o_sv = dout("s_att_v", [4, WBUF, 512])
    o_sconv = dout("s_conv", [4, 2, 2 * DFF])
    dbg = {}

    n_mixp, n_mixs, n_vs = D * T, D * 256, T * 512
    SCR = dscr("SCR", [n_mixp + n_mixs + n_vs + D * D])
    mixp_s = SCR[0:n_mixp].rearrange("(f t) -> f t", t=T)
    mixs_s = SCR[n_mixp:n_mixp + n_mixs].rearrange("(f t) -> f t", t=256)
    vs_s = SCR[n_mixp + n_mixs:n_mixp + n_mixs + n_vs].rearrange("(t c) -> t c", c=512)
    wout_s = SCR[n_mixp + n_mixs + n_vs:n_mixp + n_mixs + n_vs + D * D].rearrange("(k c) -> k c", c=D)

    SB = Arena(nc, "SB", 212480)
    PS = nc.alloc_psum_tensor("PS", [128, 4096], F32)

    def bank(k, a=0, b=512):
        return PS[:, 512 * k + a:512 * k + b]

    def dma_fc(q, sb_fn, dr_row, write):
        for c0 in range(0, 44, 8):
            c1 = min(44, c0 + 8)
            d_ = dr_row[128 * c0:128 * c1].rearrange("(c p) -> p c", p=128)
            if write:
                P.dma(q, d_, sb_fn(c0, c1), allow_slow_non_contiguous=True)
            else:
                P.dma(q, sb_fn(c0, c1), d_, allow_slow_non_contiguous=True)

    def sl(start, n, step):
        return slice(start, start + step * (n - 1) + 1, step)

    def tt(eng, out, in0, in1, op):
        P.op(eng, lambda e: e.tensor_tensor(out=out, in0=in0, in1=in1, op=op), [out], [in0, in1])

    def ts(eng, out, in0, s1, op0, s2=None, op1=None):
        ins = [in0] + [s for s in (s1, s2) if not isinstance(s, (int, float, type(None)))]
        if op1 is None:
            P.op(eng, lambda e: e.tensor_scalar(out=out, in0=in0, scalar1=s1, scalar2=None, op0=op0), [out], ins)
        else:
            P.op(eng, lambda e: e.tensor_scalar(out=out, in0=in0, scalar1=s1, scalar2=s2, op0=op0, op1=op1), [out], ins)

    def stt(out, in0, scalar, in1, op0, op1, accum=None):
        ins = [in0, in1] + ([scalar] if not isinstance(scalar, (int, float)) else [])
        outs = [out] + ([accum] if accum is not None else [])
        if accum is None:
            P.op('dve', lambda e: e.scalar_tensor_tensor(out=out, in0=in0, scalar=scalar, in1=in1, op0=op0, op1=op1), outs, ins)
        else:
            P.op('dve', lambda e: e.scalar_tensor_tensor(out=out, in0=in0, scalar=scalar, in1=in1, op0=op0, op1=op1,
                                                         accum_out=accum), outs, ins)

    def act(out, in_, func, bias=None, scale=None, accum=None):
        ins = [in_] + [s for s in (bias, scale) if not isinstance(s, (int, float, type(None)))]
        outs = [out] + ([accum] if accum is not None else [])
        kw = {}
        if bias is not None:
            kw['bias'] = bias
        if scale is not None:
            kw['scale'] = scale
        if accum is not None:
            kw['accum_out'] = accum
        P.op('act', lambda e: e.activation(out=out, in_=in_, func=func, **kw), outs, ins)

    def cp(eng, out, in_):
        if eng == 'act':
            P.op('act', lambda e: e.copy(out=out, in_=in_), [out], [in_])
        else:
            P.op(eng, lambda e: e.tensor_copy(out=out, in_=in_), [out], [in_])

    pe_hist = []

    def pe_check(out, lhsT):
        import traceback
        _, p0, p1, f0, f1 = _box(lhsT)
        rg = set(range(p0 // 32, (p1 + 31) // 32))
        _, _, _, b0, b1 = _box(out)
        bk = set(range(b0 // 2048, (b1 + 2047) // 2048))
        for (rg2, bk2, where) in pe_hist[-24:]:
            if not (rg & rg2) and (bk & bk2):
                print("PE ROW-GROUP/BANK HAZARD:", sorted(rg), sorted(bk), "vs", sorted(rg2), sorted(bk2), where,
                      "|", traceback.extract_stack(limit=4)[0].lineno)
        pe_hist.append((rg, bk, traceback.extract_stack(limit=4)[0].lineno))

    def mm(out, lhsT, rhs, start=True, stop=True, tp=None):
        pe_check(out, lhsT)
        if tp is None:
            P.op('pe', lambda e: e.matmul(out, lhsT=lhsT, rhs=rhs, start=start, stop=stop), [out], [lhsT, rhs])
        else:
            P.op('pe', lambda e: e.matmul(out, lhsT=lhsT, rhs=rhs, start=start, stop=stop, tile_position=tp), [out], [lhsT, rhs])

    def tr(out, in_, ident):
        pe_check(out, in_)
        P.op('pe', lambda e: e.transpose(out, in_, ident), [out], [in_, ident])

    def memset(eng, out, val):
        P.op(eng, lambda e: e.memset(out, val), [out], [])

    def asel(out, pattern, base, cm, cmp, fill=0.0):
        P.op('pool', lambda e: e.affine_select(out=out, in_=out, pattern=pattern, compare_op=cmp, fill=fill,
                                               base=base, channel_multiplier=cm), [out], [out])

    rr = {'n': 0}

    def rot(engs):
        rr['n'] += 1
        return engs[rr['n'] % len(engs)]

    if 'S' in stages:
        for b_ in range(4):
            for (src_, dst_) in ((ck, o_sk), (cv, o_sv)):
                for c_ in range(4):
                    P.dma('act', dst_[b_, 510 * c_:510 * (c_ + 1), :], src_[b_, 8 + 510 * c_:8 + 510 * (c_ + 1), :])

    IDF = SB.alloc((128,), F32)
    IDB = SB.alloc((128,), BF16)
    BDO = SB.alloc((128,), F32)
    BDM = SB.alloc((128,), F32)
    MSU = SB.alloc((128,), F32)
    MIU = SB.alloc((128,), F32)
    MSL = SB.alloc((128,), F32)
    MAB = SB.alloc((2, 128), F32)
    PERM = SB.alloc((128,), BF16)
    PERMF = SB.alloc((128,), F32)
    MASK2 = SB.alloc((256,), BF16)
    ONESB = SB.alloc((64,), BF16)
    PAR = SB.alloc((4, 8), F32)
    m0 = SB.mark()
    HT = SB.alloc((8, T), BF16)
    HTS = SB.alloc((8, 4, 66), BF16)
    mA = SB.mark()
    RESET = SB.alloc((512,), F32)
    tmpc = SB.alloc((256,), F32)

    memset('pool', IDF, 1.0)
    asel(IDF, [[-1, 128]], 0, 1, ALU.is_equal)
    cp('dve', IDB, IDF)
    memset('pool', BDO, 1.0)
    memset('pool', BDO[0:64, 64:128], 0.0)
    memset('pool', BDO[64:128, 0:64], 0.0)
    ts('dve', BDM, BDO, 1.0 / 64, ALU.mult)
    memset('pool', MSU, 1.0)
    asel(MSU, [[1, 128]], 0, -1, ALU.is_gt)
    tt('dve', MSU, MSU, BDO, ALU.mult)
    memset('pool', MIU, 1.0)
    asel(MIU, [[1, 128]], 0, -1, ALU.is_ge)
    tt('dve', MIU, MIU, BDO, ALU.mult)
    memset('pool', MSL, 1.0)
    asel(MSL, [[-1, 128]], 0, 1, ALU.is_gt)
    tt('dve', MSL, MSL, BDO, ALU.mult)
    cp('dve', MAB[:, 0, :], MSU)
    cp('dve', MAB[:, 1, :], MIU)
    memset('pool', tmpc[:, 0:128], 1.0)
    asel(tmpc[:, 0:128], [[-1, 128]], -32, 1, ALU.is_equal)
    memset('pool', tmpc[:, 32:64], 0.0)
    memset('pool', tmpc[:, 96:128], 0.0)
    memset('pool', tmpc[:, 128:256], 1.0)
    asel(tmpc[:, 128:256], [[-1, 128]], 32, 1, ALU.is_equal)
    memset('pool', tmpc[:, 128:160], 0.0)
    memset('pool', tmpc[:, 192:224], 0.0)
    tt('dve', PERM, tmpc[:, 0:128], tmpc[:, 128:256], ALU.add)
    tt('dve', PERMF, tmpc[:, 0:128], tmpc[:, 128:256], ALU.add)
    memset('pool', tmpc[:, 0:256], 1.0)
    asel(tmpc[:, 0:128], [[1, 128]], 0, -1, ALU.is_ge)
    asel(tmpc[:, 128:256], [[-1, 128]], 0, 1, ALU.is_ge)
    cp('dve', MASK2, tmpc[:, 0:256])
    memset('pool', RESET, 1.0)
    for c in range(8):
        memset('pool', RESET[:, 64 * c:64 * c + 1], 0.0)
    memset('pool', ONESB, 1.0)

    P_W0, P_A0, P_KK, P_KA, P_RK, P_LW, P_LB, P_OMKA = range(8)
    for i, d_ in enumerate((w0_d, a0_d, k_k_d, k_a_d, r_k_d, lnx_w_d, lnx_b_d)):
        for fc in range(4):
            P.dma('sp', PAR[:, fc, i:i + 1], d_[128 * fc:128 * (fc + 1), :])
    ts('dve', PAR[:, :, P_OMKA], PAR[:, :, P_KA], -1.0, ALU.mult, 1.0, ALU.add)
    WDU = SB.alloc((RW,), F32)
    WIU = SB.alloc((RW,), F32)
    WGU = SB.alloc((RW,), F32)
    P.dma('sp', WDU[0:64, :], w_decay_up[:, :])
    P.dma('sp', WIU[64:128, :], w_iclr_up[:, :])
    P.dma('sp', WGU[:, :], w_gate_up[:, :])
    GPRE = SB.alloc((D,), F32)
    P.dma('sp', GPRE, n_mix_pre[0:1, :].partition_broadcast(128))

    WSTh = {'b': None}
    wi = [0]

    def alloc_wst():
        WSTh['b'] = [SB.alloc((1024,), F32) for _ in range(3)]

    def load_w(dst, src, ncol, scale_pairs=None):
        i = wi[0] % 3
        wi[0] += 1
        WST = WSTh['b']
        P.dma('sp', WST[i][:, 0:ncol], src)
        if scale_pairs is None:
            cp(rot(['act', 'dve', 'pool']), dst, WST[i][:, 0:ncol])
        else:
            for (d_, sc), eng in zip(scale_pairs, ('dve', 'pool')):
                tt(eng, d_, WST[i][:, 0:ncol], sc, ALU.mult)

    W12 = SB.alloc((8, 2, NCOL_R), BF16)
    mW = SB.mark()
    alloc_wst()
    MU = SB.alloc((NCOL_R,), F32)
    OMU = SB.alloc((NCOL_R,), F32)
    P.dma('sp', MU, mu_shift[0:1, :].partition_broadcast(128))
    ts('dve', OMU, MU, -1.0, ALU.mult, 1.0, ALU.add)
    for kc in (range(8) if 'x' not in stages else []):
        rows = slice(128 * kc, 128 * (kc + 1))
        for c0, c1 in ((0, 1024), (1024, NCOL_R)):
            load_w(None, w_in[rows, c0:c1], c1 - c0,
                   scale_pairs=[(W12[:, kc, 0, c0:c1], OMU[:, c0:c1]), (W12[:, kc, 1, c0:c1], MU[:, c0:c1])])
    SB.release(mW)
    XT = [SB.alloc((D,), F32) for _ in range(2)]
    HB = [SB.alloc((D,), BF16) for _ in range(2)]
    STAT = SB.alloc((8, 4), F32)
    EPSC = SB.alloc((2,), F32)
    memset('pool', EPSC[:, 0:1], NORM_EPS)
    memset('pool', EPSC[:, 1:2], LNX_EPS)
    sti = [0]

    def norm_tile(x_rows_fn, dst_fn, want_f32_row=None, hprev_fn=None):
        i = sti[0] % 2
        si = sti[0] % 8
        sti[0] += 1
        xt, hb = XT[i], HB[i]
        x_rows_fn(xt)
        act(hb, xt, AF.Square, accum=STAT[:, si, 0:1])
        act(STAT[:, si, 1:2], STAT[:, si, 0:1], AF.Sqrt, bias=EPSC[:, 0:1], scale=1.0 / D)
        P.op('dve', lambda e: e.reciprocal(out=STAT[:, si, 2:3], in_=STAT[:, si, 1:2]), [STAT[:, si, 2:3]], [STAT[:, si, 1:2]])
        stt(hb, xt, STAT[:, si, 2:3], GPRE, ALU.mult, ALU.mult)
        if want_f32_row is not None:
            stt(xt, xt, STAT[:, si, 2:3], GPRE, ALU.mult, ALU.mult)
            want_f32_row(xt)
        if hprev_fn is not None:
            hprev_fn(hb)
        pb = bank(2 + i).bitcast(BF16)
        for kc in range(8):
            tr(pb[:, 128 * kc:128 * (kc + 1)], hb[:, 128 * kc:128 * (kc + 1)], IDB)
        return pb

    RB = {}
    for nm in ('r', 'k', 'v', 'sgw', 'a', 'g', 'cs', 'csx', 'crem', 'ein', 'eneg', 'kk', 't1', 'kkn',
               'kmod', 'bb', 'bonv', 'YT'):
        RB[nm] = SB.alloc((RT,), F32)
    BTB = SB.alloc((RT,), BF16)
    KTB = SB.alloc((RT,), BF16)
    RB['eex'] = RB['csx']
    RB['erem'] = RB['crem']
    RB['BP'] = RB['bb']
    RB['KP'] = RB['kmod']
    AR = SB.alloc((RT // 128, 2, 128), F32)
    TW = SB.alloc((RT,), F32)
    AL = SB.alloc((RT,), F32)
    SG = SB.alloc((RT,), F32)
    NSTM = RT // 128
    TOKB = SB.alloc((NSTM, 4, 128), F32)
    TOK = [TOKB[:, i_, :, :] for i_ in range(NSTM)]
    TOKH = [RB[nm_].bitcast(BF16).rearrange("p (q c) -> p q c", c=128) for nm_ in ('eneg', 'kk')]
    ARB = RB['kkn'].bitcast(BF16).rearrange("p (s a t) -> p s a t", a=2, t=128)
    NM = [SB.alloc((2, 2, 128), BF16) for _ in range(NSTM)]
    AK = [SB.alloc((2, 2, 128), BF16) for _ in range(NSTM)]
    XP = [[SB.alloc((2, 2, 128), BF16) for _ in range(2)] for _ in range(NSTM)]
    YY = [[SB.alloc((2, 128), BF16) for _ in range(2)] for _ in range(NSTM)]
    Z1 = [SB.alloc((2, 64), BF16) for _ in range(NSTM)]
    UW = [SB.alloc((2, 2, 64), BF16) for _ in range(NSTM)]
    WTs = [SB.alloc((128,), F32) for _ in range(NSTM)]
    GT = [SB.alloc((128,), F32) for _ in range(NSTM)]
    HTt = [SB.alloc((128,), F32) for _ in range(NSTM)]
    US = SB.alloc((2, 64), F32)
    SST = SB.alloc((5, 4, 64), F32)
    ROB = RB['sgw'].bitcast(BF16)[:, 0:RT]
    HTP = SB.alloc((8, RT), BF16)
    SNAT = TOK[0].rearrange("p q c -> p (q c)").rearrange("p (h j) -> p h j", j=64)

    def rwkv_fc(fc, ntok, ps_r, ps_k, ps_v, state_of_chunk, CV, out_dst):
        nst = ntok // 128
        n = ntok
        par = lambda i: PAR[:, fc, i:i + 1]
        B = {k_: v_[:, 0:n] for k_, v_ in RB.items()}
        cp('act', B['r'], ps_r)
        cp('act', B['k'], ps_k)
        cp('act', B['v'], ps_v)
        if RPRE < 1:
            return
        cols = slice(128 * fc, 128 * (fc + 1))
        pz = bank(1)
        mm(bank(4)[:, 0:n], WDU[0:64, cols], TW[0:64, 0:n])
        act(B['sgw'], bank(4)[:, 0:n], AF.Sigmoid, bias=par(P_W0))
        mm(bank(5)[:, 0:n], WIU[64:128, cols], AL[64:128, 0:n])
        act(B['a'], bank(5)[:, 0:n], AF.Sigmoid, bias=par(P_A0))
        mm(pz[:, 0:n], WGU[:, cols], SG[:, 0:n])
        cp('act', B['g'], pz[:, 0:n])
        if RPRE < 2:
            return
        P.op('dve', lambda e: e.tensor_tensor_scan(out=B['cs'], data0=RESET[:, 0:n], data1=B['sgw'], initial=0.0,
                                                   op0=ALU.mult, op1=ALU.add), [B['cs']], [RESET[:, 0:n], B['sgw']])
        if RPRE < 3:
            return
        tt('pool', B['csx'], B['cs'], B['sgw'], ALU.subtract)
        nch = n // 64
        cs3 = B['cs'].rearrange("p (c t) -> p c t", t=64)
        tt('dve', B['crem'].rearrange("p (c t) -> p c t", t=64), cs3[:, :, CV - 1:CV].to_broadcast([128, nch, 64]), cs3,
           ALU.subtract)
        if RPRE < 4:
            return
        act(B['ein'], B['cs'], AF.Exp, scale=-C0)
        act(B['eex'], B['csx'], AF.Exp, scale=-C0)
        act(B['eneg'], B['cs'], AF.Exp, scale=C0)
        act(B['erem'], B['crem'], AF.Exp, scale=-C0)
        if RPRE < 5:
            return
        ts('pool', B['kk'], B['k'], par(P_KK), ALU.mult)
        act(B['t1'], B['kk'], AF.Square)
        mm(pz[:, 0:n], BDO, B['t1'])
        act(B['t1'], pz[:, 0:n], AF.Sqrt)
        ts('dve', B['t1'], B['t1'], 1e-12, ALU.max)
        P.op('dve', lambda e: e.reciprocal(out=B['t1'], in_=B['t1']), [B['t1']], [B['t1']])
        tt('pool', B['kkn'], B['kk'], B['t1'], ALU.mult)
        if RPRE < 6:
            return
        ts('dve', B['t1'], B['a'], par(P_KA), ALU.mult, par(P_OMKA), ALU.add)
        tt('pool', B['kmod'], B['k'], B['t1'], ALU.mult)
        tt('pool', B['bb'], B['kkn'], B['a'], ALU.mult)
        if RPRE < 7:
            return
        ar = AR[:, 0:nst, :, :]
        stt(ar[:, :, 0, :], B['kkn'].rearrange("p (s t) -> p s t", t=128), -1.0,
            B['eex'].rearrange("p (s t) -> p s t", t=128), ALU.mult, ALU.mult)
        tt('pool', ar[:, :, 1, :], B['r'].rearrange("p (s t) -> p s t", t=128),
           B['ein'].rearrange("p (s t) -> p s t", t=128), ALU.mult)
        if RPRE < 8:
            return
        stt(B['t1'], B['r'], par(P_RK), B['kmod'], ALU.mult, ALU.mult)
        mm(pz[:, 0:n], BDO, B['t1'])
        tt('dve', B['bonv'], pz[:, 0:n], B['v'], ALU.mult)
        tt('dve', BTB[:, 0:n], B['bb'], B['eneg'], ALU.mult)
        tt('pool', KTB[:, 0:n], B['kmod'], B['eneg'], ALU.mult)
        tt('dve', B['BP'], B['bb'], B['erem'], ALU.mult)
        tt('pool', B['KP'], B['kmod'], B['erem'], ALU.mult)

        cp('pool', ARB[:, 0:nst, :, :], AR[:, 0:nst, :, :])
        for st in range(nst):
            tk = slice(128 * st, 128 * (st + 1))
            tok, tokb = TOK[st], TOKH[st]
            pt = bank(3)
            for qi, src in enumerate((AR[:, st, 0, :], B['BP'][:, tk], B['KP'][:, tk], B['v'][:, tk])):
                tr(pt[:, 128 * qi:128 * (qi + 1)], src, IDF)
            cp('act', tok, pt.rearrange("p (q c) -> p q c", c=128))
            cp('dve', tokb, pt.rearrange("p (q c) -> p q c", c=128))
            for pi in range(2):
                hp = slice(64 * pi, 64 * (pi + 1))
                rhs_ar = ARB[hp, st, :, :]
                mm(bank(4 + pi)[:, 0:256], BTB[hp, tk], rhs_ar)
                mm(bank(4 + pi)[:, 256:512], KTB[hp, tk], rhs_ar)
                mm(bank(6 + pi)[:, 0:128], ARB[hp, st, 0, :], BTB[hp, tk])
            for pi in range(2):
                tt('dve', NM[st][:, pi, :, :], bank(4 + pi)[:, 0:256].rearrange("p (a t) -> p a t", t=128), MAB, ALU.mult)
                tt('dve', AK[st][:, pi, :, :], bank(4 + pi)[:, 256:512].rearrange("p (a t) -> p a t", t=128), MAB, ALU.mult)
                tt('dve', YY[st][0][:, pi, :], bank(6 + pi)[:, 0:128], MSL, ALU.mult)
                cp('act', XP[st][0][:, pi, 0, :], NM[st][:, pi, 0, :])
                tt('pool', XP[st][0][:, pi, 1, :], NM[st][:, pi, 0, :], IDF, ALU.add)
        for kstep in range(6):
            for st in range(nst):
                cur = kstep % 2
                xp, yy = XP[st][cur], YY[st][cur]
                xpn, yyn = XP[st][1 - cur], YY[st][1 - cur]
                p1, p2 = (bank(1), bank(2)) if st % 2 == 0 else (bank(3), bank(0))
                if kstep == 0:
                    for pi in range(2):
                        mm(p1[:, 256 * pi:256 * pi + 128], yy[:, pi, :], xp[:, pi, 0, :])
                        mm(p2[:, 128 * pi:128 * (pi + 1)], xp[:, pi, 0, :], yy[:, pi, :])
                    cp('act', xpn[:, :, 0, :], p1.rearrange("p (h a t) -> p h a t", a=2, t=128)[:, :, 0, :])
                    cp('pool', xpn[:, :, 1, :], xp[:, :, 1, :])
                    cp('dve', yyn, p2[:, 0:256].rearrange("p (h t) -> p h t", t=128))
                elif kstep < 5:
                    for pi in range(2):
                        mm(p1[:, 256 * pi:256 * (pi + 1)], yy[:, pi, :], xp[:, pi, :, :])
                        mm(p2[:, 128 * pi:128 * (pi + 1)], xp[:, pi, 0, :], yy[:, pi, :])
                    p1v = p1.rearrange("p (h a t) -> p h a t", a=2, t=128)
                    cp('act', xpn[:, :, 0, :], p1v[:, :, 0, :])
                    tt('dve', xpn[:, :, 1, :], p1v[:, :, 1, :], xp[:, :, 1, :], ALU.add)
                    cp('act', yyn, p2[:, 0:256].rearrange("p (h t) -> p h t", t=128))
                else:
                    for pi in range(2):
                        mm(p1[:, 256 * pi + 128:256 * (pi + 1)], yy[:, pi, :], xp[:, pi, 1, :])
                    p1v = p1.rearrange("p (h a t) -> p h a t", a=2, t=128)
                    tt('dve', xpn[:, :, 1, :], p1v[:, :, 1, :], xp[:, :, 1, :], ALU.add)
        for st in range(nst):
            tok, tokb = TOK[st], TOKH[st]
            PT = XP[st][0]
            pm = bank(7)
            for pi in range(2):
                hc = slice(64 * pi, 64 * (pi + 1))
                mm(pm[:, 64 * pi:64 * (pi + 1)], AK[st][:, pi, 0, :], tokb[:, 3, hc])
            cp('act', Z1[st], pm[:, 0:128].rearrange("p (h i) -> p h i", i=64))
            for pi in range(2):
                hc = slice(64 * pi, 64 * (pi + 1))
                mm(pm[:, 128 + 128 * pi:128 + 128 * pi + 64], PT[:, pi, 1, :], Z1[st][:, pi, :])
                mm(pm[:, 128 + 128 * pi + 64:128 + 128 * (pi + 1)], PT[:, pi, 1, :], tokb[:, 0, hc])
            cp('dve', UW[st], pm[:, 128:384].rearrange("p (h a i) -> p h a i", a=2, i=64))
            for pi in range(2):
                hc = slice(64 * pi, 64 * (pi + 1))
                mm(pm[hc, 384:512], tokb[:, 0, hc], PT[:, pi, 1, :], tp=(0, 64 * pi))
            cp('act', WTs[st], pm[:, 384:512])
            pg = bank(3)
            for pi in range(2):
                hc = slice(64 * pi, 64 * (pi + 1))
                mm(pg[hc, 0:128], UW[st][:, pi, 1, :], NM[st][:, pi, 1, :], tp=(0, 64 * pi))
            tt('dve', GT[st], pg[:, 0:128], AR[:, st, 1, :], ALU.add)
            for pi in range(2):
                hc = slice(64 * pi, 64 * (pi + 1))
                mm(pg[hc, 128:256], UW[st][:, pi, 0, :], NM[st][:, pi, 1, :], start=True, stop=False, tp=(0, 64 * pi))
                mm(pg[hc, 128:256], tokb[:, 3, hc], AK[st][:, pi, 1, :], start=False, stop=True, tp=(0, 64 * pi))
            cp('act', HTt[st], pg[:, 128:256])
        for st in range(nst):
            tok = TOK[st]
            for q in range(2):
                S, chain = state_of_chunk(st, q)
                tq = slice(64 * q, 64 * (q + 1))
                tv = slice(64 * q, 64 * q + CV)
                for pi in range(2):
                    hc = slice(64 * pi, 64 * (pi + 1))
                    mm(bank(6 + pi)[tq, 128:192], WTs[st][hc, tq], S[hc, :], tp=(64 * pi, 64 * q))
                for pi in range(2):
                    tt('dve', US[tq, pi, :], bank(6 + pi)[tq, 128:192], UW[st][tq, pi, 0, :], ALU.add)
                for pi in range(2):
                    hc = slice(64 * pi, 64 * (pi + 1))
                    mm(bank(6 + pi)[hc, 192:256], S[hc, :], GT[st][hc, tq], tp=(64 * pi, 64 * pi))
                for pi in range(2):
                    hc = slice(64 * pi, 64 * (pi + 1))
                    tt('dve', B['YT'][hc, 128 * st + 64 * q:128 * st + 64 * (q + 1)], bank(6 + pi)[hc, 192:256], HTt[st][hc, tq],
                       ALU.add)
                psq = bank(6 + q)[:, 256:320]
                for pi in range(2):
                    hc = slice(64 * pi, 64 * (pi + 1))
                    mm(psq[hc, :], tok[tv, 1, hc], US[tv, pi, :], start=True, stop=False, tp=(64 * q, 64 * pi))
                    mm(psq[hc, :], tok[tv, 2, hc], tok[tv, 3, hc], start=False, stop=True, tp=(64 * q, 64 * pi))
                pcol = B['ein'][:, 128 * st + 64 * q + CV - 1:128 * st + 64 * q + CV]
                stt(S, S, pcol, psq, ALU.mult, ALU.add)
        pz = bank(1)
        mm(pz[:, 0:n], BDM, B['YT'])
        tt('dve', B['t1'], B['YT'], pz[:, 0:n], ALU.subtract)
        act(B['kk'], B['t1'], AF.Square)
        mm(pz[:, 0:n], BDM, B['kk'])
        act(B['kk'], pz[:, 0:n], AF.Sqrt, bias=EPSC[:, 1:2])
        P.op('dve', lambda e: e.reciprocal(out=B['kk'], in_=B['kk']), [B['kk']], [B['kk']])
        tt('pool', B['t1'], B['t1'], B['kk'], ALU.mult)
        ts('dve', B['t1'], B['t1'], par(P_LW), ALU.mult, par(P_LB), ALU.add)
        tt('pool', B['t1'], B['t1'], B['bonv'], ALU.add)
        tt('dve', ROB[:, 0:n], B['t1'], B['g'], ALU.mult)
        out_dst(ROB[:, 0:n])

    def rwkv_tile(ntok, cur_ap, prv_ap, state_of_chunk, CV, out_dst_fc):
        def proj(ps, col0):
            for kc in range(8):
                mm(ps, W12[:, kc, 0, col0:col0 + 128], cur_ap(kc), start=(kc == 0), stop=False)
                mm(ps, W12[:, kc, 1, col0:col0 + 128], prv_ap(kc), start=False, stop=(kc == 7))
        n = ntok
        if RPRE < -1:
            return
        pw = bank(0)
        proj(pw[:, 0:n], 1536)
        if RPRE < 0:
            return
        act(TW[0:64, 0:n], pw[0:64, 0:n], AF.Tanh)
        cp('dve', AL[64:128, 0:n], pw[64:128, 0:n])
        proj(pw[:, 0:n], 1664)
        act(SG[:, 0:n], pw[:, 0:n], AF.Sigmoid)
        for fc in range(4):
            pr, pk, pv = bank(0), bank(1), bank(2)
            proj(pr[:, 0:n], 128 * fc)
            proj(pk[:, 0:n], 512 + 128 * fc)
            proj(pv[:, 0:n], 1024 + 128 * fc)
            rwkv_fc(fc, n, pr[:, 0:n], pk[:, 0:n], pv[:, 0:n], lambda st, q, fc=fc: state_of_chunk(fc, st, q), CV,
                    lambda rob, fc=fc: out_dst_fc(fc, rob))

    for fc in range(4):
        memset('pool', SST[:, 0, fc, :], 0.0)

    for it in range(NT):
        for st in range(4):
            t0 = 512 * it + 128 * st
            last = (t0 + 128 == T)

            def ld(xt, t0=t0):
                P.dma('sp', xt, x_p[t0:t0 + 128, :])

            def f32row(hf):
                P.dma('pool', o_pshift[0:1, :], hf[127:128, :])

            pb = norm_tile(ld, None, want_f32_row=f32row if last else None)
            cp(rot(['act', 'dve']), HT[:, :, t0:t0 + 128], pb.rearrange("p (k t) -> p k t", t=128))
        for tb in (range(512 * it, 512 * (it + 1), RT) if 'R' in stages else []):
            if 'y' in stages:
                pass
            elif tb == 0:
                memset('pool', HTP[:, :, 0:1], 0.0)
                cp('dve', HTP[:, :, 1:RT], HT[:, :, 0:RT - 1])
            else:
                cp('dve', HTP, HT[:, :, tb - 1:tb - 1 + RT])
            rwkv_tile(RT,
                      lambda kc, tb=tb: HT[:, kc, tb:tb + RT],
                      lambda kc: HTP[:, kc, :],
                      lambda fc, st, q: (SST[:, 0, fc, :], True), 64,
                      lambda fc, rob, tb=tb: P.dma('pool', mixp_s[128 * fc:128 * (fc + 1), tb:tb + RT], rob))
    if 'v' in stages:
        memset('pool', SNAT, 1.0)
    for fc in (range(4) if ('R' in stages and 'z' not in stages and 'v' not in stages) else []):
        for pi in range(2):
            tr(bank(4 + pi)[0:64, 0:64], SST[64 * pi:64 * (pi + 1), 0, fc, :], IDF[64 * pi:64 * (pi + 1), 64 * pi:64 * (pi + 1)])
            cp('act', SNAT[0:64, 2 * fc + pi, :], bank(4 + pi)[0:64, 0:64])
    if 'R' in stages and 'z' not in stages and 'u' not in stages:
        P.dma('sp', o_pwkv.rearrange("h i j -> i h j"), SNAT[0:64, :, :])


    HTC = XT[0][:, 0:512].bitcast(BF16).rearrange("p (k t) -> p k t", t=128)
    HPF = TOKB.rearrange("p a q c -> p (a q c)")
    if 'S' in stages:
        memset('pool', HTS, 0.0)
        for tile in range(2):
            memset('pool', HPF, 0.0)
            for q in range(2):
                P.dma('sp', HPF[64 * q + 63:64 * q + 64, :], st_shift[2 * tile + q:2 * tile + q + 1, :])

            def ld(xt, tile=tile):
                memset('pool', xt, 0.0)
                for q in range(2):
                    P.dma('sp', xt[64 * q:64 * q + 8, :], x_s[2 * tile + q, :, :])

            def f32row(hf, tile=tile):
                for q in range(2):
                    P.dma('pool', o_sshift[2 * tile + q:2 * tile + q + 1, :], hf[64 * q + 7:64 * q + 8, :])

            def hprev(hb):
                tt('dve', hb, hb, HPF, ALU.add)

            pb = norm_tile(ld, None, want_f32_row=f32row, hprev_fn=hprev)
            pb4 = pb.rearrange("p (k q j) -> p k q j", q=2, j=64)
            cp('act', HTS[:, :, 2 * tile:2 * tile + 2, 2:65], pb4[:, :, :, 0:63])
            cp('dve', HTS[:, :, 2 * tile:2 * tile + 2, 1:2], pb4[:, :, :, 63:64])
        for tile in (range(2) if 'R' in stages else []):
            for q in range(2):
                seq = 2 * tile + q
                P.dma('sp', SNAT[0:64, :, :], st_wkv[seq].rearrange("h i j -> i h j"))
                for h_ in range(8):
                    pi, fc = h_ % 2, h_ // 2
                    tr(bank(4)[0:64, 64 * (h_ % 8):64 * (h_ % 8) + 64], SNAT[0:64, h_, :], IDF[0:64, 0:64])
                for h_ in range(8):
                    pi, fc = h_ % 2, h_ // 2
                    cp('act' if pi else 'dve', SST[64 * pi:64 * (pi + 1), 1 + seq, fc, :], bank(4)[0:64, 64 * h_:64 * h_ + 64])
            cp('dve', HTC.rearrange("p k (q j) -> p k q j", j=64), HTS[:, :, 2 * tile:2 * tile + 2, 2:66])
            cp('dve', HTP[:, :, 0:128].rearrange("p k (q j) -> p k q j", j=64), HTS[:, :, 2 * tile:2 * tile + 2, 1:65])
            rwkv_tile(128,
                      lambda kc: HTC[:, kc, :],
                      lambda kc: HTP[:, kc, 0:128],
                      lambda fc, st, q, tile=tile: (SST[:, 1 + 2 * tile + q, fc, :], False), 8,
                      lambda fc, rob, tile=tile: P.dma('pool', mixs_s[128 * fc:128 * (fc + 1), 128 * tile:128 * (tile + 1)], rob))
            for q in range(2):
                seq = 2 * tile + q
                for fc in range(4):
                    for pi in range(2):
                        tr(bank(4 + pi)[0:64, 0:64], SST[64 * pi:64 * (pi + 1), 1 + seq, fc, :],
                           IDF[64 * pi:64 * (pi + 1), 64 * pi:64 * (pi + 1)])
                        cp('act', SNAT[0:64, 2 * fc + pi, :], bank(4 + pi)[0:64, 0:64])
                P.dma('sp', o_swkv[seq].rearrange("h i j -> i h j"), SNAT[0:64, :, :])

    if debug:
        dbg['HT'] = dout("dbg_ht", [128, 8, T], BF16)
        P.dma('pool', dbg['HT'], HT)

    SB.release(mA)

    mB = SB.mark()
    WQKV = SB.alloc((8, 1536), BF16)
    mW = SB.mark()
    alloc_wst()
    for kc in range(8):
        rows = slice(128 * kc, 128 * (kc + 1))
        load_w(WQKV[:, kc, 0:1024], w_in[rows, NCOL_R:NCOL_R + 1024], 1024)
        load_w(WQKV[:, kc, 1024:1536], w_in[rows, NCOL_R + 1024:D_IN], 512)
    SB.release(mW)

    mS0 = SB.mark()
    if 'S' in stages:
        QSF = SB.alloc((4, 256), F32)
        KSF = SB.alloc((4, 256), F32)
        ATTS = SB.alloc((4, 256), BF16)
        KNEW = SB.alloc((2, 512), F32)
        VNEW = SB.alloc((2, 512), F32)
        CSS = SB.alloc((2, 128), F32)
        XBs = SB.alloc((128,), F32)
        SM = SB.alloc((8,), F32)
        ONESF = SB.alloc((64,), F32)
        RD = SB.alloc((8,), F32)
        HTC2 = SB.alloc((8, 128), BF16)
        RT1s = SB.alloc((128,), F32)
        RT2s = SB.alloc((128,), F32)
    mK = SB.mark()
    VB = [SB.alloc((512,), BF16) for _ in range(2)]
    VF = [SB.alloc((512,), F32) for _ in range(2)]
    KF = [SB.alloc((512,), F32) for _ in range(2)]
    KO = [SB.alloc((512,), F32) for _ in range(2)]
    KTMP = [SB.alloc((256,), F32) for _ in range(4)]
    CST = [SB.alloc((2, 32), F32) for _ in range(2)]

    def tokmajor_kv(lhs_ap, n_rows, i, v_bf16_dst, v_f32_dst, k_dst, cos_src, sin_src):
        pv = bank(i % 2)
        for kc in range(8):
            mm(pv, lhs_ap(kc), WQKV[:, kc, 1024:1536], start=(kc == 0), stop=(kc == 7))
        if v_bf16_dst is not None:
            cp('act', VB[i % 2], pv)
            P.dma('pool', v_bf16_dst, VB[i % 2][0:n_rows, :])
        if v_f32_dst is not None:
            cp('dve', VF[i % 2], pv)
            v_f32_dst(VF[i % 2])
        if k_dst is not None:
            pk = bank(2 + i % 2)
            for kc in range(8):
                mm(pk, lhs_ap(kc), WQKV[:, kc, 512:1024], start=(kc == 0), stop=(kc == 7))
            kf, ko, cs_ = KF[i % 2], KO[i % 2], CST[i % 2]
            cp('act', kf, pk)
            P.dma('sp', cs_[:, 0, :], cos_src)
            P.dma('sp', cs_[:, 1, :], sin_src)
            k4 = kf.rearrange("p (h a d) -> p h a d", a=2, d=32)
            o4 = ko.rearrange("p (h a d) -> p h a d", a=2, d=32)
            cb = cs_[:, 0, :].unsqueeze(1).to_broadcast([128, 8, 32])
            sb_ = cs_[:, 1, :].unsqueeze(1).to_broadcast([128, 8, 32])
            t = [x_.rearrange("p (h d) -> p h d", d=32) for x_ in KTMP]
            tt('dve', t[0], k4[:, :, 0, :], cb, ALU.mult)
            tt('pool', t[1], k4[:, :, 1, :], sb_, ALU.mult)
            tt('dve', o4[:, :, 0, :], t[0], t[1], ALU.subtract)
            tt('pool', t[2], k4[:, :, 0, :], sb_, ALU.mult)
            tt('dve', t[3], k4[:, :, 1, :], cb, ALU.mult)
            tt('pool', o4[:, :, 1, :], t[2], t[3], ALU.add)
            k_dst(ko)

    if 'B' in stages:
        for it in range(T // 128):
            t0 = 128 * it
            keep = t0 >= T - KEEP
            r0 = t0 - (T - KEEP)
            tokmajor_kv(lambda kc, t0=t0: HT[:, kc, t0:t0 + 128], 128, it, vs_s[t0:t0 + 128, :],
                        (lambda vf, r0=r0: P.dma('pool', o_pv[r0:r0 + 128, :], vf)) if keep else None,
                        (lambda ko, r0=r0: P.dma('pool', o_pk[r0:r0 + 128, :], ko)) if keep else None,
                        cost_p[r0:r0 + 128, :] if keep else None, sint_p[r0:r0 + 128, :] if keep else None)

    if 'S' in stages:
        memset('pool', ONESF, 1.0)
        memset('pool', ATTS, 0.0)
        for q in range(2):
            P.dma('sp', CSS[:, 0, 64 * q:64 * (q + 1)], cos_s[:, :])
            P.dma('sp', CSS[:, 1, 64 * q:64 * (q + 1)], sin_s[:, :])
        for tile in range(2):
            cp('dve', HTC2.rearrange("p k (q j) -> p k q j", j=64), HTS[:, :, 2 * tile:2 * tile + 2, 2:66])
            tokmajor_kv(lambda kc: HTC2[:, kc, :], 128, tile, None,
                        lambda vf, tile=tile: cp('pool', VNEW[:, tile, :], vf),
                        lambda ko, tile=tile: cp('pool', KNEW[:, tile, :], ko),
                        cost_s[:, :], sint_s[:, :])
            for q in range(2):
                seq = 2 * tile + q
                P.dma('pool', o_sk[seq, 2040:2048, :], KNEW[64 * q:64 * q + 8, tile, :])
                P.dma('pool', o_sv[seq, 2040:2048, :], VNEW[64 * q:64 * q + 8, tile, :])
            for hp in range(4):
                for which, c0_, dstF in ((0, 128 * hp, QSF), (1, 512 + 128 * hp, KSF)):
                    ps = bank(which)
                    for kc in range(8):
                        mm(ps[:, 0:128], WQKV[:, kc, c0_:c0_ + 128], HTC2[:, kc, :], start=(kc == 0), stop=(kc == 7))
                    cp('act', XBs, ps[:, 0:128])
                    pw_ = bank(7)
                    mm(pw_[:, 0:128], PERMF, XBs)
                    tt('pool', RT1s, XBs, CSS[:, 0, :], ALU.mult)
                    tt('dve', RT2s, pw_[:, 0:128], CSS[:, 1, :], ALU.mult)
                    tt('pool', dstF[:, hp, 128 * tile:128 * (tile + 1)], RT1s, RT2s, ALU.add)
    SB.release(mK)
    if 'S' in stages:
        KC = SB.alloc((17, 512), F32)
        KTC = SB.alloc((17 * 128,), F32)
        SSB = SB.alloc((17 * 128,), F32)
        LB = SB.alloc((17 * 128,), F32)
        PTs = SB.alloc((8, 17, 8), F32)
        P.dma('sp', LB[0:8, :], lb_s[:, :])
        for b_ in range(4):
            tile, q = b_ // 2, b_ % 2
            for t8 in range(0, 16, 4):
                P.dma('sp', KC[:, t8:t8 + 4, :], ck[b_, 128 * t8:128 * (t8 + 4), :].rearrange("(t p) c -> p t c", p=128))
            memset('pool', KC[:, 16, :], 0.0)
            cp('dve', KC[0:8, 16, :], KNEW[64 * q:64 * q + 8, tile, :])
            for h_ in range(8):
                pi, hp = h_ % 2, h_ // 2
                hs = slice(64 * pi, 64 * (pi + 1))
                hcol = slice(64 * h_, 64 * (h_ + 1))
                for g in range(5):
                    nt_ = 4 if g < 4 else 1
                    pk = bank(2 * (g % 2))
                    for j in range(nt_):
                        tr(pk[0:64, 128 * j:128 * (j + 1)], KC[:, 4 * g + j, hcol], IDF)
                    cp('act' if g % 2 else 'dve', KTC[hs, 512 * g:512 * g + 128 * nt_], pk[0:64, 0:128 * nt_])
                qv = QSF[hs, hp, 64 * b_:64 * b_ + 8]
                for g in range(5):
                    nn = 512 if g < 4 else 128
                    psc = bank(4 + pi + 2 * (g % 2))
                    mm(psc[0:8, 0:nn], qv, KTC[hs, 512 * g:512 * g + nn])
                    stt(SSB[0:8, 512 * g:512 * g + nn], psc[0:8, 0:nn], 0.125, LB[0:8, 512 * g:512 * g + nn], ALU.mult, ALU.add)
                P.op('dve', lambda e: e.tensor_reduce(out=SM[0:8, 0:1], in_=SSB[0:8, :], axis=AX.X, op=ALU.max),
                     [SM[0:8, 0:1]], [SSB[0:8, :]])
                ts('dve', SM[0:8, 1:2], SM[0:8, 0:1], -1.0, ALU.mult)
                act(SSB[0:8, :], SSB[0:8, :], AF.Exp, bias=SM[0:8, 1:2])
                pt_ = bank(0)
                for c_ in range(17):
                    tr(pt_[:, 8 * c_:8 * (c_ + 1)], SSB[0:8, 128 * c_:128 * (c_ + 1)], IDF[0:8, 0:8])
                cp('act', PTs[:, h_, :, :], pt_[:, 0:136].rearrange("p (c t) -> p c t", t=8))
            for t8 in range(0, 16, 4):
                P.dma('sp', KC[:, t8:t8 + 4, :], cv[b_, 128 * t8:128 * (t8 + 4), :].rearrange("(t p) c -> p t c", p=128))
            memset('pool', KC[:, 16, :], 0.0)
            cp('dve', KC[0:8, 16, :], VNEW[64 * q:64 * q + 8, tile, :])
            for h_ in range(8):
                pi, hp = h_ % 2, h_ // 2
                hs = slice(64 * pi, 64 * (pi + 1))
                hcol = slice(64 * h_, 64 * (h_ + 1))
                pn = bank(1)
                for c_ in range(17):
                    mm(pn[hs, 0:8], KC[:, c_, hcol], PTs[:, h_, c_, :], start=(c_ == 0), stop=(c_ == 16), tp=(0, 64 * pi))
                pd = bank(3)
                for c_ in range(17):
                    mm(pd[hs, 0:8], ONESF, PTs[:, h_, c_, :], start=(c_ == 0), stop=(c_ == 16), tp=(0, 64 * pi))
                cp('act', RD[hs, :], pd[hs, 0:8])
                P.op('dve', lambda e, hs=hs: e.reciprocal(out=RD[hs, :], in_=RD[hs, :]), [RD[hs, :]], [RD[hs, :]])
                tt('dve', ATTS[hs, hp, 64 * b_:64 * b_ + 8], pn[hs, 0:8], RD[hs, :], ALU.mult)
        for hp in range(4):
            P.dma('pool', mixs_s[512 + 128 * hp:512 + 128 * (hp + 1), :], ATTS[:, hp, :])
    SB.release(mS0)

    QT2 = SB.alloc((T, 2), BF16)
    KT2 = SB.alloc((T, 2), BF16)
    NV = {1: T // 128, 4: T // 128, 16: T // 128}
    VD = {d: SB.alloc((T // 128, 128), BF16) for d in (1, 4, 16)}
    NUMA = SB.alloc((2048,), F32)
    DENA = SB.alloc((2048,), F32)
    ATT = SB.alloc((2048,), BF16)
    XB = [SB.alloc((512,), BF16) for _ in range(2)]
    RT1 = [SB.alloc((512,), F32) for _ in range(2)]
    RT2 = [SB.alloc((512,), F32) for _ in range(2)]
    CSF = [SB.alloc((2, 512), F32) for _ in range(2)]
    EB = [SB.alloc((256,), BF16) for _ in range(8)]
    EM = [SB.alloc((256,), BF16) for _ in range(8)]
    PTS = [SB.alloc((2, 128), BF16) for _ in range(8)]
    BIAS = SB.alloc((8,), F32)
    DJ = [SB.alloc((128,), F32) for _ in range(2)]
    uidx = [0]

    def rope_fm(ps, xb, t1, t2, cs_, dst, n):
        cp('act', xb[:, 0:n], ps)
        pw_ = bank(7)
        mm(pw_[:, 0:n], PERM, xb[:, 0:n])
        tt('pool', t1[:, 0:n], xb[:, 0:n], cs_[:, 0, 0:n], ALU.mult)
        tt('dve', t2[:, 0:n], pw_[:, 0:n], cs_[:, 1, 0:n], ALU.mult)
        tt('pool', dst, t1[:, 0:n], t2[:, 0:n], ALU.add)

    for hp in (range(4) if ('B' in stages and BCUT >= 1) else []):
        qc = slice(128 * hp, 128 * (hp + 1))
        kcs = slice(512 + 128 * hp, 512 + 128 * (hp + 1))
        for it in range(NT):
            tb = 512 * it
            cs_ = CSF[it % 2]
            P.dma('sp', cs_[:, 0, :], cos_p[:, tb:tb + 512])
            P.dma('sp', cs_[:, 1, :], sin_p[:, tb:tb + 512])
            for which, cols, dstT in ((0, qc, QT2), (1, kcs, KT2)):
                ps = bank(which)
                for kc in range(8):
                    mm(ps, WQKV[:, kc, cols], HT[:, kc, tb:tb + 512], start=(kc == 0), stop=(kc == 7))
                rope_fm(ps, XB[which], RT1[which], RT2[which], cs_, dstT[:, tb:tb + 512, 0], 512)
        if BCUT < 2:
            continue
        for d in (1, 4, 16):
            npc = T // d // 128
            for r in range(d):
                src = vs_s[r:T:d, 128 * hp:128 * (hp + 1)].rearrange("(mt mi) c -> mi mt c", mi=128)
                for m0_ in range(0, npc, 8):
                    m1_ = min(npc, m0_ + 8)
                    P.dma('sp', VD[d][:, r * npc + m0_:r * npc + m1_, :], src[:, m0_:m1_, :])
        if BCUT < 3:
            continue
        for H in range(T // 2048):
            units = []
            for d in (1, 4, 16):
                npc = T // d // 128
                tph = 16 // d
                for r in range(d):
                    for mt in range(H * tph, (H + 1) * tph):
                        nkb = 1 if mt == 0 else 2
                        q0 = r + d * 128 * mt
                        k0 = r + d * 128 * (mt - 1) if nkb == 2 else q0
                        units.append((d, r, mt, nkb, q0, k0, q0 - 2048 * H, npc))
            GU = 4
            for g0 in range(0, len(units), GU):
                grp = units[g0:g0 + GU]
                hu = [(g, pi) for g in range(len(grp)) for pi in range(2)]
                for g, pi in hu:
                    d, r, mt, nkb, q0, k0, lo, npc = grp[g]
                    hs = slice(64 * pi, 64 * (pi + 1))
                    sb_ = bank(pi + 2 * (g % 2))[:, 256 * (g // 2):256 * (g // 2 + 1)]
                    c0 = 0 if nkb == 2 else 128
                    mm(sb_[:, c0:256], QT2[hs, sl(q0, 128, d), 0], KT2[hs, sl(k0, 128 * nkb, d), 0])
                for g, pi in hu:
                    sb_ = bank(pi + 2 * (g % 2))[:, 256 * (g // 2):256 * (g // 2 + 1)]
                    u = 2 * g + pi
                    stt(DJ[pi], sb_[:, 128:256], -0.125, IDF, ALU.mult, ALU.mult, accum=BIAS[:, u:u + 1])
                for g, pi in hu:
                    d, r, mt, nkb, q0, k0, lo, npc = grp[g]
                    sb_ = bank(pi + 2 * (g % 2))[:, 256 * (g // 2):256 * (g // 2 + 1)]
                    u = 2 * g + pi
                    c0 = 0 if nkb == 2 else 128
                    act(EB[u][:, c0:256], sb_[:, c0:256], AF.Exp, bias=BIAS[:, u:u + 1], scale=0.125)
                for g, pi in hu:
                    d, r, mt, nkb, q0, k0, lo, npc = grp[g]
                    u = 2 * g + pi
                    c0 = 0 if nkb == 2 else 128
                    tt('pool', EM[u][:, c0:256], EB[u][:, c0:256], MASK2[:, c0:256], ALU.mult)
                for g, pi in hu:
                    d, r, mt, nkb, q0, k0, lo, npc = grp[g]
                    u = 2 * g + pi
                    pt_ = bank(4 + pi).bitcast(BF16)[:, 256 * g:256 * (g + 1)]
                    for kb in range(2 - nkb, 2):
                        tr(pt_[:, 128 * kb:128 * (kb + 1)], EM[u][:, 128 * kb:128 * (kb + 1)], IDB)
                for g, pi in hu:
                    d, r, mt, nkb, q0, k0, lo, npc = grp[g]
                    u = 2 * g + pi
                    pt_ = bank(4 + pi).bitcast(BF16)[:, 256 * g:256 * (g + 1)]
                    cp('act' if pi == 0 else 'dve', PTS[u][:, 2 - nkb:2, :],
                       pt_[:, 128 * (2 - nkb):256].rearrange("p (k q) -> p k q", q=128))
                for g, pi in hu:
                    d, r, mt, nkb, q0, k0, lo, npc = grp[g]
                    u = 2 * g + pi
                    hs = slice(64 * pi, 64 * (pi + 1))
                    pn, pd = bank(6)[:, 128 * g:128 * (g + 1)], bank(7)[:, 128 * g:128 * (g + 1)]
                    for kb in range(2 - nkb, 2):
                        vt = r * npc + mt - 1 + kb
                        mm(pn[hs, :], VD[d][:, vt, hs], PTS[u][:, kb, :], start=(kb == 2 - nkb), stop=(kb == 1), tp=(0, 64 * pi))
                    for kb in range(2 - nkb, 2):
                        mm(pd[hs, :], ONESB, PTS[u][:, kb, :], start=(kb == 2 - nkb), stop=(kb == 1), tp=(0, 64 * pi))
                for g in range(len(grp)):
                    d, r, mt, nkb, q0, k0, lo, npc = grp[g]
                    pn, pd = bank(6)[:, 128 * g:128 * (g + 1)], bank(7)[:, 128 * g:128 * (g + 1)]
                    dstn = NUMA[:, sl(lo, 128, d)]
                    dstd = DENA[:, sl(lo, 128, d)]
                    if d == 1:
                        cp('dve', dstn, pn)
                        cp('act', dstd, pd)
                    else:
                        tt('dve', dstn, pn, dstn, ALU.add)
                        tt('dve', dstd, pd, dstd, ALU.add)
            P.op('dve', lambda e: e.reciprocal(out=DENA, in_=DENA), [DENA], [DENA])
            tt('pool', ATT, NUMA, DENA, ALU.mult)
            P.dma('pool', mixp_s[512 + 128 * hp:512 + 128 * (hp + 1), 2048 * H:2048 * (H + 1)], ATT)
    if debug:
        dbg['mix'] = dout("dbg_mix", [D, T], BF16)
        P.dma('pool', dbg['mix'], mixp_s)
    SB.release(m0)

    mC = SB.mark()
    WU = SB.alloc((8, 2 * DFF), BF16)
    WD = SB.alloc((22, D), BF16)
    GV = SB.alloc((3, D), F32)
    CW = SB.alloc((44, 4), F32)
    CAR = SB.alloc((5, 44, 2), F32)
    WOS = [SB.alloc((2, D), BF16) for _ in range(2)]
    mW = SB.mark()
    if 'C' in stages:
        alloc_wst()
        for i, gsrc in enumerate((n_mix_post, n_ffn_pre, n_ffn_post)):
            P.dma('sp', GV[:, i, :], gsrc[0:1, :].partition_broadcast(128))
        for j in range(3):
            dma_fc('sp', lambda c0, c1, j=j: CW[:, c0:c1, j], cw_d[j, :], False)
        dma_fc('sp', lambda c0, c1: CW[:, c0:c1, 3], cb_d.rearrange("f o -> (f o)"), False)
        for kc in range(8):
            rows = slice(128 * kc, 128 * (kc + 1))
            for j in range(0, 2 * DFF, 1024):
                n_ = min(1024, 2 * DFF - j)
                load_w(WU[:, kc, j:j + n_], w_up[rows, j:j + n_], n_)
        for fcn in range(22):
            load_w(WD[:, fcn, :], w_dn[128 * fcn:128 * (fcn + 1), :], 1024)
        for kc in range(8):
            load_w(WOS[kc % 2][:, 0, :], w_out[128 * kc:128 * (kc + 1), :], 1024)
            P.dma('pool', wout_s[128 * kc:128 * (kc + 1), :], WOS[kc % 2][:, 0, :])
    SB.release(mW)
    MIXT = [SB.alloc((8, 128), BF16) for _ in range(2)]
    WOF = [SB.alloc((512,), F32) for _ in range(2)]
    XC = [SB.alloc((D,), F32) for _ in range(2)]
    H2B = SB.alloc((D,), BF16)
    H2T = SB.alloc((8, 128), BF16)
    UE = [[SB.alloc((4, 130), F32)] * 2 for _ in range(2)]
    CGU = [SB.alloc((4, 128), F32) for _ in range(2)]
    CT = [SB.alloc((128,), F32) for _ in range(4)]
    FTB = [SB.alloc((22, 128), BF16) for _ in range(2)]
    JK = SB.alloc((D,), BF16)
    ST2 = SB.alloc((8, 4), F32)
    EPS2 = SB.alloc((1,), F32)
    ci = [0]
    pend = [None]

    def ffn_tile(*a_, **kw_):
        g_ = ffn_tile_gen(*a_, **kw_)
        next(g_)
        ffn_flush()
        next(g_)
        next(g_)
        pend[0] = g_

    def ffn_flush():
        if pend[0] is not None:
            for _ in pend[0]:
                pass
            pend[0] = None

    def rms_from(src_halves, si, col):
        act(JK[:, 0:512], src_halves[0], AF.Square, accum=ST2[:, si, 0:1])
        act(JK[:, 512:1024], src_halves[1], AF.Square, accum=ST2[:, si, 1:2])
        tt('dve', ST2[:, si, 0:1], ST2[:, si, 0:1], ST2[:, si, 1:2], ALU.add)
        act(ST2[:, si, 1:2], ST2[:, si, 0:1], AF.Sqrt, bias=EPS2[:, 0:1], scale=1.0 / D)
        P.op('dve', lambda e: e.reciprocal(out=ST2[:, si, col:col + 1], in_=ST2[:, si, 1:2]),
             [ST2[:, si, col:col + 1]], [ST2[:, si, 1:2]])

    def ffn_tile_gen(mix_src, x_src, y_dst, stream, segs, conv_out=None):
        i = ci[0] % 2
        si = ci[0] % 8
        ci[0] += 1
        mixt, xc = MIXT[i], XC[i]
        FT = FTB[i]
        P.dma('sp', mixt, mix_src.rearrange("(k p) t -> p k t", p=128))
        x_src(xc)
        for kq in range(4):
            wos = WOS[kq % 2]
            P.dma('sp', wos, wout_s[256 * kq:256 * (kq + 1), :].rearrange("(k p) c -> p k c", p=128))
            for k2 in range(2):
                kc = 2 * kq + k2
                for half in range(2):
                    mm(bank(half), mixt[:, kc, :], wos[:, k2, 512 * half:512 * (half + 1)], start=(kc == 0), stop=(kc == 7))
        yield 0
        rms_from((bank(0), bank(1)), si, 2)
        for half in range(2):
            hs_ = slice(512 * half, 512 * (half + 1))
            stt(WOF[half][:, 0:512], bank(half), ST2[:, si, 2:3], GV[:, 0, hs_], ALU.mult, ALU.mult)
            tt('pool', xc[:, hs_], xc[:, hs_], WOF[half][:, 0:512], ALU.add)
        rms_from((xc[:, 0:512], xc[:, 512:1024]), si, 3)
        stt(H2B, xc, ST2[:, si, 3:4], GV[:, 1, :], ALU.mult, ALU.mult)
        pb = bank(2).bitcast(BF16)
        for kc in range(8):
            tr(pb[:, 128 * kc:128 * (kc + 1)], H2B[:, 128 * kc:128 * (kc + 1)], IDB)
        cp('act', H2T, pb.rearrange("p (k t) -> p k t", t=128))
        g0 = 0
        gi = 0
        while g0 < 22:
            ng = min(4, 22 - g0)
            for side in range(2):
                ps = bank(3 + 2 * (gi % 2) + side)
                for c in range(ng):
                    ch = 22 * side + g0 + c
                    for kc in range(8):
                        mm(ps[:, 128 * c:128 * (c + 1)], WU[:, kc, 128 * ch:128 * (ch + 1)], H2T[:, kc, :],
                           start=(kc == 0), stop=(kc == 7))
            for side in range(2):
                ps = bank(3 + 2 * (gi % 2) + side)
                ue = UE[side][gi % 2]
                chs = slice(22 * side + g0, 22 * side + g0 + ng)
                cp('act', ue[:, 0:ng, 2:130], ps[:, 0:128 * ng].rearrange("p (c t) -> p c t", t=128))
                for (col0, ncols, cinit) in segs:
                    if cinit is None:
                        cp('pool', ue[:, 0:ng, col0:col0 + 2], CAR[:, stream, chs, :])
                    else:
                        cinit(ue[:, 0:ng, col0:col0 + 2], chs)
                for c in range(ng):
                    ch = 22 * side + g0 + c
                    t_ = CT[(2 * c + side) % 4]
                    ts('dve', t_, ue[:, c, 0:128], CW[:, ch, 0:1], ALU.mult, CW[:, ch, 3:4], ALU.add)
                    stt(t_, ue[:, c, 1:129], CW[:, ch, 1:2], t_, ALU.mult, ALU.add)
                    stt(CGU[side][:, c, :], ue[:, c, 2:130], CW[:, ch, 2:3], t_, ALU.mult, ALU.add)
                last_col = segs[-1][0] + segs[-1][1]
                cp('pool', CAR[:, stream, chs, :], ue[:, 0:ng, last_col:last_col + 2])
                if conv_out is not None:
                    conv_out(ue, chs, ng)
            act(CGU[0][:, 0:ng, :], CGU[0][:, 0:ng, :], AF.Silu)
            tt('pool', FT[:, g0:g0 + ng, :], CGU[0][:, 0:ng, :], CGU[1][:, 0:ng, :], ALU.mult)
            g0 += ng
            gi += 1
            if gi == 2:
                yield 1
        yield 2
        dbk = (bank(7), bank(2))
        for half in range(2):
            ps = dbk[half]
            for fcn in range(22):
                mm(ps, FT[:, fcn, :], WD[:, fcn, 512 * half:512 * (half + 1)], start=(fcn == 0), stop=(fcn == 21))
        rms_from(dbk, si, 2)
        for half in range(2):
            hs_ = slice(512 * half, 512 * (half + 1))
            stt(WOF[half][:, 0:512], dbk[half], ST2[:, si, 2:3], GV[:, 2, hs_], ALU.mult, ALU.mult)
            tt('pool', xc[:, hs_], xc[:, hs_], WOF[half][:, 0:512], ALU.add)
        y_dst(xc)

    if 'C' in stages:
        memset('pool', EPS2, NORM_EPS)
        memset('pool', CAR, 0.0)
        for it in range(T // 128):
            t0 = 128 * it
            ffn_tile(mixp_s[:, t0:t0 + 128],
                     lambda xc, t0=t0: P.dma('sp', xc, x_p[t0:t0 + 128, :]),
                     lambda xc, t0=t0: P.dma('pool', y_p[t0:t0 + 128, :], xc),
                     0, [(0, 128, None)])
        ffn_flush()
        if 'S' in stages:
            CONVS = SB.alloc((4, 44, 2), F32)
            SCONV = SB.alloc((4, 44, 2), F32)
            for seq in range(4):
                for r_ in range(2):
                    dma_fc('sp', lambda c0, c1, seq=seq, r_=r_: CONVS[:, seq, c0:c1, r_], st_conv[seq, r_, :], False)
            for tile in range(2):
                def xs_(xc, tile=tile):
                    memset('pool', xc, 0.0)
                    for q in range(2):
                        P.dma('sp', xc[64 * q:64 * q + 8, :], x_s[2 * tile + q, :, :])

                def ys_(xc, tile=tile):
                    for q in range(2):
                        P.dma('pool', y_s[2 * tile + q, :, :], xc[64 * q:64 * q + 8, :])

                def mk_init(seq):
                    return lambda dst, chs, seq=seq: cp('pool', dst, CONVS[:, seq, chs, :])

                def cout(ue, chs, ng, tile=tile):
                    for q in range(2):
                        cp('pool', SCONV[:, 2 * tile + q, chs, :], ue[:, 0:ng, 2 + 64 * q + 6:2 + 64 * q + 8])

                ffn_tile(mixs_s[:, 128 * tile:128 * (tile + 1)], xs_, ys_, 1,
                         [(0, 64, mk_init(2 * tile)), (64, 64, mk_init(2 * tile + 1))], conv_out=cout)
            ffn_flush()
            for seq in range(4):
                for r_ in range(2):
                    dma_fc('pool', lambda c0, c1, seq=seq, r_=r_: SCONV[:, seq, c0:c1, r_], o_sconv[seq, r_, :], True)
        for r_ in range(2):
            dma_fc('pool', lambda c0, c1, r_=r_: CAR[:, 0, c0:c1, r_], o_pconv[r_, :], True)
    SB.release(mC)

    P.final_wait_all('sp')
    P.build()
    return nc


def _rope_tables(T):
    half = 32
    inv = (np.float32(10000.0) ** (-np.arange(half, dtype=np.float32) / np.float32(half))).astype(np.float32)

    def tab(pos):
        ang = pos.astype(np.float32)[:, None] * inv[None, :]
        return np.cos(ang).astype(np.float32), np.sin(ang).astype(np.float32)
    cp_, sp_ = tab(np.arange(T))
    cs_, ss_ = tab(PAST_LEN + np.arange(64))
    def fm(c, s):
        cf = np.concatenate([c.T, c.T], 0)
        sf = np.concatenate([-s.T, s.T], 0)
        return (np.ascontiguousarray(np.concatenate([cf, cf], 0)), np.ascontiguousarray(np.concatenate([sf, sf], 0)))
    return fm(cp_, sp_), fm(cs_, ss_), (cp_, sp_), (cs_, ss_)


def _lb_table():
    cnt = np.zeros((8, 17 * 128), np.float64)
    for t in range(8):
        for d in (1, 4, 16):
            for j in range(129):
                idx = WBUF + t - j * d
                if idx >= 0:
                    cnt[t, idx] += 1
    lb = np.where(cnt > 0, np.log(np.maximum(cnt, 1)), -30000.0)
    return lb.astype(np.float32)


def _core_inputs(inp, c, T):
    (cfp, sfp), (cfs, sfs), (ctp, stp), (cts, sts) = _rope_tables(T)
    KEEP = min(WBUF, T)
    f = lambda a: np.ascontiguousarray(np.asarray(a, dtype=np.float32))
    d = {
        'x_prompt': f(inp['x_prompt'][c, :T]),
        'x_sample': f(inp['x_sample'][4 * c:4 * c + 4]),
        'state_rwkv_shift': f(inp['state_rwkv_shift'][0, 4 * c:4 * c + 4]),
        'state_rwkv_wkv': f(inp['state_rwkv_wkv'][0, 4 * c:4 * c + 4]),
        'cache_att_k': f(np.asarray(inp['cache_att_k'][0, 4 * c:4 * c + 4]).reshape(4, WBUF, 512)),
        'cache_att_v': f(np.asarray(inp['cache_att_v'][0, 4 * c:4 * c + 4]).reshape(4, WBUF, 512)),
        'state_ffn_conv': f(inp['state_ffn_conv'][0, 4 * c:4 * c + 4]),
        'norm_mix_pre': f(inp['norm_mix_pre']), 'norm_mix_post': f(inp['norm_mix_post']),
        'norm_ffn_pre': f(inp['norm_ffn_pre']), 'norm_ffn_post': f(inp['norm_ffn_post']),
        'w_in': f(inp['w_in'][0]), 'mu_shift': f(inp['mu_shift']),
        'w0': f(inp['w0']).reshape(512, 1), 'w_decay_up': f(inp['w_decay_up'][0]), 'a0': f(inp['a0']).reshape(512, 1),
        'w_iclr_up': f(inp['w_iclr_up'][0]), 'w_gate_up': f(inp['w_gate_up'][0]),
        'k_k': f(inp['k_k']).reshape(512, 1), 'k_a': f(inp['k_a']).reshape(512, 1), 'r_k': f(inp['r_k']).reshape(512, 1),
        'lnx_w': f(inp['lnx_w']).reshape(512, 1), 'lnx_b': f(inp['lnx_b']).reshape(512, 1),
        'w_out': f(inp['w_out'][0]), 'w_ffn_up': f(inp['w_ffn_up'][0]), 'ffn_conv_w': f(inp['ffn_conv_w'][0]),
        'ffn_conv_b': f(inp['ffn_conv_b']).reshape(2 * DFF, 1), 'w_ffn_down': f(inp['w_ffn_down'][0]),
        'cos_p': cfp, 'sin_p': sfp, 'cos_s': cfs, 'sin_s': sfs,
        'cost_p': np.ascontiguousarray(ctp[T - KEEP:]), 'sint_p': np.ascontiguousarray(stp[T - KEEP:]),
        'cost_s': np.ascontiguousarray(np.tile(cts[:64], (2, 1))), 'sint_s': np.ascontiguousarray(np.tile(sts[:64], (2, 1))),
        'lb_s': _lb_table(),
    }
    return d


_NC_CACHE = {}


def kernel(**inputs):
    T = 4096
    n = 8
    if T not in _NC_CACHE:
        _NC_CACHE[T] = build_program(T=T, debug=False, stages='NRBCS')
    nc = _NC_CACHE[T]
    in_maps = [_core_inputs(inputs, c, T) for c in range(n)]
    res = run_bass_kernel_spmd(nc, in_maps, core_ids=list(range(n)))
    rs = res.results
    g = lambda k: [np.asarray(r[k], dtype=np.float32) for r in rs]
    y_p = np.stack(g('y_prompt'), 0)
    y_s = np.concatenate(g('y_sample'), 0)
    p_shift = np.stack([a.reshape(D) for a in g('p_shift')], 0)[None]
    p_wkv = np.stack(g('p_wkv'), 0)[None]
    p_k = np.stack([a.reshape(WBUF, 8, 64) for a in g('p_att_k')], 0)[None]
    p_v = np.stack([a.reshape(WBUF, 8, 64) for a in g('p_att_v')], 0)[None]
    p_conv = np.stack(g('p_conv'), 0)[None]
    s_shift = np.concatenate(g('s_shift'), 0)[None]
    s_wkv = np.concatenate(g('s_wkv'), 0)[None]
    s_k = np.concatenate([a.reshape(4, WBUF, 8, 64) for a in g('s_att_k')], 0)[None]
    s_v = np.concatenate([a.reshape(4, WBUF, 8, 64) for a in g('s_att_v')], 0)[None]
    s_conv = np.concatenate(g('s_conv'), 0)[None]
    return (y_p, y_s, p_shift, p_wkv, p_k, p_v, p_conv, s_shift, s_wkv, s_k, s_v, s_conv)
```

```python
import contextlib
import os
RCUT = int(os.environ.get('RCUT', '9'))
RPRE = int(os.environ.get('RPRE', '99'))
RSUB = int(os.environ.get('RSUB', '99'))
BCUT = int(os.environ.get('BCUT', '99'))
import numpy as np
import ml_dtypes
import concourse.bass as bass
import concourse.mybir as mybir
from concourse.bass_utils import run_bass_kernel_spmd

F32 = mybir.dt.float32
BF16 = mybir.dt.bfloat16
AF = mybir.ActivationFunctionType
ALU = mybir.AluOpType
AX = mybir.AxisListType

D = 1024
HD = 64
RW = 512
NCOL_R = 1792
D_IN = 3328
DFF = 2816
PAST_LEN = 16384
WBUF = 2048
NORM_EPS = 1e-6
LNX_EPS = 64e-5
C0 = float(np.exp(-0.5))

_ESZ = {F32: 4, BF16: 2}


def _box(ap):
    name = ap.tensor.name
    dims = ap.ap
    off = int(ap.offset)
    esz = _ESZ.get(ap.dtype, 4)
    if 'DRAM' in str(ap.space).upper():
        ext = sum((c - 1) * abs(s) for s, c in dims)
        return (name, 0, 1, off * esz, (off + ext + 1) * esz)
    pstride = dims[0][0]
    pcount = dims[0][1]
    if pstride == 0:
        p0, f0 = 0, off
    else:
        p0 = off // pstride
        f0 = off - p0 * pstride
    ext = sum((c - 1) * abs(s) for s, c in dims[1:])
    b0, b1 = f0 * esz, (f0 + ext + 1) * esz
    if 'PSUM' in str(ap.space).upper():
        return (name, (p0 // 32) * 32, ((p0 + pcount + 31) // 32) * 32, (b0 // 2048) * 2048, ((b1 + 2047) // 2048) * 2048)
    return (name, p0, p0 + pcount, b0, b1)


class Prog:
    COMPUTE = ('pe', 'act', 'dve', 'pool')

    def __init__(self, nc, n_dma_sems=(20, 12, 8)):
        self.nc = nc
        self.ops = {e: [] for e in ('pe', 'act', 'dve', 'pool', 'sp')}
        self.nseq = {e: 0 for e in self.COMPUTE}
        self.waited = {}
        self.recs = {}
        self.dmaq = {'sp': [n_dma_sems[0], 0], 'pool': [n_dma_sems[1], 0], 'act': [n_dma_sems[2], 0]}
        self.sems = {}

    def _deps(self, eng, outs, ins, waiter=None):
        waiter = waiter or eng
        deps = {}
        for ap, is_w in [(a, True) for a in outs] + [(a, False) for a in ins]:
            name, p0, p1, f0, f1 = _box(ap)
            for r in self.recs.get(name, ()):
                if r[0] < p1 and p0 < r[1] and r[2] < f1 and f0 < r[3]:
                    if is_w or r[6] or (name == 'PS' and r[7] != eng):
                        if r[7] == eng and eng in self.COMPUTE:
                            if eng == 'pe':
                                continue
                            if not (r[6] and not is_w):
                                continue
                        if deps.get(r[4], 0) < r[5]:
                            deps[r[4]] = r[5]
        out = []
        for key, val in deps.items():
            if self.waited.get((waiter, key), 0) < val:
                self.waited[(waiter, key)] = val
                out.append((key, val))
        return out

    def _record(self, eng, outs, ins, key, val):
        for ap, is_w in [(a, True) for a in outs] + [(a, False) for a in ins]:
            name, p0, p1, f0, f1 = _box(ap)
            lst = self.recs.setdefault(name, [])
            if is_w:
                lst[:] = [r for r in lst if not (p0 <= r[0] and r[1] <= p1 and f0 <= r[2] and r[3] <= f1)]
            else:
                lst[:] = [r for r in lst if not ((not r[6]) and r[4] == key and p0 <= r[0] and r[1] <= p1
                                                 and f0 <= r[2] and r[3] <= f1)]
            lst.append([p0, p1, f0, f1, key, val, is_w, eng])

    def op(self, eng, fn, outs, ins):
        waits = self._deps(eng, outs, ins)
        self.nseq[eng] += 1
        key = 'c_' + eng
        self._record(eng, outs, ins, key, self.nseq[eng])
        self.ops[eng].append(('c', waits, fn, key))

    def dma(self, q, out, in_, **kw):
        ns, cnt = self.dmaq[q]
        slot = cnt % ns
        rnd = cnt // ns
        self.dmaq[q][1] += 1
        key = 'd_%s_%d' % (q, slot)
        waits = self._deps('dma_' + q, [out], [in_], waiter=q)
        if rnd > 0 and self.waited.get((q, key), 0) < 16 * rnd:
            self.waited[(q, key)] = 16 * rnd
            waits.append((key, 16 * rnd))
        self._record('dma_' + q, [out], [in_], key, 16 * (rnd + 1))
        self.ops[q].append(('d', waits, (out, in_, kw), key))

    def final_wait_all(self, eng='sp'):
        waits = []
        for q, (ns, cnt) in self.dmaq.items():
            for slot in range(min(ns, cnt)):
                nuse = (cnt - slot + ns - 1) // ns
                waits.append(('d_%s_%d' % (q, slot), 16 * nuse))
        for e in self.COMPUTE:
            if self.nseq[e] > 0:
                waits.append(('c_' + e, self.nseq[e]))
        self.ops[eng].append(('w', waits, None, None))

    def build(self):
        nc = self.nc
        keys = set()
        for e, lst in self.ops.items():
            for kind, waits, payload, key in lst:
                if key:
                    keys.add(key)
                for k, v in waits:
                    keys.add(k)
        with contextlib.ExitStack() as st:
            for k in sorted(keys):
                self.sems[k] = st.enter_context(nc.semaphore(k))
            block = st.enter_context(nc.Block())
            sems = self.sems

            def runner(lst):
                def run(e):
                    for kind, waits, payload, key in lst:
                        for k, v in waits:
                            e.wait_ge(sems[k], v)
                        if kind == 'c':
                            payload(e).then_inc(sems[key], 1)
                        elif kind == 'd':
                            out, in_, kw = payload
                            e.dma_start(out=out, in_=in_, **kw).then_inc(sems[key], 16)
                return run

            block.tensor(runner(self.ops['pe']))
            block.scalar(runner(self.ops['act']))
            block.vector(runner(self.ops['dve']))
            block.gpsimd(runner(self.ops['pool']))
            block.sync(runner(self.ops['sp']))


class Arena:
    def __init__(self, nc, name, nbytes):
        self.t = nc.alloc_sbuf_tensor(name, [128, nbytes // 4], F32)
        self.cap = nbytes
        self.off = 0
        self.peak = 0

    def alloc(self, shape, dt):
        shape = tuple(int(s) for s in shape)
        n = int(np.prod(shape))
        nb = (n * _ESZ[dt] + 63) // 64 * 64
        off = self.off
        self.off += nb
        self.peak = max(self.peak, self.off)
        assert self.off <= self.cap, ("SBUF arena overflow", self.off, self.cap)
        ap = self.t[:, off // 4:(off + nb) // 4]
        if dt != F32:
            ap = ap.bitcast(dt)
        ap = ap[:, 0:n]
        if len(shape) > 1:
            names = ['a%d' % i for i in range(len(shape))]
            pat = "p (%s) -> p %s" % (' '.join(names), ' '.join(names))
            ap = ap.rearrange(pat, **{nm: s for nm, s in zip(names, shape)})
        return ap

    def mark(self):
        return self.off

    def release(self, m):
        self.off = m


def build_program(T=4096, debug=False, stages='ABC'):
    RT = 256
    NT = T // 512
    nc = bass.Bass("TRN2", target_bir_lowering=False)
    P = Prog(nc)
    KEEP = min(WBUF, T)

    def din(name, shape, dt=F32):
        return nc.dram_tensor(name, list(shape), dt, kind="ExternalInput").ap()

    def dout(name, shape, dt=F32):
        return nc.dram_tensor(name, list(shape), dt, kind="ExternalOutput").ap()

    def dscr(name, shape, dt=BF16):
        return nc.dram_tensor(name, list(shape), dt, kind="Internal").ap()

    x_p = din("x_prompt", [T, D])
    x_s = din("x_sample", [4, 8, D])
    st_shift = din("state_rwkv_shift", [4, D])
    st_wkv = din("state_rwkv_wkv", [4, 8, 64, 64])
    ck = din("cache_att_k", [4, WBUF, 512])
    cv = din("cache_att_v", [4, WBUF, 512])
    st_conv = din("state_ffn_conv", [4, 2, 2 * DFF])
    n_mix_pre = din("norm_mix_pre", [1, D])
    n_mix_post = din("norm_mix_post", [1, D])
    n_ffn_pre = din("norm_ffn_pre", [1, D])
    n_ffn_post = din("norm_ffn_post", [1, D])
    w_in = din("w_in", [D, D_IN])
    mu_shift = din("mu_shift", [1, NCOL_R])
    w0_d = din("w0", [RW, 1])
    w_decay_up = din("w_decay_up", [64, RW])
    a0_d = din("a0", [RW, 1])
    w_iclr_up = din("w_iclr_up", [64, RW])
    w_gate_up = din("w_gate_up", [128, RW])
    k_k_d = din("k_k", [RW, 1])
    k_a_d = din("k_a", [RW, 1])
    r_k_d = din("r_k", [RW, 1])
    lnx_w_d = din("lnx_w", [RW, 1])
    lnx_b_d = din("lnx_b", [RW, 1])
    w_out = din("w_out", [D, D])
    w_up = din("w_ffn_up", [D, 2 * DFF])
    cw_d = din("ffn_conv_w", [3, 2 * DFF])
    cb_d = din("ffn_conv_b", [2 * DFF, 1])
    w_dn = din("w_ffn_down", [DFF, D])
    cos_p = din("cos_p", [128, T])
    sin_p = din("sin_p", [128, T])
    cos_s = din("cos_s", [128, 64])
    sin_s = din("sin_s", [128, 64])
    cost_p = din("cost_p", [KEEP, 32])
    sint_p = din("sint_p", [KEEP, 32])
    cost_s = din("cost_s", [128, 32])
    sint_s = din("sint_s", [128, 32])

    lb_s = din("lb_s", [8, 17 * 128])

    y_p = dout("y_prompt", [T, D])
    y_s = dout("y_sample", [4, 8, D])
    o_pshift = dout("p_shift", [1, D])
    o_pwkv = dout("p_wkv", [8, 64, 64])
    o_pk = dout("p_att_k", [KEEP, 512])
    o_pv = dout("p_att_v", [KEEP, 512])
    o_pconv = dout("p_conv", [2, 2 * DFF])
    o_sshift = dout("s_shift", [4, D])
    o_swkv = dout("s_wkv", [4, 8, 64, 64])
    o_sk = dout("s_att_k", [4, WBUF, 512])
    o_sv = dout("s_att_v", [4, WBUF, 512])
    o_sconv = dout("s_conv", [4, 2, 2 * DFF])
    dbg = {}

    n_mixp, n_mixs, n_vs = D * T, D * 256, T * 512
    SCR = dscr("SCR", [n_mixp + n_mixs + n_vs + D * D])
    mixp_s = SCR[0:n_mixp].rearrange("(f t) -> f t", t=T)
    mixs_s = SCR[n_mixp:n_mixp + n_mixs].rearrange("(f t) -> f t", t=256)
    vs_s = SCR[n_mixp + n_mixs:n_mixp + n_mixs + n_vs].rearrange("(t c) -> t c", c=512)
    wout_s = SCR[n_mixp + n_mixs + n_vs:n_mixp + n_mixs + n_vs + D * D].rearrange("(k c) -> k c", c=D)

    SB = Arena(nc, "SB", 212480)
    PS = nc.alloc_psum_tensor("PS", [128, 4096], F32)

    def bank(k, a=0, b=512):
        return PS[:, 512 * k + a:512 * k + b]

    def dma_fc(q, sb_fn, dr_row, write):
        for c0 in range(0, 44, 8):
            c1 = min(44, c0 + 8)
            d_ = dr_row[128 * c0:128 * c1].rearrange("(c p) -> p c", p=128)
            if write:
                P.dma(q, d_, sb_fn(c0, c1), allow_slow_non_contiguous=True)
            else:
                P.dma(q, sb_fn(c0, c1), d_, allow_slow_non_contiguous=True)

    def sl(start, n, step):
        return slice(start, start + step * (n - 1) + 1, step)

    def tt(eng, out, in0, in1, op):
        P.op(eng, lambda e: e.tensor_tensor(out=out, in0=in0, in1=in1, op=op), [out], [in0, in1])

    def ts(eng, out, in0, s1, op0, s2=None, op1=None):
        ins = [in0] + [s for s in (s1, s2) if not isinstance(s, (int, float, type(None)))]
        if op1 is None:
            P.op(eng, lambda e: e.tensor_scalar(out=out, in0=in0, scalar1=s1, scalar2=None, op0=op0), [out], ins)
        else:
            P.op(eng, lambda e: e.tensor_scalar(out=out, in0=in0, scalar1=s1, scalar2=s2, op0=op0, op1=op1), [out], ins)

    def stt(out, in0, scalar, in1, op0, op1, accum=None):
        ins = [in0, in1] + ([scalar] if not isinstance(scalar, (int, float)) else [])
        outs = [out] + ([accum] if accum is not None else [])
        if accum is None:
            P.op('dve', lambda e: e.scalar_tensor_tensor(out=out, in0=in0, scalar=scalar, in1=in1, op0=op0, op1=op1), outs, ins)
        else:
            P.op('dve', lambda e: e.scalar_tensor_tensor(out=out, in0=in0, scalar=scalar, in1=in1, op0=op0, op1=op1,
                                                         accum_out=accum), outs, ins)

    def act(out, in_, func, bias=None, scale=None, accum=None):
        ins = [in_] + [s for s in (bias, scale) if not isinstance(s, (int, float, type(None)))]
        outs = [out] + ([accum] if accum is not None else [])
        kw = {}
        if bias is not None:
            kw['bias'] = bias
        if scale is not None:
            kw['scale'] = scale
        if accum is not None:
            kw['accum_out'] = accum
        P.op('act', lambda e: e.activation(out=out, in_=in_, func=func, **kw), outs, ins)

    def cp(eng, out, in_):
        if eng == 'act':
            P.op('act', lambda e: e.copy(out=out, in_=in_), [out], [in_])
        else:
            P.op(eng, lambda e: e.tensor_copy(out=out, in_=in_), [out], [in_])

    pe_hist = []

    def pe_check(out, lhsT):
        import traceback
        _, p0, p1, f0, f1 = _box(lhsT)
        rg = set(range(p0 // 32, (p1 + 31) // 32))
        _, _, _, b0, b1 = _box(out)
        bk = set(range(b0 // 2048, (b1 + 2047) // 2048))
        for (rg2, bk2, where) in pe_hist[-24:]:
            if not (rg & rg2) and (bk & bk2):
                print("PE ROW-GROUP/BANK HAZARD:", sorted(rg), sorted(bk), "vs", sorted(rg2), sorted(bk2), where,
                      "|", traceback.extract_stack(limit=4)[0].lineno)
        pe_hist.append((rg, bk, traceback.extract_stack(limit=4)[0].lineno))

    def mm(out, lhsT, rhs, start=True, stop=True, tp=None):
        pe_check(out, lhsT)
        if tp is None:
            P.op('pe', lambda e: e.matmul(out, lhsT=lhsT, rhs=rhs, start=start, stop=stop), [out], [lhsT, rhs])
        else:
            P.op('pe', lambda e: e.matmul(out, lhsT=lhsT, rhs=rhs, start=start, stop=stop, tile_position=tp), [out], [lhsT, rhs])

    def tr(out, in_, ident):
        pe_check(out, in_)
        P.op('pe', lambda e: e.transpose(out, in_, ident), [out], [in_, ident])

    def memset(eng, out, val):
        P.op(eng, lambda e: e.memset(out, val), [out], [])

    def asel(out, pattern, base, cm, cmp, fill=0.0):
        P.op('pool', lambda e: e.affine_select(out=out, in_=out, pattern=pattern, compare_op=cmp, fill=fill,
                                               base=base, channel_multiplier=cm), [out], [out])

    rr = {'n': 0}

    def rot(engs):
        rr['n'] += 1
        return engs[rr['n'] % len(engs)]

    if 'S' in stages:
        for b_ in range(4):
            for (src_, dst_) in ((ck, o_sk), (cv, o_sv)):
                for c_ in range(4):
                    P.dma('act', dst_[b_, 510 * c_:510 * (c_ + 1), :], src_[b_, 8 + 510 * c_:8 + 510 * (c_ + 1), :])

    IDF = SB.alloc((128,), F32)
    IDB = SB.alloc((128,), BF16)
    BDO = SB.alloc((128,), F32)
    BDM = SB.alloc((128,), F32)
    MSU = SB.alloc((128,), F32)
    MIU = SB.alloc((128,), F32)
    MSL = SB.alloc((128,), F32)
    MAB = SB.alloc((2, 128), F32)
    PERM = SB.alloc((128,), BF16)
    PERMF = SB.alloc((128,), F32)
    MASK2 = SB.alloc((256,), BF16)
    ONESB = SB.alloc((64,), BF16)
    PAR = SB.alloc((4, 8), F32)
    m0 = SB.mark()
    HT = SB.alloc((8, T), BF16)
    HTS = SB.alloc((8, 4, 66), BF16)
    mA = SB.mark()
    RESET = SB.alloc((512,), F32)
    tmpc = SB.alloc((256,), F32)

    memset('pool', IDF, 1.0)
    asel(IDF, [[-1, 128]], 0, 1, ALU.is_equal)
    cp('dve', IDB, IDF)
    memset('pool', BDO, 1.0)
    memset('pool', BDO[0:64, 64:128], 0.0)
    memset('pool', BDO[64:128, 0:64], 0.0)
    ts('dve', BDM, BDO, 1.0 / 64, ALU.mult)
    memset('pool', MSU, 1.0)
    asel(MSU, [[1, 128]], 0, -1, ALU.is_gt)
    tt('dve', MSU, MSU, BDO, ALU.mult)
    memset('pool', MIU, 1.0)
    asel(MIU, [[1, 128]], 0, -1, ALU.is_ge)
    tt('dve', MIU, MIU, BDO, ALU.mult)
    memset('pool', MSL, 1.0)
    asel(MSL, [[-1, 128]], 0, 1, ALU.is_gt)
    tt('dve', MSL, MSL, BDO, ALU.mult)
    cp('dve', MAB[:, 0, :], MSU)
    cp('dve', MAB[:, 1, :], MIU)
    memset('pool', tmpc[:, 0:128], 1.0)
    asel(tmpc[:, 0:128], [[-1, 128]], -32, 1, ALU.is_equal)
    memset('pool', tmpc[:, 32:64], 0.0)
    memset('pool', tmpc[:, 96:128], 0.0)
    memset('pool', tmpc[:, 128:256], 1.0)
    asel(tmpc[:, 128:256], [[-1, 128]], 32, 1, ALU.is_equal)
    memset('pool', tmpc[:, 128:160], 0.0)
    memset('pool', tmpc[:, 192:224], 0.0)
    tt('dve', PERM, tmpc[:, 0:128], tmpc[:, 128:256], ALU.add)
    tt('dve', PERMF, tmpc[:, 0:128], tmpc[:, 128:256], ALU.add)
    memset('pool', tmpc[:, 0:256], 1.0)
    asel(tmpc[:, 0:128], [[1, 128]], 0, -1, ALU.is_ge)
    asel(tmpc[:, 128:256], [[-1, 128]], 0, 1, ALU.is_ge)
    cp('dve', MASK2, tmpc[:, 0:256])
    memset('pool', RESET, 1.0)
    for c in range(8):
        memset('pool', RESET[:, 64 * c:64 * c + 1], 0.0)
    memset('pool', ONESB, 1.0)

    P_W0, P_A0, P_KK, P_KA, P_RK, P_LW, P_LB, P_OMKA = range(8)
    for i, d_ in enumerate((w0_d, a0_d, k_k_d, k_a_d, r_k_d, lnx_w_d, lnx_b_d)):
        for fc in range(4):
            P.dma('sp', PAR[:, fc, i:i + 1], d_[128 * fc:128 * (fc + 1), :])
    ts('dve', PAR[:, :, P_OMKA], PAR[:, :, P_KA], -1.0, ALU.mult, 1.0, ALU.add)
    WDU = SB.alloc((RW,), F32)
    WIU = SB.alloc((RW,), F32)
    WGU = SB.alloc((RW,), F32)
    P.dma('sp', WDU[0:64, :], w_decay_up[:, :])
    P.dma('sp', WIU[64:128, :], w_iclr_up[:, :])
    P.dma('sp', WGU[:, :], w_gate_up[:, :])
    GPRE = SB.alloc((D,), F32)
    P.dma('sp', GPRE, n_mix_pre[0:1, :].partition_broadcast(128))

    WSTh = {'b': None}
    wi = [0]

    def alloc_wst():
        WSTh['b'] = [SB.alloc((1024,), F32) for _ in range(3)]

    def load_w(dst, src, ncol, scale_pairs=None):
        i = wi[0] % 3
        wi[0] += 1
        WST = WSTh['b']
        P.dma('sp', WST[i][:, 0:ncol], src)
        if scale_pairs is None:
            cp(rot(['act', 'dve', 'pool']), dst, WST[i][:, 0:ncol])
        else:
            for (d_, sc), eng in zip(scale_pairs, ('dve', 'pool')):
                tt(eng, d_, WST[i][:, 0:ncol], sc, ALU.mult)

    W12 = SB.alloc((8, 2, NCOL_R), BF16)
    mW = SB.mark()
    alloc_wst()
    MU = SB.alloc((NCOL_R,), F32)
    OMU = SB.alloc((NCOL_R,), F32)
    P.dma('sp', MU, mu_shift[0:1, :].partition_broadcast(128))
    ts('dve', OMU, MU, -1.0, ALU.mult, 1.0, ALU.add)
    for kc in (range(8) if 'x' not in stages else []):
        rows = slice(128 * kc, 128 * (kc + 1))
        for c0, c1 in ((0, 1024), (1024, NCOL_R)):
            load_w(None, w_in[rows, c0:c1], c1 - c0,
                   scale_pairs=[(W12[:, kc, 0, c0:c1], OMU[:, c0:c1]), (W12[:, kc, 1, c0:c1], MU[:, c0:c1])])
    SB.release(mW)
    XT = [SB.alloc((D,), F32) for _ in range(2)]
    HB = [SB.alloc((D,), BF16) for _ in range(2)]
    STAT = SB.alloc((8, 4), F32)
    EPSC = SB.alloc((2,), F32)
    memset('pool', EPSC[:, 0:1], NORM_EPS)
    memset('pool', EPSC[:, 1:2], LNX_EPS)
    sti = [0]

    def norm_tile(x_rows_fn, dst_fn, want_f32_row=None, hprev_fn=None):
        i = sti[0] % 2
        si = sti[0] % 8
        sti[0] += 1
        xt, hb = XT[i], HB[i]
        x_rows_fn(xt)
        act(hb, xt, AF.Square, accum=STAT[:, si, 0:1])
        act(STAT[:, si, 1:2], STAT[:, si, 0:1], AF.Sqrt, bias=EPSC[:, 0:1], scale=1.0 / D)
        P.op('dve', lambda e: e.reciprocal(out=STAT[:, si, 2:3], in_=STAT[:, si, 1:2]), [STAT[:, si, 2:3]], [STAT[:, si, 1:2]])
        stt(hb, xt, STAT[:, si, 2:3], GPRE, ALU.mult, ALU.mult)
        if want_f32_row is not None:
            stt(xt, xt, STAT[:, si, 2:3], GPRE, ALU.mult, ALU.mult)
            want_f32_row(xt)
        if hprev_fn is not None:
            hprev_fn(hb)
        pb = bank(2 + i).bitcast(BF16)
        for kc in range(8):
            tr(pb[:, 128 * kc:128 * (kc + 1)], hb[:, 128 * kc:128 * (kc + 1)], IDB)
        return pb

    RB = {}
    for nm in ('r', 'k', 'v', 'sgw', 'a', 'g', 'cs', 'csx', 'crem', 'ein', 'eneg', 'kk', 't1', 'kkn',
               'kmod', 'bb', 'bonv', 'YT'):
        RB[nm] = SB.alloc((RT,), F32)
    BTB = SB.alloc((RT,), BF16)
    KTB = SB.alloc((RT,), BF16)
    RB['eex'] = RB['csx']
    RB['erem'] = RB['crem']
    RB['BP'] = RB['bb']
    RB['KP'] = RB['kmod']
    AR = SB.alloc((RT // 128, 2, 128), F32)
    TW = SB.alloc((RT,), F32)
    AL = SB.alloc((RT,), F32)
    SG = SB.alloc((RT,), F32)
    NSTM = RT // 128
    TOKB = SB.alloc((NSTM, 4, 128), F32)
    TOK = [TOKB[:, i_, :, :] for i_ in range(NSTM)]
    TOKH = [RB[nm_].bitcast(BF16).rearrange("p (q c) -> p q c", c=128) for nm_ in ('eneg', 'kk')]
    ARB = RB['kkn'].bitcast(BF16).rearrange("p (s a t) -> p s a t", a=2, t=128)
    NM = [SB.alloc((2, 2, 128), BF16) for _ in range(NSTM)]
    AK = [SB.alloc((2, 2, 128), BF16) for _ in range(NSTM)]
    XP = [[SB.alloc((2, 2, 128), BF16) for _ in range(2)] for _ in range(NSTM)]
    YY = [[SB.alloc((2, 128), BF16) for _ in range(2)] for _ in range(NSTM)]
    Z1 = [SB.alloc((2, 64), BF16) for _ in range(NSTM)]
    UW = [SB.alloc((2, 2, 64), BF16) for _ in range(NSTM)]
    WTs = [SB.alloc((128,), F32) for _ in range(NSTM)]
    GT = [SB.alloc((128,), F32) for _ in range(NSTM)]
    HTt = [SB.alloc((128,), F32) for _ in range(NSTM)]
    US = SB.alloc((2, 64), F32)
    SST = SB.alloc((5, 4, 64), F32)
    ROB = RB['sgw'].bitcast(BF16)[:, 0:RT]
    HTP = SB.alloc((8, RT), BF16)
    SNAT = TOK[0].rearrange("p q c -> p (q c)").rearrange("p (h j) -> p h j", j=64)

    def rwkv_fc(fc, ntok, ps_r, ps_k, ps_v, state_of_chunk, CV, out_dst):
        nst = ntok // 128
        n = ntok
        par = lambda i: PAR[:, fc, i:i + 1]
        B = {k_: v_[:, 0:n] for k_, v_ in RB.items()}
        cp('act', B['r'], ps_r)
        cp('act', B['k'], ps_k)
        cp('act', B['v'], ps_v)
        if RPRE < 1:
            return
        cols = slice(128 * fc, 128 * (fc + 1))
        pz = bank(1)
        mm(bank(4)[:, 0:n], WDU[0:64, cols], TW[0:64, 0:n])
        act(B['sgw'], bank(4)[:, 0:n], AF.Sigmoid, bias=par(P_W0))
        mm(bank(5)[:, 0:n], WIU[64:128, cols], AL[64:128, 0:n])
        act(B['a'], bank(5)[:, 0:n], AF.Sigmoid, bias=par(P_A0))
        mm(pz[:, 0:n], WGU[:, cols], SG[:, 0:n])
        cp('act', B['g'], pz[:, 0:n])
        if RPRE < 2:
            return
        P.op('dve', lambda e: e.tensor_tensor_scan(out=B['cs'], data0=RESET[:, 0:n], data1=B['sgw'], initial=0.0,
                                                   op0=ALU.mult, op1=ALU.add), [B['cs']], [RESET[:, 0:n], B['sgw']])
        if RPRE < 3:
            return
        tt('pool', B['csx'], B['cs'], B['sgw'], ALU.subtract)
        nch = n // 64
        cs3 = B['cs'].rearrange("p (c t) -> p c t", t=64)
        tt('dve', B['crem'].rearrange("p (c t) -> p c t", t=64), cs3[:, :, CV - 1:CV].to_broadcast([128, nch, 64]), cs3,
           ALU.subtract)
        if RPRE < 4:
            return
        act(B['ein'], B['cs'], AF.Exp, scale=-C0)
        act(B['eex'], B['csx'], AF.Exp, scale=-C0)
        act(B['eneg'], B['cs'], AF.Exp, scale=C0)
        act(B['erem'], B['crem'], AF.Exp, scale=-C0)
        if RPRE < 5:
            return
        ts('pool', B['kk'], B['k'], par(P_KK), ALU.mult)
        act(B['t1'], B['kk'], AF.Square)
        mm(pz[:, 0:n], BDO, B['t1'])
        act(B['t1'], pz[:, 0:n], AF.Sqrt)
        ts('dve', B['t1'], B['t1'], 1e-12, ALU.max)
        P.op('dve', lambda e: e.reciprocal(out=B['t1'], in_=B['t1']), [B['t1']], [B['t1']])
        tt('pool', B['kkn'], B['kk'], B['t1'], ALU.mult)
        if RPRE < 6:
            return
        ts('dve', B['t1'], B['a'], par(P_KA), ALU.mult, par(P_OMKA), ALU.add)
        tt('pool', B['kmod'], B['k'], B['t1'], ALU.mult)
        tt('pool', B['bb'], B['kkn'], B['a'], ALU.mult)
        if RPRE < 7:
            return
        ar = AR[:, 0:nst, :, :]
        stt(ar[:, :, 0, :], B['kkn'].rearrange("p (s t) -> p s t", t=128), -1.0,
            B['eex'].rearrange("p (s t) -> p s t", t=128), ALU.mult, ALU.mult)
        tt('pool', ar[:, :, 1, :], B['r'].rearrange("p (s t) -> p s t", t=128),
           B['ein'].rearrange("p (s t) -> p s t", t=128), ALU.mult)
        if RPRE < 8:
            return
        stt(B['t1'], B['r'], par(P_RK), B['kmod'], ALU.mult, ALU.mult)
        mm(pz[:, 0:n], BDO, B['t1'])
        tt('dve', B['bonv'], pz[:, 0:n], B['v'], ALU.mult)
        tt('dve', BTB[:, 0:n], B['bb'], B['eneg'], ALU.mult)
        tt('pool', KTB[:, 0:n], B['kmod'], B['eneg'], ALU.mult)
        tt('dve', B['BP'], B['bb'], B['erem'], ALU.mult)
        tt('pool', B['KP'], B['kmod'], B['erem'], ALU.mult)

        cp('pool', ARB[:, 0:nst, :, :], AR[:, 0:nst, :, :])
        for st in range(nst):
            tk = slice(128 * st, 128 * (st + 1))
            tok, tokb = TOK[st], TOKH[st]
            pt = bank(3)
            for qi, src in enumerate((AR[:, st, 0, :], B['BP'][:, tk], B['KP'][:, tk], B['v'][:, tk])):
                tr(pt[:, 128 * qi:128 * (qi + 1)], src, IDF)
            cp('act', tok, pt.rearrange("p (q c) -> p q c", c=128))
            cp('dve', tokb, pt.rearrange("p (q c) -> p q c", c=128))
            for pi in range(2):
                hp = slice(64 * pi, 64 * (pi + 1))
                rhs_ar = ARB[hp, st, :, :]
                mm(bank(4 + pi)[:, 0:256], BTB[hp, tk], rhs_ar)
                mm(bank(4 + pi)[:, 256:512], KTB[hp, tk], rhs_ar)
                mm(bank(6 + pi)[:, 0:128], ARB[hp, st, 0, :], BTB[hp, tk])
            for pi in range(2):
                tt('dve', NM[st][:, pi, :, :], bank(4 + pi)[:, 0:256].rearrange("p (a t) -> p a t", t=128), MAB, ALU.mult)
                tt('dve', AK[st][:, pi, :, :], bank(4 + pi)[:, 256:512].rearrange("p (a t) -> p a t", t=128), MAB, ALU.mult)
                tt('dve', YY[st][0][:, pi, :], bank(6 + pi)[:, 0:128], MSL, ALU.mult)
                cp('act', XP[st][0][:, pi, 0, :], NM[st][:, pi, 0, :])
                tt('pool', XP[st][0][:, pi, 1, :], NM[st][:, pi, 0, :], IDF, ALU.add)
        for kstep in range(6):
            for st in range(nst):
                cur = kstep % 2
                xp, yy = XP[st][cur], YY[st][cur]
                xpn, yyn = XP[st][1 - cur], YY[st][1 - cur]
                p1, p2 = (bank(1), bank(2)) if st % 2 == 0 else (bank(3), bank(0))
                if kstep == 0:
                    for pi in range(2):
                        mm(p1[:, 256 * pi:256 * pi + 128], yy[:, pi, :], xp[:, pi, 0, :])
                        mm(p2[:, 128 * pi:128 * (pi + 1)], xp[:, pi, 0, :], yy[:, pi, :])
                    cp('act', xpn[:, :, 0, :], p1.rearrange("p (h a t) -> p h a t", a=2, t=128)[:, :, 0, :])
                    cp('pool', xpn[:, :, 1, :], xp[:, :, 1, :])
                    cp('dve', yyn, p2[:, 0:256].rearrange("p (h t) -> p h t", t=128))
                elif kstep < 5:
                    for pi in range(2):
                        mm(p1[:, 256 * pi:256 * (pi + 1)], yy[:, pi, :], xp[:, pi, :, :])
                        mm(p2[:, 128 * pi:128 * (pi + 1)], xp[:, pi, 0, :], yy[:, pi, :])
                    p1v = p1.rearrange("p (h a t) -> p h a t", a=2, t=128)
                    cp('act', xpn[:, :, 0, :], p1v[:, :, 0, :])
                    tt('dve', xpn[:, :, 1, :], p1v[:, :, 1, :], xp[:, :, 1, :], ALU.add)
                    cp('act', yyn, p2[:, 0:256].rearrange("p (h t) -> p h t", t=128))
                else:
                    for pi in range(2):
                        mm(p1[:, 256 * pi + 128:256 * (pi + 1)], yy[:, pi, :], xp[:, pi, 1, :])
                    p1v = p1.rearrange("p (h a t) -> p h a t", a=2, t=128)
                    tt('dve', xpn[:, :, 1, :], p1v[:, :, 1, :], xp[:, :, 1, :], ALU.add)
        for st in range(nst):
            tok, tokb = TOK[st], TOKH[st]
            PT = XP[st][0]
            pm = bank(7)
            for pi in range(2):
                hc = slice(64 * pi, 64 * (pi + 1))
                mm(pm[:, 64 * pi:64 * (pi + 1)], AK[st][:, pi, 0, :], tokb[:, 3, hc])
            cp('act', Z1[st], pm[:, 0:128].rearrange("p (h i) -> p h i", i=64))
            for pi in range(2):
                hc = slice(64 * pi, 64 * (pi + 1))
                mm(pm[:, 128 + 128 * pi:128 + 128 * pi + 64], PT[:, pi, 1, :], Z1[st][:, pi, :])
                mm(pm[:, 128 + 128 * pi + 64:128 + 128 * (pi + 1)], PT[:, pi, 1, :], tokb[:, 0, hc])
            cp('dve', UW[st], pm[:, 128:384].rearrange("p (h a i) -> p h a i", a=2, i=64))
            for pi in range(2):
                hc = slice(64 * pi, 64 * (pi + 1))
                mm(pm[hc, 384:512], tokb[:, 0, hc], PT[:, pi, 1, :], tp=(0, 64 * pi))
            cp('act', WTs[st], pm[:, 384:512])
            pg = bank(3)
            for pi in range(2):
                hc = slice(64 * pi, 64 * (pi + 1))
                mm(pg[hc, 0:128], UW[st][:, pi, 1, :], NM[st][:, pi, 1, :], tp=(0, 64 * pi))
            tt('dve', GT[st], pg[:, 0:128], AR[:, st, 1, :], ALU.add)
            for pi in range(2):
                hc = slice(64 * pi, 64 * (pi + 1))
                mm(pg[hc, 128:256], UW[st][:, pi, 0, :], NM[st][:, pi, 1, :], start=True, stop=False, tp=(0, 64 * pi))
                mm(pg[hc, 128:256], tokb[:, 3, hc], AK[st][:, pi, 1, :], start=False, stop=True, tp=(0, 64 * pi))
            cp('act', HTt[st], pg[:, 128:256])
        for st in range(nst):
            tok = TOK[st]
            for q in range(2):
                S, chain = state_of_chunk(st, q)
                tq = slice(64 * q, 64 * (q + 1))
                tv = slice(64 * q, 64 * q + CV)
                for pi in range(2):
                    hc = slice(64 * pi, 64 * (pi + 1))
                    mm(bank(6 + pi)[tq, 128:192], WTs[st][hc, tq], S[hc, :], tp=(64 * pi, 64 * q))
                for pi in range(2):
                    tt('dve', US[tq, pi, :], bank(6 + pi)[tq, 128:192], UW[st][tq, pi, 0, :], ALU.add)
                for pi in range(2):
                    hc = slice(64 * pi, 64 * (pi + 1))
                    mm(bank(6 + pi)[hc, 192:256], S[hc, :], GT[st][hc, tq], tp=(64 * pi, 64 * pi))
                for pi in range(2):
                    hc = slice(64 * pi, 64 * (pi + 1))
                    tt('dve', B['YT'][hc, 128 * st + 64 * q:128 * st + 64 * (q + 1)], bank(6 + pi)[hc, 192:256], HTt[st][hc, tq],
                       ALU.add)
                psq = bank(6 + q)[:, 256:320]
                for pi in range(2):
                    hc = slice(64 * pi, 64 * (pi + 1))
                    mm(psq[hc, :], tok[tv, 1, hc], US[tv, pi, :], start=True, stop=False, tp=(64 * q, 64 * pi))
                    mm(psq[hc, :], tok[tv, 2, hc], tok[tv, 3, hc], start=False, stop=True, tp=(64 * q, 64 * pi))
                pcol = B['ein'][:, 128 * st + 64 * q + CV - 1:128 * st + 64 * q + CV]
                stt(S, S, pcol, psq, ALU.mult, ALU.add)
        pz = bank(1)
        mm(pz[:, 0:n], BDM, B['YT'])
        tt('dve', B['t1'], B['YT'], pz[:, 0:n], ALU.subtract)
        act(B['kk'], B['t1'], AF.Square)
        mm(pz[:, 0:n], BDM, B['kk'])
        act(B['kk'], pz[:, 0:n], AF.Sqrt, bias=EPSC[:, 1:2])
        P.op('dve', lambda e: e.reciprocal(out=B['kk'], in_=B['kk']), [B['kk']], [B['kk']])
        tt('pool', B['t1'], B['t1'], B['kk'], ALU.mult)
        ts('dve', B['t1'], B['t1'], par(P_LW), ALU.mult, par(P_LB), ALU.add)
        tt('pool', B['t1'], B['t1'], B['bonv'], ALU.add)
        tt('dve', ROB[:, 0:n], B['t1'], B['g'], ALU.mult)
        out_dst(ROB[:, 0:n])

    def rwkv_tile(ntok, cur_ap, prv_ap, state_of_chunk, CV, out_dst_fc):
        def proj(ps, col0):
            for kc in range(8):
                mm(ps, W12[:, kc, 0, col0:col0 + 128], cur_ap(kc), start=(kc == 0), stop=False)
                mm(ps, W12[:, kc, 1, col0:col0 + 128], prv_ap(kc), start=False, stop=(kc == 7))
        n = ntok
        if RPRE < -1:
            return
        pw = bank(0)
        proj(pw[:, 0:n], 1536)
        if RPRE < 0:
            return
        act(TW[0:64, 0:n], pw[0:64, 0:n], AF.Tanh)
        cp('dve', AL[64:128, 0:n], pw[64:128, 0:n])
        proj(pw[:, 0:n], 1664)
        act(SG[:, 0:n], pw[:, 0:n], AF.Sigmoid)
        for fc in range(4):
            pr, pk, pv = bank(0), bank(1), bank(2)
            proj(pr[:, 0:n], 128 * fc)
            proj(pk[:, 0:n], 512 + 128 * fc)
            proj(pv[:, 0:n], 1024 + 128 * fc)
            rwkv_fc(fc, n, pr[:, 0:n], pk[:, 0:n], pv[:, 0:n], lambda st, q, fc=fc: state_of_chunk(fc, st, q), CV,
                    lambda rob, fc=fc: out_dst_fc(fc, rob))

    for fc in range(4):
        memset('pool', SST[:, 0, fc, :], 0.0)

    for it in range(NT):
        for st in range(4):
            t0 = 512 * it + 128 * st
            last = (t0 + 128 == T)

            def ld(xt, t0=t0):
                P.dma('sp', xt, x_p[t0:t0 + 128, :])

            def f32row(hf):
                P.dma('pool', o_pshift[0:1, :], hf[127:128, :])

            pb = norm_tile(ld, None, want_f32_row=f32row if last else None)
            cp(rot(['act', 'dve']), HT[:, :, t0:t0 + 128], pb.rearrange("p (k t) -> p k t", t=128))
        for tb in (range(512 * it, 512 * (it + 1), RT) if 'R' in stages else []):
            if 'y' in stages:
                pass
            elif tb == 0:
                memset('pool', HTP[:, :, 0:1], 0.0)
                cp('dve', HTP[:, :, 1:RT], HT[:, :, 0:RT - 1])
            else:
                cp('dve', HTP, HT[:, :, tb - 1:tb - 1 + RT])
            rwkv_tile(RT,
                      lambda kc, tb=tb: HT[:, kc, tb:tb + RT],
                      lambda kc: HTP[:, kc, :],
                      lambda fc, st, q: (SST[:, 0, fc, :], True), 64,
                      lambda fc, rob, tb=tb: P.dma('pool', mixp_s[128 * fc:128 * (fc + 1), tb:tb + RT], rob))
    if 'v' in stages:
        memset('pool', SNAT, 1.0)
    for fc in (range(4) if ('R' in stages and 'z' not in stages and 'v' not in stages) else []):
        for pi in range(2):
            tr(bank(4 + pi)[0:64, 0:64], SST[64 * pi:64 * (pi + 1), 0, fc, :], IDF[64 * pi:64 * (pi + 1), 64 * pi:64 * (pi + 1)])
            cp('act', SNAT[0:64, 2 * fc + pi, :], bank(4 + pi)[0:64, 0:64])
    if 'R' in stages and 'z' not in stages and 'u' not in stages:
        P.dma('sp', o_pwkv.rearrange("h i j -> i h j"), SNAT[0:64, :, :])


    HTC = XT[0][:, 0:512].bitcast(BF16).rearrange("p (k t) -> p k t", t=128)
    HPF = TOKB.rearrange("p a q c -> p (a q c)")
    if 'S' in stages:
        memset('pool', HTS, 0.0)
        for tile in range(2):
            memset('pool', HPF, 0.0)
            for q in range(2):
                P.dma('sp', HPF[64 * q + 63:64 * q + 64, :], st_shift[2 * tile + q:2 * tile + q + 1, :])

            def ld(xt, tile=tile):
                memset('pool', xt, 0.0)
                for q in range(2):
                    P.dma('sp', xt[64 * q:64 * q + 8, :], x_s[2 * tile + q, :, :])

            def f32row(hf, tile=tile):
                for q in range(2):
                    P.dma('pool', o_sshift[2 * tile + q:2 * tile + q + 1, :], hf[64 * q + 7:64 * q + 8, :])

            def hprev(hb):
                tt('dve', hb, hb, HPF, ALU.add)

            pb = norm_tile(ld, None, want_f32_row=f32row, hprev_fn=hprev)
            pb4 = pb.rearrange("p (k q j) -> p k q j", q=2, j=64)
            cp('act', HTS[:, :, 2 * tile:2 * tile + 2, 2:65], pb4[:, :, :, 0:63])
            cp('dve', HTS[:, :, 2 * tile:2 * tile + 2, 1:2], pb4[:, :, :, 63:64])
        for tile in (range(2) if 'R' in stages else []):
            for q in range(2):
                seq = 2 * tile + q
                P.dma('sp', SNAT[0:64, :, :], st_wkv[seq].rearrange("h i j -> i h j"))
                for h_ in range(8):
                    pi, fc = h_ % 2, h_ // 2
                    tr(bank(4)[0:64, 64 * (h_ % 8):64 * (h_ % 8) + 64], SNAT[0:64, h_, :], IDF[0:64, 0:64])
                for h_ in range(8):
                    pi, fc = h_ % 2, h_ // 2
                    cp('act' if pi else 'dve', SST[64 * pi:64 * (pi + 1), 1 + seq, fc, :], bank(4)[0:64, 64 * h_:64 * h_ + 64])
            cp('dve', HTC.rearrange("p k (q j) -> p k q j", j=64), HTS[:, :, 2 * tile:2 * tile + 2, 2:66])
            cp('dve', HTP[:, :, 0:128].rearrange("p k (q j) -> p k q j", j=64), HTS[:, :, 2 * tile:2 * tile + 2, 1:65])
            rwkv_tile(128,
                      lambda kc: HTC[:, kc, :],
                      lambda kc: HTP[:, kc, 0:128],
                      lambda fc, st, q, tile=tile: (SST[:, 1 + 2 * tile + q, fc, :], False), 8,
                      lambda fc, rob, tile=tile: P.dma('pool', mixs_s[128 * fc:128 * (fc + 1), 128 * tile:128 * (tile + 1)], rob))
            for q in range(2):
                seq = 2 * tile + q
                for fc in range(4):
                    for pi in range(2):
                        tr(bank(4 + pi)[0:64, 0:64], SST[64 * pi:64 * (pi + 1), 1 + seq, fc, :],
                           IDF[64 * pi:64 * (pi + 1), 64 * pi:64 * (pi + 1)])
                        cp('act', SNAT[0:64, 2 * fc + pi, :], bank(4 + pi)[0:64, 0:64])
                P.dma('sp', o_swkv[seq].rearrange("h i j -> i h j"), SNAT[0:64, :, :])

    if debug:
        dbg['HT'] = dout("dbg_ht", [128, 8, T], BF16)
        P.dma('pool', dbg['HT'], HT)

    SB.release(mA)

    mB = SB.mark()
    WQKV = SB.alloc((8, 1536), BF16)
    mW = SB.mark()
    alloc_wst()
    for kc in range(8):
        rows = slice(128 * kc, 128 * (kc + 1))
        load_w(WQKV[:, kc, 0:1024], w_in[rows, NCOL_R:NCOL_R + 1024], 1024)
        load_w(WQKV[:, kc, 1024:1536], w_in[rows, NCOL_R + 1024:D_IN], 512)
    SB.release(mW)

    mS0 = SB.mark()
    if 'S' in stages:
        QSF = SB.alloc((4, 256), F32)
        KSF = SB.alloc((4, 256), F32)
        ATTS = SB.alloc((4, 256), BF16)
        KNEW = SB.alloc((2, 512), F32)
        VNEW = SB.alloc((2, 512), F32)
        CSS = SB.alloc((2, 128), F32)
        XBs = SB.alloc((128,), F32)
        SM = SB.alloc((8,), F32)
        ONESF = SB.alloc((64,), F32)
        RD = SB.alloc((8,), F32)
        HTC2 = SB.alloc((8, 128), BF16)
        RT1s = SB.alloc((128,), F32)
        RT2s = SB.alloc((128,), F32)
    mK = SB.mark()
    VB = [SB.alloc((512,), BF16) for _ in range(2)]
    VF = [SB.alloc((512,), F32) for _ in range(2)]
    KF = [SB.alloc((512,), F32) for _ in range(2)]
    KO = [SB.alloc((512,), F32) for _ in range(2)]
    KTMP = [SB.alloc((256,), F32) for _ in range(4)]
    CST = [SB.alloc((2, 32), F32) for _ in range(2)]

    def tokmajor_kv(lhs_ap, n_rows, i, v_bf16_dst, v_f32_dst, k_dst, cos_src, sin_src):
        pv = bank(i % 2)
        for kc in range(8):
            mm(pv, lhs_ap(kc), WQKV[:, kc, 1024:1536], start=(kc == 0), stop=(kc == 7))
        if v_bf16_dst is not None:
            cp('act', VB[i % 2], pv)
            P.dma('pool', v_bf16_dst, VB[i % 2][0:n_rows, :])
        if v_f32_dst is not None:
            cp('dve', VF[i % 2], pv)
            v_f32_dst(VF[i % 2])
        if k_dst is not None:
            pk = bank(2 + i % 2)
            for kc in range(8):
                mm(pk, lhs_ap(kc), WQKV[:, kc, 512:1024], start=(kc == 0), stop=(kc == 7))
            kf, ko, cs_ = KF[i % 2], KO[i % 2], CST[i % 2]
            cp('act', kf, pk)
            P.dma('sp', cs_[:, 0, :], cos_src)
            P.dma('sp', cs_[:, 1, :], sin_src)
            k4 = kf.rearrange("p (h a d) -> p h a d", a=2, d=32)
            o4 = ko.rearrange("p (h a d) -> p h a d", a=2, d=32)
            cb = cs_[:, 0, :].unsqueeze(1).to_broadcast([128, 8, 32])
            sb_ = cs_[:, 1, :].unsqueeze(1).to_broadcast([128, 8, 32])
            t = [x_.rearrange("p (h d) -> p h d", d=32) for x_ in KTMP]
            tt('dve', t[0], k4[:, :, 0, :], cb, ALU.mult)
            tt('pool', t[1], k4[:, :, 1, :], sb_, ALU.mult)
            tt('dve', o4[:, :, 0, :], t[0], t[1], ALU.subtract)
            tt('pool', t[2], k4[:, :, 0, :], sb_, ALU.mult)
            tt('dve', t[3], k4[:, :, 1, :], cb, ALU.mult)
            tt('pool', o4[:, :, 1, :], t[2], t[3], ALU.add)
            k_dst(ko)

    if 'B' in stages:
        for it in range(T // 128):
            t0 = 128 * it
            keep = t0 >= T - KEEP
            r0 = t0 - (T - KEEP)
            tokmajor_kv(lambda kc, t0=t0: HT[:, kc, t0:t0 + 128], 128, it, vs_s[t0:t0 + 128, :],
                        (lambda vf, r0=r0: P.dma('pool', o_pv[r0:r0 + 128, :], vf)) if keep else None,
                        (lambda ko, r0=r0: P.dma('pool', o_pk[r0:r0 + 128, :], ko)) if keep else None,
                        cost_p[r0:r0 + 128, :] if keep else None, sint_p[r0:r0 + 128, :] if keep else None)

    if 'S' in stages:
        memset('pool', ONESF, 1.0)
        memset('pool', ATTS, 0.0)
        for q in range(2):
            P.dma('sp', CSS[:, 0, 64 * q:64 * (q + 1)], cos_s[:, :])
            P.dma('sp', CSS[:, 1, 64 * q:64 * (q + 1)], sin_s[:, :])
        for tile in range(2):
            cp('dve', HTC2.rearrange("p k (q j) -> p k q j", j=64), HTS[:, :, 2 * tile:2 * tile + 2, 2:66])
            tokmajor_kv(lambda kc: HTC2[:, kc, :], 128, tile, None,
                        lambda vf, tile=tile: cp('pool', VNEW[:, tile, :], vf),
                        lambda ko, tile=tile: cp('pool', KNEW[:, tile, :], ko),
                        cost_s[:, :], sint_s[:, :])
            for q in range(2):
                seq = 2 * tile + q
                P.dma('pool', o_sk[seq, 2040:2048, :], KNEW[64 * q:64 * q + 8, tile, :])
                P.dma('pool', o_sv[seq, 2040:2048, :], VNEW[64 * q:64 * q + 8, tile, :])
            for hp in range(4):
                for which, c0_, dstF in ((0, 128 * hp, QSF), (1, 512 + 128 * hp, KSF)):
                    ps = bank(which)
                    for kc in range(8):
                        mm(ps[:, 0:128], WQKV[:, kc, c0_:c0_ + 128], HTC2[:, kc, :], start=(kc == 0), stop=(kc == 7))
                    cp('act', XBs, ps[:, 0:128])
                    pw_ = bank(7)
                    mm(pw_[:, 0:128], PERMF, XBs)
                    tt('pool', RT1s, XBs, CSS[:, 0, :], ALU.mult)
                    tt('dve', RT2s, pw_[:, 0:128], CSS[:, 1, :], ALU.mult)
                    tt('pool', dstF[:, hp, 128 * tile:128 * (tile + 1)], RT1s, RT2s, ALU.add)
    SB.release(mK)
    if 'S' in stages:
        KC = SB.alloc((17, 512), F32)
        KTC = SB.alloc((17 * 128,), F32)
        SSB = SB.alloc((17 * 128,), F32)
        LB = SB.alloc((17 * 128,), F32)
        PTs = SB.alloc((8, 17, 8), F32)
        P.dma('sp', LB[0:8, :], lb_s[:, :])
        for b_ in range(4):
            tile, q = b_ // 2, b_ % 2
            for t8 in range(0, 16, 4):
                P.dma('sp', KC[:, t8:t8 + 4, :], ck[b_, 128 * t8:128 * (t8 + 4), :].rearrange("(t p) c -> p t c", p=128))
            memset('pool', KC[:, 16, :], 0.0)
            cp('dve', KC[0:8, 16, :], KNEW[64 * q:64 * q + 8, tile, :])
            for h_ in range(8):
                pi, hp = h_ % 2, h_ // 2
                hs = slice(64 * pi, 64 * (pi + 1))
                hcol = slice(64 * h_, 64 * (h_ + 1))
                for g in range(5):
                    nt_ = 4 if g < 4 else 1
                    pk = bank(2 * (g % 2))
                    for j in range(nt_):
                        tr(pk[0:64, 128 * j:128 * (j + 1)], KC[:, 4 * g + j, hcol], IDF)
                    cp('act' if g % 2 else 'dve', KTC[hs, 512 * g:512 * g + 128 * nt_], pk[0:64, 0:128 * nt_])
                qv = QSF[hs, hp, 64 * b_:64 * b_ + 8]
                for g in range(5):
                    nn = 512 if g < 4 else 128
                    psc = bank(4 + pi + 2 * (g % 2))
                    mm(psc[0:8, 0:nn], qv, KTC[hs, 512 * g:512 * g + nn])
                    stt(SSB[0:8, 512 * g:512 * g + nn], psc[0:8, 0:nn], 0.125, LB[0:8, 512 * g:512 * g + nn], ALU.mult, ALU.add)
                P.op('dve', lambda e: e.tensor_reduce(out=SM[0:8, 0:1], in_=SSB[0:8, :], axis=AX.X, op=ALU.max),
                     [SM[0:8, 0:1]], [SSB[0:8, :]])
                ts('dve', SM[0:8, 1:2], SM[0:8, 0:1], -1.0, ALU.mult)
                act(SSB[0:8, :], SSB[0:8, :], AF.Exp, bias=SM[0:8, 1:2])
                pt_ = bank(0)
                for c_ in range(17):
                    tr(pt_[:, 8 * c_:8 * (c_ + 1)], SSB[0:8, 128 * c_:128 * (c_ + 1)], IDF[0:8, 0:8])
                cp('act', PTs[:, h_, :, :], pt_[:, 0:136].rearrange("p (c t) -> p c t", t=8))
            for t8 in range(0, 16, 4):
                P.dma('sp', KC[:, t8:t8 + 4, :], cv[b_, 128 * t8:128 * (t8 + 4), :].rearrange("(t p) c -> p t c", p=128))
            memset('pool', KC[:, 16, :], 0.0)
            cp('dve', KC[0:8, 16, :], VNEW[64 * q:64 * q + 8, tile, :])
            for h_ in range(8):
                pi, hp = h_ % 2, h_ // 2
                hs = slice(64 * pi, 64 * (pi + 1))
                hcol = slice(64 * h_, 64 * (h_ + 1))
                pn = bank(1)
                for c_ in range(17):
                    mm(pn[hs, 0:8], KC[:, c_, hcol], PTs[:, h_, c_, :], start=(c_ == 0), stop=(c_ == 16), tp=(0, 64 * pi))
                pd = bank(3)
                for c_ in range(17):
                    mm(pd[hs, 0:8], ONESF, PTs[:, h_, c_, :], start=(c_ == 0), stop=(c_ == 16), tp=(0, 64 * pi))
                cp('act', RD[hs, :], pd[hs, 0:8])
                P.op('dve', lambda e, hs=hs: e.reciprocal(out=RD[hs, :], in_=RD[hs, :]), [RD[hs, :]], [RD[hs, :]])
                tt('dve', ATTS[hs, hp, 64 * b_:64 * b_ + 8], pn[hs, 0:8], RD[hs, :], ALU.mult)
        for hp in range(4):
            P.dma('pool', mixs_s[512 + 128 * hp:512 + 128 * (hp + 1), :], ATTS[:, hp, :])
    SB.release(mS0)

    QT2 = SB.alloc((T, 2), BF16)
    KT2 = SB.alloc((T, 2), BF16)
    NV = {1: T // 128, 4: T // 128, 16: T // 128}
    VD = {d: SB.alloc((T // 128, 128), BF16) for d in (1, 4, 16)}
    NUMA = SB.alloc((2048,), F32)
    DENA = SB.alloc((2048,), F32)
    ATT = SB.alloc((2048,), BF16)
    XB = [SB.alloc((512,), BF16) for _ in range(2)]
    RT1 = [SB.alloc((512,), F32) for _ in range(2)]
    RT2 = [SB.alloc((512,), F32) for _ in range(2)]
    CSF = [SB.alloc((2, 512), F32) for _ in range(2)]
    EB = [SB.alloc((256,), BF16) for _ in range(8)]
    EM = [SB.alloc((256,), BF16) for _ in range(8)]
    PTS = [SB.alloc((2, 128), BF16) for _ in range(8)]
    BIAS = SB.alloc((8,), F32)
    DJ = [SB.alloc((128,), F32) for _ in range(2)]
    uidx = [0]

    def rope_fm(ps, xb, t1, t2, cs_, dst, n):
        cp('act', xb[:, 0:n], ps)
        pw_ = bank(7)
        mm(pw_[:, 0:n], PERM, xb[:, 0:n])
        tt('pool', t1[:, 0:n], xb[:, 0:n], cs_[:, 0, 0:n], ALU.mult)
        tt('dve', t2[:, 0:n], pw_[:, 0:n], cs_[:, 1, 0:n], ALU.mult)
        tt('pool', dst, t1[:, 0:n], t2[:, 0:n], ALU.add)

    for hp in (range(4) if ('B' in stages and BCUT >= 1) else []):
        qc = slice(128 * hp, 128 * (hp + 1))
        kcs = slice(512 + 128 * hp, 512 + 128 * (hp + 1))
        for it in range(NT):
            tb = 512 * it
            cs_ = CSF[it % 2]
            P.dma('sp', cs_[:, 0, :], cos_p[:, tb:tb + 512])
            P.dma('sp', cs_[:, 1, :], sin_p[:, tb:tb + 512])
            for which, cols, dstT in ((0, qc, QT2), (1, kcs, KT2)):
                ps = bank(which)
                for kc in range(8):
                    mm(ps, WQKV[:, kc, cols], HT[:, kc, tb:tb + 512], start=(kc == 0), stop=(kc == 7))
                rope_fm(ps, XB[which], RT1[which], RT2[which], cs_, dstT[:, tb:tb + 512, 0], 512)
        if BCUT < 2:
            continue
        for d in (1, 4, 16):
            npc = T // d // 128
            for r in range(d):
                src = vs_s[r:T:d, 128 * hp:128 * (hp + 1)].rearrange("(mt mi) c -> mi mt c", mi=128)
                for m0_ in range(0, npc, 8):
                    m1_ = min(npc, m0_ + 8)
                    P.dma('sp', VD[d][:, r * npc + m0_:r * npc + m1_, :], src[:, m0_:m1_, :])
        if BCUT < 3:
            continue
        for H in range(T // 2048):
            units = []
            for d in (1, 4, 16):
                npc = T // d // 128
                tph = 16 // d
                for r in range(d):
                    for mt in range(H * tph, (H + 1) * tph):
                        nkb = 1 if mt == 0 else 2
                        q0 = r + d * 128 * mt
                        k0 = r + d * 128 * (mt - 1) if nkb == 2 else q0
                        units.append((d, r, mt, nkb, q0, k0, q0 - 2048 * H, npc))
            GU = 4
            for g0 in range(0, len(units), GU):
                grp = units[g0:g0 + GU]
                hu = [(g, pi) for g in range(len(grp)) for pi in range(2)]
                for g, pi in hu:
                    d, r, mt, nkb, q0, k0, lo, npc = grp[g]
                    hs = slice(64 * pi, 64 * (pi + 1))
                    sb_ = bank(pi + 2 * (g % 2))[:, 256 * (g // 2):256 * (g // 2 + 1)]
                    c0 = 0 if nkb == 2 else 128
                    mm(sb_[:, c0:256], QT2[hs, sl(q0, 128, d), 0], KT2[hs, sl(k0, 128 * nkb, d), 0])
                for g, pi in hu:
                    sb_ = bank(pi + 2 * (g % 2))[:, 256 * (g // 2):256 * (g // 2 + 1)]
                    u = 2 * g + pi
                    stt(DJ[pi], sb_[:, 128:256], -0.125, IDF, ALU.mult, ALU.mult, accum=BIAS[:, u:u + 1])
                for g, pi in hu:
                    d, r, mt, nkb, q0, k0, lo, npc = grp[g]
                    sb_ = bank(pi + 2 * (g % 2))[:, 256 * (g // 2):256 * (g // 2 + 1)]
                    u = 2 * g + pi
                    c0 = 0 if nkb == 2 else 128
                    act(EB[u][:, c0:256], sb_[:, c0:256], AF.Exp, bias=BIAS[:, u:u + 1], scale=0.125)
                for g, pi in hu:
                    d, r, mt, nkb, q0, k0, lo, npc = grp[g]
                    u = 2 * g + pi
                    c0 = 0 if nkb == 2 else 128
                    tt('pool', EM[u][:, c0:256], EB[u][:, c0:256], MASK2[:, c0:256], ALU.mult)
                for g, pi in hu:
                    d, r, mt, nkb, q0, k0, lo, npc = grp[g]
                    u = 2 * g + pi
                    pt_ = bank(4 + pi).bitcast(BF16)[:, 256 * g:256 * (g + 1)]
                    for kb in range(2 - nkb, 2):
                        tr(pt_[:, 128 * kb:128 * (kb + 1)], EM[u][:, 128 * kb:128 * (kb + 1)], IDB)
                for g, pi in hu:
                    d, r, mt, nkb, q0, k0, lo, npc = grp[g]
                    u = 2 * g + pi
                    pt_ = bank(4 + pi).bitcast(BF16)[:, 256 * g:256 * (g + 1)]
                    cp('act' if pi == 0 else 'dve', PTS[u][:, 2 - nkb:2, :],
                       pt_[:, 128 * (2 - nkb):256].rearrange("p (k q) -> p k q", q=128))
                for g, pi in hu:
                    d, r, mt, nkb, q0, k0, lo, npc = grp[g]
                    u = 2 * g + pi
                    hs = slice(64 * pi, 64 * (pi + 1))
                    pn, pd = bank(6)[:, 128 * g:128 * (g + 1)], bank(7)[:, 128 * g:128 * (g + 1)]
                    for kb in range(2 - nkb, 2):
                        vt = r * npc + mt - 1 + kb
                        mm(pn[hs, :], VD[d][:, vt, hs], PTS[u][:, kb, :], start=(kb == 2 - nkb), stop=(kb == 1), tp=(0, 64 * pi))
                    for kb in range(2 - nkb, 2):
                        mm(pd[hs, :], ONESB, PTS[u][:, kb, :], start=(kb == 2 - nkb), stop=(kb == 1), tp=(0, 64 * pi))
                for g in range(len(grp)):
                    d, r, mt, nkb, q0, k0, lo, npc = grp[g]
                    pn, pd = bank(6)[:, 128 * g:128 * (g + 1)], bank(7)[:, 128 * g:128 * (g + 1)]
                    dstn = NUMA[:, sl(lo, 128, d)]
                    dstd = DENA[:, sl(lo, 128, d)]
                    if d == 1:
                        cp('dve', dstn, pn)
                        cp('act', dstd, pd)
                    else:
                        tt('dve', dstn, pn, dstn, ALU.add)
                        tt('dve', dstd, pd, dstd, ALU.add)
            P.op('dve', lambda e: e.reciprocal(out=DENA, in_=DENA), [DENA], [DENA])
            tt('pool', ATT, NUMA, DENA, ALU.mult)
            P.dma('pool', mixp_s[512 + 128 * hp:512 + 128 * (hp + 1), 2048 * H:2048 * (H + 1)], ATT)
    if debug:
        dbg['mix'] = dout("dbg_mix", [D, T], BF16)
        P.dma('pool', dbg['mix'], mixp_s)
    SB.release(m0)

    mC = SB.mark()
    WU = SB.alloc((8, 2 * DFF), BF16)
    WD = SB.alloc((22, D), BF16)
    GV = SB.alloc((3, D), F32)
    CW = SB.alloc((44, 4), F32)
    CAR = SB.alloc((5, 44, 2), F32)
    WOS = [SB.alloc((2, D), BF16) for _ in range(2)]
    mW = SB.mark()
    if 'C' in stages:
        alloc_wst()
        for i, gsrc in enumerate((n_mix_post, n_ffn_pre, n_ffn_post)):
            P.dma('sp', GV[:, i, :], gsrc[0:1, :].partition_broadcast(128))
        for j in range(3):
            dma_fc('sp', lambda c0, c1, j=j: CW[:, c0:c1, j], cw_d[j, :], False)
        dma_fc('sp', lambda c0, c1: CW[:, c0:c1, 3], cb_d.rearrange("f o -> (f o)"), False)
        for kc in range(8):
            rows = slice(128 * kc, 128 * (kc + 1))
            for j in range(0, 2 * DFF, 1024):
                n_ = min(1024, 2 * DFF - j)
                load_w(WU[:, kc, j:j + n_], w_up[rows, j:j + n_], n_)
        for fcn in range(22):
            load_w(WD[:, fcn, :], w_dn[128 * fcn:128 * (fcn + 1), :], 1024)
        for kc in range(8):
            load_w(WOS[kc % 2][:, 0, :], w_out[128 * kc:128 * (kc + 1), :], 1024)
            P.dma('pool', wout_s[128 * kc:128 * (kc + 1), :], WOS[kc % 2][:, 0, :])
    SB.release(mW)
    MIXT = [SB.alloc((8, 128), BF16) for _ in range(2)]
    WOF = [SB.alloc((512,), F32) for _ in range(2)]
    XC = [SB.alloc((D,), F32) for _ in range(2)]
    H2B = SB.alloc((D,), BF16)
    H2T = SB.alloc((8, 128), BF16)
    UE = [[SB.alloc((4, 130), F32)] * 2 for _ in range(2)]
    CGU = [SB.alloc((4, 128), F32) for _ in range(2)]
    CT = [SB.alloc((128,), F32) for _ in range(4)]
    FTB = [SB.alloc((22, 128), BF16) for _ in range(2)]
    JK = SB.alloc((D,), BF16)
    ST2 = SB.alloc((8, 4), F32)
    EPS2 = SB.alloc((1,), F32)
    ci = [0]
    pend = [None]

    def ffn_tile(*a_, **kw_):
        g_ = ffn_tile_gen(*a_, **kw_)
        next(g_)
        ffn_flush()
        next(g_)
        next(g_)
        pend[0] = g_

    def ffn_flush():
        if pend[0] is not None:
            for _ in pend[0]:
                pass
            pend[0] = None

    def rms_from(src_halves, si, col):
        act(JK[:, 0:512], src_halves[0], AF.Square, accum=ST2[:, si, 0:1])
        act(JK[:, 512:1024], src_halves[1], AF.Square, accum=ST2[:, si, 1:2])
        tt('dve', ST2[:, si, 0:1], ST2[:, si, 0:1], ST2[:, si, 1:2], ALU.add)
        act(ST2[:, si, 1:2], ST2[:, si, 0:1], AF.Sqrt, bias=EPS2[:, 0:1], scale=1.0 / D)
        P.op('dve', lambda e: e.reciprocal(out=ST2[:, si, col:col + 1], in_=ST2[:, si, 1:2]),
             [ST2[:, si, col:col + 1]], [ST2[:, si, 1:2]])

    def ffn_tile_gen(mix_src, x_src, y_dst, stream, segs, conv_out=None):
        i = ci[0] % 2
        si = ci[0] % 8
        ci[0] += 1
        mixt, xc = MIXT[i], XC[i]
        FT = FTB[i]
        P.dma('sp', mixt, mix_src.rearrange("(k p) t -> p k t", p=128))
        x_src(xc)
        for kq in range(4):
            wos = WOS[kq % 2]
            P.dma('sp', wos, wout_s[256 * kq:256 * (kq + 1), :].rearrange("(k p) c -> p k c", p=128))
            for k2 in range(2):
                kc = 2 * kq + k2
                for half in range(2):
                    mm(bank(half), mixt[:, kc, :], wos[:, k2, 512 * half:512 * (half + 1)], start=(kc == 0), stop=(kc == 7))
        yield 0
        rms_from((bank(0), bank(1)), si, 2)
        for half in range(2):
            hs_ = slice(512 * half, 512 * (half + 1))
            stt(WOF[half][:, 0:512], bank(half), ST2[:, si, 2:3], GV[:, 0, hs_], ALU.mult, ALU.mult)
            tt('pool', xc[:, hs_], xc[:, hs_], WOF[half][:, 0:512], ALU.add)
        rms_from((xc[:, 0:512], xc[:, 512:1024]), si, 3)
        stt(H2B, xc, ST2[:, si, 3:4], GV[:, 1, :], ALU.mult, ALU.mult)
        pb = bank(2).bitcast(BF16)
        for kc in range(8):
            tr(pb[:, 128 * kc:128 * (kc + 1)], H2B[:, 128 * kc:128 * (kc + 1)], IDB)
        cp('act', H2T, pb.rearrange("p (k t) -> p k t", t=128))
        g0 = 0
        gi = 0
        while g0 < 22:
            ng = min(4, 22 - g0)
            for side in range(2):
                ps = bank(3 + 2 * (gi % 2) + side)
                for c in range(ng):
                    ch = 22 * side + g0 + c
                    for kc in range(8):
                        mm(ps[:, 128 * c:128 * (c + 1)], WU[:, kc, 128 * ch:128 * (ch + 1)], H2T[:, kc, :],
                           start=(kc == 0), stop=(kc == 7))
            for side in range(2):
                ps = bank(3 + 2 * (gi % 2) + side)
                ue = UE[side][gi % 2]
                chs = slice(22 * side + g0, 22 * side + g0 + ng)
                cp('act', ue[:, 0:ng, 2:130], ps[:, 0:128 * ng].rearrange("p (c t) -> p c t", t=128))
                for (col0, ncols, cinit) in segs:
                    if cinit is None:
                        cp('pool', ue[:, 0:ng, col0:col0 + 2], CAR[:, stream, chs, :])
                    else:
                        cinit(ue[:, 0:ng, col0:col0 + 2], chs)
                for c in range(ng):
                    ch = 22 * side + g0 + c
                    t_ = CT[(2 * c + side) % 4]
                    act(t_, ue[:, c, 0:128], AF.Identity, bias=CW[:, ch, 3:4], scale=CW[:, ch, 0:1])
                    stt(t_, ue[:, c, 1:129], CW[:, ch, 1:2], t_, ALU.mult, ALU.add)
                    stt(CGU[side][:, c, :], ue[:, c, 2:130], CW[:, ch, 2:3], t_, ALU.mult, ALU.add)
                last_col = segs[-1][0] + segs[-1][1]
                cp('pool', CAR[:, stream, chs, :], ue[:, 0:ng, last_col:last_col + 2])
                if conv_out is not None:
                    conv_out(ue, chs, ng)
            act(CGU[0][:, 0:ng, :], CGU[0][:, 0:ng, :], AF.Silu)
            tt('pool', FT[:, g0:g0 + ng, :], CGU[0][:, 0:ng, :], CGU[1][:, 0:ng, :], ALU.mult)
            g0 += ng
            gi += 1
            if gi == 2:
                yield 1
        yield 2
        dbk = (bank(7), bank(2))
        for half in range(2):
            ps = dbk[half]
            for fcn in range(22):
                mm(ps, FT[:, fcn, :], WD[:, fcn, 512 * half:512 * (half + 1)], start=(fcn == 0), stop=(fcn == 21))
        rms_from(dbk, si, 2)
        for half in range(2):
            hs_ = slice(512 * half, 512 * (half + 1))
            stt(WOF[half][:, 0:512], dbk[half], ST2[:, si, 2:3], GV[:, 2, hs_], ALU.mult, ALU.mult)
            tt('pool', xc[:, hs_], xc[:, hs_], WOF[half][:, 0:512], ALU.add)
        y_dst(xc)

    if 'C' in stages:
        memset('pool', EPS2, NORM_EPS)
        memset('pool', CAR, 0.0)
        for it in range(T // 128):
            t0 = 128 * it
            ffn_tile(mixp_s[:, t0:t0 + 128],
                     lambda xc, t0=t0: P.dma('sp', xc, x_p[t0:t0 + 128, :]),
                     lambda xc, t0=t0: P.dma('pool', y_p[t0:t0 + 128, :], xc),
                     0, [(0, 128, None)])
        ffn_flush()
        if 'S' in stages:
            CONVS = SB.alloc((4, 44, 2), F32)
            SCONV = SB.alloc((4, 44, 2), F32)
            for seq in range(4):
                for r_ in range(2):
                    dma_fc('sp', lambda c0, c1, seq=seq, r_=r_: CONVS[:, seq, c0:c1, r_], st_conv[seq, r_, :], False)
            for tile in range(2):
                def xs_(xc, tile=tile):
                    memset('pool', xc, 0.0)
                    for q in range(2):
                        P.dma('sp', xc[64 * q:64 * q + 8, :], x_s[2 * tile + q, :, :])

                def ys_(xc, tile=tile):
                    for q in range(2):
                        P.dma('pool', y_s[2 * tile + q, :, :], xc[64 * q:64 * q + 8, :])

                def mk_init(seq):
                    return lambda dst, chs, seq=seq: cp('pool', dst, CONVS[:, seq, chs, :])

                def cout(ue, chs, ng, tile=tile):
                    for q in range(2):
                        cp('pool', SCONV[:, 2 * tile + q, chs, :], ue[:, 0:ng, 2 + 64 * q + 6:2 + 64 * q + 8])

                ffn_tile(mixs_s[:, 128 * tile:128 * (tile + 1)], xs_, ys_, 1,
                         [(0, 64, mk_init(2 * tile)), (64, 64, mk_init(2 * tile + 1))], conv_out=cout)
            ffn_flush()
            for seq in range(4):
                for r_ in range(2):
                    dma_fc('pool', lambda c0, c1, seq=seq, r_=r_: SCONV[:, seq, c0:c1, r_], o_sconv[seq, r_, :], True)
        for r_ in range(2):
            dma_fc('pool', lambda c0, c1, r_=r_: CAR[:, 0, c0:c1, r_], o_pconv[r_, :], True)
    SB.release(mC)

    P.final_wait_all('sp')
    P.build()
    return nc


def _rope_tables(T):
    half = 32
    inv = (np.float32(10000.0) ** (-np.arange(half, dtype=np.float32) / np.float32(half))).astype(np.float32)

    def tab(pos):
        ang = pos.astype(np.float32)[:, None] * inv[None, :]
        return np.cos(ang).astype(np.float32), np.sin(ang).astype(np.float32)
    cp_, sp_ = tab(np.arange(T))
    cs_, ss_ = tab(PAST_LEN + np.arange(64))
    def fm(c, s):
        cf = np.concatenate([c.T, c.T], 0)
        sf = np.concatenate([-s.T, s.T], 0)
        return (np.ascontiguousarray(np.concatenate([cf, cf], 0)), np.ascontiguousarray(np.concatenate([sf, sf], 0)))
    return fm(cp_, sp_), fm(cs_, ss_), (cp_, sp_), (cs_, ss_)


def _lb_table():
    cnt = np.zeros((8, 17 * 128), np.float64)
    for t in range(8):
        for d in (1, 4, 16):
            for j in range(129):
                idx = WBUF + t - j * d
                if idx >= 0:
                    cnt[t, idx] += 1
    lb = np.where(cnt > 0, np.log(np.maximum(cnt, 1)), -30000.0)
    return lb.astype(np.float32)


def _core_inputs(inp, c, T):
    (cfp, sfp), (cfs, sfs), (ctp, stp), (cts, sts) = _rope_tables(T)
    KEEP = min(WBUF, T)
    f = lambda a: np.ascontiguousarray(np.asarray(a, dtype=np.float32))
    d = {
        'x_prompt': f(inp['x_prompt'][c, :T]),
        'x_sample': f(inp['x_sample'][4 * c:4 * c + 4]),
        'state_rwkv_shift': f(inp['state_rwkv_shift'][0, 4 * c:4 * c + 4]),
        'state_rwkv_wkv': f(inp['state_rwkv_wkv'][0, 4 * c:4 * c + 4]),
        'cache_att_k': f(np.asarray(inp['cache_att_k'][0, 4 * c:4 * c + 4]).reshape(4, WBUF, 512)),
        'cache_att_v': f(np.asarray(inp['cache_att_v'][0, 4 * c:4 * c + 4]).reshape(4, WBUF, 512)),
        'state_ffn_conv': f(inp['state_ffn_conv'][0, 4 * c:4 * c + 4]),
        'norm_mix_pre': f(inp['norm_mix_pre']), 'norm_mix_post': f(inp['norm_mix_post']),
        'norm_ffn_pre': f(inp['norm_ffn_pre']), 'norm_ffn_post': f(inp['norm_ffn_post']),
        'w_in': f(inp['w_in'][0]), 'mu_shift': f(inp['mu_shift']),
        'w0': f(inp['w0']).reshape(512, 1), 'w_decay_up': f(inp['w_decay_up'][0]), 'a0': f(inp['a0']).reshape(512, 1),
        'w_iclr_up': f(inp['w_iclr_up'][0]), 'w_gate_up': f(inp['w_gate_up'][0]),
        'k_k': f(inp['k_k']).reshape(512, 1), 'k_a': f(inp['k_a']).reshape(512, 1), 'r_k': f(inp['r_k']).reshape(512, 1),
        'lnx_w': f(inp['lnx_w']).reshape(512, 1), 'lnx_b': f(inp['lnx_b']).reshape(512, 1),
        'w_out': f(inp['w_out'][0]), 'w_ffn_up': f(inp['w_ffn_up'][0]), 'ffn_conv_w': f(inp['ffn_conv_w'][0]),
        'ffn_conv_b': f(inp['ffn_conv_b']).reshape(2 * DFF, 1), 'w_ffn_down': f(inp['w_ffn_down'][0]),
        'cos_p': cfp, 'sin_p': sfp, 'cos_s': cfs, 'sin_s': sfs,
        'cost_p': np.ascontiguousarray(ctp[T - KEEP:]), 'sint_p': np.ascontiguousarray(stp[T - KEEP:]),
        'cost_s': np.ascontiguousarray(np.tile(cts[:64], (2, 1))), 'sint_s': np.ascontiguousarray(np.tile(sts[:64], (2, 1))),
        'lb_s': _lb_table(),
    }
    return d


_NC_CACHE = {}


def kernel(**inputs):
    T = 4096
    n = 8
    if T not in _NC_CACHE:
        _NC_CACHE[T] = build_program(T=T, debug=False, stages='NRBCS')
    nc = _NC_CACHE[T]
    in_maps = [_core_inputs(inputs, c, T) for c in range(n)]
    res = run_bass_kernel_spmd(nc, in_maps, core_ids=list(range(n)))
    rs = res.results
    g = lambda k: [np.asarray(r[k], dtype=np.float32) for r in rs]
    y_p = np.stack(g('y_prompt'), 0)
    y_s = np.concatenate(g('y_sample'), 0)
    p_shift = np.stack([a.reshape(D) for a in g('p_shift')], 0)[None]
    p_wkv = np.stack(g('p_wkv'), 0)[None]
    p_k = np.stack([a.reshape(WBUF, 8, 64) for a in g('p_att_k')], 0)[None]
    p_v = np.stack([a.reshape(WBUF, 8, 64) for a in g('p_att_v')], 0)[None]
    p_conv = np.stack(g('p_conv'), 0)[None]
    s_shift = np.concatenate(g('s_shift'), 0)[None]
    s_wkv = np.concatenate(g('s_wkv'), 0)[None]
    s_k = np.concatenate([a.reshape(4, WBUF, 8, 64) for a in g('s_att_k')], 0)[None]
    s_v = np.concatenate([a.reshape(4, WBUF, 8, 64) for a in g('s_att_v')], 0)[None]
    s_conv = np.concatenate(g('s_conv'), 0)[None]
    return (y_p, y_s, p_shift, p_wkv, p_k, p_v, p_conv, s_shift, s_wkv, s_k, s_v, s_conv)
```

```python
import contextlib
import os
RCUT = int(os.environ.get('RCUT', '9'))
RPRE = int(os.environ.get('RPRE', '99'))
RSUB = int(os.environ.get('RSUB', '99'))
BCUT = int(os.environ.get('BCUT', '99'))
import numpy as np
import ml_dtypes
import concourse.bass as bass
import concourse.mybir as mybir
from concourse.bass_utils import run_bass_kernel_spmd

F32 = mybir.dt.float32
BF16 = mybir.dt.bfloat16
AF = mybir.ActivationFunctionType
ALU = mybir.AluOpType
AX = mybir.AxisListType

D = 1024
HD = 64
RW = 512
NCOL_R = 1792
D_IN = 3328
DFF = 2816
PAST_LEN = 16384
WBUF = 2048
NORM_EPS = 1e-6
LNX_EPS = 64e-5
C0 = float(np.exp(-0.5))

_ESZ = {F32: 4, BF16: 2}


def _box(ap):
    name = ap.tensor.name
    dims = ap.ap
    off = int(ap.offset)
    esz = _ESZ.get(ap.dtype, 4)
    if 'DRAM' in str(ap.space).upper():
        ext = sum((c - 1) * abs(s) for s, c in dims)
        return (name, 0, 1, off * esz, (off + ext + 1) * esz)
    pstride = dims[0][0]
    pcount = dims[0][1]
    if pstride == 0:
        p0, f0 = 0, off
    else:
        p0 = off // pstride
        f0 = off - p0 * pstride
    ext = sum((c - 1) * abs(s) for s, c in dims[1:])
    b0, b1 = f0 * esz, (f0 + ext + 1) * esz
    if 'PSUM' in str(ap.space).upper():
        return (name, (p0 // 32) * 32, ((p0 + pcount + 31) // 32) * 32, (b0 // 2048) * 2048, ((b1 + 2047) // 2048) * 2048)
    return (name, p0, p0 + pcount, b0, b1)


class Prog:
    COMPUTE = ('pe', 'act', 'dve', 'pool')

    def __init__(self, nc, n_dma_sems=(20, 12, 8)):
        self.nc = nc
        self.ops = {e: [] for e in ('pe', 'act', 'dve', 'pool', 'sp')}
        self.nseq = {e: 0 for e in self.COMPUTE}
        self.waited = {}
        self.recs = {}
        self.dmaq = {'sp': [n_dma_sems[0], 0], 'pool': [n_dma_sems[1], 0], 'act': [n_dma_sems[2], 0]}
        self.sems = {}

    def _deps(self, eng, outs, ins, waiter=None):
        waiter = waiter or eng
        deps = {}
        for ap, is_w in [(a, True) for a in outs] + [(a, False) for a in ins]:
            name, p0, p1, f0, f1 = _box(ap)
            for r in self.recs.get(name, ()):
                if r[0] < p1 and p0 < r[1] and r[2] < f1 and f0 < r[3]:
                    if is_w or r[6] or (name == 'PS' and r[7] != eng):
                        if r[7] == eng and eng in self.COMPUTE:
                            if eng == 'pe':
                                continue
                            if not (r[6] and not is_w):
                                continue
                        if deps.get(r[4], 0) < r[5]:
                            deps[r[4]] = r[5]
        out = []
        for key, val in deps.items():
            if self.waited.get((waiter, key), 0) < val:
                self.waited[(waiter, key)] = val
                out.append((key, val))
        return out

    def _record(self, eng, outs, ins, key, val):
        for ap, is_w in [(a, True) for a in outs] + [(a, False) for a in ins]:
            name, p0, p1, f0, f1 = _box(ap)
            lst = self.recs.setdefault(name, [])
            if is_w:
                lst[:] = [r for r in lst if not (p0 <= r[0] and r[1] <= p1 and f0 <= r[2] and r[3] <= f1)]
            else:
                lst[:] = [r for r in lst if not ((not r[6]) and r[4] == key and p0 <= r[0] and r[1] <= p1
                                                 and f0 <= r[2] and r[3] <= f1)]
            lst.append([p0, p1, f0, f1, key, val, is_w, eng])

    def op(self, eng, fn, outs, ins):
        waits = self._deps(eng, outs, ins)
        self.nseq[eng] += 1
        key = 'c_' + eng
        self._record(eng, outs, ins, key, self.nseq[eng])
        self.ops[eng].append(('c', waits, fn, key))

    def dma(self, q, out, in_, **kw):
        ns, cnt = self.dmaq[q]
        slot = cnt % ns
        rnd = cnt // ns
        self.dmaq[q][1] += 1
        key = 'd_%s_%d' % (q, slot)
        waits = self._deps('dma_' + q, [out], [in_], waiter=q)
        if rnd > 0 and self.waited.get((q, key), 0) < 16 * rnd:
            self.waited[(q, key)] = 16 * rnd
            waits.append((key, 16 * rnd))
        self._record('dma_' + q, [out], [in_], key, 16 * (rnd + 1))
        self.ops[q].append(('d', waits, (out, in_, kw), key))

    def final_wait_all(self, eng='sp'):
        waits = []
        for q, (ns, cnt) in self.dmaq.items():
            for slot in range(min(ns, cnt)):
                nuse = (cnt - slot + ns - 1) // ns
                waits.append(('d_%s_%d' % (q, slot), 16 * nuse))
        for e in self.COMPUTE:
            if self.nseq[e] > 0:
                waits.append(('c_' + e, self.nseq[e]))
        self.ops[eng].append(('w', waits, None, None))

    def build(self):
        nc = self.nc
        keys = set()
        for e, lst in self.ops.items():
            for kind, waits, payload, key in lst:
                if key:
                    keys.add(key)
                for k, v in waits:
                    keys.add(k)
        with contextlib.ExitStack() as st:
            for k in sorted(keys):
                self.sems[k] = st.enter_context(nc.semaphore(k))
            block = st.enter_context(nc.Block())
            sems = self.sems

            def runner(lst):
                def run(e):
                    for kind, waits, payload, key in lst:
                        for k, v in waits:
                            e.wait_ge(sems[k], v)
                        if kind == 'c':
                            payload(e).then_inc(sems[key], 1)
                        elif kind == 'd':
                            out, in_, kw = payload
                            e.dma_start(out=out, in_=in_, **kw).then_inc(sems[key], 16)
                return run

            block.tensor(runner(self.ops['pe']))
            block.scalar(runner(self.ops['act']))
            block.vector(runner(self.ops['dve']))
            block.gpsimd(runner(self.ops['pool']))
            block.sync(runner(self.ops['sp']))


class Arena:
    def __init__(self, nc, name, nbytes):
        self.t = nc.alloc_sbuf_tensor(name, [128, nbytes // 4], F32)
        self.cap = nbytes
        self.off = 0
        self.peak = 0

    def alloc(self, shape, dt):
        shape = tuple(int(s) for s in shape)
        n = int(np.prod(shape))
        nb = (n * _ESZ[dt] + 63) // 64 * 64
        off = self.off
        self.off += nb
        self.peak = max(self.peak, self.off)
        assert self.off <= self.cap, ("SBUF arena overflow", self.off, self.cap)
        ap = self.t[:, off // 4:(off + nb) // 4]
        if dt != F32:
            ap = ap.bitcast(dt)
        ap = ap[:, 0:n]
        if len(shape) > 1:
            names = ['a%d' % i for i in range(len(shape))]
            pat = "p (%s) -> p %s" % (' '.join(names), ' '.join(names))
            ap = ap.rearrange(pat, **{nm: s for nm, s in zip(names, shape)})
        return ap

    def mark(self):
        return self.off

    def release(self, m):
        self.off = m


def build_program(T=4096, debug=False, stages='ABC'):
    RT = 256
    NT = T // 512
    nc = bass.Bass("TRN2", target_bir_lowering=False)
    P = Prog(nc)
    KEEP = min(WBUF, T)

    def din(name, shape, dt=F32):
        return nc.dram_tensor(name, list(shape), dt, kind="ExternalInput").ap()

    def dout(name, shape, dt=F32):
        return nc.dram_tensor(name, list(shape), dt, kind="ExternalOutput").ap()

    def dscr(name, shape, dt=BF16):
        return nc.dram_tensor(name, list(shape), dt, kind="Internal").ap()

    x_p = din("x_prompt", [T, D])
    x_s = din("x_sample", [4, 8, D])
    st_shift = din("state_rwkv_shift", [4, D])
    st_wkv = din("state_rwkv_wkv", [4, 8, 64, 64])
    ck = din("cache_att_k", [4, WBUF, 512])
    cv = din("cache_att_v", [4, WBUF, 512])
    st_conv = din("state_ffn_conv", [4, 2, 2 * DFF])
    n_mix_pre = din("norm_mix_pre", [1, D])
    n_mix_post = din("norm_mix_post", [1, D])
    n_ffn_pre = din("norm_ffn_pre", [1, D])
    n_ffn_post = din("norm_ffn_post", [1, D])
    w_in = din("w_in", [D, D_IN])
    mu_shift = din("mu_shift", [1, NCOL_R])
    w0_d = din("w0", [RW, 1])
    w_decay_up = din("w_decay_up", [64, RW])
    a0_d = din("a0", [RW, 1])
    w_iclr_up = din("w_iclr_up", [64, RW])
    w_gate_up = din("w_gate_up", [128, RW])
    k_k_d = din("k_k", [RW, 1])
    k_a_d = din("k_a", [RW, 1])
    r_k_d = din("r_k", [RW, 1])
    lnx_w_d = din("lnx_w", [RW, 1])
    lnx_b_d = din("lnx_b", [RW, 1])
    w_out = din("w_out", [D, D])
    w_up = din("w_ffn_up", [D, 2 * DFF])
    cw_d = din("ffn_conv_w", [3, 2 * DFF])
    cb_d = din("ffn_conv_b", [2 * DFF, 1])
    w_dn = din("w_ffn_down", [DFF, D])
    cos_p = din("cos_p", [128, T])
    sin_p = din("sin_p", [128, T])
    cos_s = din("cos_s", [128, 64])
    sin_s = din("sin_s", [128, 64])
    cost_p = din("cost_p", [KEEP, 32])
    sint_p = din("sint_p", [KEEP, 32])
    cost_s = din("cost_s", [128, 32])
    sint_s = din("sint_s", [128, 32])

    lb_s = din("lb_s", [8, 17 * 128])

    y_p = dout("y_prompt", [T, D])
    y_s = dout("y_sample", [4, 8, D])
    o_pshift = dout("p_shift", [1, D])
    o_pwkv = dout("p_wkv", [8, 64, 64])
    o_pk = dout("p_att_k", [KEEP, 512])
    o_pv = dout("p_att_v", [KEEP, 512])
    o_pconv = dout("p_conv", [2, 2 * DFF])
    o_sshift = dout("s_shift", [4, D])
    o_swkv = dout("s_wkv", [4, 8, 64, 64])
    o_sk = dout("s_att_k", [4, WBUF, 512])
    o_sv = dout("s_att_v", [4, WBUF, 512])
    o_sconv = dout("s_conv", [4, 2, 2 * DFF])
    dbg = {}

    n_mixp, n_mixs, n_vs = D * T, D * 256, T * 512
    SCR = dscr("SCR", [n_mixp + n_mixs + n_vs + D * D])
    mixp_s = SCR[0:n_mixp].rearrange("(f t) -> f t", t=T)
    mixs_s = SCR[n_mixp:n_mixp + n_mixs].rearrange("(f t) -> f t", t=256)
    vs_s = SCR[n_mixp + n_mixs:n_mixp + n_mixs + n_vs].rearrange("(t c) -> t c", c=512)
    wout_s = SCR[n_mixp + n_mixs + n_vs:n_mixp + n_mixs + n_vs + D * D].rearrange("(k c) -> k c", c=D)

    SB = Arena(nc, "SB", 212480)
    PS = nc.alloc_psum_tensor("PS", [128, 4096], F32)

    def bank(k, a=0, b=512):
        return PS[:, 512 * k + a:512 * k + b]

    def dma_fc(q, sb_fn, dr_row, write):
        for c0 in range(0, 44, 8):
            c1 = min(44, c0 + 8)
            d_ = dr_row[128 * c0:128 * c1].rearrange("(c p) -> p c", p=128)
            if write:
                P.dma(q, d_, sb_fn(c0, c1), allow_slow_non_contiguous=True)
            else:
                P.dma(q, sb_fn(c0, c1), d_, allow_slow_non_contiguous=True)

    def sl(start, n, step):
        return slice(start, start + step * (n - 1) + 1, step)

    def tt(eng, out, in0, in1, op):
        P.op(eng, lambda e: e.tensor_tensor(out=out, in0=in0, in1=in1, op=op), [out], [in0, in1])

    def ts(eng, out, in0, s1, op0, s2=None, op1=None):
        ins = [in0] + [s for s in (s1, s2) if not isinstance(s, (int, float, type(None)))]
        if op1 is None:
            P.op(eng, lambda e: e.tensor_scalar(out=out, in0=in0, scalar1=s1, scalar2=None, op0=op0), [out], ins)
        else:
            P.op(eng, lambda e: e.tensor_scalar(out=out, in0=in0, scalar1=s1, scalar2=s2, op0=op0, op1=op1), [out], ins)

    def stt(out, in0, scalar, in1, op0, op1, accum=None):
        ins = [in0, in1] + ([scalar] if not isinstance(scalar, (int, float)) else [])
        outs = [out] + ([accum] if accum is not None else [])
        if accum is None:
            P.op('dve', lambda e: e.scalar_tensor_tensor(out=out, in0=in0, scalar=scalar, in1=in1, op0=op0, op1=op1), outs, ins)
        else:
            P.op('dve', lambda e: e.scalar_tensor_tensor(out=out, in0=in0, scalar=scalar, in1=in1, op0=op0, op1=op1,
                                                         accum_out=accum), outs, ins)

    def act(out, in_, func, bias=None, scale=None, accum=None):
        ins = [in_] + [s for s in (bias, scale) if not isinstance(s, (int, float, type(None)))]
        outs = [out] + ([accum] if accum is not None else [])
        kw = {}
        if bias is not None:
            kw['bias'] = bias
        if scale is not None:
            kw['scale'] = scale
        if accum is not None:
            kw['accum_out'] = accum
        P.op('act', lambda e: e.activation(out=out, in_=in_, func=func, **kw), outs, ins)

    def cp(eng, out, in_):
        if eng == 'act':
            P.op('act', lambda e: e.copy(out=out, in_=in_), [out], [in_])
        else:
            P.op(eng, lambda e: e.tensor_copy(out=out, in_=in_), [out], [in_])

    pe_hist = []

    def pe_check(out, lhsT):
        import traceback
        _, p0, p1, f0, f1 = _box(lhsT)
        rg = set(range(p0 // 32, (p1 + 31) // 32))
        _, _, _, b0, b1 = _box(out)
        bk = set(range(b0 // 2048, (b1 + 2047) // 2048))
        for (rg2, bk2, where) in pe_hist[-24:]:
            if not (rg & rg2) and (bk & bk2):
                print("PE ROW-GROUP/BANK HAZARD:", sorted(rg), sorted(bk), "vs", sorted(rg2), sorted(bk2), where,
                      "|", traceback.extract_stack(limit=4)[0].lineno)
        pe_hist.append((rg, bk, traceback.extract_stack(limit=4)[0].lineno))

    def mm(out, lhsT, rhs, start=True, stop=True, tp=None):
        pe_check(out, lhsT)
        if tp is None:
            P.op('pe', lambda e: e.matmul(out, lhsT=lhsT, rhs=rhs, start=start, stop=stop), [out], [lhsT, rhs])
        else:
            P.op('pe', lambda e: e.matmul(out, lhsT=lhsT, rhs=rhs, start=start, stop=stop, tile_position=tp), [out], [lhsT, rhs])

    def tr(out, in_, ident):
        pe_check(out, in_)
        P.op('pe', lambda e: e.transpose(out, in_, ident), [out], [in_, ident])

    def memset(eng, out, val):
        P.op(eng, lambda e: e.memset(out, val), [out], [])

    def asel(out, pattern, base, cm, cmp, fill=0.0):
        P.op('pool', lambda e: e.affine_select(out=out, in_=out, pattern=pattern, compare_op=cmp, fill=fill,
                                               base=base, channel_multiplier=cm), [out], [out])

    rr = {'n': 0}

    def rot(engs):
        rr['n'] += 1
        return engs[rr['n'] % len(engs)]

    if 'S' in stages:
        for b_ in range(4):
            for (src_, dst_) in ((ck, o_sk), (cv, o_sv)):
                for c_ in range(4):
                    P.dma('act', dst_[b_, 510 * c_:510 * (c_ + 1), :], src_[b_, 8 + 510 * c_:8 + 510 * (c_ + 1), :])

    IDF = SB.alloc((128,), F32)
    IDB = SB.alloc((128,), BF16)
    BDO = SB.alloc((128,), F32)
    BDM = SB.alloc((128,), F32)
    MSU = SB.alloc((128,), F32)
    MIU = SB.alloc((128,), F32)
    MSL = SB.alloc((128,), F32)
    MAB = SB.alloc((2, 128), F32)
    PERM = SB.alloc((128,), BF16)
    PERMF = SB.alloc((128,), F32)
    MASK2 = SB.alloc((256,), BF16)
    ONESB = SB.alloc((64,), BF16)
    PAR = SB.alloc((4, 8), F32)
    m0 = SB.mark()
    HT = SB.alloc((8, T), BF16)
    HTS = SB.alloc((8, 4, 66), BF16)
    mA = SB.mark()
    RESET = SB.alloc((512,), F32)
    tmpc = SB.alloc((256,), F32)

    memset('pool', IDF, 1.0)
    asel(IDF, [[-1, 128]], 0, 1, ALU.is_equal)
    cp('dve', IDB, IDF)
    memset('pool', BDO, 1.0)
    memset('pool', BDO[0:64, 64:128], 0.0)
    memset('pool', BDO[64:128, 0:64], 0.0)
    ts('dve', BDM, BDO, 1.0 / 64, ALU.mult)
    memset('pool', MSU, 1.0)
    asel(MSU, [[1, 128]], 0, -1, ALU.is_gt)
    tt('dve', MSU, MSU, BDO, ALU.mult)
    memset('pool', MIU, 1.0)
    asel(MIU, [[1, 128]], 0, -1, ALU.is_ge)
    tt('dve', MIU, MIU, BDO, ALU.mult)
    memset('pool', MSL, 1.0)
    asel(MSL, [[-1, 128]], 0, 1, ALU.is_gt)
    tt('dve', MSL, MSL, BDO, ALU.mult)
    cp('dve', MAB[:, 0, :], MSU)
    cp('dve', MAB[:, 1, :], MIU)
    memset('pool', tmpc[:, 0:128], 1.0)
    asel(tmpc[:, 0:128], [[-1, 128]], -32, 1, ALU.is_equal)
    memset('pool', tmpc[:, 32:64], 0.0)
    memset('pool', tmpc[:, 96:128], 0.0)
    memset('pool', tmpc[:, 128:256], 1.0)
    asel(tmpc[:, 128:256], [[-1, 128]], 32, 1, ALU.is_equal)
    memset('pool', tmpc[:, 128:160], 0.0)
    memset('pool', tmpc[:, 192:224], 0.0)
    tt('dve', PERM, tmpc[:, 0:128], tmpc[:, 128:256], ALU.add)
    tt('dve', PERMF, tmpc[:, 0:128], tmpc[:, 128:256], ALU.add)
    memset('pool', tmpc[:, 0:256], 1.0)
    asel(tmpc[:, 0:128], [[1, 128]], 0, -1, ALU.is_ge)
    asel(tmpc[:, 128:256], [[-1, 128]], 0, 1, ALU.is_ge)
    cp('dve', MASK2, tmpc[:, 0:256])
    memset('pool', RESET, 1.0)
    for c in range(8):
        memset('pool', RESET[:, 64 * c:64 * c + 1], 0.0)
    memset('pool', ONESB, 1.0)

    P_W0, P_A0, P_KK, P_KA, P_RK, P_LW, P_LB, P_OMKA = range(8)
    for i, d_ in enumerate((w0_d, a0_d, k_k_d, k_a_d, r_k_d, lnx_w_d, lnx_b_d)):
        for fc in range(4):
            P.dma('sp', PAR[:, fc, i:i + 1], d_[128 * fc:128 * (fc + 1), :])
    ts('dve', PAR[:, :, P_OMKA], PAR[:, :, P_KA], -1.0, ALU.mult, 1.0, ALU.add)
    WDU = SB.alloc((RW,), F32)
    WIU = SB.alloc((RW,), F32)
    WGU = SB.alloc((RW,), F32)
    P.dma('sp', WDU[0:64, :], w_decay_up[:, :])
    P.dma('sp', WIU[64:128, :], w_iclr_up[:, :])
    P.dma('sp', WGU[:, :], w_gate_up[:, :])
    GPRE = SB.alloc((D,), F32)
    P.dma('sp', GPRE, n_mix_pre[0:1, :].partition_broadcast(128))

    WSTh = {'b': None}
    wi = [0]

    def alloc_wst():
        WSTh['b'] = [SB.alloc((1024,), F32) for _ in range(3)]

    def load_w(dst, src, ncol, scale_pairs=None):
        i = wi[0] % 3
        wi[0] += 1
        WST = WSTh['b']
        P.dma('sp', WST[i][:, 0:ncol], src)
        if scale_pairs is None:
            cp(rot(['act', 'dve', 'pool']), dst, WST[i][:, 0:ncol])
        else:
            for (d_, sc), eng in zip(scale_pairs, ('dve', 'pool')):
                tt(eng, d_, WST[i][:, 0:ncol], sc, ALU.mult)

    W12 = SB.alloc((8, 2, NCOL_R), BF16)
    mW = SB.mark()
    alloc_wst()
    MU = SB.alloc((NCOL_R,), F32)
    OMU = SB.alloc((NCOL_R,), F32)
    P.dma('sp', MU, mu_shift[0:1, :].partition_broadcast(128))
    ts('dve', OMU, MU, -1.0, ALU.mult, 1.0, ALU.add)
    for kc in (range(8) if 'x' not in stages else []):
        rows = slice(128 * kc, 128 * (kc + 1))
        for c0, c1 in ((0, 1024), (1024, NCOL_R)):
            load_w(None, w_in[rows, c0:c1], c1 - c0,
                   scale_pairs=[(W12[:, kc, 0, c0:c1], OMU[:, c0:c1]), (W12[:, kc, 1, c0:c1], MU[:, c0:c1])])
    SB.release(mW)
    XT = [SB.alloc((D,), F32) for _ in range(2)]
    HB = [SB.alloc((D,), BF16) for _ in range(2)]
    STAT = SB.alloc((8, 4), F32)
    EPSC = SB.alloc((2,), F32)
    memset('pool', EPSC[:, 0:1], NORM_EPS)
    memset('pool', EPSC[:, 1:2], LNX_EPS)
    sti = [0]

    def norm_tile(x_rows_fn, dst_fn, want_f32_row=None, hprev_fn=None):
        i = sti[0] % 2
        si = sti[0] % 8
        sti[0] += 1
        xt, hb = XT[i], HB[i]
        x_rows_fn(xt)
        act(hb, xt, AF.Square, accum=STAT[:, si, 0:1])
        act(STAT[:, si, 1:2], STAT[:, si, 0:1], AF.Sqrt, bias=EPSC[:, 0:1], scale=1.0 / D)
        P.op('dve', lambda e: e.reciprocal(out=STAT[:, si, 2:3], in_=STAT[:, si, 1:2]), [STAT[:, si, 2:3]], [STAT[:, si, 1:2]])
        stt(hb, xt, STAT[:, si, 2:3], GPRE, ALU.mult, ALU.mult)
        if want_f32_row is not None:
            stt(xt, xt, STAT[:, si, 2:3], GPRE, ALU.mult, ALU.mult)
            want_f32_row(xt)
        if hprev_fn is not None:
            hprev_fn(hb)
        pb = bank(2 + i).bitcast(BF16)
        for kc in range(8):
            tr(pb[:, 128 * kc:128 * (kc + 1)], hb[:, 128 * kc:128 * (kc + 1)], IDB)
        return pb

    RB = {}
    for nm in ('r', 'k', 'v', 'sgw', 'a', 'g', 'cs', 'csx', 'crem', 'ein', 'eneg', 'kk', 't1', 'kkn',
               'kmod', 'bb', 'bonv', 'YT'):
        RB[nm] = SB.alloc((RT,), F32)
    BTB = SB.alloc((RT,), BF16)
    KTB = SB.alloc((RT,), BF16)
    RB['eex'] = RB['csx']
    RB['erem'] = RB['crem']
    RB['BP'] = RB['bb']
    RB['KP'] = RB['kmod']
    AR = SB.alloc((RT // 128, 2, 128), F32)
    TW = SB.alloc((RT,), F32)
    AL = SB.alloc((RT,), F32)
    SG = SB.alloc((RT,), F32)
    NSTM = RT // 128
    TOKB = SB.alloc((NSTM, 4, 128), F32)
    TOK = [TOKB[:, i_, :, :] for i_ in range(NSTM)]
    TOKH = [RB[nm_].bitcast(BF16).rearrange("p (q c) -> p q c", c=128) for nm_ in ('eneg', 'kk')]
    ARB = RB['kkn'].bitcast(BF16).rearrange("p (s a t) -> p s a t", a=2, t=128)
    NM = [SB.alloc((2, 2, 128), BF16) for _ in range(NSTM)]
    AK = [SB.alloc((2, 2, 128), BF16) for _ in range(NSTM)]
    XP = [[SB.alloc((2, 2, 128), BF16) for _ in range(2)] for _ in range(NSTM)]
    YY = [[SB.alloc((2, 128), BF16) for _ in range(2)] for _ in range(NSTM)]
    Z1 = [SB.alloc((2, 64), BF16) for _ in range(NSTM)]
    UW = [SB.alloc((2, 2, 64), BF16) for _ in range(NSTM)]
    WTs = [SB.alloc((128,), F32) for _ in range(NSTM)]
    GT = [SB.alloc((128,), F32) for _ in range(NSTM)]
    HTt = [SB.alloc((128,), F32) for _ in range(NSTM)]
    US = SB.alloc((2, 64), F32)
    SST = SB.alloc((5, 4, 64), F32)
    ROB = RB['sgw'].bitcast(BF16)[:, 0:RT]
    HTP = SB.alloc((8, RT), BF16)
    SNAT = TOK[0].rearrange("p q c -> p (q c)").rearrange("p (h j) -> p h j", j=64)

    def rwkv_fc(fc, ntok, ps_r, ps_k, ps_v, state_of_chunk, CV, out_dst):
        nst = ntok // 128
        n = ntok
        par = lambda i: PAR[:, fc, i:i + 1]
        B = {k_: v_[:, 0:n] for k_, v_ in RB.items()}
        cp('act', B['r'], ps_r)
        cp('act', B['k'], ps_k)
        cp('act', B['v'], ps_v)
        if RPRE < 1:
            return
        cols = slice(128 * fc, 128 * (fc + 1))
        pz = bank(1)
        mm(bank(4)[:, 0:n], WDU[0:64, cols], TW[0:64, 0:n])
        act(B['sgw'], bank(4)[:, 0:n], AF.Sigmoid, bias=par(P_W0))
        mm(bank(5)[:, 0:n], WIU[64:128, cols], AL[64:128, 0:n])
        act(B['a'], bank(5)[:, 0:n], AF.Sigmoid, bias=par(P_A0))
        mm(pz[:, 0:n], WGU[:, cols], SG[:, 0:n])
        cp('act', B['g'], pz[:, 0:n])
        if RPRE < 2:
            return
        P.op('dve', lambda e: e.tensor_tensor_scan(out=B['cs'], data0=RESET[:, 0:n], data1=B['sgw'], initial=0.0,
                                                   op0=ALU.mult, op1=ALU.add), [B['cs']], [RESET[:, 0:n], B['sgw']])
        if RPRE < 3:
            return
        tt('dve', B['csx'], B['cs'], B['sgw'], ALU.subtract)
        nch = n // 64
        cs3 = B['cs'].rearrange("p (c t) -> p c t", t=64)
        tt('dve', B['crem'].rearrange("p (c t) -> p c t", t=64), cs3[:, :, CV - 1:CV].to_broadcast([128, nch, 64]), cs3,
           ALU.subtract)
        if RPRE < 4:
            return
        act(B['ein'], B['cs'], AF.Exp, scale=-C0)
        act(B['eex'], B['csx'], AF.Exp, scale=-C0)
        act(B['eneg'], B['cs'], AF.Exp, scale=C0)
        act(B['erem'], B['crem'], AF.Exp, scale=-C0)
        if RPRE < 5:
            return
        ts('dve', B['kk'], B['k'], par(P_KK), ALU.mult)
        act(B['t1'], B['kk'], AF.Square)
        mm(pz[:, 0:n], BDO, B['t1'])
        act(B['t1'], pz[:, 0:n], AF.Sqrt)
        ts('dve', B['t1'], B['t1'], 1e-12, ALU.max)
        P.op('dve', lambda e: e.reciprocal(out=B['t1'], in_=B['t1']), [B['t1']], [B['t1']])
        tt('dve', B['kkn'], B['kk'], B['t1'], ALU.mult)
        if RPRE < 6:
            return
        ts('dve', B['t1'], B['a'], par(P_KA), ALU.mult, par(P_OMKA), ALU.add)
        tt('pool', B['kmod'], B['k'], B['t1'], ALU.mult)
        tt('dve', B['bb'], B['kkn'], B['a'], ALU.mult)
        if RPRE < 7:
            return
        ar = AR[:, 0:nst, :, :]
        stt(ar[:, :, 0, :], B['kkn'].rearrange("p (s t) -> p s t", t=128), -1.0,
            B['eex'].rearrange("p (s t) -> p s t", t=128), ALU.mult, ALU.mult)
        tt('pool', ar[:, :, 1, :], B['r'].rearrange("p (s t) -> p s t", t=128),
           B['ein'].rearrange("p (s t) -> p s t", t=128), ALU.mult)
        if RPRE < 8:
            return
        stt(B['t1'], B['r'], par(P_RK), B['kmod'], ALU.mult, ALU.mult)
        mm(pz[:, 0:n], BDO, B['t1'])
        tt('dve', B['bonv'], pz[:, 0:n], B['v'], ALU.mult)
        tt('dve', BTB[:, 0:n], B['bb'], B['eneg'], ALU.mult)
        tt('pool', KTB[:, 0:n], B['kmod'], B['eneg'], ALU.mult)
        tt('dve', B['BP'], B['bb'], B['erem'], ALU.mult)
        tt('pool', B['KP'], B['kmod'], B['erem'], ALU.mult)

        cp('pool', ARB[:, 0:nst, :, :], AR[:, 0:nst, :, :])
        for st in range(nst):
            tk = slice(128 * st, 128 * (st + 1))
            tok, tokb = TOK[st], TOKH[st]
            pt = bank(3)
            for qi, src in enumerate((AR[:, st, 0, :], B['BP'][:, tk], B['KP'][:, tk], B['v'][:, tk])):
                tr(pt[:, 128 * qi:128 * (qi + 1)], src, IDF)
            cp('act', tok, pt.rearrange("p (q c) -> p q c", c=128))
            cp('dve', tokb, pt.rearrange("p (q c) -> p q c", c=128))
            for pi in range(2):
                hp = slice(64 * pi, 64 * (pi + 1))
                rhs_ar = ARB[hp, st, :, :]
                mm(bank(4 + pi)[:, 0:256], BTB[hp, tk], rhs_ar)
                mm(bank(4 + pi)[:, 256:512], KTB[hp, tk], rhs_ar)
                mm(bank(6 + pi)[:, 0:128], ARB[hp, st, 0, :], BTB[hp, tk])
            for pi in range(2):
                tt('dve', NM[st][:, pi, :, :], bank(4 + pi)[:, 0:256].rearrange("p (a t) -> p a t", t=128), MAB, ALU.mult)
                tt('dve', AK[st][:, pi, :, :], bank(4 + pi)[:, 256:512].rearrange("p (a t) -> p a t", t=128), MAB, ALU.mult)
                tt('dve', YY[st][0][:, pi, :], bank(6 + pi)[:, 0:128], MSL, ALU.mult)
                cp('act', XP[st][0][:, pi, 0, :], NM[st][:, pi, 0, :])
                tt('pool', XP[st][0][:, pi, 1, :], NM[st][:, pi, 0, :], IDF, ALU.add)
        for kstep in range(6):
            for st in range(nst):
                cur = kstep % 2
                xp, yy = XP[st][cur], YY[st][cur]
                xpn, yyn = XP[st][1 - cur], YY[st][1 - cur]
                p1, p2 = (bank(1), bank(2)) if st % 2 == 0 else (bank(3), bank(0))
                if kstep == 0:
                    for pi in range(2):
                        mm(p1[:, 256 * pi:256 * pi + 128], yy[:, pi, :], xp[:, pi, 0, :])
                        mm(p2[:, 128 * pi:128 * (pi + 1)], xp[:, pi, 0, :], yy[:, pi, :])
                    cp('act', xpn[:, :, 0, :], p1.rearrange("p (h a t) -> p h a t", a=2, t=128)[:, :, 0, :])
                    cp('pool', xpn[:, :, 1, :], xp[:, :, 1, :])
                    cp('dve', yyn, p2[:, 0:256].rearrange("p (h t) -> p h t", t=128))
                elif kstep < 5:
                    for pi in range(2):
                        mm(p1[:, 256 * pi:256 * (pi + 1)], yy[:, pi, :], xp[:, pi, :, :])
                        mm(p2[:, 128 * pi:128 * (pi + 1)], xp[:, pi, 0, :], yy[:, pi, :])
                    p1v = p1.rearrange("p (h a t) -> p h a t", a=2, t=128)
                    cp('act', xpn[:, :, 0, :], p1v[:, :, 0, :])
                    tt('dve', xpn[:, :, 1, :], p1v[:, :, 1, :], xp[:, :, 1, :], ALU.add)
                    cp('act', yyn, p2[:, 0:256].rearrange("p (h t) -> p h t", t=128))
                else:
                    for pi in range(2):
                        mm(p1[:, 256 * pi + 128:256 * (pi + 1)], yy[:, pi, :], xp[:, pi, 1, :])
                    p1v = p1.rearrange("p (h a t) -> p h a t", a=2, t=128)
                    tt('dve', xpn[:, :, 1, :], p1v[:, :, 1, :], xp[:, :, 1, :], ALU.add)
        def pmb(st):
            return bank(7) if st % 2 == 0 else bank(2)

        def pgb(st):
            return bank(3) if st % 2 == 0 else bank(1)
        for st in range(nst):
            tokb, pm = TOKH[st], pmb(st)
            for pi in range(2):
                hc = slice(64 * pi, 64 * (pi + 1))
                mm(pm[:, 64 * pi:64 * (pi + 1)], AK[st][:, pi, 0, :], tokb[:, 3, hc])
            cp('act', Z1[st], pm[:, 0:128].rearrange("p (h i) -> p h i", i=64))
        for st in range(nst):
            tokb, pm, PT = TOKH[st], pmb(st), XP[st][0]
            for pi in range(2):
                hc = slice(64 * pi, 64 * (pi + 1))
                mm(pm[:, 128 + 128 * pi:128 + 128 * pi + 64], PT[:, pi, 1, :], Z1[st][:, pi, :])
                mm(pm[:, 128 + 128 * pi + 64:128 + 128 * (pi + 1)], PT[:, pi, 1, :], tokb[:, 0, hc])
            for pi in range(2):
                hc = slice(64 * pi, 64 * (pi + 1))
                mm(pm[hc, 384:512], tokb[:, 0, hc], PT[:, pi, 1, :], tp=(0, 64 * pi))
            cp('dve', UW[st], pm[:, 128:384].rearrange("p (h a i) -> p h a i", a=2, i=64))
            cp('act', WTs[st], pm[:, 384:512])
        for st in range(nst):
            tokb, pg = TOKH[st], pgb(st)
            for pi in range(2):
                hc = slice(64 * pi, 64 * (pi + 1))
                mm(pg[hc, 0:128], UW[st][:, pi, 1, :], NM[st][:, pi, 1, :], tp=(0, 64 * pi))
            for pi in range(2):
                hc = slice(64 * pi, 64 * (pi + 1))
                mm(pg[hc, 128:256], UW[st][:, pi, 0, :], NM[st][:, pi, 1, :], start=True, stop=False, tp=(0, 64 * pi))
                mm(pg[hc, 128:256], tokb[:, 3, hc], AK[st][:, pi, 1, :], start=False, stop=True, tp=(0, 64 * pi))
            tt('dve', GT[st], pg[:, 0:128], AR[:, st, 1, :], ALU.add)
            cp('act', HTt[st], pg[:, 128:256])
        for st in range(nst):
            tok = TOK[st]
            for q in range(2):
                S, chain = state_of_chunk(st, q)
                tq = slice(64 * q, 64 * (q + 1))
                tv = slice(64 * q, 64 * q + CV)
                for pi in range(2):
                    hc = slice(64 * pi, 64 * (pi + 1))
                    mm(bank(6 + pi)[tq, 128:192], WTs[st][hc, tq], S[hc, :], tp=(64 * pi, 64 * q))
                for pi in range(2):
                    tt('dve', US[tq, pi, :], bank(6 + pi)[tq, 128:192], UW[st][tq, pi, 0, :], ALU.add)
                for pi in range(2):
                    hc = slice(64 * pi, 64 * (pi + 1))
                    mm(bank(6 + pi)[hc, 192:256], S[hc, :], GT[st][hc, tq], tp=(64 * pi, 64 * pi))
                for pi in range(2):
                    hc = slice(64 * pi, 64 * (pi + 1))
                    tt('dve', B['YT'][hc, 128 * st + 64 * q:128 * st + 64 * (q + 1)], bank(6 + pi)[hc, 192:256], HTt[st][hc, tq],
                       ALU.add)
                psq = bank(6 + q)[:, 256:320]
                for pi in range(2):
                    hc = slice(64 * pi, 64 * (pi + 1))
                    mm(psq[hc, :], tok[tv, 1, hc], US[tv, pi, :], start=True, stop=False, tp=(64 * q, 64 * pi))
                    mm(psq[hc, :], tok[tv, 2, hc], tok[tv, 3, hc], start=False, stop=True, tp=(64 * q, 64 * pi))
                pcol = B['ein'][:, 128 * st + 64 * q + CV - 1:128 * st + 64 * q + CV]
                stt(S, S, pcol, psq, ALU.mult, ALU.add)
        pz = bank(1)
        mm(pz[:, 0:n], BDM, B['YT'])
        tt('dve', B['t1'], B['YT'], pz[:, 0:n], ALU.subtract)
        act(B['kk'], B['t1'], AF.Square)
        mm(pz[:, 0:n], BDM, B['kk'])
        act(B['kk'], pz[:, 0:n], AF.Sqrt, bias=EPSC[:, 1:2])
        P.op('dve', lambda e: e.reciprocal(out=B['kk'], in_=B['kk']), [B['kk']], [B['kk']])
        tt('pool', B['t1'], B['t1'], B['kk'], ALU.mult)
        ts('dve', B['t1'], B['t1'], par(P_LW), ALU.mult, par(P_LB), ALU.add)
        tt('pool', B['t1'], B['t1'], B['bonv'], ALU.add)
        tt('dve', ROB[:, 0:n], B['t1'], B['g'], ALU.mult)
        out_dst(ROB[:, 0:n])

    def rwkv_tile(ntok, cur_ap, prv_ap, state_of_chunk, CV, out_dst_fc):
        def proj(ps, col0):
            for kc in range(8):
                mm(ps, W12[:, kc, 0, col0:col0 + 128], cur_ap(kc), start=(kc == 0), stop=False)
                mm(ps, W12[:, kc, 1, col0:col0 + 128], prv_ap(kc), start=False, stop=(kc == 7))
        n = ntok
        if RPRE < -1:
            return
        pw = bank(0)
        proj(pw[:, 0:n], 1536)
        if RPRE < 0:
            return
        act(TW[0:64, 0:n], pw[0:64, 0:n], AF.Tanh)
        cp('dve', AL[64:128, 0:n], pw[64:128, 0:n])
        proj(pw[:, 0:n], 1664)
        act(SG[:, 0:n], pw[:, 0:n], AF.Sigmoid)
        for fc in range(4):
            pr, pk, pv = bank(0), bank(1), bank(2)
            proj(pr[:, 0:n], 128 * fc)
            proj(pk[:, 0:n], 512 + 128 * fc)
            proj(pv[:, 0:n], 1024 + 128 * fc)
            rwkv_fc(fc, n, pr[:, 0:n], pk[:, 0:n], pv[:, 0:n], lambda st, q, fc=fc: state_of_chunk(fc, st, q), CV,
                    lambda rob, fc=fc: out_dst_fc(fc, rob))

    for fc in range(4):
        memset('pool', SST[:, 0, fc, :], 0.0)

    for it in range(NT):
        for st in range(4):
            t0 = 512 * it + 128 * st
            last = (t0 + 128 == T)

            def ld(xt, t0=t0):
                P.dma('sp', xt, x_p[t0:t0 + 128, :])

            def f32row(hf):
                P.dma('pool', o_pshift[0:1, :], hf[127:128, :])

            pb = norm_tile(ld, None, want_f32_row=f32row if last else None)
            cp(rot(['act', 'dve']), HT[:, :, t0:t0 + 128], pb.rearrange("p (k t) -> p k t", t=128))
        for tb in (range(512 * it, 512 * (it + 1), RT) if 'R' in stages else []):
            if 'y' in stages:
                pass
            elif tb == 0:
                memset('pool', HTP[:, :, 0:1], 0.0)
                cp('dve', HTP[:, :, 1:RT], HT[:, :, 0:RT - 1])
            else:
                cp('dve', HTP, HT[:, :, tb - 1:tb - 1 + RT])
            rwkv_tile(RT,
                      lambda kc, tb=tb: HT[:, kc, tb:tb + RT],
                      lambda kc: HTP[:, kc, :],
                      lambda fc, st, q: (SST[:, 0, fc, :], True), 64,
                      lambda fc, rob, tb=tb: P.dma('pool', mixp_s[128 * fc:128 * (fc + 1), tb:tb + RT], rob))
    if 'v' in stages:
        memset('pool', SNAT, 1.0)
    for fc in (range(4) if ('R' in stages and 'z' not in stages and 'v' not in stages) else []):
        for pi in range(2):
            tr(bank(4 + pi)[0:64, 0:64], SST[64 * pi:64 * (pi + 1), 0, fc, :], IDF[64 * pi:64 * (pi + 1), 64 * pi:64 * (pi + 1)])
            cp('act', SNAT[0:64, 2 * fc + pi, :], bank(4 + pi)[0:64, 0:64])
    if 'R' in stages and 'z' not in stages and 'u' not in stages:
        P.dma('sp', o_pwkv.rearrange("h i j -> i h j"), SNAT[0:64, :, :])


    HTC = XT[0][:, 0:512].bitcast(BF16).rearrange("p (k t) -> p k t", t=128)
    HPF = TOKB.rearrange("p a q c -> p (a q c)")
    if 'S' in stages:
        memset('pool', HTS, 0.0)
        for tile in range(2):
            memset('pool', HPF, 0.0)
            for q in range(2):
                P.dma('sp', HPF[64 * q + 63:64 * q + 64, :], st_shift[2 * tile + q:2 * tile + q + 1, :])

            def ld(xt, tile=tile):
                memset('pool', xt, 0.0)
                for q in range(2):
                    P.dma('sp', xt[64 * q:64 * q + 8, :], x_s[2 * tile + q, :, :])

            def f32row(hf, tile=tile):
                for q in range(2):
                    P.dma('pool', o_sshift[2 * tile + q:2 * tile + q + 1, :], hf[64 * q + 7:64 * q + 8, :])

            def hprev(hb):
                tt('dve', hb, hb, HPF, ALU.add)

            pb = norm_tile(ld, None, want_f32_row=f32row, hprev_fn=hprev)
            pb4 = pb.rearrange("p (k q j) -> p k q j", q=2, j=64)
            cp('act', HTS[:, :, 2 * tile:2 * tile + 2, 2:65], pb4[:, :, :, 0:63])
            cp('dve', HTS[:, :, 2 * tile:2 * tile + 2, 1:2], pb4[:, :, :, 63:64])
        for tile in (range(2) if 'R' in stages else []):
            for q in range(2):
                seq = 2 * tile + q
                P.dma('sp', SNAT[0:64, :, :], st_wkv[seq].rearrange("h i j -> i h j"))
                for h_ in range(8):
                    pi, fc = h_ % 2, h_ // 2
                    tr(bank(4)[0:64, 64 * (h_ % 8):64 * (h_ % 8) + 64], SNAT[0:64, h_, :], IDF[0:64, 0:64])
                for h_ in range(8):
                    pi, fc = h_ % 2, h_ // 2
                    cp('act' if pi else 'dve', SST[64 * pi:64 * (pi + 1), 1 + seq, fc, :], bank(4)[0:64, 64 * h_:64 * h_ + 64])
            cp('dve', HTC.rearrange("p k (q j) -> p k q j", j=64), HTS[:, :, 2 * tile:2 * tile + 2, 2:66])
            cp('dve', HTP[:, :, 0:128].rearrange("p k (q j) -> p k q j", j=64), HTS[:, :, 2 * tile:2 * tile + 2, 1:65])
            rwkv_tile(128,
                      lambda kc: HTC[:, kc, :],
                      lambda kc: HTP[:, kc, 0:128],
                      lambda fc, st, q, tile=tile: (SST[:, 1 + 2 * tile + q, fc, :], False), 8,
                      lambda fc, rob, tile=tile: P.dma('pool', mixs_s[128 * fc:128 * (fc + 1), 128 * tile:128 * (tile + 1)], rob))
            for q in range(2):
                seq = 2 * tile + q
                for fc in range(4):
                    for pi in range(2):
                        tr(bank(4 + pi)[0:64, 0:64], SST[64 * pi:64 * (pi + 1), 1 + seq, fc, :],
                           IDF[64 * pi:64 * (pi + 1), 64 * pi:64 * (pi + 1)])
                        cp('act', SNAT[0:64, 2 * fc + pi, :], bank(4 + pi)[0:64, 0:64])
                P.dma('sp', o_swkv[seq].rearrange("h i j -> i h j"), SNAT[0:64, :, :])

    if debug:
        dbg['HT'] = dout("dbg_ht", [128, 8, T], BF16)
        P.dma('pool', dbg['HT'], HT)

    SB.release(mA)

    mB = SB.mark()
    WQKV = SB.alloc((8, 1536), BF16)
    mW = SB.mark()
    alloc_wst()
    for kc in range(8):
        rows = slice(128 * kc, 128 * (kc + 1))
        load_w(WQKV[:, kc, 0:1024], w_in[rows, NCOL_R:NCOL_R + 1024], 1024)
        load_w(WQKV[:, kc, 1024:1536], w_in[rows, NCOL_R + 1024:D_IN], 512)
    SB.release(mW)

    mS0 = SB.mark()
    if 'S' in stages:
        QSF = SB.alloc((4, 256), F32)
        KSF = SB.alloc((4, 256), F32)
        ATTS = SB.alloc((4, 256), BF16)
        KNEW = SB.alloc((2, 512), F32)
        VNEW = SB.alloc((2, 512), F32)
        CSS = SB.alloc((2, 128), F32)
        XBs = SB.alloc((128,), F32)
        SM = SB.alloc((8,), F32)
        ONESF = SB.alloc((64,), F32)
        RD = SB.alloc((8,), F32)
        HTC2 = SB.alloc((8, 128), BF16)
        RT1s = SB.alloc((128,), F32)
        RT2s = SB.alloc((128,), F32)
    mK = SB.mark()
    VB = [SB.alloc((512,), BF16) for _ in range(2)]
    VF = [SB.alloc((512,), F32) for _ in range(2)]
    KF = [SB.alloc((512,), F32) for _ in range(2)]
    KO = [SB.alloc((512,), F32) for _ in range(2)]
    KTMP = [SB.alloc((256,), F32) for _ in range(4)]
    CST = [SB.alloc((2, 32), F32) for _ in range(2)]

    def tokmajor_kv(lhs_ap, n_rows, i, v_bf16_dst, v_f32_dst, k_dst, cos_src, sin_src):
        pv = bank(i % 2)
        for kc in range(8):
            mm(pv, lhs_ap(kc), WQKV[:, kc, 1024:1536], start=(kc == 0), stop=(kc == 7))
        if v_bf16_dst is not None:
            cp('act', VB[i % 2], pv)
            P.dma('pool', v_bf16_dst, VB[i % 2][0:n_rows, :])
        if v_f32_dst is not None:
            cp('dve', VF[i % 2], pv)
            v_f32_dst(VF[i % 2])
        if k_dst is not None:
            pk = bank(2 + i % 2)
            for kc in range(8):
                mm(pk, lhs_ap(kc), WQKV[:, kc, 512:1024], start=(kc == 0), stop=(kc == 7))
            kf, ko, cs_ = KF[i % 2], KO[i % 2], CST[i % 2]
            cp('act', kf, pk)
            P.dma('sp', cs_[:, 0, :], cos_src)
            P.dma('sp', cs_[:, 1, :], sin_src)
            k4 = kf.rearrange("p (h a d) -> p h a d", a=2, d=32)
            o4 = ko.rearrange("p (h a d) -> p h a d", a=2, d=32)
            cb = cs_[:, 0, :].unsqueeze(1).to_broadcast([128, 8, 32])
            sb_ = cs_[:, 1, :].unsqueeze(1).to_broadcast([128, 8, 32])
            t = [x_.rearrange("p (h d) -> p h d", d=32) for x_ in KTMP]
            tt('dve', t[0], k4[:, :, 0, :], cb, ALU.mult)
            tt('pool', t[1], k4[:, :, 1, :], sb_, ALU.mult)
            tt('dve', o4[:, :, 0, :], t[0], t[1], ALU.subtract)
            tt('pool', t[2], k4[:, :, 0, :], sb_, ALU.mult)
            tt('dve', t[3], k4[:, :, 1, :], cb, ALU.mult)
            tt('pool', o4[:, :, 1, :], t[2], t[3], ALU.add)
            k_dst(ko)

    if 'B' in stages:
        for it in range(T // 128):
            t0 = 128 * it
            keep = t0 >= T - KEEP
            r0 = t0 - (T - KEEP)
            tokmajor_kv(lambda kc, t0=t0: HT[:, kc, t0:t0 + 128], 128, it, vs_s[t0:t0 + 128, :],
                        (lambda vf, r0=r0: P.dma('pool', o_pv[r0:r0 + 128, :], vf)) if keep else None,
                        (lambda ko, r0=r0: P.dma('pool', o_pk[r0:r0 + 128, :], ko)) if keep else None,
                        cost_p[r0:r0 + 128, :] if keep else None, sint_p[r0:r0 + 128, :] if keep else None)

    if 'S' in stages:
        memset('pool', ONESF, 1.0)
        memset('pool', ATTS, 0.0)
        for q in range(2):
            P.dma('sp', CSS[:, 0, 64 * q:64 * (q + 1)], cos_s[:, :])
            P.dma('sp', CSS[:, 1, 64 * q:64 * (q + 1)], sin_s[:, :])
        for tile in range(2):
            cp('dve', HTC2.rearrange("p k (q j) -> p k q j", j=64), HTS[:, :, 2 * tile:2 * tile + 2, 2:66])
            tokmajor_kv(lambda kc: HTC2[:, kc, :], 128, tile, None,
                        lambda vf, tile=tile: cp('pool', VNEW[:, tile, :], vf),
                        lambda ko, tile=tile: cp('pool', KNEW[:, tile, :], ko),
                        cost_s[:, :], sint_s[:, :])
            for q in range(2):
                seq = 2 * tile + q
                P.dma('pool', o_sk[seq, 2040:2048, :], KNEW[64 * q:64 * q + 8, tile, :])
                P.dma('pool', o_sv[seq, 2040:2048, :], VNEW[64 * q:64 * q + 8, tile, :])
            for hp in range(4):
                for which, c0_, dstF in ((0, 128 * hp, QSF), (1, 512 + 128 * hp, KSF)):
                    ps = bank(which)
                    for kc in range(8):
                        mm(ps[:, 0:128], WQKV[:, kc, c0_:c0_ + 128], HTC2[:, kc, :], start=(kc == 0), stop=(kc == 7))
                    cp('act', XBs, ps[:, 0:128])
                    pw_ = bank(7)
                    mm(pw_[:, 0:128], PERMF, XBs)
                    tt('pool', RT1s, XBs, CSS[:, 0, :], ALU.mult)
                    tt('dve', RT2s, pw_[:, 0:128], CSS[:, 1, :], ALU.mult)
                    tt('pool', dstF[:, hp, 128 * tile:128 * (tile + 1)], RT1s, RT2s, ALU.add)
    SB.release(mK)
    if 'S' in stages:
        KC = SB.alloc((17, 512), F32)
        KTC = SB.alloc((17 * 128,), F32)
        SSB = SB.alloc((17 * 128,), F32)
        LB = SB.alloc((17 * 128,), F32)
        PTs = SB.alloc((8, 17, 8), F32)
        P.dma('sp', LB[0:8, :], lb_s[:, :])
        for b_ in range(4):
            tile, q = b_ // 2, b_ % 2
            for t8 in range(0, 16, 4):
                P.dma('sp', KC[:, t8:t8 + 4, :], ck[b_, 128 * t8:128 * (t8 + 4), :].rearrange("(t p) c -> p t c", p=128))
            memset('pool', KC[:, 16, :], 0.0)
            cp('dve', KC[0:8, 16, :], KNEW[64 * q:64 * q + 8, tile, :])
            for h_ in range(8):
                pi, hp = h_ % 2, h_ // 2
                hs = slice(64 * pi, 64 * (pi + 1))
                hcol = slice(64 * h_, 64 * (h_ + 1))
                for g in range(5):
                    nt_ = 4 if g < 4 else 1
                    pk = bank(2 * (g % 2))
                    for j in range(nt_):
                        tr(pk[0:64, 128 * j:128 * (j + 1)], KC[:, 4 * g + j, hcol], IDF)
                    cp('act' if g % 2 else 'dve', KTC[hs, 512 * g:512 * g + 128 * nt_], pk[0:64, 0:128 * nt_])
                qv = QSF[hs, hp, 64 * b_:64 * b_ + 8]
                for g in range(5):
                    nn = 512 if g < 4 else 128
                    psc = bank(4 + pi + 2 * (g % 2))
                    mm(psc[0:8, 0:nn], qv, KTC[hs, 512 * g:512 * g + nn])
                    stt(SSB[0:8, 512 * g:512 * g + nn], psc[0:8, 0:nn], 0.125, LB[0:8, 512 * g:512 * g + nn], ALU.mult, ALU.add)
                P.op('dve', lambda e: e.tensor_reduce(out=SM[0:8, 0:1], in_=SSB[0:8, :], axis=AX.X, op=ALU.max),
                     [SM[0:8, 0:1]], [SSB[0:8, :]])
                ts('dve', SM[0:8, 1:2], SM[0:8, 0:1], -1.0, ALU.mult)
                act(SSB[0:8, :], SSB[0:8, :], AF.Exp, bias=SM[0:8, 1:2])
                pt_ = bank(0)
                for c_ in range(17):
                    tr(pt_[:, 8 * c_:8 * (c_ + 1)], SSB[0:8, 128 * c_:128 * (c_ + 1)], IDF[0:8, 0:8])
                cp('act', PTs[:, h_, :, :], pt_[:, 0:136].rearrange("p (c t) -> p c t", t=8))
            for t8 in range(0, 16, 4):
                P.dma('sp', KC[:, t8:t8 + 4, :], cv[b_, 128 * t8:128 * (t8 + 4), :].rearrange("(t p) c -> p t c", p=128))
            memset('pool', KC[:, 16, :], 0.0)
            cp('dve', KC[0:8, 16, :], VNEW[64 * q:64 * q + 8, tile, :])
            for h_ in range(8):
                pi, hp = h_ % 2, h_ // 2
                hs = slice(64 * pi, 64 * (pi + 1))
                hcol = slice(64 * h_, 64 * (h_ + 1))
                pn = bank(1)
                for c_ in range(17):
                    mm(pn[hs, 0:8], KC[:, c_, hcol], PTs[:, h_, c_, :], start=(c_ == 0), stop=(c_ == 16), tp=(0, 64 * pi))
                pd = bank(3)
                for c_ in range(17):
                    mm(pd[hs, 0:8], ONESF, PTs[:, h_, c_, :], start=(c_ == 0), stop=(c_ == 16), tp=(0, 64 * pi))
                cp('act', RD[hs, :], pd[hs, 0:8])
                P.op('dve', lambda e, hs=hs: e.reciprocal(out=RD[hs, :], in_=RD[hs, :]), [RD[hs, :]], [RD[hs, :]])
                tt('dve', ATTS[hs, hp, 64 * b_:64 * b_ + 8], pn[hs, 0:8], RD[hs, :], ALU.mult)
        for hp in range(4):
            P.dma('pool', mixs_s[512 + 128 * hp:512 + 128 * (hp + 1), :], ATTS[:, hp, :])
    SB.release(mS0)

    QT2 = SB.alloc((T, 2), BF16)
    KT2 = SB.alloc((T, 2), BF16)
    NV = {1: T // 128, 4: T // 128, 16: T // 128}
    VD = {d: SB.alloc((T // 128, 128), BF16) for d in (1, 4, 16)}
    NUMA = SB.alloc((2048,), F32)
    DENA = SB.alloc((2048,), F32)
    ATT = SB.alloc((2048,), BF16)
    XB = [SB.alloc((512,), BF16) for _ in range(2)]
    RT1 = [SB.alloc((512,), F32) for _ in range(2)]
    RT2 = [SB.alloc((512,), F32) for _ in range(2)]
    CSF = [SB.alloc((2, 512), F32) for _ in range(2)]
    EB = [SB.alloc((256,), BF16) for _ in range(8)]
    EM = [SB.alloc((256,), BF16) for _ in range(8)]
    PTS = [SB.alloc((2, 128), BF16) for _ in range(8)]
    BIAS = SB.alloc((8,), F32)
    DJ = [SB.alloc((128,), F32) for _ in range(2)]
    uidx = [0]

    def rope_fm(ps, xb, t1, t2, cs_, dst, n):
        cp('act', xb[:, 0:n], ps)
        pw_ = bank(7)
        mm(pw_[:, 0:n], PERM, xb[:, 0:n])
        tt('pool', t1[:, 0:n], xb[:, 0:n], cs_[:, 0, 0:n], ALU.mult)
        tt('dve', t2[:, 0:n], pw_[:, 0:n], cs_[:, 1, 0:n], ALU.mult)
        tt('pool', dst, t1[:, 0:n], t2[:, 0:n], ALU.add)

    for hp in (range(4) if ('B' in stages and BCUT >= 1) else []):
        qc = slice(128 * hp, 128 * (hp + 1))
        kcs = slice(512 + 128 * hp, 512 + 128 * (hp + 1))
        for it in range(NT):
            tb = 512 * it
            cs_ = CSF[it % 2]
            P.dma('sp', cs_[:, 0, :], cos_p[:, tb:tb + 512])
            P.dma('sp', cs_[:, 1, :], sin_p[:, tb:tb + 512])
            for which, cols, dstT in ((0, qc, QT2), (1, kcs, KT2)):
                ps = bank(which)
                for kc in range(8):
                    mm(ps, WQKV[:, kc, cols], HT[:, kc, tb:tb + 512], start=(kc == 0), stop=(kc == 7))
                rope_fm(ps, XB[which], RT1[which], RT2[which], cs_, dstT[:, tb:tb + 512, 0], 512)
        if BCUT < 2:
            continue
        for d in (1, 4, 16):
            npc = T // d // 128
            for r in range(d):
                src = vs_s[r:T:d, 128 * hp:128 * (hp + 1)].rearrange("(mt mi) c -> mi mt c", mi=128)
                for m0_ in range(0, npc, 8):
                    m1_ = min(npc, m0_ + 8)
                    P.dma('sp', VD[d][:, r * npc + m0_:r * npc + m1_, :], src[:, m0_:m1_, :])
        if BCUT < 3:
            continue
        for H in range(T // 2048):
            units = []
            for d in (1, 4, 16):
                npc = T // d // 128
                tph = 16 // d
                for r in range(d):
                    for mt in range(H * tph, (H + 1) * tph):
                        nkb = 1 if mt == 0 else 2
                        q0 = r + d * 128 * mt
                        k0 = r + d * 128 * (mt - 1) if nkb == 2 else q0
                        units.append((d, r, mt, nkb, q0, k0, q0 - 2048 * H, npc))
            GU = 4
            for g0 in range(0, len(units), GU):
                grp = units[g0:g0 + GU]
                hu = [(g, pi) for g in range(len(grp)) for pi in range(2)]
                for g, pi in hu:
                    d, r, mt, nkb, q0, k0, lo, npc = grp[g]
                    hs = slice(64 * pi, 64 * (pi + 1))
                    sb_ = bank(pi + 2 * (g % 2))[:, 256 * (g // 2):256 * (g // 2 + 1)]
                    c0 = 0 if nkb == 2 else 128
                    mm(sb_[:, c0:256], QT2[hs, sl(q0, 128, d), 0], KT2[hs, sl(k0, 128 * nkb, d), 0])
                for g, pi in hu:
                    sb_ = bank(pi + 2 * (g % 2))[:, 256 * (g // 2):256 * (g // 2 + 1)]
                    u = 2 * g + pi
                    stt(DJ[pi], sb_[:, 128:256], -0.125, IDF, ALU.mult, ALU.mult, accum=BIAS[:, u:u + 1])
                for g, pi in hu:
                    d, r, mt, nkb, q0, k0, lo, npc = grp[g]
                    sb_ = bank(pi + 2 * (g % 2))[:, 256 * (g // 2):256 * (g // 2 + 1)]
                    u = 2 * g + pi
                    c0 = 0 if nkb == 2 else 128
                    act(EB[u][:, c0:256], sb_[:, c0:256], AF.Exp, bias=BIAS[:, u:u + 1], scale=0.125)
                for g, pi in hu:
                    d, r, mt, nkb, q0, k0, lo, npc = grp[g]
                    u = 2 * g + pi
                    c0 = 0 if nkb == 2 else 128
                    tt('pool', EM[u][:, c0:256], EB[u][:, c0:256], MASK2[:, c0:256], ALU.mult)
                for g, pi in hu:
                    d, r, mt, nkb, q0, k0, lo, npc = grp[g]
                    u = 2 * g + pi
                    pt_ = bank(4 + pi).bitcast(BF16)[:, 256 * g:256 * (g + 1)]
                    for kb in range(2 - nkb, 2):
                        tr(pt_[:, 128 * kb:128 * (kb + 1)], EM[u][:, 128 * kb:128 * (kb + 1)], IDB)
                for g, pi in hu:
                    d, r, mt, nkb, q0, k0, lo, npc = grp[g]
                    u = 2 * g + pi
                    pt_ = bank(4 + pi).bitcast(BF16)[:, 256 * g:256 * (g + 1)]
                    cp('act' if pi == 0 else 'dve', PTS[u][:, 2 - nkb:2, :],
                       pt_[:, 128 * (2 - nkb):256].rearrange("p (k q) -> p k q", q=128))
                for g, pi in hu:
                    d, r, mt, nkb, q0, k0, lo, npc = grp[g]
                    u = 2 * g + pi
                    hs = slice(64 * pi, 64 * (pi + 1))
                    pn, pd = bank(6)[:, 128 * g:128 * (g + 1)], bank(7)[:, 128 * g:128 * (g + 1)]
                    for kb in range(2 - nkb, 2):
                        vt = r * npc + mt - 1 + kb
                        mm(pn[hs, :], VD[d][:, vt, hs], PTS[u][:, kb, :], start=(kb == 2 - nkb), stop=(kb == 1), tp=(0, 64 * pi))
                    for kb in range(2 - nkb, 2):
                        mm(pd[hs, :], ONESB, PTS[u][:, kb, :], start=(kb == 2 - nkb), stop=(kb == 1), tp=(0, 64 * pi))
                for g in range(len(grp)):
                    d, r, mt, nkb, q0, k0, lo, npc = grp[g]
                    pn, pd = bank(6)[:, 128 * g:128 * (g + 1)], bank(7)[:, 128 * g:128 * (g + 1)]
                    dstn = NUMA[:, sl(lo, 128, d)]
                    dstd = DENA[:, sl(lo, 128, d)]
                    if d == 1:
                        cp('dve', dstn, pn)
                        cp('act', dstd, pd)
                    else:
                        tt('dve', dstn, pn, dstn, ALU.add)
                        tt('dve', dstd, pd, dstd, ALU.add)
            P.op('dve', lambda e: e.reciprocal(out=DENA, in_=DENA), [DENA], [DENA])
            tt('pool', ATT, NUMA, DENA, ALU.mult)
            P.dma('pool', mixp_s[512 + 128 * hp:512 + 128 * (hp + 1), 2048 * H:2048 * (H + 1)], ATT)
    if debug:
        dbg['mix'] = dout("dbg_mix", [D, T], BF16)
        P.dma('pool', dbg['mix'], mixp_s)
    SB.release(m0)

    mC = SB.mark()
    WU = SB.alloc((8, 2 * DFF), BF16)
    WD = SB.alloc((22, D), BF16)
    GV = SB.alloc((3, D), F32)
    CW = SB.alloc((44, 4), F32)
    CAR = SB.alloc((5, 44, 2), F32)
    WOS = [SB.alloc((2, D), BF16) for _ in range(2)]
    mW = SB.mark()
    if 'C' in stages:
        alloc_wst()
        for i, gsrc in enumerate((n_mix_post, n_ffn_pre, n_ffn_post)):
            P.dma('sp', GV[:, i, :], gsrc[0:1, :].partition_broadcast(128))
        for j in range(3):
            dma_fc('sp', lambda c0, c1, j=j: CW[:, c0:c1, j], cw_d[j, :], False)
        dma_fc('sp', lambda c0, c1: CW[:, c0:c1, 3], cb_d.rearrange("f o -> (f o)"), False)
        for kc in range(8):
            rows = slice(128 * kc, 128 * (kc + 1))
            for j in range(0, 2 * DFF, 1024):
                n_ = min(1024, 2 * DFF - j)
                load_w(WU[:, kc, j:j + n_], w_up[rows, j:j + n_], n_)
        for fcn in range(22):
            load_w(WD[:, fcn, :], w_dn[128 * fcn:128 * (fcn + 1), :], 1024)
        for kc in range(8):
            load_w(WOS[kc % 2][:, 0, :], w_out[128 * kc:128 * (kc + 1), :], 1024)
            P.dma('pool', wout_s[128 * kc:128 * (kc + 1), :], WOS[kc % 2][:, 0, :])
    SB.release(mW)
    MIXT = [SB.alloc((8, 128), BF16) for _ in range(2)]
    WOF = [SB.alloc((512,), F32) for _ in range(2)]
    XC = [SB.alloc((D,), F32) for _ in range(2)]
    H2B = SB.alloc((D,), BF16)
    H2T = SB.alloc((8, 128), BF16)
    UE = [[SB.alloc((4, 130), F32)] * 2 for _ in range(2)]
    CGU = [SB.alloc((4, 128), F32) for _ in range(2)]
    CT = [SB.alloc((128,), F32) for _ in range(4)]
    FTB = [SB.alloc((22, 128), BF16) for _ in range(2)]
    JK = SB.alloc((D,), BF16)
    ST2 = SB.alloc((8, 4), F32)
    EPS2 = SB.alloc((1,), F32)
    ci = [0]
    pend = [None]

    def ffn_tile(*a_, **kw_):
        g_ = ffn_tile_gen(*a_, **kw_)
        next(g_)
        ffn_flush()
        next(g_)
        next(g_)
        pend[0] = g_

    def ffn_flush():
        if pend[0] is not None:
            for _ in pend[0]:
                pass
            pend[0] = None

    def rms_from(src_halves, si, col):
        act(JK[:, 0:512], src_halves[0], AF.Square, accum=ST2[:, si, 0:1])
        act(JK[:, 512:1024], src_halves[1], AF.Square, accum=ST2[:, si, 1:2])
        tt('dve', ST2[:, si, 0:1], ST2[:, si, 0:1], ST2[:, si, 1:2], ALU.add)
        act(ST2[:, si, 1:2], ST2[:, si, 0:1], AF.Sqrt, bias=EPS2[:, 0:1], scale=1.0 / D)
        P.op('dve', lambda e: e.reciprocal(out=ST2[:, si, col:col + 1], in_=ST2[:, si, 1:2]),
             [ST2[:, si, col:col + 1]], [ST2[:, si, 1:2]])

    def ffn_tile_gen(mix_src, x_src, y_dst, stream, segs, conv_out=None):
        i = ci[0] % 2
        si = ci[0] % 8
        ci[0] += 1
        mixt, xc = MIXT[i], XC[i]
        FT = FTB[i]
        P.dma('sp', mixt, mix_src.rearrange("(k p) t -> p k t", p=128))
        x_src(xc)
        for kq in range(4):
            wos = WOS[kq % 2]
            P.dma('sp', wos, wout_s[256 * kq:256 * (kq + 1), :].rearrange("(k p) c -> p k c", p=128))
            for k2 in range(2):
                kc = 2 * kq + k2
                for half in range(2):
                    mm(bank(half), mixt[:, kc, :], wos[:, k2, 512 * half:512 * (half + 1)], start=(kc == 0), stop=(kc == 7))
        yield 0
        rms_from((bank(0), bank(1)), si, 2)
        for half in range(2):
            hs_ = slice(512 * half, 512 * (half + 1))
            stt(WOF[half][:, 0:512], bank(half), ST2[:, si, 2:3], GV[:, 0, hs_], ALU.mult, ALU.mult)
            tt('pool', xc[:, hs_], xc[:, hs_], WOF[half][:, 0:512], ALU.add)
        rms_from((xc[:, 0:512], xc[:, 512:1024]), si, 3)
        stt(H2B, xc, ST2[:, si, 3:4], GV[:, 1, :], ALU.mult, ALU.mult)
        pb = bank(2).bitcast(BF16)
        for kc in range(8):
            tr(pb[:, 128 * kc:128 * (kc + 1)], H2B[:, 128 * kc:128 * (kc + 1)], IDB)
        cp('act', H2T, pb.rearrange("p (k t) -> p k t", t=128))
        g0 = 0
        gi = 0
        while g0 < 22:
            ng = min(4, 22 - g0)
            for side in range(2):
                ps = bank(3 + 2 * (gi % 2) + side)
                for c in range(ng):
                    ch = 22 * side + g0 + c
                    for kc in range(8):
                        mm(ps[:, 128 * c:128 * (c + 1)], WU[:, kc, 128 * ch:128 * (ch + 1)], H2T[:, kc, :],
                           start=(kc == 0), stop=(kc == 7))
            for side in range(2):
                ps = bank(3 + 2 * (gi % 2) + side)
                ue = UE[side][gi % 2]
                chs = slice(22 * side + g0, 22 * side + g0 + ng)
                cp('act', ue[:, 0:ng, 2:130], ps[:, 0:128 * ng].rearrange("p (c t) -> p c t", t=128))
                for (col0, ncols, cinit) in segs:
                    if cinit is None:
                        cp('pool', ue[:, 0:ng, col0:col0 + 2], CAR[:, stream, chs, :])
                    else:
                        cinit(ue[:, 0:ng, col0:col0 + 2], chs)
                for c in range(ng):
                    ch = 22 * side + g0 + c
                    t_ = CT[(2 * c + side) % 4]
                    act(t_, ue[:, c, 0:128], AF.Identity, bias=CW[:, ch, 3:4], scale=CW[:, ch, 0:1])
                    stt(t_, ue[:, c, 1:129], CW[:, ch, 1:2], t_, ALU.mult, ALU.add)
                    stt(CGU[side][:, c, :], ue[:, c, 2:130], CW[:, ch, 2:3], t_, ALU.mult, ALU.add)
                last_col = segs[-1][0] + segs[-1][1]
                cp('pool', CAR[:, stream, chs, :], ue[:, 0:ng, last_col:last_col + 2])
                if conv_out is not None:
                    conv_out(ue, chs, ng)
            act(CGU[0][:, 0:ng, :], CGU[0][:, 0:ng, :], AF.Silu)
            tt('pool', FT[:, g0:g0 + ng, :], CGU[0][:, 0:ng, :], CGU[1][:, 0:ng, :], ALU.mult)
            g0 += ng
            gi += 1
            if gi == 2:
                yield 1
        yield 2
        dbk = (bank(7), bank(2))
        for half in range(2):
            ps = dbk[half]
            for fcn in range(22):
                mm(ps, FT[:, fcn, :], WD[:, fcn, 512 * half:512 * (half + 1)], start=(fcn == 0), stop=(fcn == 21))
        rms_from(dbk, si, 2)
        for half in range(2):
            hs_ = slice(512 * half, 512 * (half + 1))
            stt(WOF[half][:, 0:512], dbk[half], ST2[:, si, 2:3], GV[:, 2, hs_], ALU.mult, ALU.mult)
            tt('pool', xc[:, hs_], xc[:, hs_], WOF[half][:, 0:512], ALU.add)
        y_dst(xc)

    if 'C' in stages:
        memset('pool', EPS2, NORM_EPS)
        memset('pool', CAR, 0.0)
        for it in range(T // 128):
            t0 = 128 * it
            ffn_tile(mixp_s[:, t0:t0 + 128],
                     lambda xc, t0=t0: P.dma('sp', xc, x_p[t0:t0 + 128, :]),
                     lambda xc, t0=t0: P.dma('pool', y_p[t0:t0 + 128, :], xc),
                     0, [(0, 128, None)])
        ffn_flush()
        if 'S' in stages:
            CONVS = SB.alloc((4, 44, 2), F32)
            SCONV = SB.alloc((4, 44, 2), F32)
            for seq in range(4):
                for r_ in range(2):
                    dma_fc('sp', lambda c0, c1, seq=seq, r_=r_: CONVS[:, seq, c0:c1, r_], st_conv[seq, r_, :], False)
            for tile in range(2):
                def xs_(xc, tile=tile):
                    memset('pool', xc, 0.0)
                    for q in range(2):
                        P.dma('sp', xc[64 * q:64 * q + 8, :], x_s[2 * tile + q, :, :])

                def ys_(xc, tile=tile):
                    for q in range(2):
                        P.dma('pool', y_s[2 * tile + q, :, :], xc[64 * q:64 * q + 8, :])

                def mk_init(seq):
                    return lambda dst, chs, seq=seq: cp('pool', dst, CONVS[:, seq, chs, :])

                def cout(ue, chs, ng, tile=tile):
                    for q in range(2):
                        cp('pool', SCONV[:, 2 * tile + q, chs, :], ue[:, 0:ng, 2 + 64 * q + 6:2 + 64 * q + 8])

                ffn_tile(mixs_s[:, 128 * tile:128 * (tile + 1)], xs_, ys_, 1,
                         [(0, 64, mk_init(2 * tile)), (64, 64, mk_init(2 * tile + 1))], conv_out=cout)
            ffn_flush()
            for seq in range(4):
                for r_ in range(2):
                    dma_fc('pool', lambda c0, c1, seq=seq, r_=r_: SCONV[:, seq, c0:c1, r_], o_sconv[seq, r_, :], True)
        for r_ in range(2):
            dma_fc('pool', lambda c0, c1, r_=r_: CAR[:, 0, c0:c1, r_], o_pconv[r_, :], True)
    SB.release(mC)

    P.final_wait_all('sp')
    P.build()
    return nc


def _rope_tables(T):
    half = 32
    inv = (np.float32(10000.0) ** (-np.arange(half, dtype=np.float32) / np.float32(half))).astype(np.float32)

    def tab(pos):
        ang = pos.astype(np.float32)[:, None] * inv[None, :]
        return np.cos(ang).astype(np.float32), np.sin(ang).astype(np.float32)
    cp_, sp_ = tab(np.arange(T))
    cs_, ss_ = tab(PAST_LEN + np.arange(64))
    def fm(c, s):
        cf = np.concatenate([c.T, c.T], 0)
        sf = np.concatenate([-s.T, s.T], 0)
        return (np.ascontiguousarray(np.concatenate([cf, cf], 0)), np.ascontiguousarray(np.concatenate([sf, sf], 0)))
    return fm(cp_, sp_), fm(cs_, ss_), (cp_, sp_), (cs_, ss_)


def _lb_table():
    cnt = np.zeros((8, 17 * 128), np.float64)
    for t in range(8):
        for d in (1, 4, 16):
            for j in range(129):
                idx = WBUF + t - j * d
                if idx >= 0:
                    cnt[t, idx] += 1
    lb = np.where(cnt > 0, np.log(np.maximum(cnt, 1)), -30000.0)
    return lb.astype(np.float32)


def _core_inputs(inp, c, T):
    (cfp, sfp), (cfs, sfs), (ctp, stp), (cts, sts) = _rope_tables(T)
    KEEP = min(WBUF, T)
    f = lambda a: np.ascontiguousarray(np.asarray(a, dtype=np.float32))
    d = {
        'x_prompt': f(inp['x_prompt'][c, :T]),
        'x_sample': f(inp['x_sample'][4 * c:4 * c + 4]),
        'state_rwkv_shift': f(inp['state_rwkv_shift'][0, 4 * c:4 * c + 4]),
        'state_rwkv_wkv': f(inp['state_rwkv_wkv'][0, 4 * c:4 * c + 4]),
        'cache_att_k': f(np.asarray(inp['cache_att_k'][0, 4 * c:4 * c + 4]).reshape(4, WBUF, 512)),
        'cache_att_v': f(np.asarray(inp['cache_att_v'][0, 4 * c:4 * c + 4]).reshape(4, WBUF, 512)),
        'state_ffn_conv': f(inp['state_ffn_conv'][0, 4 * c:4 * c + 4]),
        'norm_mix_pre': f(inp['norm_mix_pre']), 'norm_mix_post': f(inp['norm_mix_post']),
        'norm_ffn_pre': f(inp['norm_ffn_pre']), 'norm_ffn_post': f(inp['norm_ffn_post']),
        'w_in': f(inp['w_in'][0]), 'mu_shift': f(inp['mu_shift']),
        'w0': f(inp['w0']).reshape(512, 1), 'w_decay_up': f(inp['w_decay_up'][0]), 'a0': f(inp['a0']).reshape(512, 1),
        'w_iclr_up': f(inp['w_iclr_up'][0]), 'w_gate_up': f(inp['w_gate_up'][0]),
        'k_k': f(inp['k_k']).reshape(512, 1), 'k_a': f(inp['k_a']).reshape(512, 1), 'r_k': f(inp['r_k']).reshape(512, 1),
        'lnx_w': f(inp['lnx_w']).reshape(512, 1), 'lnx_b': f(inp['lnx_b']).reshape(512, 1),
        'w_out': f(inp['w_out'][0]), 'w_ffn_up': f(inp['w_ffn_up'][0]), 'ffn_conv_w': f(inp['ffn_conv_w'][0]),
        'ffn_conv_b': f(inp['ffn_conv_b']).reshape(2 * DFF, 1), 'w_ffn_down': f(inp['w_ffn_down'][0]),
        'cos_p': cfp, 'sin_p': sfp, 'cos_s': cfs, 'sin_s': sfs,
        'cost_p': np.ascontiguousarray(ctp[T - KEEP:]), 'sint_p': np.ascontiguousarray(stp[T - KEEP:]),
        'cost_s': np.ascontiguousarray(np.tile(cts[:64], (2, 1))), 'sint_s': np.ascontiguousarray(np.tile(sts[:64], (2, 1))),
        'lb_s': _lb_table(),
    }
    return d


_NC_CACHE = {}


def kernel(**inputs):
    T = 4096
    n = 8
    if T not in _NC_CACHE:
        _NC_CACHE[T] = build_program(T=T, debug=False, stages='NRBCS')
    nc = _NC_CACHE[T]
    in_maps = [_core_inputs(inputs, c, T) for c in range(n)]
    res = run_bass_kernel_spmd(nc, in_maps, core_ids=list(range(n)))
    rs = res.results
    g = lambda k: [np.asarray(r[k], dtype=np.float32) for r in rs]
    y_p = np.stack(g('y_prompt'), 0)
    y_s = np.concatenate(g('y_sample'), 0)
    p_shift = np.stack([a.reshape(D) for a in g('p_shift')], 0)[None]
    p_wkv = np.stack(g('p_wkv'), 0)[None]
    p_k = np.stack([a.reshape(WBUF, 8, 64) for a in g('p_att_k')], 0)[None]
    p_v = np.stack([a.reshape(WBUF, 8, 64) for a in g('p_att_v')], 0)[None]
    p_conv = np.stack(g('p_conv'), 0)[None]
    s_shift = np.concatenate(g('s_shift'), 0)[None]
    s_wkv = np.concatenate(g('s_wkv'), 0)[None]
    s_k = np.concatenate([a.reshape(4, WBUF, 8, 64) for a in g('s_att_k')], 0)[None]
    s_v = np.concatenate([a.reshape(4, WBUF, 8, 64) for a in g('s_att_v')], 0)[None]
    s_conv = np.concatenate(g('s_conv'), 0)[None]
    return (y_p, y_s, p_shift, p_wkv, p_k, p_v, p_conv, s_shift, s_wkv, s_k, s_v, s_conv)
```

```python
import contextlib
import os
RCUT = int(os.environ.get('RCUT', '9'))
RPRE = int(os.environ.get('RPRE', '99'))
RSUB = int(os.environ.get('RSUB', '99'))
BCUT = int(os.environ.get('BCUT', '99'))
import numpy as np
import ml_dtypes
import concourse.bass as bass
import concourse.mybir as mybir
from concourse.bass_utils import run_bass_kernel_spmd

F32 = mybir.dt.float32
BF16 = mybir.dt.bfloat16
AF = mybir.ActivationFunctionType
ALU = mybir.AluOpType
AX = mybir.AxisListType

D = 1024
HD = 64
RW = 512
NCOL_R = 1792
D_IN = 3328
DFF = 2816
PAST_LEN = 16384
WBUF = 2048
NORM_EPS = 1e-6
LNX_EPS = 64e-5
C0 = float(np.exp(-0.5))

_ESZ = {F32: 4, BF16: 2}


def _box(ap):
    name = ap.tensor.name
    dims = ap.ap
    off = int(ap.offset)
    esz = _ESZ.get(ap.dtype, 4)
    if 'DRAM' in str(ap.space).upper():
        ext = sum((c - 1) * abs(s) for s, c in dims)
        return (name, 0, 1, off * esz, (off + ext + 1) * esz)
    pstride = dims[0][0]
    pcount = dims[0][1]
    if pstride == 0:
        p0, f0 = 0, off
    else:
        p0 = off // pstride
        f0 = off - p0 * pstride
    ext = sum((c - 1) * abs(s) for s, c in dims[1:])
    b0, b1 = f0 * esz, (f0 + ext + 1) * esz
    if 'PSUM' in str(ap.space).upper():
        return (name, (p0 // 32) * 32, ((p0 + pcount + 31) // 32) * 32, (b0 // 2048) * 2048, ((b1 + 2047) // 2048) * 2048)
    return (name, p0, p0 + pcount, b0, b1)


class Prog:
    COMPUTE = ('pe', 'act', 'dve', 'pool')

    def __init__(self, nc, n_dma_sems=(20, 12, 8)):
        self.nc = nc
        self.ops = {e: [] for e in ('pe', 'act', 'dve', 'pool', 'sp')}
        self.nseq = {e: 0 for e in self.COMPUTE}
        self.waited = {}
        self.recs = {}
        self.dmaq = {'sp': [n_dma_sems[0], 0], 'pool': [n_dma_sems[1], 0], 'act': [n_dma_sems[2], 0]}
        self.sems = {}

    def _deps(self, eng, outs, ins, waiter=None):
        waiter = waiter or eng
        deps = {}
        for ap, is_w in [(a, True) for a in outs] + [(a, False) for a in ins]:
            name, p0, p1, f0, f1 = _box(ap)
            for r in self.recs.get(name, ()):
                if r[0] < p1 and p0 < r[1] and r[2] < f1 and f0 < r[3]:
                    if is_w or r[6] or (name == 'PS' and r[7] != eng):
                        if r[7] == eng and eng in self.COMPUTE:
                            if eng == 'pe':
                                continue
                            if not (r[6] and not is_w):
                                continue
                        if deps.get(r[4], 0) < r[5]:
                            deps[r[4]] = r[5]
        out = []
        for key, val in deps.items():
            if self.waited.get((waiter, key), 0) < val:
                self.waited[(waiter, key)] = val
                out.append((key, val))
        return out

    def _record(self, eng, outs, ins, key, val):
        for ap, is_w in [(a, True) for a in outs] + [(a, False) for a in ins]:
            name, p0, p1, f0, f1 = _box(ap)
            lst = self.recs.setdefault(name, [])
            if is_w:
                lst[:] = [r for r in lst if not (p0 <= r[0] and r[1] <= p1 and f0 <= r[2] and r[3] <= f1)]
            else:
                lst[:] = [r for r in lst if not ((not r[6]) and r[4] == key and p0 <= r[0] and r[1] <= p1
                                                 and f0 <= r[2] and r[3] <= f1)]
            lst.append([p0, p1, f0, f1, key, val, is_w, eng])

    def op(self, eng, fn, outs, ins):
        waits = self._deps(eng, outs, ins)
        self.nseq[eng] += 1
        key = 'c_' + eng
        self._record(eng, outs, ins, key, self.nseq[eng])
        self.ops[eng].append(('c', waits, fn, key))

    def dma(self, q, out, in_, **kw):
        ns, cnt = self.dmaq[q]
        slot = cnt % ns
        rnd = cnt // ns
        self.dmaq[q][1] += 1
        key = 'd_%s_%d' % (q, slot)
        waits = self._deps('dma_' + q, [out], [in_], waiter=q)
        if rnd > 0 and self.waited.get((q, key), 0) < 16 * rnd:
            self.waited[(q, key)] = 16 * rnd
            waits.append((key, 16 * rnd))
        self._record('dma_' + q, [out], [in_], key, 16 * (rnd + 1))
        self.ops[q].append(('d', waits, (out, in_, kw), key))

    def final_wait_all(self, eng='sp'):
        waits = []
        for q, (ns, cnt) in self.dmaq.items():
            for slot in range(min(ns, cnt)):
                nuse = (cnt - slot + ns - 1) // ns
                waits.append(('d_%s_%d' % (q, slot), 16 * nuse))
        for e in self.COMPUTE:
            if self.nseq[e] > 0:
                waits.append(('c_' + e, self.nseq[e]))
        self.ops[eng].append(('w', waits, None, None))

    def build(self):
        nc = self.nc
        keys = set()
        for e, lst in self.ops.items():
            for kind, waits, payload, key in lst:
                if key:
                    keys.add(key)
                for k, v in waits:
                    keys.add(k)
        with contextlib.ExitStack() as st:
            for k in sorted(keys):
                self.sems[k] = st.enter_context(nc.semaphore(k))
            block = st.enter_context(nc.Block())
            sems = self.sems

            def runner(lst):
                def run(e):
                    for kind, waits, payload, key in lst:
                        for k, v in waits:
                            e.wait_ge(sems[k], v)
                        if kind == 'c':
                            payload(e).then_inc(sems[key], 1)
                        elif kind == 'd':
                            out, in_, kw = payload
                            e.dma_start(out=out, in_=in_, **kw).then_inc(sems[key], 16)
                return run

            block.tensor(runner(self.ops['pe']))
            block.scalar(runner(self.ops['act']))
            block.vector(runner(self.ops['dve']))
            block.gpsimd(runner(self.ops['pool']))
            block.sync(runner(self.ops['sp']))


class Arena:
    def __init__(self, nc, name, nbytes):
        self.t = nc.alloc_sbuf_tensor(name, [128, nbytes // 4], F32)
        self.cap = nbytes
        self.off = 0
        self.peak = 0

    def alloc(self, shape, dt):
        shape = tuple(int(s) for s in shape)
        n = int(np.prod(shape))
        nb = (n * _ESZ[dt] + 63) // 64 * 64
        off = self.off
        self.off += nb
        self.peak = max(self.peak, self.off)
        assert self.off <= self.cap, ("SBUF arena overflow", self.off, self.cap)
        ap = self.t[:, off // 4:(off + nb) // 4]
        if dt != F32:
            ap = ap.bitcast(dt)
        ap = ap[:, 0:n]
        if len(shape) > 1:
            names = ['a%d' % i for i in range(len(shape))]
            pat = "p (%s) -> p %s" % (' '.join(names), ' '.join(names))
            ap = ap.rearrange(pat, **{nm: s for nm, s in zip(names, shape)})
        return ap

    def mark(self):
        return self.off

    def release(self, m):
        self.off = m


def build_program(T=4096, debug=False, stages='ABC'):
    RT = 512
    NT = T // 512
    nc = bass.Bass("TRN2", target_bir_lowering=False)
    P = Prog(nc)
    KEEP = min(WBUF, T)

    def din(name, shape, dt=F32):
        return nc.dram_tensor(name, list(shape), dt, kind="ExternalInput").ap()

    def dout(name, shape, dt=F32):
        return nc.dram_tensor(name, list(shape), dt, kind="ExternalOutput").ap()

    def dscr(name, shape, dt=BF16):
        return nc.dram_tensor(name, list(shape), dt, kind="Internal").ap()

    x_p = din("x_prompt", [T, D])
    x_s = din("x_sample", [4, 8, D])
    st_shift = din("state_rwkv_shift", [4, D])
    st_wkv = din("state_rwkv_wkv", [4, 8, 64, 64])
    ck = din("cache_att_k", [4, WBUF, 512])
    cv = din("cache_att_v", [4, WBUF, 512])
    st_conv = din("state_ffn_conv", [4, 2, 2 * DFF])
    n_mix_pre = din("norm_mix_pre", [1, D])
    n_mix_post = din("norm_mix_post", [1, D])
    n_ffn_pre = din("norm_ffn_pre", [1, D])
    n_ffn_post = din("norm_ffn_post", [1, D])
    w_in = din("w_in", [D, D_IN])
    mu_shift = din("mu_shift", [1, NCOL_R])
    w0_d = din("w0", [RW, 1])
    w_decay_up = din("w_decay_up", [64, RW])
    a0_d = din("a0", [RW, 1])
    w_iclr_up = din("w_iclr_up", [64, RW])
    w_gate_up = din("w_gate_up", [128, RW])
    k_k_d = din("k_k", [RW, 1])
    k_a_d = din("k_a", [RW, 1])
    r_k_d = din("r_k", [RW, 1])
    lnx_w_d = din("lnx_w", [RW, 1])
    lnx_b_d = din("lnx_b", [RW, 1])
    w_out = din("w_out", [D, D])
    w_up = din("w_ffn_up", [D, 2 * DFF])
    cw_d = din("ffn_conv_w", [3, 2 * DFF])
    cb_d = din("ffn_conv_b", [2 * DFF, 1])
    w_dn = din("w_ffn_down", [DFF, D])
    cos_p = din("cos_p", [128, T])
    sin_p = din("sin_p", [128, T])
    cos_s = din("cos_s", [128, 64])
    sin_s = din("sin_s", [128, 64])
    cost_p = din("cost_p", [KEEP, 32])
    sint_p = din("sint_p", [KEEP, 32])
    cost_s = din("cost_s", [128, 32])
    sint_s = din("sint_s", [128, 32])

    lb_s = din("lb_s", [8, 17 * 128])

    y_p = dout("y_prompt", [T, D])
    y_s = dout("y_sample", [4, 8, D])
    o_pshift = dout("p_shift", [1, D])
    o_pwkv = dout("p_wkv", [8, 64, 64])
    o_pk = dout("p_att_k", [KEEP, 512])
    o_pv = dout("p_att_v", [KEEP, 512])
    o_pconv = dout("p_conv", [2, 2 * DFF])
    o_sshift = dout("s_shift", [4, D])
    o_swkv = dout("s_wkv", [4, 8, 64, 64])
    o_sk = dout("s_att_k", [4, WBUF, 512])
    o_sv = dout("s_att_v", [4, WBUF, 512])
    o_sconv = dout("s_conv", [4, 2, 2 * DFF])
    dbg = {}

    n_mixp, n_mixs, n_vs = D * T, D * 256, T * 512
    SCR = dscr("SCR", [n_mixp + n_mixs + n_vs + D * D + D * T])
    mixp_s = SCR[0:n_mixp].rearrange("(f t) -> f t", t=T)
    mixs_s = SCR[n_mixp:n_mixp + n_mixs].rearrange("(f t) -> f t", t=256)
    vs_s = SCR[n_mixp + n_mixs:n_mixp + n_mixs + n_vs].rearrange("(t c) -> t c", c=512)
    wout_s = SCR[n_mixp + n_mixs + n_vs:n_mixp + n_mixs + n_vs + D * D].rearrange("(k c) -> k c", c=D)
    o_h_ = n_mixp + n_mixs + n_vs + D * D
    hts_s = SCR[o_h_:o_h_ + D * T].rearrange("(f t) -> f t", t=T)

    SB = Arena(nc, "SB", 212480)
    PS = nc.alloc_psum_tensor("PS", [128, 4096], F32)

    def bank(k, a=0, b=512):
        return PS[:, 512 * k + a:512 * k + b]

    def dma_fc(q, sb_fn, dr_row, write):
        for c0 in range(0, 44, 8):
            c1 = min(44, c0 + 8)
            d_ = dr_row[128 * c0:128 * c1].rearrange("(c p) -> p c", p=128)
            if write:
                P.dma(q, d_, sb_fn(c0, c1), allow_slow_non_contiguous=True)
            else:
                P.dma(q, sb_fn(c0, c1), d_, allow_slow_non_contiguous=True)

    def sl(start, n, step):
        return slice(start, start + step * (n - 1) + 1, step)

    def tt(eng, out, in0, in1, op):
        P.op(eng, lambda e: e.tensor_tensor(out=out, in0=in0, in1=in1, op=op), [out], [in0, in1])

    def ts(eng, out, in0, s1, op0, s2=None, op1=None):
        ins = [in0] + [s for s in (s1, s2) if not isinstance(s, (int, float, type(None)))]
        if op1 is None:
            P.op(eng, lambda e: e.tensor_scalar(out=out, in0=in0, scalar1=s1, scalar2=None, op0=op0), [out], ins)
        else:
            P.op(eng, lambda e: e.tensor_scalar(out=out, in0=in0, scalar1=s1, scalar2=s2, op0=op0, op1=op1), [out], ins)

    def stt(out, in0, scalar, in1, op0, op1, accum=None):
        ins = [in0, in1] + ([scalar] if not isinstance(scalar, (int, float)) else [])
        outs = [out] + ([accum] if accum is not None else [])
        if accum is None:
            P.op('dve', lambda e: e.scalar_tensor_tensor(out=out, in0=in0, scalar=scalar, in1=in1, op0=op0, op1=op1), outs, ins)
        else:
            P.op('dve', lambda e: e.scalar_tensor_tensor(out=out, in0=in0, scalar=scalar, in1=in1, op0=op0, op1=op1,
                                                         accum_out=accum), outs, ins)

    def act(out, in_, func, bias=None, scale=None, accum=None):
        ins = [in_] + [s for s in (bias, scale) if not isinstance(s, (int, float, type(None)))]
        outs = [out] + ([accum] if accum is not None else [])
        kw = {}
        if bias is not None:
            kw['bias'] = bias
        if scale is not None:
            kw['scale'] = scale
        if accum is not None:
            kw['accum_out'] = accum
        P.op('act', lambda e: e.activation(out=out, in_=in_, func=func, **kw), outs, ins)

    def cp(eng, out, in_):
        if eng == 'act':
            P.op('act', lambda e: e.copy(out=out, in_=in_), [out], [in_])
        else:
            P.op(eng, lambda e: e.tensor_copy(out=out, in_=in_), [out], [in_])

    pe_hist = []

    def pe_check(out, lhsT):
        import traceback
        _, p0, p1, f0, f1 = _box(lhsT)
        rg = set(range(p0 // 32, (p1 + 31) // 32))
        _, _, _, b0, b1 = _box(out)
        bk = set(range(b0 // 2048, (b1 + 2047) // 2048))
        for (rg2, bk2, where) in pe_hist[-24:]:
            if not (rg & rg2) and (bk & bk2):
                print("PE ROW-GROUP/BANK HAZARD:", sorted(rg), sorted(bk), "vs", sorted(rg2), sorted(bk2), where,
                      "|", traceback.extract_stack(limit=4)[0].lineno)
        pe_hist.append((rg, bk, traceback.extract_stack(limit=4)[0].lineno))

    def mm(out, lhsT, rhs, start=True, stop=True, tp=None):
        pe_check(out, lhsT)
        if tp is None:
            P.op('pe', lambda e: e.matmul(out, lhsT=lhsT, rhs=rhs, start=start, stop=stop), [out], [lhsT, rhs])
        else:
            P.op('pe', lambda e: e.matmul(out, lhsT=lhsT, rhs=rhs, start=start, stop=stop, tile_position=tp), [out], [lhsT, rhs])

    def tr(out, in_, ident):
        pe_check(out, in_)
        P.op('pe', lambda e: e.transpose(out, in_, ident), [out], [in_, ident])

    def memset(eng, out, val):
        P.op(eng, lambda e: e.memset(out, val), [out], [])

    def asel(out, pattern, base, cm, cmp, fill=0.0):
        P.op('pool', lambda e: e.affine_select(out=out, in_=out, pattern=pattern, compare_op=cmp, fill=fill,
                                               base=base, channel_multiplier=cm), [out], [out])

    rr = {'n': 0}

    def rot(engs):
        rr['n'] += 1
        return engs[rr['n'] % len(engs)]

    if 'S' in stages:
        for b_ in range(4):
            for (src_, dst_) in ((ck, o_sk), (cv, o_sv)):
                for c_ in range(4):
                    P.dma('act', dst_[b_, 510 * c_:510 * (c_ + 1), :], src_[b_, 8 + 510 * c_:8 + 510 * (c_ + 1), :])

    IDF = SB.alloc((128,), F32)
    IDB = SB.alloc((128,), BF16)
    BDO = SB.alloc((128,), F32)
    BDM = SB.alloc((128,), F32)
    MSU = SB.alloc((128,), F32)
    MIU = SB.alloc((128,), F32)
    MSL = SB.alloc((128,), F32)
    MAB = SB.alloc((2, 128), F32)
    PERM = SB.alloc((128,), BF16)
    PERMF = SB.alloc((128,), F32)
    MASK2 = SB.alloc((256,), BF16)
    ONESB = SB.alloc((64,), BF16)
    PAR = SB.alloc((4, 8), F32)
    m0 = SB.mark()
    HTS = SB.alloc((8, 4, 66), BF16)
    mA = SB.mark()
    RESET = SB.alloc((512,), F32)
    tmpc = SB.alloc((256,), F32)

    memset('pool', IDF, 1.0)
    asel(IDF, [[-1, 128]], 0, 1, ALU.is_equal)
    cp('dve', IDB, IDF)
    memset('pool', BDO, 1.0)
    memset('pool', BDO[0:64, 64:128], 0.0)
    memset('pool', BDO[64:128, 0:64], 0.0)
    ts('dve', BDM, BDO, 1.0 / 64, ALU.mult)
    memset('pool', MSU, 1.0)
    asel(MSU, [[1, 128]], 0, -1, ALU.is_gt)
    tt('dve', MSU, MSU, BDO, ALU.mult)
    memset('pool', MIU, 1.0)
    asel(MIU, [[1, 128]], 0, -1, ALU.is_ge)
    tt('dve', MIU, MIU, BDO, ALU.mult)
    memset('pool', MSL, 1.0)
    asel(MSL, [[-1, 128]], 0, 1, ALU.is_gt)
    tt('dve', MSL, MSL, BDO, ALU.mult)
    cp('dve', MAB[:, 0, :], MSU)
    cp('dve', MAB[:, 1, :], MIU)
    memset('pool', tmpc[:, 0:128], 1.0)
    asel(tmpc[:, 0:128], [[-1, 128]], -32, 1, ALU.is_equal)
    memset('pool', tmpc[:, 32:64], 0.0)
    memset('pool', tmpc[:, 96:128], 0.0)
    memset('pool', tmpc[:, 128:256], 1.0)
    asel(tmpc[:, 128:256], [[-1, 128]], 32, 1, ALU.is_equal)
    memset('pool', tmpc[:, 128:160], 0.0)
    memset('pool', tmpc[:, 192:224], 0.0)
    tt('dve', PERM, tmpc[:, 0:128], tmpc[:, 128:256], ALU.add)
    tt('dve', PERMF, tmpc[:, 0:128], tmpc[:, 128:256], ALU.add)
    memset('pool', tmpc[:, 0:256], 1.0)
    asel(tmpc[:, 0:128], [[1, 128]], 0, -1, ALU.is_ge)
    asel(tmpc[:, 128:256], [[-1, 128]], 0, 1, ALU.is_ge)
    cp('dve', MASK2, tmpc[:, 0:256])
    memset('pool', RESET, 1.0)
    for c in range(8):
        memset('pool', RESET[:, 64 * c:64 * c + 1], 0.0)
    memset('pool', ONESB, 1.0)

    P_W0, P_A0, P_KK, P_KA, P_RK, P_LW, P_LB, P_OMKA = range(8)
    for i, d_ in enumerate((w0_d, a0_d, k_k_d, k_a_d, r_k_d, lnx_w_d, lnx_b_d)):
        for fc in range(4):
            P.dma('sp', PAR[:, fc, i:i + 1], d_[128 * fc:128 * (fc + 1), :])
    ts('dve', PAR[:, :, P_OMKA], PAR[:, :, P_KA], -1.0, ALU.mult, 1.0, ALU.add)
    WDU = SB.alloc((RW,), F32)
    WIU = SB.alloc((RW,), F32)
    WGU = SB.alloc((RW,), F32)
    P.dma('sp', WDU[0:64, :], w_decay_up[:, :])
    P.dma('sp', WIU[64:128, :], w_iclr_up[:, :])
    P.dma('sp', WGU[:, :], w_gate_up[:, :])
    GPRE = SB.alloc((D,), F32)
    P.dma('sp', GPRE, n_mix_pre[0:1, :].partition_broadcast(128))

    WSTh = {'b': None}
    wi = [0]

    def alloc_wst():
        WSTh['b'] = [SB.alloc((1024,), F32) for _ in range(3)]

    def load_w(dst, src, ncol, scale_pairs=None):
        i = wi[0] % 3
        wi[0] += 1
        WST = WSTh['b']
        P.dma('sp', WST[i][:, 0:ncol], src)
        if scale_pairs is None:
            cp(rot(['act', 'dve', 'pool']), dst, WST[i][:, 0:ncol])
        else:
            for (d_, sc), eng in zip(scale_pairs, ('dve', 'pool')):
                tt(eng, d_, WST[i][:, 0:ncol], sc, ALU.mult)

    W12 = SB.alloc((8, 2, NCOL_R), BF16)
    mW = SB.mark()
    alloc_wst()
    MU = SB.alloc((NCOL_R,), F32)
    OMU = SB.alloc((NCOL_R,), F32)
    P.dma('sp', MU, mu_shift[0:1, :].partition_broadcast(128))
    ts('dve', OMU, MU, -1.0, ALU.mult, 1.0, ALU.add)
    for kc in (range(8) if 'x' not in stages else []):
        rows = slice(128 * kc, 128 * (kc + 1))
        for c0, c1 in ((0, 1024), (1024, NCOL_R)):
            load_w(None, w_in[rows, c0:c1], c1 - c0,
                   scale_pairs=[(W12[:, kc, 0, c0:c1], OMU[:, c0:c1]), (W12[:, kc, 1, c0:c1], MU[:, c0:c1])])
    SB.release(mW)
    XT = [SB.alloc((D,), F32) for _ in range(2)]
    HB = [SB.alloc((D,), BF16) for _ in range(2)]
    STAT = SB.alloc((8, 4), F32)
    EPSC = SB.alloc((2,), F32)
    memset('pool', EPSC[:, 0:1], NORM_EPS)
    memset('pool', EPSC[:, 1:2], LNX_EPS)
    sti = [0]

    def norm_tile(x_rows_fn, dst_fn, want_f32_row=None, hprev_fn=None):
        i = sti[0] % 2
        si = sti[0] % 8
        sti[0] += 1
        xt, hb = XT[i], HB[i]
        x_rows_fn(xt)
        act(hb, xt, AF.Square, accum=STAT[:, si, 0:1])
        act(STAT[:, si, 1:2], STAT[:, si, 0:1], AF.Sqrt, bias=EPSC[:, 0:1], scale=1.0 / D)
        P.op('dve', lambda e: e.reciprocal(out=STAT[:, si, 2:3], in_=STAT[:, si, 1:2]), [STAT[:, si, 2:3]], [STAT[:, si, 1:2]])
        stt(hb, xt, STAT[:, si, 2:3], GPRE, ALU.mult, ALU.mult)
        if want_f32_row is not None:
            stt(xt, xt, STAT[:, si, 2:3], GPRE, ALU.mult, ALU.mult)
            want_f32_row(xt)
        if hprev_fn is not None:
            hprev_fn(hb)
        pb = bank(2 + i).bitcast(BF16)
        for kc in range(8):
            tr(pb[:, 128 * kc:128 * (kc + 1)], hb[:, 128 * kc:128 * (kc + 1)], IDB)
        return pb

    RB = {}
    for nm in ('r', 'k', 'v', 'sgw', 'a', 'g', 'cs', 'csx', 'crem', 'ein', 'eneg', 'kk', 't1', 'kkn',
               'kmod', 'bb', 'bonv', 'YT'):
        RB[nm] = SB.alloc((RT,), F32)
    BTB = SB.alloc((RT,), BF16)
    KTB = SB.alloc((RT,), BF16)
    RB['eex'] = RB['csx']
    RB['erem'] = RB['crem']
    RB['BP'] = RB['bb']
    RB['KP'] = RB['kmod']
    AR = SB.alloc((RT // 128, 2, 128), F32)
    TW = SB.alloc((RT,), F32)
    AL = SB.alloc((RT,), F32)
    SG = SB.alloc((RT,), F32)
    NSTM = RT // 128
    TOKB = SB.alloc((NSTM, 4, 128), F32)
    TOK = [TOKB[:, i_, :, :] for i_ in range(NSTM)]
    TOKH = [RB[('eneg', 'kk')[(i_ * 512) // (2 * RT)]].bitcast(BF16)[:, (i_ * 512) % (2 * RT):(i_ * 512) % (2 * RT) + 512]
            .rearrange("p (q c) -> p q c", c=128) for i_ in range(NSTM)]
    ARB = RB['kkn'].bitcast(BF16).rearrange("p (s a t) -> p s a t", a=2, t=128)
    NM = [SB.alloc((2, 2, 128), BF16) for _ in range(NSTM)]
    AK = [SB.alloc((2, 2, 128), BF16) for _ in range(NSTM)]
    XP = [[SB.alloc((2, 2, 128), BF16) for _ in range(2)] for _ in range(NSTM)]
    YY = [[SB.alloc((2, 128), BF16) for _ in range(2)] for _ in range(NSTM)]
    Z1 = [SB.alloc((2, 64), BF16) for _ in range(NSTM)]
    UW = [SB.alloc((2, 2, 64), BF16) for _ in range(NSTM)]
    WTs = [SB.alloc((128,), F32) for _ in range(NSTM)]
    GT = [SB.alloc((128,), F32) for _ in range(NSTM)]
    HTt = [SB.alloc((128,), F32) for _ in range(NSTM)]
    US = SB.alloc((2, 64), F32)
    SST = SB.alloc((5, 4, 64), F32)
    ROB = RB['sgw'].bitcast(BF16)[:, 0:RT]
    HTP = SB.alloc((8, RT), BF16)
    HTCUR = [SB.alloc((8, 512), BF16) for _ in range(2)]
    HLAST = SB.alloc((8, 2), BF16)
    SNAT = TOK[0].rearrange("p q c -> p (q c)").rearrange("p (h j) -> p h j", j=64)

    def rwkv_fc(fc, ntok, ps_r, ps_k, ps_v, state_of_chunk, CV, out_dst):
        nst = ntok // 128
        n = ntok
        par = lambda i: PAR[:, fc, i:i + 1]
        B = {k_: v_[:, 0:n] for k_, v_ in RB.items()}
        cp('act', B['r'], ps_r)
        cp('act', B['k'], ps_k)
        cp('act', B['v'], ps_v)
        if RPRE < 1:
            return
        cols = slice(128 * fc, 128 * (fc + 1))
        pz = bank(1)
        mm(bank(4)[:, 0:n], WDU[0:64, cols], TW[0:64, 0:n])
        act(B['sgw'], bank(4)[:, 0:n], AF.Sigmoid, bias=par(P_W0))
        mm(bank(5)[:, 0:n], WIU[64:128, cols], AL[64:128, 0:n])
        act(B['a'], bank(5)[:, 0:n], AF.Sigmoid, bias=par(P_A0))
        mm(pz[:, 0:n], WGU[:, cols], SG[:, 0:n])
        cp('act', B['g'], pz[:, 0:n])
        if RPRE < 2:
            return
        P.op('dve', lambda e: e.tensor_tensor_scan(out=B['cs'], data0=RESET[:, 0:n], data1=B['sgw'], initial=0.0,
                                                   op0=ALU.mult, op1=ALU.add), [B['cs']], [RESET[:, 0:n], B['sgw']])
        if RPRE < 3:
            return
        tt('dve', B['csx'], B['cs'], B['sgw'], ALU.subtract)
        nch = n // 64
        cs3 = B['cs'].rearrange("p (c t) -> p c t", t=64)
        tt('dve', B['crem'].rearrange("p (c t) -> p c t", t=64), cs3[:, :, CV - 1:CV].to_broadcast([128, nch, 64]), cs3,
           ALU.subtract)
        if RPRE < 4:
            return
        act(B['ein'], B['cs'], AF.Exp, scale=-C0)
        act(B['eex'], B['csx'], AF.Exp, scale=-C0)
        act(B['eneg'], B['cs'], AF.Exp, scale=C0)
        act(B['erem'], B['crem'], AF.Exp, scale=-C0)
        if RPRE < 5:
            return
        ts('dve', B['kk'], B['k'], par(P_KK), ALU.mult)
        act(B['t1'], B['kk'], AF.Square)
        mm(pz[:, 0:n], BDO, B['t1'])
        act(B['t1'], pz[:, 0:n], AF.Sqrt)
        ts('dve', B['t1'], B['t1'], 1e-12, ALU.max)
        P.op('dve', lambda e: e.reciprocal(out=B['t1'], in_=B['t1']), [B['t1']], [B['t1']])
        tt('dve', B['kkn'], B['kk'], B['t1'], ALU.mult)
        if RPRE < 6:
            return
        ts('dve', B['t1'], B['a'], par(P_KA), ALU.mult, par(P_OMKA), ALU.add)
        tt('pool', B['kmod'], B['k'], B['t1'], ALU.mult)
        tt('dve', B['bb'], B['kkn'], B['a'], ALU.mult)
        if RPRE < 7:
            return
        ar = AR[:, 0:nst, :, :]
        stt(ar[:, :, 0, :], B['kkn'].rearrange("p (s t) -> p s t", t=128), -1.0,
            B['eex'].rearrange("p (s t) -> p s t", t=128), ALU.mult, ALU.mult)
        tt('pool', ar[:, :, 1, :], B['r'].rearrange("p (s t) -> p s t", t=128),
           B['ein'].rearrange("p (s t) -> p s t", t=128), ALU.mult)
        if RPRE < 8:
            return
        stt(B['t1'], B['r'], par(P_RK), B['kmod'], ALU.mult, ALU.mult)
        mm(pz[:, 0:n], BDO, B['t1'])
        tt('dve', B['bonv'], pz[:, 0:n], B['v'], ALU.mult)
        tt('dve', BTB[:, 0:n], B['bb'], B['eneg'], ALU.mult)
        tt('pool', KTB[:, 0:n], B['kmod'], B['eneg'], ALU.mult)
        tt('dve', B['BP'], B['bb'], B['erem'], ALU.mult)
        tt('pool', B['KP'], B['kmod'], B['erem'], ALU.mult)

        cp('pool', ARB[:, 0:nst, :, :], AR[:, 0:nst, :, :])
        for st in range(nst):
            tk = slice(128 * st, 128 * (st + 1))
            tok, tokb = TOK[st], TOKH[st]
            pt = bank(3)
            for qi, src in enumerate((AR[:, st, 0, :], B['BP'][:, tk], B['KP'][:, tk], B['v'][:, tk])):
                tr(pt[:, 128 * qi:128 * (qi + 1)], src, IDF)
            cp('act', tok, pt.rearrange("p (q c) -> p q c", c=128))
            cp('dve', tokb, pt.rearrange("p (q c) -> p q c", c=128))
            for pi in range(2):
                hp = slice(64 * pi, 64 * (pi + 1))
                rhs_ar = ARB[hp, st, :, :]
                mm(bank(4 + pi)[:, 0:256], BTB[hp, tk], rhs_ar)
                mm(bank(4 + pi)[:, 256:512], KTB[hp, tk], rhs_ar)
                mm(bank(6 + pi)[:, 0:128], ARB[hp, st, 0, :], BTB[hp, tk])
            for pi in range(2):
                tt('dve', NM[st][:, pi, :, :], bank(4 + pi)[:, 0:256].rearrange("p (a t) -> p a t", t=128), MAB, ALU.mult)
                tt('dve', AK[st][:, pi, :, :], bank(4 + pi)[:, 256:512].rearrange("p (a t) -> p a t", t=128), MAB, ALU.mult)
                tt('dve', YY[st][0][:, pi, :], bank(6 + pi)[:, 0:128], MSL, ALU.mult)
                cp('act', XP[st][0][:, pi, 0, :], NM[st][:, pi, 0, :])
                tt('pool', XP[st][0][:, pi, 1, :], NM[st][:, pi, 0, :], IDF, ALU.add)
        for kstep in range(6):
            for st in range(nst):
                cur = kstep % 2
                xp, yy = XP[st][cur], YY[st][cur]
                xpn, yyn = XP[st][1 - cur], YY[st][1 - cur]
                p1, p2 = (bank(1), bank(2)) if st % 2 == 0 else (bank(3), bank(0))
                if kstep == 0:
                    for pi in range(2):
                        mm(p1[:, 256 * pi:256 * pi + 128], yy[:, pi, :], xp[:, pi, 0, :])
                        mm(p2[:, 128 * pi:128 * (pi + 1)], xp[:, pi, 0, :], yy[:, pi, :])
                    cp('act', xpn[:, :, 0, :], p1.rearrange("p (h a t) -> p h a t", a=2, t=128)[:, :, 0, :])
                    cp('pool', xpn[:, :, 1, :], xp[:, :, 1, :])
                    cp('dve', yyn, p2[:, 0:256].rearrange("p (h t) -> p h t", t=128))
                elif kstep < 5:
                    for pi in range(2):
                        mm(p1[:, 256 * pi:256 * (pi + 1)], yy[:, pi, :], xp[:, pi, :, :])
                        mm(p2[:, 128 * pi:128 * (pi + 1)], xp[:, pi, 0, :], yy[:, pi, :])
                    p1v = p1.rearrange("p (h a t) -> p h a t", a=2, t=128)
                    cp('act', xpn[:, :, 0, :], p1v[:, :, 0, :])
                    tt('dve', xpn[:, :, 1, :], p1v[:, :, 1, :], xp[:, :, 1, :], ALU.add)
                    cp('act', yyn, p2[:, 0:256].rearrange("p (h t) -> p h t", t=128))
                else:
                    for pi in range(2):
                        mm(p1[:, 256 * pi + 128:256 * (pi + 1)], yy[:, pi, :], xp[:, pi, 1, :])
                    p1v = p1.rearrange("p (h a t) -> p h a t", a=2, t=128)
                    tt('dve', xpn[:, :, 1, :], p1v[:, :, 1, :], xp[:, :, 1, :], ALU.add)
        def pmb(st):
            return bank(7) if st % 2 == 0 else bank(2)

        def pgb(st):
            return bank(3) if st % 2 == 0 else bank(1)
        for st in range(nst):
            tokb, pm = TOKH[st], pmb(st)
            for pi in range(2):
                hc = slice(64 * pi, 64 * (pi + 1))
                mm(pm[:, 64 * pi:64 * (pi + 1)], AK[st][:, pi, 0, :], tokb[:, 3, hc])
            cp('act', Z1[st], pm[:, 0:128].rearrange("p (h i) -> p h i", i=64))
        for st in range(nst):
            tokb, pm, PT = TOKH[st], pmb(st), XP[st][0]
            for pi in range(2):
                hc = slice(64 * pi, 64 * (pi + 1))
                mm(pm[:, 128 + 128 * pi:128 + 128 * pi + 64], PT[:, pi, 1, :], Z1[st][:, pi, :])
                mm(pm[:, 128 + 128 * pi + 64:128 + 128 * (pi + 1)], PT[:, pi, 1, :], tokb[:, 0, hc])
            for pi in range(2):
                hc = slice(64 * pi, 64 * (pi + 1))
                mm(pm[hc, 384:512], tokb[:, 0, hc], PT[:, pi, 1, :], tp=(0, 64 * pi))
            cp('dve', UW[st], pm[:, 128:384].rearrange("p (h a i) -> p h a i", a=2, i=64))
            cp('act', WTs[st], pm[:, 384:512])
        for st in range(nst):
            tokb, pg = TOKH[st], pgb(st)
            for pi in range(2):
                hc = slice(64 * pi, 64 * (pi + 1))
                mm(pg[hc, 0:128], UW[st][:, pi, 1, :], NM[st][:, pi, 1, :], tp=(0, 64 * pi))
            for pi in range(2):
                hc = slice(64 * pi, 64 * (pi + 1))
                mm(pg[hc, 128:256], UW[st][:, pi, 0, :], NM[st][:, pi, 1, :], start=True, stop=False, tp=(0, 64 * pi))
                mm(pg[hc, 128:256], tokb[:, 3, hc], AK[st][:, pi, 1, :], start=False, stop=True, tp=(0, 64 * pi))
            tt('dve', GT[st], pg[:, 0:128], AR[:, st, 1, :], ALU.add)
            cp('act', HTt[st], pg[:, 128:256])
        for st in range(nst):
            tok = TOK[st]
            for q in range(2):
                S, chain = state_of_chunk(st, q)
                tq = slice(64 * q, 64 * (q + 1))
                tv = slice(64 * q, 64 * q + CV)
                for pi in range(2):
                    hc = slice(64 * pi, 64 * (pi + 1))
                    mm(bank(6 + pi)[tq, 128:192], WTs[st][hc, tq], S[hc, :], tp=(64 * pi, 64 * q))
                for pi in range(2):
                    tt('dve', US[tq, pi, :], bank(6 + pi)[tq, 128:192], UW[st][tq, pi, 0, :], ALU.add)
                for pi in range(2):
                    hc = slice(64 * pi, 64 * (pi + 1))
                    mm(bank(6 + pi)[hc, 192:256], S[hc, :], GT[st][hc, tq], tp=(64 * pi, 64 * pi))
                for pi in range(2):
                    hc = slice(64 * pi, 64 * (pi + 1))
                    tt('dve', B['YT'][hc, 128 * st + 64 * q:128 * st + 64 * (q + 1)], bank(6 + pi)[hc, 192:256], HTt[st][hc, tq],
                       ALU.add)
                psq = bank(6 + q)[:, 256:320]
                for pi in range(2):
                    hc = slice(64 * pi, 64 * (pi + 1))
                    mm(psq[hc, :], tok[tv, 1, hc], US[tv, pi, :], start=True, stop=False, tp=(64 * q, 64 * pi))
                    mm(psq[hc, :], tok[tv, 2, hc], tok[tv, 3, hc], start=False, stop=True, tp=(64 * q, 64 * pi))
                pcol = B['ein'][:, 128 * st + 64 * q + CV - 1:128 * st + 64 * q + CV]
                stt(S, S, pcol, psq, ALU.mult, ALU.add)
        pz = bank(1)
        mm(pz[:, 0:n], BDM, B['YT'])
        tt('dve', B['t1'], B['YT'], pz[:, 0:n], ALU.subtract)
        act(B['kk'], B['t1'], AF.Square)
        mm(pz[:, 0:n], BDM, B['kk'])
        act(B['kk'], pz[:, 0:n], AF.Sqrt, bias=EPSC[:, 1:2])
        P.op('dve', lambda e: e.reciprocal(out=B['kk'], in_=B['kk']), [B['kk']], [B['kk']])
        tt('pool', B['t1'], B['t1'], B['kk'], ALU.mult)
        ts('dve', B['t1'], B['t1'], par(P_LW), ALU.mult, par(P_LB), ALU.add)
        tt('pool', B['t1'], B['t1'], B['bonv'], ALU.add)
        tt('dve', ROB[:, 0:n], B['t1'], B['g'], ALU.mult)
        out_dst(ROB[:, 0:n])

    def rwkv_tile(ntok, cur_ap, prv_ap, state_of_chunk, CV, out_dst_fc):
        def proj(ps, col0):
            for kc in range(8):
                mm(ps, W12[:, kc, 0, col0:col0 + 128], cur_ap(kc), start=(kc == 0), stop=False)
                mm(ps, W12[:, kc, 1, col0:col0 + 128], prv_ap(kc), start=False, stop=(kc == 7))
        n = ntok
        if RPRE < -1:
            return
        pw = bank(0)
        proj(pw[:, 0:n], 1536)
        if RPRE < 0:
            return
        act(TW[0:64, 0:n], pw[0:64, 0:n], AF.Tanh)
        cp('dve', AL[64:128, 0:n], pw[64:128, 0:n])
        proj(pw[:, 0:n], 1664)
        act(SG[:, 0:n], pw[:, 0:n], AF.Sigmoid)
        for fc in range(4):
            pr, pk, pv = bank(0), bank(1), bank(2)
            proj(pr[:, 0:n], 128 * fc)
            proj(pk[:, 0:n], 512 + 128 * fc)
            proj(pv[:, 0:n], 1024 + 128 * fc)
            rwkv_fc(fc, n, pr[:, 0:n], pk[:, 0:n], pv[:, 0:n], lambda st, q, fc=fc: state_of_chunk(fc, st, q), CV,
                    lambda rob, fc=fc: out_dst_fc(fc, rob))

    for fc in range(4):
        memset('pool', SST[:, 0, fc, :], 0.0)

    for it in range(NT):
        for st in range(4):
            t0 = 512 * it + 128 * st
            last = (t0 + 128 == T)

            def ld(xt, t0=t0):
                P.dma('sp', xt, x_p[t0:t0 + 128, :])

            def f32row(hf):
                P.dma('pool', o_pshift[0:1, :], hf[127:128, :])

            pb = norm_tile(ld, None, want_f32_row=f32row if last else None)
            cp(rot(['act', 'dve']), HTCUR[it % 2][:, :, 128 * st:128 * (st + 1)], pb.rearrange("p (k t) -> p k t", t=128))
        hcur = HTCUR[it % 2]
        P.dma('pool', hts_s[:, 512 * it:512 * (it + 1)].rearrange("(k p) t -> p k t", p=128), hcur)
        for lo in (range(0, 512, RT) if 'R' in stages else []):
            tb = 512 * it + lo
            if lo == 0:
                if it == 0:
                    memset('pool', HTP[:, :, 0:1], 0.0)
                else:
                    cp('dve', HTP[:, :, 0:1], HLAST[:, :, 0:1])
                cp('dve', HTP[:, :, 1:RT], hcur[:, :, 0:RT - 1])
            else:
                cp('dve', HTP, hcur[:, :, lo - 1:lo - 1 + RT])
            if lo + RT == 512:
                cp('dve', HLAST[:, :, 0:1], hcur[:, :, 511:512])
            rwkv_tile(RT,
                      lambda kc, lo=lo, hcur=hcur: hcur[:, kc, lo:lo + RT],
                      lambda kc: HTP[:, kc, :],
                      lambda fc, st, q: (SST[:, 0, fc, :], True), 64,
                      lambda fc, rob, tb=tb: P.dma('pool', mixp_s[128 * fc:128 * (fc + 1), tb:tb + RT], rob))
    if 'v' in stages:
        memset('pool', SNAT, 1.0)
    for fc in (range(4) if ('R' in stages and 'z' not in stages and 'v' not in stages) else []):
        for pi in range(2):
            tr(bank(4 + pi)[0:64, 0:64], SST[64 * pi:64 * (pi + 1), 0, fc, :], IDF[64 * pi:64 * (pi + 1), 64 * pi:64 * (pi + 1)])
            cp('act', SNAT[0:64, 2 * fc + pi, :], bank(4 + pi)[0:64, 0:64])
    if 'R' in stages and 'z' not in stages and 'u' not in stages:
        P.dma('sp', o_pwkv.rearrange("h i j -> i h j"), SNAT[0:64, :, :])


    HTC = XT[0][:, 0:512].bitcast(BF16).rearrange("p (k t) -> p k t", t=128)
    HPF = TOKB.rearrange("p a q c -> p (a q c)")[:, 0:D]
    if 'S' in stages:
        memset('pool', HTS, 0.0)
        for tile in range(2):
            memset('pool', HPF, 0.0)
            for q in range(2):
                P.dma('sp', HPF[64 * q + 63:64 * q + 64, :], st_shift[2 * tile + q:2 * tile + q + 1, :])

            def ld(xt, tile=tile):
                memset('pool', xt, 0.0)
                for q in range(2):
                    P.dma('sp', xt[64 * q:64 * q + 8, :], x_s[2 * tile + q, :, :])

            def f32row(hf, tile=tile):
                for q in range(2):
                    P.dma('pool', o_sshift[2 * tile + q:2 * tile + q + 1, :], hf[64 * q + 7:64 * q + 8, :])

            def hprev(hb):
                tt('dve', hb, hb, HPF, ALU.add)

            pb = norm_tile(ld, None, want_f32_row=f32row, hprev_fn=hprev)
            pb4 = pb.rearrange("p (k q j) -> p k q j", q=2, j=64)
            cp('act', HTS[:, :, 2 * tile:2 * tile + 2, 2:65], pb4[:, :, :, 0:63])
            cp('dve', HTS[:, :, 2 * tile:2 * tile + 2, 1:2], pb4[:, :, :, 63:64])
        for tile in (range(2) if 'R' in stages else []):
            for q in range(2):
                seq = 2 * tile + q
                P.dma('sp', SNAT[0:64, :, :], st_wkv[seq].rearrange("h i j -> i h j"))
                for h_ in range(8):
                    pi, fc = h_ % 2, h_ // 2
                    tr(bank(4)[0:64, 64 * (h_ % 8):64 * (h_ % 8) + 64], SNAT[0:64, h_, :], IDF[0:64, 0:64])
                for h_ in range(8):
                    pi, fc = h_ % 2, h_ // 2
                    cp('act' if pi else 'dve', SST[64 * pi:64 * (pi + 1), 1 + seq, fc, :], bank(4)[0:64, 64 * h_:64 * h_ + 64])
            cp('dve', HTC.rearrange("p k (q j) -> p k q j", j=64), HTS[:, :, 2 * tile:2 * tile + 2, 2:66])
            cp('dve', HTP[:, :, 0:128].rearrange("p k (q j) -> p k q j", j=64), HTS[:, :, 2 * tile:2 * tile + 2, 1:65])
            rwkv_tile(128,
                      lambda kc: HTC[:, kc, :],
                      lambda kc: HTP[:, kc, 0:128],
                      lambda fc, st, q, tile=tile: (SST[:, 1 + 2 * tile + q, fc, :], False), 8,
                      lambda fc, rob, tile=tile: P.dma('pool', mixs_s[128 * fc:128 * (fc + 1), 128 * tile:128 * (tile + 1)], rob))
            for q in range(2):
                seq = 2 * tile + q
                for fc in range(4):
                    for pi in range(2):
                        tr(bank(4 + pi)[0:64, 0:64], SST[64 * pi:64 * (pi + 1), 1 + seq, fc, :],
                           IDF[64 * pi:64 * (pi + 1), 64 * pi:64 * (pi + 1)])
                        cp('act', SNAT[0:64, 2 * fc + pi, :], bank(4 + pi)[0:64, 0:64])
                P.dma('sp', o_swkv[seq].rearrange("h i j -> i h j"), SNAT[0:64, :, :])


    SB.release(mA)

    mB = SB.mark()
    HT = SB.alloc((8, T), BF16)
    for kc in range(8):
        P.dma('sp', HT[:, kc, :], hts_s[128 * kc:128 * (kc + 1), :])
    WQKV = SB.alloc((8, 1536), BF16)
    mW = SB.mark()
    alloc_wst()
    for kc in range(8):
        rows = slice(128 * kc, 128 * (kc + 1))
        load_w(WQKV[:, kc, 0:1024], w_in[rows, NCOL_R:NCOL_R + 1024], 1024)
        load_w(WQKV[:, kc, 1024:1536], w_in[rows, NCOL_R + 1024:D_IN], 512)
    SB.release(mW)

    mS0 = SB.mark()
    if 'S' in stages:
        QSF = SB.alloc((4, 256), F32)
        KSF = SB.alloc((4, 256), F32)
        ATTS = SB.alloc((4, 256), BF16)
        KNEW = SB.alloc((2, 512), F32)
        VNEW = SB.alloc((2, 512), F32)
        CSS = SB.alloc((2, 128), F32)
        XBs = SB.alloc((128,), F32)
        SM = SB.alloc((8,), F32)
        ONESF = SB.alloc((64,), F32)
        RD = SB.alloc((8,), F32)
        HTC2 = SB.alloc((8, 128), BF16)
        RT1s = SB.alloc((128,), F32)
        RT2s = SB.alloc((128,), F32)
    mK = SB.mark()
    VB = [SB.alloc((512,), BF16) for _ in range(2)]
    VF = [SB.alloc((512,), F32) for _ in range(2)]
    KF = [SB.alloc((512,), F32) for _ in range(2)]
    KO = [SB.alloc((512,), F32) for _ in range(2)]
    KTMP = [SB.alloc((256,), F32) for _ in range(4)]
    CST = [SB.alloc((2, 32), F32) for _ in range(2)]

    def tokmajor_kv(lhs_ap, n_rows, i, v_bf16_dst, v_f32_dst, k_dst, cos_src, sin_src):
        pv = bank(i % 2)
        for kc in range(8):
            mm(pv, lhs_ap(kc), WQKV[:, kc, 1024:1536], start=(kc == 0), stop=(kc == 7))
        if v_bf16_dst is not None:
            cp('act', VB[i % 2], pv)
            P.dma('pool', v_bf16_dst, VB[i % 2][0:n_rows, :])
        if v_f32_dst is not None:
            cp('dve', VF[i % 2], pv)
            v_f32_dst(VF[i % 2])
        if k_dst is not None:
            pk = bank(2 + i % 2)
            for kc in range(8):
                mm(pk, lhs_ap(kc), WQKV[:, kc, 512:1024], start=(kc == 0), stop=(kc == 7))
            kf, ko, cs_ = KF[i % 2], KO[i % 2], CST[i % 2]
            cp('act', kf, pk)
            P.dma('sp', cs_[:, 0, :], cos_src)
            P.dma('sp', cs_[:, 1, :], sin_src)
            k4 = kf.rearrange("p (h a d) -> p h a d", a=2, d=32)
            o4 = ko.rearrange("p (h a d) -> p h a d", a=2, d=32)
            cb = cs_[:, 0, :].unsqueeze(1).to_broadcast([128, 8, 32])
            sb_ = cs_[:, 1, :].unsqueeze(1).to_broadcast([128, 8, 32])
            t = [x_.rearrange("p (h d) -> p h d", d=32) for x_ in KTMP]
            tt('dve', t[0], k4[:, :, 0, :], cb, ALU.mult)
            tt('pool', t[1], k4[:, :, 1, :], sb_, ALU.mult)
            tt('dve', o4[:, :, 0, :], t[0], t[1], ALU.subtract)
            tt('pool', t[2], k4[:, :, 0, :], sb_, ALU.mult)
            tt('dve', t[3], k4[:, :, 1, :], cb, ALU.mult)
            tt('pool', o4[:, :, 1, :], t[2], t[3], ALU.add)
            k_dst(ko)

    if 'B' in stages:
        for it in range(T // 128):
            t0 = 128 * it
            keep = t0 >= T - KEEP
            r0 = t0 - (T - KEEP)
            tokmajor_kv(lambda kc, t0=t0: HT[:, kc, t0:t0 + 128], 128, it, vs_s[t0:t0 + 128, :],
                        (lambda vf, r0=r0: P.dma('pool', o_pv[r0:r0 + 128, :], vf)) if keep else None,
                        (lambda ko, r0=r0: P.dma('pool', o_pk[r0:r0 + 128, :], ko)) if keep else None,
                        cost_p[r0:r0 + 128, :] if keep else None, sint_p[r0:r0 + 128, :] if keep else None)

    if 'S' in stages:
        memset('pool', ONESF, 1.0)
        memset('pool', ATTS, 0.0)
        for q in range(2):
            P.dma('sp', CSS[:, 0, 64 * q:64 * (q + 1)], cos_s[:, :])
            P.dma('sp', CSS[:, 1, 64 * q:64 * (q + 1)], sin_s[:, :])
        for tile in range(2):
            cp('dve', HTC2.rearrange("p k (q j) -> p k q j", j=64), HTS[:, :, 2 * tile:2 * tile + 2, 2:66])
            tokmajor_kv(lambda kc: HTC2[:, kc, :], 128, tile, None,
                        lambda vf, tile=tile: cp('pool', VNEW[:, tile, :], vf),
                        lambda ko, tile=tile: cp('pool', KNEW[:, tile, :], ko),
                        cost_s[:, :], sint_s[:, :])
            for q in range(2):
                seq = 2 * tile + q
                P.dma('pool', o_sk[seq, 2040:2048, :], KNEW[64 * q:64 * q + 8, tile, :])
                P.dma('pool', o_sv[seq, 2040:2048, :], VNEW[64 * q:64 * q + 8, tile, :])
            for hp in range(4):
                for which, c0_, dstF in ((0, 128 * hp, QSF), (1, 512 + 128 * hp, KSF)):
                    ps = bank(which)
                    for kc in range(8):
                        mm(ps[:, 0:128], WQKV[:, kc, c0_:c0_ + 128], HTC2[:, kc, :], start=(kc == 0), stop=(kc == 7))
                    cp('act', XBs, ps[:, 0:128])
                    pw_ = bank(7)
                    mm(pw_[:, 0:128], PERMF, XBs)
                    tt('pool', RT1s, XBs, CSS[:, 0, :], ALU.mult)
                    tt('dve', RT2s, pw_[:, 0:128], CSS[:, 1, :], ALU.mult)
                    tt('pool', dstF[:, hp, 128 * tile:128 * (tile + 1)], RT1s, RT2s, ALU.add)
    SB.release(mK)
    if 'S' in stages:
        KC = SB.alloc((17, 512), F32)
        KTC = SB.alloc((17 * 128,), F32)
        SSB = SB.alloc((17 * 128,), F32)
        LB = SB.alloc((17 * 128,), F32)
        PTs = SB.alloc((8, 17, 8), F32)
        P.dma('sp', LB[0:8, :], lb_s[:, :])
        for b_ in range(4):
            tile, q = b_ // 2, b_ % 2
            for t8 in range(0, 16, 4):
                P.dma('sp', KC[:, t8:t8 + 4, :], ck[b_, 128 * t8:128 * (t8 + 4), :].rearrange("(t p) c -> p t c", p=128))
            memset('pool', KC[:, 16, :], 0.0)
            cp('dve', KC[0:8, 16, :], KNEW[64 * q:64 * q + 8, tile, :])
            for h_ in range(8):
                pi, hp = h_ % 2, h_ // 2
                hs = slice(64 * pi, 64 * (pi + 1))
                hcol = slice(64 * h_, 64 * (h_ + 1))
                for g in range(5):
                    nt_ = 4 if g < 4 else 1
                    pk = bank(2 * (g % 2))
                    for j in range(nt_):
                        tr(pk[0:64, 128 * j:128 * (j + 1)], KC[:, 4 * g + j, hcol], IDF)
                    cp('act' if g % 2 else 'dve', KTC[hs, 512 * g:512 * g + 128 * nt_], pk[0:64, 0:128 * nt_])
                qv = QSF[hs, hp, 64 * b_:64 * b_ + 8]
                for g in range(5):
                    nn = 512 if g < 4 else 128
                    psc = bank(4 + pi + 2 * (g % 2))
                    mm(psc[0:8, 0:nn], qv, KTC[hs, 512 * g:512 * g + nn])
                    stt(SSB[0:8, 512 * g:512 * g + nn], psc[0:8, 0:nn], 0.125, LB[0:8, 512 * g:512 * g + nn], ALU.mult, ALU.add)
                P.op('dve', lambda e: e.tensor_reduce(out=SM[0:8, 0:1], in_=SSB[0:8, :], axis=AX.X, op=ALU.max),
                     [SM[0:8, 0:1]], [SSB[0:8, :]])
                ts('dve', SM[0:8, 1:2], SM[0:8, 0:1], -1.0, ALU.mult)
                act(SSB[0:8, :], SSB[0:8, :], AF.Exp, bias=SM[0:8, 1:2])
                pt_ = bank(0)
                for c_ in range(17):
                    tr(pt_[:, 8 * c_:8 * (c_ + 1)], SSB[0:8, 128 * c_:128 * (c_ + 1)], IDF[0:8, 0:8])
                cp('act', PTs[:, h_, :, :], pt_[:, 0:136].rearrange("p (c t) -> p c t", t=8))
            for t8 in range(0, 16, 4):
                P.dma('sp', KC[:, t8:t8 + 4, :], cv[b_, 128 * t8:128 * (t8 + 4), :].rearrange("(t p) c -> p t c", p=128))
            memset('pool', KC[:, 16, :], 0.0)
            cp('dve', KC[0:8, 16, :], VNEW[64 * q:64 * q + 8, tile, :])
            for h_ in range(8):
                pi, hp = h_ % 2, h_ // 2
                hs = slice(64 * pi, 64 * (pi + 1))
                hcol = slice(64 * h_, 64 * (h_ + 1))
                pn = bank(1)
                for c_ in range(17):
                    mm(pn[hs, 0:8], KC[:, c_, hcol], PTs[:, h_, c_, :], start=(c_ == 0), stop=(c_ == 16), tp=(0, 64 * pi))
                pd = bank(3)
                for c_ in range(17):
                    mm(pd[hs, 0:8], ONESF, PTs[:, h_, c_, :], start=(c_ == 0), stop=(c_ == 16), tp=(0, 64 * pi))
                cp('act', RD[hs, :], pd[hs, 0:8])
                P.op('dve', lambda e, hs=hs: e.reciprocal(out=RD[hs, :], in_=RD[hs, :]), [RD[hs, :]], [RD[hs, :]])
                tt('dve', ATTS[hs, hp, 64 * b_:64 * b_ + 8], pn[hs, 0:8], RD[hs, :], ALU.mult)
        for hp in range(4):
            P.dma('pool', mixs_s[512 + 128 * hp:512 + 128 * (hp + 1), :], ATTS[:, hp, :])
    SB.release(mS0)

    QT2 = SB.alloc((T, 2), BF16)
    KT2 = SB.alloc((T, 2), BF16)
    NV = {1: T // 128, 4: T // 128, 16: T // 128}
    VD = {d: SB.alloc((T // 128, 128), BF16) for d in (1, 4, 16)}
    NUMA = SB.alloc((2048,), F32)
    DENA = SB.alloc((2048,), F32)
    ATT = SB.alloc((2048,), BF16)
    XB = [SB.alloc((512,), BF16) for _ in range(2)]
    RT1 = [SB.alloc((512,), F32) for _ in range(2)]
    RT2 = [SB.alloc((512,), F32) for _ in range(2)]
    CSF = [SB.alloc((2, 512), F32) for _ in range(2)]
    EB = [SB.alloc((256,), BF16) for _ in range(8)]
    EM = [SB.alloc((256,), BF16) for _ in range(8)]
    PTS = [SB.alloc((2, 128), BF16) for _ in range(8)]
    BIAS = SB.alloc((8,), F32)
    DJ = [SB.alloc((128,), F32) for _ in range(2)]
    uidx = [0]

    def rope_fm(ps, xb, t1, t2, cs_, dst, n):
        cp('act', xb[:, 0:n], ps)
        pw_ = bank(7)
        mm(pw_[:, 0:n], PERM, xb[:, 0:n])
        tt('pool', t1[:, 0:n], xb[:, 0:n], cs_[:, 0, 0:n], ALU.mult)
        tt('dve', t2[:, 0:n], pw_[:, 0:n], cs_[:, 1, 0:n], ALU.mult)
        tt('pool', dst, t1[:, 0:n], t2[:, 0:n], ALU.add)

    for hp in (range(4) if ('B' in stages and BCUT >= 1) else []):
        qc = slice(128 * hp, 128 * (hp + 1))
        kcs = slice(512 + 128 * hp, 512 + 128 * (hp + 1))
        for it in range(NT):
            tb = 512 * it
            cs_ = CSF[it % 2]
            P.dma('sp', cs_[:, 0, :], cos_p[:, tb:tb + 512])
            P.dma('sp', cs_[:, 1, :], sin_p[:, tb:tb + 512])
            for which, cols, dstT in ((0, qc, QT2), (1, kcs, KT2)):
                ps = bank(which)
                for kc in range(8):
                    mm(ps, WQKV[:, kc, cols], HT[:, kc, tb:tb + 512], start=(kc == 0), stop=(kc == 7))
                rope_fm(ps, XB[which], RT1[which], RT2[which], cs_, dstT[:, tb:tb + 512, 0], 512)
        if BCUT < 2:
            continue
        for d in (1, 4, 16):
            npc = T // d // 128
            for r in range(d):
                src = vs_s[r:T:d, 128 * hp:128 * (hp + 1)].rearrange("(mt mi) c -> mi mt c", mi=128)
                for m0_ in range(0, npc, 8):
                    m1_ = min(npc, m0_ + 8)
                    P.dma('sp', VD[d][:, r * npc + m0_:r * npc + m1_, :], src[:, m0_:m1_, :])
        if BCUT < 3:
            continue
        for H in range(T // 2048):
            units = []
            for d in (1, 4, 16):
                npc = T // d // 128
                tph = 16 // d
                for r in range(d):
                    for mt in range(H * tph, (H + 1) * tph):
                        nkb = 1 if mt == 0 else 2
                        q0 = r + d * 128 * mt
                        k0 = r + d * 128 * (mt - 1) if nkb == 2 else q0
                        units.append((d, r, mt, nkb, q0, k0, q0 - 2048 * H, npc))
            GU = 4
            for g0 in range(0, len(units), GU):
                grp = units[g0:g0 + GU]
                hu = [(g, pi) for g in range(len(grp)) for pi in range(2)]
                for g, pi in hu:
                    d, r, mt, nkb, q0, k0, lo, npc = grp[g]
                    hs = slice(64 * pi, 64 * (pi + 1))
                    sb_ = bank(pi + 2 * (g % 2))[:, 256 * (g // 2):256 * (g // 2 + 1)]
                    c0 = 0 if nkb == 2 else 128
                    mm(sb_[:, c0:256], QT2[hs, sl(q0, 128, d), 0], KT2[hs, sl(k0, 128 * nkb, d), 0])
                for g, pi in hu:
                    sb_ = bank(pi + 2 * (g % 2))[:, 256 * (g // 2):256 * (g // 2 + 1)]
                    u = 2 * g + pi
                    stt(DJ[pi], sb_[:, 128:256], -0.125, IDF, ALU.mult, ALU.mult, accum=BIAS[:, u:u + 1])
                for g, pi in hu:
                    d, r, mt, nkb, q0, k0, lo, npc = grp[g]
                    sb_ = bank(pi + 2 * (g % 2))[:, 256 * (g // 2):256 * (g // 2 + 1)]
                    u = 2 * g + pi
                    c0 = 0 if nkb == 2 else 128
                    act(EB[u][:, c0:256], sb_[:, c0:256], AF.Exp, bias=BIAS[:, u:u + 1], scale=0.125)
                for g, pi in hu:
                    d, r, mt, nkb, q0, k0, lo, npc = grp[g]
                    u = 2 * g + pi
                    c0 = 0 if nkb == 2 else 128
                    tt('pool', EM[u][:, c0:256], EB[u][:, c0:256], MASK2[:, c0:256], ALU.mult)
                for g, pi in hu:
                    d, r, mt, nkb, q0, k0, lo, npc = grp[g]
                    u = 2 * g + pi
                    pt_ = bank(4 + pi).bitcast(BF16)[:, 256 * g:256 * (g + 1)]
                    for kb in range(2 - nkb, 2):
                        tr(pt_[:, 128 * kb:128 * (kb + 1)], EM[u][:, 128 * kb:128 * (kb + 1)], IDB)
                for g, pi in hu:
                    d, r, mt, nkb, q0, k0, lo, npc = grp[g]
                    u = 2 * g + pi
                    pt_ = bank(4 + pi).bitcast(BF16)[:, 256 * g:256 * (g + 1)]
                    cp('act' if pi == 0 else 'dve', PTS[u][:, 2 - nkb:2, :],
                       pt_[:, 128 * (2 - nkb):256].rearrange("p (k q) -> p k q", q=128))
                for g, pi in hu:
                    d, r, mt, nkb, q0, k0, lo, npc = grp[g]
                    u = 2 * g + pi
                    hs = slice(64 * pi, 64 * (pi + 1))
                    pn, pd = bank(6)[:, 128 * g:128 * (g + 1)], bank(7)[:, 128 * g:128 * (g + 1)]
                    for kb in range(2 - nkb, 2):
                        vt = r * npc + mt - 1 + kb
                        mm(pn[hs, :], VD[d][:, vt, hs], PTS[u][:, kb, :], start=(kb == 2 - nkb), stop=(kb == 1), tp=(0, 64 * pi))
                    for kb in range(2 - nkb, 2):
                        mm(pd[hs, :], ONESB, PTS[u][:, kb, :], start=(kb == 2 - nkb), stop=(kb == 1), tp=(0, 64 * pi))
                for g in range(len(grp)):
                    d, r, mt, nkb, q0, k0, lo, npc = grp[g]
                    pn, pd = bank(6)[:, 128 * g:128 * (g + 1)], bank(7)[:, 128 * g:128 * (g + 1)]
                    dstn = NUMA[:, sl(lo, 128, d)]
                    dstd = DENA[:, sl(lo, 128, d)]
                    if d == 1:
                        cp('dve', dstn, pn)
                        cp('act', dstd, pd)
                    else:
                        tt('dve', dstn, pn, dstn, ALU.add)
                        tt('dve', dstd, pd, dstd, ALU.add)
            P.op('dve', lambda e: e.reciprocal(out=DENA, in_=DENA), [DENA], [DENA])
            tt('pool', ATT, NUMA, DENA, ALU.mult)
            P.dma('pool', mixp_s[512 + 128 * hp:512 + 128 * (hp + 1), 2048 * H:2048 * (H + 1)], ATT)
    if debug:
        dbg['mix'] = dout("dbg_mix", [D, T], BF16)
        P.dma('pool', dbg['mix'], mixp_s)
    SB.release(m0)

    mC = SB.mark()
    WU = SB.alloc((8, 2 * DFF), BF16)
    WD = SB.alloc((22, D), BF16)
    GV = SB.alloc((3, D), F32)
    CW = SB.alloc((44, 4), F32)
    CAR = SB.alloc((5, 44, 2), F32)
    WOS = [SB.alloc((2, D), BF16) for _ in range(2)]
    mW = SB.mark()
    if 'C' in stages:
        alloc_wst()
        for i, gsrc in enumerate((n_mix_post, n_ffn_pre, n_ffn_post)):
            P.dma('sp', GV[:, i, :], gsrc[0:1, :].partition_broadcast(128))
        for j in range(3):
            dma_fc('sp', lambda c0, c1, j=j: CW[:, c0:c1, j], cw_d[j, :], False)
        dma_fc('sp', lambda c0, c1: CW[:, c0:c1, 3], cb_d.rearrange("f o -> (f o)"), False)
        for kc in range(8):
            rows = slice(128 * kc, 128 * (kc + 1))
            for j in range(0, 2 * DFF, 1024):
                n_ = min(1024, 2 * DFF - j)
                load_w(WU[:, kc, j:j + n_], w_up[rows, j:j + n_], n_)
        for fcn in range(22):
            load_w(WD[:, fcn, :], w_dn[128 * fcn:128 * (fcn + 1), :], 1024)
        for kc in range(8):
            load_w(WOS[kc % 2][:, 0, :], w_out[128 * kc:128 * (kc + 1), :], 1024)
            P.dma('pool', wout_s[128 * kc:128 * (kc + 1), :], WOS[kc % 2][:, 0, :])
    SB.release(mW)
    MIXT = [SB.alloc((8, 128), BF16) for _ in range(2)]
    WOF = [SB.alloc((512,), F32) for _ in range(2)]
    XC = [SB.alloc((D,), F32) for _ in range(2)]
    H2B = SB.alloc((D,), BF16)
    H2T = SB.alloc((8, 128), BF16)
    UE = [[SB.alloc((4, 130), F32)] * 2 for _ in range(2)]
    CGU = [SB.alloc((4, 128), F32) for _ in range(2)]
    CT = [SB.alloc((128,), F32) for _ in range(4)]
    FTB = [SB.alloc((22, 128), BF16) for _ in range(2)]
    JK = SB.alloc((D,), BF16)
    ST2 = SB.alloc((8, 4), F32)
    EPS2 = SB.alloc((1,), F32)
    ci = [0]
    pend = [None]

    def ffn_tile(*a_, **kw_):
        g_ = ffn_tile_gen(*a_, **kw_)
        next(g_)
        ffn_flush()
        next(g_)
        next(g_)
        pend[0] = g_

    def ffn_flush():
        if pend[0] is not None:
            for _ in pend[0]:
                pass
            pend[0] = None

    def rms_from(src_halves, si, col):
        act(JK[:, 0:512], src_halves[0], AF.Square, accum=ST2[:, si, 0:1])
        act(JK[:, 512:1024], src_halves[1], AF.Square, accum=ST2[:, si, 1:2])
        tt('dve', ST2[:, si, 0:1], ST2[:, si, 0:1], ST2[:, si, 1:2], ALU.add)
        act(ST2[:, si, 1:2], ST2[:, si, 0:1], AF.Sqrt, bias=EPS2[:, 0:1], scale=1.0 / D)
        P.op('dve', lambda e: e.reciprocal(out=ST2[:, si, col:col + 1], in_=ST2[:, si, 1:2]),
             [ST2[:, si, col:col + 1]], [ST2[:, si, 1:2]])

    def ffn_tile_gen(mix_src, x_src, y_dst, stream, segs, conv_out=None):
        i = ci[0] % 2
        si = ci[0] % 8
        ci[0] += 1
        mixt, xc = MIXT[i], XC[i]
        FT = FTB[i]
        P.dma('sp', mixt, mix_src.rearrange("(k p) t -> p k t", p=128))
        x_src(xc)
        for kq in range(4):
            wos = WOS[kq % 2]
            P.dma('sp', wos, wout_s[256 * kq:256 * (kq + 1), :].rearrange("(k p) c -> p k c", p=128))
            for k2 in range(2):
                kc = 2 * kq + k2
                for half in range(2):
                    mm(bank(half), mixt[:, kc, :], wos[:, k2, 512 * half:512 * (half + 1)], start=(kc == 0), stop=(kc == 7))
        yield 0
        rms_from((bank(0), bank(1)), si, 2)
        for half in range(2):
            hs_ = slice(512 * half, 512 * (half + 1))
            stt(WOF[half][:, 0:512], bank(half), ST2[:, si, 2:3], GV[:, 0, hs_], ALU.mult, ALU.mult)
            tt('pool', xc[:, hs_], xc[:, hs_], WOF[half][:, 0:512], ALU.add)
        rms_from((xc[:, 0:512], xc[:, 512:1024]), si, 3)
        stt(H2B, xc, ST2[:, si, 3:4], GV[:, 1, :], ALU.mult, ALU.mult)
        pb = bank(2).bitcast(BF16)
        for kc in range(8):
            tr(pb[:, 128 * kc:128 * (kc + 1)], H2B[:, 128 * kc:128 * (kc + 1)], IDB)
        cp('act', H2T, pb.rearrange("p (k t) -> p k t", t=128))
        g0 = 0
        gi = 0
        while g0 < 22:
            ng = min(4, 22 - g0)
            for side in range(2):
                ps = bank(3 + 2 * (gi % 2) + side)
                for c in range(ng):
                    ch = 22 * side + g0 + c
                    for kc in range(8):
                        mm(ps[:, 128 * c:128 * (c + 1)], WU[:, kc, 128 * ch:128 * (ch + 1)], H2T[:, kc, :],
                           start=(kc == 0), stop=(kc == 7))
            for side in range(2):
                ps = bank(3 + 2 * (gi % 2) + side)
                ue = UE[side][gi % 2]
                chs = slice(22 * side + g0, 22 * side + g0 + ng)
                cp('act', ue[:, 0:ng, 2:130], ps[:, 0:128 * ng].rearrange("p (c t) -> p c t", t=128))
                for (col0, ncols, cinit) in segs:
                    if cinit is None:
                        cp('pool', ue[:, 0:ng, col0:col0 + 2], CAR[:, stream, chs, :])
                    else:
                        cinit(ue[:, 0:ng, col0:col0 + 2], chs)
                for c in range(ng):
                    ch = 22 * side + g0 + c
                    t_ = CT[(2 * c + side) % 4]
                    act(t_, ue[:, c, 0:128], AF.Identity, bias=CW[:, ch, 3:4], scale=CW[:, ch, 0:1])
                    stt(t_, ue[:, c, 1:129], CW[:, ch, 1:2], t_, ALU.mult, ALU.add)
                    stt(CGU[side][:, c, :], ue[:, c, 2:130], CW[:, ch, 2:3], t_, ALU.mult, ALU.add)
                last_col = segs[-1][0] + segs[-1][1]
                cp('pool', CAR[:, stream, chs, :], ue[:, 0:ng, last_col:last_col + 2])
                if conv_out is not None:
                    conv_out(ue, chs, ng)
            act(CGU[0][:, 0:ng, :], CGU[0][:, 0:ng, :], AF.Silu)
            tt('pool', FT[:, g0:g0 + ng, :], CGU[0][:, 0:ng, :], CGU[1][:, 0:ng, :], ALU.mult)
            g0 += ng
            gi += 1
            if gi == 2:
                yield 1
        yield 2
        dbk = (bank(7), bank(2))
        for half in range(2):
            ps = dbk[half]
            for fcn in range(22):
                mm(ps, FT[:, fcn, :], WD[:, fcn, 512 * half:512 * (half + 1)], start=(fcn == 0), stop=(fcn == 21))
        rms_from(dbk, si, 2)
        for half in range(2):
            hs_ = slice(512 * half, 512 * (half + 1))
            stt(WOF[half][:, 0:512], dbk[half], ST2[:, si, 2:3], GV[:, 2, hs_], ALU.mult, ALU.mult)
            tt('pool', xc[:, hs_], xc[:, hs_], WOF[half][:, 0:512], ALU.add)
        y_dst(xc)

    if 'C' in stages:
        memset('pool', EPS2, NORM_EPS)
        memset('pool', CAR, 0.0)
        for it in range(T // 128):
            t0 = 128 * it
            ffn_tile(mixp_s[:, t0:t0 + 128],
                     lambda xc, t0=t0: P.dma('sp', xc, x_p[t0:t0 + 128, :]),
                     lambda xc, t0=t0: P.dma('pool', y_p[t0:t0 + 128, :], xc),
                     0, [(0, 128, None)])
        ffn_flush()
        if 'S' in stages:
            CONVS = SB.alloc((4, 44, 2), F32)
            SCONV = SB.alloc((4, 44, 2), F32)
            for seq in range(4):
                for r_ in range(2):
                    dma_fc('sp', lambda c0, c1, seq=seq, r_=r_: CONVS[:, seq, c0:c1, r_], st_conv[seq, r_, :], False)
            for tile in range(2):
                def xs_(xc, tile=tile):
                    memset('pool', xc, 0.0)
                    for q in range(2):
                        P.dma('sp', xc[64 * q:64 * q + 8, :], x_s[2 * tile + q, :, :])

                def ys_(xc, tile=tile):
                    for q in range(2):
                        P.dma('pool', y_s[2 * tile + q, :, :], xc[64 * q:64 * q + 8, :])

                def mk_init(seq):
                    return lambda dst, chs, seq=seq: cp('pool', dst, CONVS[:, seq, chs, :])

                def cout(ue, chs, ng, tile=tile):
                    for q in range(2):
                        cp('pool', SCONV[:, 2 * tile + q, chs, :], ue[:, 0:ng, 2 + 64 * q + 6:2 + 64 * q + 8])

                ffn_tile(mixs_s[:, 128 * tile:128 * (tile + 1)], xs_, ys_, 1,
                         [(0, 64, mk_init(2 * tile)), (64, 64, mk_init(2 * tile + 1))], conv_out=cout)
            ffn_flush()
            for seq in range(4):
                for r_ in range(2):
                    dma_fc('pool', lambda c0, c1, seq=seq, r_=r_: SCONV[:, seq, c0:c1, r_], o_sconv[seq, r_, :], True)
        for r_ in range(2):
            dma_fc('pool', lambda c0, c1, r_=r_: CAR[:, 0, c0:c1, r_], o_pconv[r_, :], True)
    SB.release(mC)

    P.final_wait_all('sp')
    P.build()
    return nc


def _rope_tables(T):
    half = 32
    inv = (np.float32(10000.0) ** (-np.arange(half, dtype=np.float32) / np.float32(half))).astype(np.float32)

    def tab(pos):
        ang = pos.astype(np.float32)[:, None] * inv[None, :]
        return np.cos(ang).astype(np.float32), np.sin(ang).astype(np.float32)
    cp_, sp_ = tab(np.arange(T))
    cs_, ss_ = tab(PAST_LEN + np.arange(64))
    def fm(c, s):
        cf = np.concatenate([c.T, c.T], 0)
        sf = np.concatenate([-s.T, s.T], 0)
        return (np.ascontiguousarray(np.concatenate([cf, cf], 0)), np.ascontiguousarray(np.concatenate([sf, sf], 0)))
    return fm(cp_, sp_), fm(cs_, ss_), (cp_, sp_), (cs_, ss_)


def _lb_table():
    cnt = np.zeros((8, 17 * 128), np.float64)
    for t in range(8):
        for d in (1, 4, 16):
            for j in range(129):
                idx = WBUF + t - j * d
                if idx >= 0:
                    cnt[t, idx] += 1
    lb = np.where(cnt > 0, np.log(np.maximum(cnt, 1)), -30000.0)
    return lb.astype(np.float32)


def _core_inputs(inp, c, T):
    (cfp, sfp), (cfs, sfs), (ctp, stp), (cts, sts) = _rope_tables(T)
    KEEP = min(WBUF, T)
    f = lambda a: np.ascontiguousarray(np.asarray(a, dtype=np.float32))
    d = {
        'x_prompt': f(inp['x_prompt'][c, :T]),
        'x_sample': f(inp['x_sample'][4 * c:4 * c + 4]),
        'state_rwkv_shift': f(inp['state_rwkv_shift'][0, 4 * c:4 * c + 4]),
        'state_rwkv_wkv': f(inp['state_rwkv_wkv'][0, 4 * c:4 * c + 4]),
        'cache_att_k': f(np.asarray(inp['cache_att_k'][0, 4 * c:4 * c + 4]).reshape(4, WBUF, 512)),
        'cache_att_v': f(np.asarray(inp['cache_att_v'][0, 4 * c:4 * c + 4]).reshape(4, WBUF, 512)),
        'state_ffn_conv': f(inp['state_ffn_conv'][0, 4 * c:4 * c + 4]),
        'norm_mix_pre': f(inp['norm_mix_pre']), 'norm_mix_post': f(inp['norm_mix_post']),
        'norm_ffn_pre': f(inp['norm_ffn_pre']), 'norm_ffn_post': f(inp['norm_ffn_post']),
        'w_in': f(inp['w_in'][0]), 'mu_shift': f(inp['mu_shift']),
        'w0': f(inp['w0']).reshape(512, 1), 'w_decay_up': f(inp['w_decay_up'][0]), 'a0': f(inp['a0']).reshape(512, 1),
        'w_iclr_up': f(inp['w_iclr_up'][0]), 'w_gate_up': f(inp['w_gate_up'][0]),
        'k_k': f(inp['k_k']).reshape(512, 1), 'k_a': f(inp['k_a']).reshape(512, 1), 'r_k': f(inp['r_k']).reshape(512, 1),
        'lnx_w': f(inp['lnx_w']).reshape(512, 1), 'lnx_b': f(inp['lnx_b']).reshape(512, 1),
        'w_out': f(inp['w_out'][0]), 'w_ffn_up': f(inp['w_ffn_up'][0]), 'ffn_conv_w': f(inp['ffn_conv_w'][0]),
        'ffn_conv_b': f(inp['ffn_conv_b']).reshape(2 * DFF, 1), 'w_ffn_down': f(inp['w_ffn_down'][0]),
        'cos_p': cfp, 'sin_p': sfp, 'cos_s': cfs, 'sin_s': sfs,
        'cost_p': np.ascontiguousarray(ctp[T - KEEP:]), 'sint_p': np.ascontiguousarray(stp[T - KEEP:]),
        'cost_s': np.ascontiguousarray(np.tile(cts[:64], (2, 1))), 'sint_s': np.ascontiguousarray(np.tile(sts[:64], (2, 1))),
        'lb_s': _lb_table(),
    }
    return d


_NC_CACHE = {}


def kernel(**inputs):
    T = 4096
    n = 8
    if T not in _NC_CACHE:
        _NC_CACHE[T] = build_program(T=T, debug=False, stages='NRBCS')
    nc = _NC_CACHE[T]
    in_maps = [_core_inputs(inputs, c, T) for c in range(n)]
    res = run_bass_kernel_spmd(nc, in_maps, core_ids=list(range(n)))
    rs = res.results
    g = lambda k: [np.asarray(r[k], dtype=np.float32) for r in rs]
    y_p = np.stack(g('y_prompt'), 0)
    y_s = np.concatenate(g('y_sample'), 0)
    p_shift = np.stack([a.reshape(D) for a in g('p_shift')], 0)[None]
    p_wkv = np.stack(g('p_wkv'), 0)[None]
    p_k = np.stack([a.reshape(WBUF, 8, 64) for a in g('p_att_k')], 0)[None]
    p_v = np.stack([a.reshape(WBUF, 8, 64) for a in g('p_att_v')], 0)[None]
    p_conv = np.stack(g('p_conv'), 0)[None]
    s_shift = np.concatenate(g('s_shift'), 0)[None]
    s_wkv = np.concatenate(g('s_wkv'), 0)[None]
    s_k = np.concatenate([a.reshape(4, WBUF, 8, 64) for a in g('s_att_k')], 0)[None]
    s_v = np.concatenate([a.reshape(4, WBUF, 8, 64) for a in g('s_att_v')], 0)[None]
    s_conv = np.concatenate(g('s_conv'), 0)[None]
    return (y_p, y_s, p_shift, p_wkv, p_k, p_v, p_conv, s_shift, s_wkv, s_k, s_v, s_conv)
```

```python
import contextlib
import os
RCUT = int(os.environ.get('RCUT', '9'))
RPRE = int(os.environ.get('RPRE', '99'))
RSUB = int(os.environ.get('RSUB', '99'))
BCUT = int(os.environ.get('BCUT', '99'))
import numpy as np
import ml_dtypes
import concourse.bass as bass
import concourse.mybir as mybir
from concourse.bass_utils import run_bass_kernel_spmd

F32 = mybir.dt.float32
BF16 = mybir.dt.bfloat16
AF = mybir.ActivationFunctionType
ALU = mybir.AluOpType
AX = mybir.AxisListType

D = 1024
HD = 64
RW = 512
NCOL_R = 1792
D_IN = 3328
DFF = 2816
PAST_LEN = 16384
WBUF = 2048
NORM_EPS = 1e-6
LNX_EPS = 64e-5
C0 = float(np.exp(-0.5))

_ESZ = {F32: 4, BF16: 2}


def _box(ap):
    name = ap.tensor.name
    dims = ap.ap
    off = int(ap.offset)
    esz = _ESZ.get(ap.dtype, 4)
    if 'DRAM' in str(ap.space).upper():
        ext = sum((c - 1) * abs(s) for s, c in dims)
        return (name, 0, 1, off * esz, (off + ext + 1) * esz)
    pstride = dims[0][0]
    pcount = dims[0][1]
    if pstride == 0:
        p0, f0 = 0, off
    else:
        p0 = off // pstride
        f0 = off - p0 * pstride
    ext = sum((c - 1) * abs(s) for s, c in dims[1:])
    b0, b1 = f0 * esz, (f0 + ext + 1) * esz
    if 'PSUM' in str(ap.space).upper():
        return (name, (p0 // 32) * 32, ((p0 + pcount + 31) // 32) * 32, (b0 // 2048) * 2048, ((b1 + 2047) // 2048) * 2048)
    return (name, p0, p0 + pcount, b0, b1)


class Prog:
    COMPUTE = ('pe', 'act', 'dve', 'pool')

    def __init__(self, nc, n_dma_sems=(20, 12, 8)):
        self.nc = nc
        self.ops = {e: [] for e in ('pe', 'act', 'dve', 'pool', 'sp')}
        self.nseq = {e: 0 for e in self.COMPUTE}
        self.waited = {}
        self.recs = {}
        self.dmaq = {'sp': [n_dma_sems[0], 0], 'pool': [n_dma_sems[1], 0], 'act': [n_dma_sems[2], 0]}
        self.sems = {}

    def _deps(self, eng, outs, ins, waiter=None):
        waiter = waiter or eng
        deps = {}
        for ap, is_w in [(a, True) for a in outs] + [(a, False) for a in ins]:
            name, p0, p1, f0, f1 = _box(ap)
            for r in self.recs.get(name, ()):
                if r[0] < p1 and p0 < r[1] and r[2] < f1 and f0 < r[3]:
                    if is_w or r[6] or (name == 'PS' and r[7] != eng):
                        if r[7] == eng and eng in self.COMPUTE:
                            if eng == 'pe':
                                continue
                            if not (r[6] and not is_w):
                                continue
                        if deps.get(r[4], 0) < r[5]:
                            deps[r[4]] = r[5]
        out = []
        for key, val in deps.items():
            if self.waited.get((waiter, key), 0) < val:
                self.waited[(waiter, key)] = val
                out.append((key, val))
        return out

    def _record(self, eng, outs, ins, key, val):
        for ap, is_w in [(a, True) for a in outs] + [(a, False) for a in ins]:
            name, p0, p1, f0, f1 = _box(ap)
            lst = self.recs.setdefault(name, [])
            if is_w:
                lst[:] = [r for r in lst if not (p0 <= r[0] and r[1] <= p1 and f0 <= r[2] and r[3] <= f1)]
            else:
                lst[:] = [r for r in lst if not ((not r[6]) and r[4] == key and p0 <= r[0] and r[1] <= p1
                                                 and f0 <= r[2] and r[3] <= f1)]
            lst.append([p0, p1, f0, f1, key, val, is_w, eng])

    def op(self, eng, fn, outs, ins):
        waits = self._deps(eng, outs, ins)
        self.nseq[eng] += 1
        key = 'c_' + eng
        self._record(eng, outs, ins, key, self.nseq[eng])
        self.ops[eng].append(('c', waits, fn, key))

    def dma(self, q, out, in_, **kw):
        ns, cnt = self.dmaq[q]
        slot = cnt % ns
        rnd = cnt // ns
        self.dmaq[q][1] += 1
        key = 'd_%s_%d' % (q, slot)
        waits = self._deps('dma_' + q, [out], [in_], waiter=q)
        if rnd > 0 and self.waited.get((q, key), 0) < 16 * rnd:
            self.waited[(q, key)] = 16 * rnd
            waits.append((key, 16 * rnd))
        self._record('dma_' + q, [out], [in_], key, 16 * (rnd + 1))
        self.ops[q].append(('d', waits, (out, in_, kw), key))

    def final_wait_all(self, eng='sp'):
        waits = []
        for q, (ns, cnt) in self.dmaq.items():
            for slot in range(min(ns, cnt)):
                nuse = (cnt - slot + ns - 1) // ns
                waits.append(('d_%s_%d' % (q, slot), 16 * nuse))
        for e in self.COMPUTE:
            if self.nseq[e] > 0:
                waits.append(('c_' + e, self.nseq[e]))
        self.ops[eng].append(('w', waits, None, None))

    def build(self):
        nc = self.nc
        keys = set()
        for e, lst in self.ops.items():
            for kind, waits, payload, key in lst:
                if key:
                    keys.add(key)
                for k, v in waits:
                    keys.add(k)
        with contextlib.ExitStack() as st:
            for k in sorted(keys):
                self.sems[k] = st.enter_context(nc.semaphore(k))
            block = st.enter_context(nc.Block())
            sems = self.sems

            def runner(lst):
                def run(e):
                    for kind, waits, payload, key in lst:
                        for k, v in waits:
                            e.wait_ge(sems[k], v)
                        if kind == 'c':
                            payload(e).then_inc(sems[key], 1)
                        elif kind == 'd':
                            out, in_, kw = payload
                            e.dma_start(out=out, in_=in_, **kw).then_inc(sems[key], 16)
                return run

            block.tensor(runner(self.ops['pe']))
            block.scalar(runner(self.ops['act']))
            block.vector(runner(self.ops['dve']))
            block.gpsimd(runner(self.ops['pool']))
            block.sync(runner(self.ops['sp']))


class Arena:
    def __init__(self, nc, name, nbytes):
        self.t = nc.alloc_sbuf_tensor(name, [128, nbytes // 4], F32)
        self.cap = nbytes
        self.off = 0
        self.peak = 0

    def alloc(self, shape, dt):
        shape = tuple(int(s) for s in shape)
        n = int(np.prod(shape))
        nb = (n * _ESZ[dt] + 63) // 64 * 64
        off = self.off
        self.off += nb
        self.peak = max(self.peak, self.off)
        assert self.off <= self.cap, ("SBUF arena overflow", self.off, self.cap)
        ap = self.t[:, off // 4:(off + nb) // 4]
        if dt != F32:
            ap = ap.bitcast(dt)
        ap = ap[:, 0:n]
        if len(shape) > 1:
            names = ['a%d' % i for i in range(len(shape))]
            pat = "p (%s) -> p %s" % (' '.join(names), ' '.join(names))
            ap = ap.rearrange(pat, **{nm: s for nm, s in zip(names, shape)})
        return ap

    def mark(self):
        return self.off

    def release(self, m):
        self.off = m


def build_program(T=4096, debug=False, stages='ABC'):
    RT = 512
    NT = T // 512
    nc = bass.Bass("TRN2", target_bir_lowering=False)
    P = Prog(nc)
    KEEP = min(WBUF, T)

    def din(name, shape, dt=F32):
        return nc.dram_tensor(name, list(shape), dt, kind="ExternalInput").ap()

    def dout(name, shape, dt=F32):
        return nc.dram_tensor(name, list(shape), dt, kind="ExternalOutput").ap()

    def dscr(name, shape, dt=BF16):
        return nc.dram_tensor(name, list(shape), dt, kind="Internal").ap()

    x_p = din("x_prompt", [T, D])
    x_s = din("x_sample", [4, 8, D])
    st_shift = din("state_rwkv_shift", [4, D])
    st_wkv = din("state_rwkv_wkv", [4, 8, 64, 64])
    ck = din("cache_att_k", [4, WBUF, 512])
    cv = din("cache_att_v", [4, WBUF, 512])
    st_conv = din("state_ffn_conv", [4, 2, 2 * DFF])
    n_mix_pre = din("norm_mix_pre", [1, D])
    n_mix_post = din("norm_mix_post", [1, D])
    n_ffn_pre = din("norm_ffn_pre", [1, D])
    n_ffn_post = din("norm_ffn_post", [1, D])
    w_in = din("w_in", [D, D_IN])
    mu_shift = din("mu_shift", [1, NCOL_R])
    w0_d = din("w0", [RW, 1])
    w_decay_up = din("w_decay_up", [64, RW])
    a0_d = din("a0", [RW, 1])
    w_iclr_up = din("w_iclr_up", [64, RW])
    w_gate_up = din("w_gate_up", [128, RW])
    k_k_d = din("k_k", [RW, 1])
    k_a_d = din("k_a", [RW, 1])
    r_k_d = din("r_k", [RW, 1])
    lnx_w_d = din("lnx_w", [RW, 1])
    lnx_b_d = din("lnx_b", [RW, 1])
    w_out = din("w_out", [D, D])
    w_up = din("w_ffn_up", [D, 2 * DFF])
    cw_d = din("ffn_conv_w", [3, 2 * DFF])
    cb_d = din("ffn_conv_b", [2 * DFF, 1])
    w_dn = din("w_ffn_down", [DFF, D])
    cos_p = din("cos_p", [128, T])
    sin_p = din("sin_p", [128, T])
    cos_s = din("cos_s", [128, 64])
    sin_s = din("sin_s", [128, 64])
    cost_p = din("cost_p", [KEEP, 32])
    sint_p = din("sint_p", [KEEP, 32])
    cost_s = din("cost_s", [128, 32])
    sint_s = din("sint_s", [128, 32])

    lb_s = din("lb_s", [8, 17 * 128])

    y_p = dout("y_prompt", [T, D])
    y_s = dout("y_sample", [4, 8, D])
    o_pshift = dout("p_shift", [1, D])
    o_pwkv = dout("p_wkv", [8, 64, 64])
    o_pk = dout("p_att_k", [KEEP, 512])
    o_pv = dout("p_att_v", [KEEP, 512])
    o_pconv = dout("p_conv", [2, 2 * DFF])
    o_sshift = dout("s_shift", [4, D])
    o_swkv = dout("s_wkv", [4, 8, 64, 64])
    o_sk = dout("s_att_k", [4, WBUF, 512])
    o_sv = dout("s_att_v", [4, WBUF, 512])
    o_sconv = dout("s_conv", [4, 2, 2 * DFF])
    dbg = {}

    n_mixp, n_mixs, n_vs = D * T, D * 256, T * 512
    SCR = dscr("SCR", [n_mixp + n_mixs + n_vs + D * D + D * T])
    mixp_s = SCR[0:n_mixp].rearrange("(f t) -> f t", t=T)
    mixs_s = SCR[n_mixp:n_mixp + n_mixs].rearrange("(f t) -> f t", t=256)
    vs_s = SCR[n_mixp + n_mixs:n_mixp + n_mixs + n_vs].rearrange("(t c) -> t c", c=512)
    wout_s = SCR[n_mixp + n_mixs + n_vs:n_mixp + n_mixs + n_vs + D * D].rearrange("(k c) -> k c", c=D)
    o_h_ = n_mixp + n_mixs + n_vs + D * D
    hts_s = SCR[o_h_:o_h_ + D * T].rearrange("(f t) -> f t", t=T)

    SB = Arena(nc, "SB", 212480)
    PS = nc.alloc_psum_tensor("PS", [128, 4096], F32)

    def bank(k, a=0, b=512):
        return PS[:, 512 * k + a:512 * k + b]

    def dma_fc(q, sb_fn, dr_row, write):
        for c0 in range(0, 44, 8):
            c1 = min(44, c0 + 8)
            d_ = dr_row[128 * c0:128 * c1].rearrange("(c p) -> p c", p=128)
            if write:
                P.dma(q, d_, sb_fn(c0, c1), allow_slow_non_contiguous=True)
            else:
                P.dma(q, sb_fn(c0, c1), d_, allow_slow_non_contiguous=True)

    def sl(start, n, step):
        return slice(start, start + step * (n - 1) + 1, step)

    def tt(eng, out, in0, in1, op):
        P.op(eng, lambda e: e.tensor_tensor(out=out, in0=in0, in1=in1, op=op), [out], [in0, in1])

    def ts(eng, out, in0, s1, op0, s2=None, op1=None):
        ins = [in0] + [s for s in (s1, s2) if not isinstance(s, (int, float, type(None)))]
        if op1 is None:
            P.op(eng, lambda e: e.tensor_scalar(out=out, in0=in0, scalar1=s1, scalar2=None, op0=op0), [out], ins)
        else:
            P.op(eng, lambda e: e.tensor_scalar(out=out, in0=in0, scalar1=s1, scalar2=s2, op0=op0, op1=op1), [out], ins)

    def stt(out, in0, scalar, in1, op0, op1, accum=None):
        ins = [in0, in1] + ([scalar] if not isinstance(scalar, (int, float)) else [])
        outs = [out] + ([accum] if accum is not None else [])
        if accum is None:
            P.op('dve', lambda e: e.scalar_tensor_tensor(out=out, in0=in0, scalar=scalar, in1=in1, op0=op0, op1=op1), outs, ins)
        else:
            P.op('dve', lambda e: e.scalar_tensor_tensor(out=out, in0=in0, scalar=scalar, in1=in1, op0=op0, op1=op1,
                                                         accum_out=accum), outs, ins)

    def act(out, in_, func, bias=None, scale=None, accum=None):
        ins = [in_] + [s for s in (bias, scale) if not isinstance(s, (int, float, type(None)))]
        outs = [out] + ([accum] if accum is not None else [])
        kw = {}
        if bias is not None:
            kw['bias'] = bias
        if scale is not None:
            kw['scale'] = scale
        if accum is not None:
            kw['accum_out'] = accum
        P.op('act', lambda e: e.activation(out=out, in_=in_, func=func, **kw), outs, ins)

    def cp(eng, out, in_):
        if eng == 'act':
            P.op('act', lambda e: e.copy(out=out, in_=in_), [out], [in_])
        else:
            P.op(eng, lambda e: e.tensor_copy(out=out, in_=in_), [out], [in_])

    pe_hist = []

    def pe_check(out, lhsT):
        import traceback
        _, p0, p1, f0, f1 = _box(lhsT)
        rg = set(range(p0 // 32, (p1 + 31) // 32))
        _, _, _, b0, b1 = _box(out)
        bk = set(range(b0 // 2048, (b1 + 2047) // 2048))
        for (rg2, bk2, where) in pe_hist[-24:]:
            if not (rg & rg2) and (bk & bk2):
                print("PE ROW-GROUP/BANK HAZARD:", sorted(rg), sorted(bk), "vs", sorted(rg2), sorted(bk2), where,
                      "|", traceback.extract_stack(limit=4)[0].lineno)
        pe_hist.append((rg, bk, traceback.extract_stack(limit=4)[0].lineno))

    def mm(out, lhsT, rhs, start=True, stop=True, tp=None):
        pe_check(out, lhsT)
        if tp is None:
            P.op('pe', lambda e: e.matmul(out, lhsT=lhsT, rhs=rhs, start=start, stop=stop), [out], [lhsT, rhs])
        else:
            P.op('pe', lambda e: e.matmul(out, lhsT=lhsT, rhs=rhs, start=start, stop=stop, tile_position=tp), [out], [lhsT, rhs])

    def tr(out, in_, ident):
        pe_check(out, in_)
        P.op('pe', lambda e: e.transpose(out, in_, ident), [out], [in_, ident])

    def memset(eng, out, val):
        P.op(eng, lambda e: e.memset(out, val), [out], [])

    def asel(out, pattern, base, cm, cmp, fill=0.0):
        P.op('pool', lambda e: e.affine_select(out=out, in_=out, pattern=pattern, compare_op=cmp, fill=fill,
                                               base=base, channel_multiplier=cm), [out], [out])

    rr = {'n': 0}

    def rot(engs):
        rr['n'] += 1
        return engs[rr['n'] % len(engs)]

    if 'S' in stages:
        for b_ in range(4):
            for (src_, dst_) in ((ck, o_sk), (cv, o_sv)):
                for c_ in range(4):
                    P.dma('act', dst_[b_, 510 * c_:510 * (c_ + 1), :], src_[b_, 8 + 510 * c_:8 + 510 * (c_ + 1), :])

    IDF = SB.alloc((128,), F32)
    IDB = SB.alloc((128,), BF16)
    BDO = SB.alloc((128,), F32)
    BDM = SB.alloc((128,), F32)
    MSU = SB.alloc((128,), F32)
    MIU = SB.alloc((128,), F32)
    MSL = SB.alloc((128,), F32)
    MAB = SB.alloc((2, 128), F32)
    PERM = SB.alloc((128,), BF16)
    PERMF = SB.alloc((128,), F32)
    MASK2 = SB.alloc((256,), BF16)
    ONESB = SB.alloc((64,), BF16)
    PAR = SB.alloc((4, 8), F32)
    m0 = SB.mark()
    HTS = SB.alloc((8, 4, 66), BF16)
    mA = SB.mark()
    RESET = SB.alloc((512,), F32)
    tmpc = SB.alloc((256,), F32)

    memset('pool', IDF, 1.0)
    asel(IDF, [[-1, 128]], 0, 1, ALU.is_equal)
    cp('dve', IDB, IDF)
    memset('pool', BDO, 1.0)
    memset('pool', BDO[0:64, 64:128], 0.0)
    memset('pool', BDO[64:128, 0:64], 0.0)
    ts('dve', BDM, BDO, 1.0 / 64, ALU.mult)
    memset('pool', MSU, 1.0)
    asel(MSU, [[1, 128]], 0, -1, ALU.is_gt)
    tt('dve', MSU, MSU, BDO, ALU.mult)
    memset('pool', MIU, 1.0)
    asel(MIU, [[1, 128]], 0, -1, ALU.is_ge)
    tt('dve', MIU, MIU, BDO, ALU.mult)
    memset('pool', MSL, 1.0)
    asel(MSL, [[-1, 128]], 0, 1, ALU.is_gt)
    tt('dve', MSL, MSL, BDO, ALU.mult)
    cp('dve', MAB[:, 0, :], MSU)
    cp('dve', MAB[:, 1, :], MIU)
    memset('pool', tmpc[:, 0:128], 1.0)
    asel(tmpc[:, 0:128], [[-1, 128]], -32, 1, ALU.is_equal)
    memset('pool', tmpc[:, 32:64], 0.0)
    memset('pool', tmpc[:, 96:128], 0.0)
    memset('pool', tmpc[:, 128:256], 1.0)
    asel(tmpc[:, 128:256], [[-1, 128]], 32, 1, ALU.is_equal)
    memset('pool', tmpc[:, 128:160], 0.0)
    memset('pool', tmpc[:, 192:224], 0.0)
    tt('dve', PERM, tmpc[:, 0:128], tmpc[:, 128:256], ALU.add)
    tt('dve', PERMF, tmpc[:, 0:128], tmpc[:, 128:256], ALU.add)
    memset('pool', tmpc[:, 0:256], 1.0)
    asel(tmpc[:, 0:128], [[1, 128]], 0, -1, ALU.is_ge)
    asel(tmpc[:, 128:256], [[-1, 128]], 0, 1, ALU.is_ge)
    cp('dve', MASK2, tmpc[:, 0:256])
    memset('pool', RESET, 1.0)
    for c in range(8):
        memset('pool', RESET[:, 64 * c:64 * c + 1], 0.0)
    memset('pool', ONESB, 1.0)

    P_W0, P_A0, P_KK, P_KA, P_RK, P_LW, P_LB, P_OMKA = range(8)
    for i, d_ in enumerate((w0_d, a0_d, k_k_d, k_a_d, r_k_d, lnx_w_d, lnx_b_d)):
        for fc in range(4):
            P.dma('sp', PAR[:, fc, i:i + 1], d_[128 * fc:128 * (fc + 1), :])
    ts('dve', PAR[:, :, P_OMKA], PAR[:, :, P_KA], -1.0, ALU.mult, 1.0, ALU.add)
    WDU = SB.alloc((RW,), F32)
    WIU = SB.alloc((RW,), F32)
    WGU = SB.alloc((RW,), F32)
    P.dma('sp', WDU[0:64, :], w_decay_up[:, :])
    P.dma('sp', WIU[64:128, :], w_iclr_up[:, :])
    P.dma('sp', WGU[:, :], w_gate_up[:, :])
    GPRE = SB.alloc((D,), F32)
    P.dma('sp', GPRE, n_mix_pre[0:1, :].partition_broadcast(128))

    WSTh = {'b': None}
    wi = [0]

    def alloc_wst():
        WSTh['b'] = [SB.alloc((1024,), F32) for _ in range(3)]

    def load_w(dst, src, ncol, scale_pairs=None):
        i = wi[0] % 3
        wi[0] += 1
        WST = WSTh['b']
        P.dma('sp', WST[i][:, 0:ncol], src)
        if scale_pairs is None:
            cp(rot(['act', 'dve', 'pool']), dst, WST[i][:, 0:ncol])
        else:
            for (d_, sc), eng in zip(scale_pairs, ('dve', 'pool')):
                tt(eng, d_, WST[i][:, 0:ncol], sc, ALU.mult)

    W12 = SB.alloc((8, 2, NCOL_R), BF16)
    mW = SB.mark()
    alloc_wst()
    MU = SB.alloc((NCOL_R,), F32)
    OMU = SB.alloc((NCOL_R,), F32)
    P.dma('sp', MU, mu_shift[0:1, :].partition_broadcast(128))
    ts('dve', OMU, MU, -1.0, ALU.mult, 1.0, ALU.add)
    for kc in (range(8) if 'x' not in stages else []):
        rows = slice(128 * kc, 128 * (kc + 1))
        for c0, c1 in ((0, 1024), (1024, NCOL_R)):
            load_w(None, w_in[rows, c0:c1], c1 - c0,
                   scale_pairs=[(W12[:, kc, 0, c0:c1], OMU[:, c0:c1]), (W12[:, kc, 1, c0:c1], MU[:, c0:c1])])
    SB.release(mW)
    XT = [SB.alloc((D,), F32) for _ in range(2)]
    HB = [SB.alloc((D,), BF16) for _ in range(2)]
    STAT = SB.alloc((8, 4), F32)
    EPSC = SB.alloc((2,), F32)
    memset('pool', EPSC[:, 0:1], NORM_EPS)
    memset('pool', EPSC[:, 1:2], LNX_EPS)
    sti = [0]

    def norm_tile(x_rows_fn, dst_fn, want_f32_row=None, hprev_fn=None):
        i = sti[0] % 2
        si = sti[0] % 8
        sti[0] += 1
        xt, hb = XT[i], HB[i]
        x_rows_fn(xt)
        act(hb, xt, AF.Square, accum=STAT[:, si, 0:1])
        act(STAT[:, si, 1:2], STAT[:, si, 0:1], AF.Sqrt, bias=EPSC[:, 0:1], scale=1.0 / D)
        P.op('dve', lambda e: e.reciprocal(out=STAT[:, si, 2:3], in_=STAT[:, si, 1:2]), [STAT[:, si, 2:3]], [STAT[:, si, 1:2]])
        stt(hb, xt, STAT[:, si, 2:3], GPRE, ALU.mult, ALU.mult)
        if want_f32_row is not None:
            stt(xt, xt, STAT[:, si, 2:3], GPRE, ALU.mult, ALU.mult)
            want_f32_row(xt)
        if hprev_fn is not None:
            hprev_fn(hb)
        pb = bank(2 + i).bitcast(BF16)
        for kc in range(8):
            tr(pb[:, 128 * kc:128 * (kc + 1)], hb[:, 128 * kc:128 * (kc + 1)], IDB)
        return pb

    RB = {}
    for nm in ('r', 'k', 'v', 'sgw', 'a', 'g', 'cs', 'csx', 'crem', 'ein', 'eneg', 'kk', 't1', 'kkn',
               'kmod', 'bb', 'bonv', 'YT'):
        RB[nm] = SB.alloc((RT,), F32)
    BTB = SB.alloc((RT,), BF16)
    KTB = SB.alloc((RT,), BF16)
    RB['eex'] = RB['csx']
    RB['erem'] = RB['crem']
    RB['BP'] = RB['bb']
    RB['KP'] = RB['kmod']
    AR = SB.alloc((RT // 128, 2, 128), F32)
    TW = SB.alloc((RT,), F32)
    AL = SB.alloc((RT,), F32)
    SG = SB.alloc((RT,), F32)
    NSTM = RT // 128
    TOKB = SB.alloc((NSTM, 4, 128), F32)
    TOK = [TOKB[:, i_, :, :] for i_ in range(NSTM)]
    TOKH = [RB[('eneg', 'kk')[(i_ * 512) // (2 * RT)]].bitcast(BF16)[:, (i_ * 512) % (2 * RT):(i_ * 512) % (2 * RT) + 512]
            .rearrange("p (q c) -> p q c", c=128) for i_ in range(NSTM)]
    ARB = RB['kkn'].bitcast(BF16).rearrange("p (s a t) -> p s a t", a=2, t=128)
    NM = [SB.alloc((2, 2, 128), BF16) for _ in range(NSTM)]
    AK = [SB.alloc((2, 2, 128), BF16) for _ in range(NSTM)]
    XP = [[SB.alloc((2, 2, 128), BF16) for _ in range(2)] for _ in range(NSTM)]
    YY = [[SB.alloc((2, 128), BF16) for _ in range(2)] for _ in range(NSTM)]
    Z1 = [SB.alloc((2, 64), BF16) for _ in range(NSTM)]
    UW = [SB.alloc((2, 2, 64), BF16) for _ in range(NSTM)]
    WTs = [SB.alloc((128,), F32) for _ in range(NSTM)]
    GT = [SB.alloc((128,), F32) for _ in range(NSTM)]
    HTt = [SB.alloc((128,), F32) for _ in range(NSTM)]
    US = SB.alloc((2, 64), F32)
    SST = SB.alloc((5, 4, 64), F32)
    ROB = RB['sgw'].bitcast(BF16)[:, 0:RT]
    HTP = SB.alloc((8, RT), BF16)
    HTCUR = [SB.alloc((8, 512), BF16) for _ in range(2)]
    HLAST = SB.alloc((8, 2), BF16)
    SNAT = TOK[0].rearrange("p q c -> p (q c)").rearrange("p (h j) -> p h j", j=64)

    def rwkv_fc(fc, ntok, ps_r, ps_k, ps_v, state_of_chunk, CV, out_dst):
        nst = ntok // 128
        n = ntok
        par = lambda i: PAR[:, fc, i:i + 1]
        B = {k_: v_[:, 0:n] for k_, v_ in RB.items()}
        cp('act', B['r'], ps_r)
        cp('act', B['k'], ps_k)
        cp('act', B['v'], ps_v)
        if RPRE < 1:
            return
        cols = slice(128 * fc, 128 * (fc + 1))
        pz = bank(1)
        mm(bank(4)[:, 0:n], WDU[0:64, cols], TW[0:64, 0:n])
        act(B['sgw'], bank(4)[:, 0:n], AF.Sigmoid, bias=par(P_W0))
        mm(bank(5)[:, 0:n], WIU[64:128, cols], AL[64:128, 0:n])
        act(B['a'], bank(5)[:, 0:n], AF.Sigmoid, bias=par(P_A0))
        mm(pz[:, 0:n], WGU[:, cols], SG[:, 0:n])
        cp('act', B['g'], pz[:, 0:n])
        if RPRE < 2:
            return
        P.op('dve', lambda e: e.tensor_tensor_scan(out=B['cs'], data0=RESET[:, 0:n], data1=B['sgw'], initial=0.0,
                                                   op0=ALU.mult, op1=ALU.add), [B['cs']], [RESET[:, 0:n], B['sgw']])
        if RPRE < 3:
            return
        tt('dve', B['csx'], B['cs'], B['sgw'], ALU.subtract)
        nch = n // 64
        cs3 = B['cs'].rearrange("p (c t) -> p c t", t=64)
        tt('dve', B['crem'].rearrange("p (c t) -> p c t", t=64), cs3[:, :, CV - 1:CV].to_broadcast([128, nch, 64]), cs3,
           ALU.subtract)
        if RPRE < 4:
            return
        act(B['ein'], B['cs'], AF.Exp, scale=-C0)
        act(B['eex'], B['csx'], AF.Exp, scale=-C0)
        act(B['eneg'], B['cs'], AF.Exp, scale=C0)
        act(B['erem'], B['crem'], AF.Exp, scale=-C0)
        if RPRE < 5:
            return
        ts('dve', B['kk'], B['k'], par(P_KK), ALU.mult)
        act(B['t1'], B['kk'], AF.Square)
        mm(pz[:, 0:n], BDO, B['t1'])
        act(B['t1'], pz[:, 0:n], AF.Sqrt)
        ts('dve', B['t1'], B['t1'], 1e-12, ALU.max)
        P.op('dve', lambda e: e.reciprocal(out=B['t1'], in_=B['t1']), [B['t1']], [B['t1']])
        tt('dve', B['kkn'], B['kk'], B['t1'], ALU.mult)
        if RPRE < 6:
            return
        ts('dve', B['t1'], B['a'], par(P_KA), ALU.mult, par(P_OMKA), ALU.add)
        tt('dve', B['kmod'], B['k'], B['t1'], ALU.mult)
        tt('dve', B['bb'], B['kkn'], B['a'], ALU.mult)
        if RPRE < 7:
            return
        ar = AR[:, 0:nst, :, :]
        stt(ar[:, :, 0, :], B['kkn'].rearrange("p (s t) -> p s t", t=128), -1.0,
            B['eex'].rearrange("p (s t) -> p s t", t=128), ALU.mult, ALU.mult)
        tt('pool', ar[:, :, 1, :], B['r'].rearrange("p (s t) -> p s t", t=128),
           B['ein'].rearrange("p (s t) -> p s t", t=128), ALU.mult)
        if RPRE < 8:
            return
        stt(B['t1'], B['r'], par(P_RK), B['kmod'], ALU.mult, ALU.mult)
        mm(pz[:, 0:n], BDO, B['t1'])
        tt('dve', B['bonv'], pz[:, 0:n], B['v'], ALU.mult)
        tt('dve', BTB[:, 0:n], B['bb'], B['eneg'], ALU.mult)
        tt('dve', KTB[:, 0:n], B['kmod'], B['eneg'], ALU.mult)
        tt('dve', B['BP'], B['bb'], B['erem'], ALU.mult)
        tt('pool', B['KP'], B['kmod'], B['erem'], ALU.mult)

        cp('act', ARB[:, 0:nst, :, :], AR[:, 0:nst, :, :])
        for st in range(nst):
            tk = slice(128 * st, 128 * (st + 1))
            tok, tokb = TOK[st], TOKH[st]
            pt = bank(3)
            for qi, src in enumerate((AR[:, st, 0, :], B['BP'][:, tk], B['KP'][:, tk], B['v'][:, tk])):
                tr(pt[:, 128 * qi:128 * (qi + 1)], src, IDF)
            cp('act', tok, pt.rearrange("p (q c) -> p q c", c=128))
            cp('dve', tokb, pt.rearrange("p (q c) -> p q c", c=128))
            for pi in range(2):
                hp = slice(64 * pi, 64 * (pi + 1))
                rhs_ar = ARB[hp, st, :, :]
                mm(bank(4 + pi)[:, 0:256], BTB[hp, tk], rhs_ar)
                mm(bank(4 + pi)[:, 256:512], KTB[hp, tk], rhs_ar)
                mm(bank(6 + pi)[:, 0:128], ARB[hp, st, 0, :], BTB[hp, tk])
            for pi in range(2):
                tt('dve', NM[st][:, pi, :, :], bank(4 + pi)[:, 0:256].rearrange("p (a t) -> p a t", t=128), MAB, ALU.mult)
                tt('dve', AK[st][:, pi, :, :], bank(4 + pi)[:, 256:512].rearrange("p (a t) -> p a t", t=128), MAB, ALU.mult)
                tt('dve', YY[st][0][:, pi, :], bank(6 + pi)[:, 0:128], MSL, ALU.mult)
                cp('act', XP[st][0][:, pi, 0, :], NM[st][:, pi, 0, :])
                tt('pool', XP[st][0][:, pi, 1, :], NM[st][:, pi, 0, :], IDF, ALU.add)
        for kstep in range(6):
            for st in range(nst):
                cur = kstep % 2
                xp, yy = XP[st][cur], YY[st][cur]
                xpn, yyn = XP[st][1 - cur], YY[st][1 - cur]
                p1, p2 = (bank(1), bank(2)) if st % 2 == 0 else (bank(3), bank(0))
                if kstep == 0:
                    for pi in range(2):
                        mm(p1[:, 256 * pi:256 * pi + 128], yy[:, pi, :], xp[:, pi, 0, :])
                        mm(p2[:, 128 * pi:128 * (pi + 1)], xp[:, pi, 0, :], yy[:, pi, :])
                    cp('act', xpn[:, :, 0, :], p1.rearrange("p (h a t) -> p h a t", a=2, t=128)[:, :, 0, :])
                    cp('pool', xpn[:, :, 1, :], xp[:, :, 1, :])
                    cp('dve', yyn, p2[:, 0:256].rearrange("p (h t) -> p h t", t=128))
                elif kstep < 5:
                    for pi in range(2):
                        mm(p1[:, 256 * pi:256 * (pi + 1)], yy[:, pi, :], xp[:, pi, :, :])
                        mm(p2[:, 128 * pi:128 * (pi + 1)], xp[:, pi, 0, :], yy[:, pi, :])
                    p1v = p1.rearrange("p (h a t) -> p h a t", a=2, t=128)
                    cp('act', xpn[:, :, 0, :], p1v[:, :, 0, :])
                    tt('dve', xpn[:, :, 1, :], p1v[:, :, 1, :], xp[:, :, 1, :], ALU.add)
                    cp('act', yyn, p2[:, 0:256].rearrange("p (h t) -> p h t", t=128))
                else:
                    for pi in range(2):
                        mm(p1[:, 256 * pi + 128:256 * (pi + 1)], yy[:, pi, :], xp[:, pi, 1, :])
                    p1v = p1.rearrange("p (h a t) -> p h a t", a=2, t=128)
                    tt('dve', xpn[:, :, 1, :], p1v[:, :, 1, :], xp[:, :, 1, :], ALU.add)
        def pmb(st):
            return bank(7) if st % 2 == 0 else bank(2)

        def pgb(st):
            return bank(3) if st % 2 == 0 else bank(1)
        for st in range(nst):
            tokb, pm = TOKH[st], pmb(st)
            for pi in range(2):
                hc = slice(64 * pi, 64 * (pi + 1))
                mm(pm[:, 64 * pi:64 * (pi + 1)], AK[st][:, pi, 0, :], tokb[:, 3, hc])
            cp('act', Z1[st], pm[:, 0:128].rearrange("p (h i) -> p h i", i=64))
        for st in range(nst):
            tokb, pm, PT = TOKH[st], pmb(st), XP[st][0]
            for pi in range(2):
                hc = slice(64 * pi, 64 * (pi + 1))
                mm(pm[:, 128 + 128 * pi:128 + 128 * pi + 64], PT[:, pi, 1, :], Z1[st][:, pi, :])
                mm(pm[:, 128 + 128 * pi + 64:128 + 128 * (pi + 1)], PT[:, pi, 1, :], tokb[:, 0, hc])
            for pi in range(2):
                hc = slice(64 * pi, 64 * (pi + 1))
                mm(pm[hc, 384:512], tokb[:, 0, hc], PT[:, pi, 1, :], tp=(0, 64 * pi))
            cp('dve', UW[st], pm[:, 128:384].rearrange("p (h a i) -> p h a i", a=2, i=64))
            cp('act', WTs[st], pm[:, 384:512])
        for st in range(nst):
            tokb, pg = TOKH[st], pgb(st)
            for pi in range(2):
                hc = slice(64 * pi, 64 * (pi + 1))
                mm(pg[hc, 0:128], UW[st][:, pi, 1, :], NM[st][:, pi, 1, :], tp=(0, 64 * pi))
            for pi in range(2):
                hc = slice(64 * pi, 64 * (pi + 1))
                mm(pg[hc, 128:256], UW[st][:, pi, 0, :], NM[st][:, pi, 1, :], start=True, stop=False, tp=(0, 64 * pi))
                mm(pg[hc, 128:256], tokb[:, 3, hc], AK[st][:, pi, 1, :], start=False, stop=True, tp=(0, 64 * pi))
            tt('dve', GT[st], pg[:, 0:128], AR[:, st, 1, :], ALU.add)
            cp('act', HTt[st], pg[:, 128:256])
        for st in range(nst):
            tok = TOK[st]
            for q in range(2):
                S, chain = state_of_chunk(st, q)
                tq = slice(64 * q, 64 * (q + 1))
                tv = slice(64 * q, 64 * q + CV)
                for pi in range(2):
                    hc = slice(64 * pi, 64 * (pi + 1))
                    mm(bank(6 + pi)[tq, 128:192], WTs[st][hc, tq], S[hc, :], tp=(64 * pi, 64 * q))
                for pi in range(2):
                    tt('dve', US[tq, pi, :], bank(6 + pi)[tq, 128:192], UW[st][tq, pi, 0, :], ALU.add)
                for pi in range(2):
                    hc = slice(64 * pi, 64 * (pi + 1))
                    mm(bank(6 + pi)[hc, 192:256], S[hc, :], GT[st][hc, tq], tp=(64 * pi, 64 * pi))
                for pi in range(2):
                    hc = slice(64 * pi, 64 * (pi + 1))
                    tt('dve', B['YT'][hc, 128 * st + 64 * q:128 * st + 64 * (q + 1)], bank(6 + pi)[hc, 192:256], HTt[st][hc, tq],
                       ALU.add)
                psq = bank(6 + q)[:, 256:320]
                for pi in range(2):
                    hc = slice(64 * pi, 64 * (pi + 1))
                    mm(psq[hc, :], tok[tv, 1, hc], US[tv, pi, :], start=True, stop=False, tp=(64 * q, 64 * pi))
                    mm(psq[hc, :], tok[tv, 2, hc], tok[tv, 3, hc], start=False, stop=True, tp=(64 * q, 64 * pi))
                pcol = B['ein'][:, 128 * st + 64 * q + CV - 1:128 * st + 64 * q + CV]
                stt(S, S, pcol, psq, ALU.mult, ALU.add)
        pz = bank(1)
        mm(pz[:, 0:n], BDM, B['YT'])
        tt('dve', B['t1'], B['YT'], pz[:, 0:n], ALU.subtract)
        act(B['kk'], B['t1'], AF.Square)
        mm(pz[:, 0:n], BDM, B['kk'])
        act(B['kk'], pz[:, 0:n], AF.Sqrt, bias=EPSC[:, 1:2])
        P.op('dve', lambda e: e.reciprocal(out=B['kk'], in_=B['kk']), [B['kk']], [B['kk']])
        tt('pool', B['t1'], B['t1'], B['kk'], ALU.mult)
        ts('dve', B['t1'], B['t1'], par(P_LW), ALU.mult, par(P_LB), ALU.add)
        tt('pool', B['t1'], B['t1'], B['bonv'], ALU.add)
        tt('dve', ROB[:, 0:n], B['t1'], B['g'], ALU.mult)
        out_dst(ROB[:, 0:n])

    def rwkv_tile(ntok, cur_ap, prv_ap, state_of_chunk, CV, out_dst_fc):
        def proj(ps, col0):
            for kc in range(8):
                mm(ps, W12[:, kc, 0, col0:col0 + 128], cur_ap(kc), start=(kc == 0), stop=False)
                mm(ps, W12[:, kc, 1, col0:col0 + 128], prv_ap(kc), start=False, stop=(kc == 7))
        n = ntok
        if RPRE < -1:
            return
        pw = bank(0)
        proj(pw[:, 0:n], 1536)
        if RPRE < 0:
            return
        act(TW[0:64, 0:n], pw[0:64, 0:n], AF.Tanh)
        cp('dve', AL[64:128, 0:n], pw[64:128, 0:n])
        proj(pw[:, 0:n], 1664)
        act(SG[:, 0:n], pw[:, 0:n], AF.Sigmoid)
        for fc in range(4):
            pr, pk, pv = bank(0), bank(1), bank(2)
            proj(pr[:, 0:n], 128 * fc)
            proj(pk[:, 0:n], 512 + 128 * fc)
            proj(pv[:, 0:n], 1024 + 128 * fc)
            rwkv_fc(fc, n, pr[:, 0:n], pk[:, 0:n], pv[:, 0:n], lambda st, q, fc=fc: state_of_chunk(fc, st, q), CV,
                    lambda rob, fc=fc: out_dst_fc(fc, rob))

    for fc in range(4):
        memset('pool', SST[:, 0, fc, :], 0.0)

    for it in range(NT):
        for st in range(4):
            t0 = 512 * it + 128 * st
            last = (t0 + 128 == T)

            def ld(xt, t0=t0):
                P.dma('sp', xt, x_p[t0:t0 + 128, :])

            def f32row(hf):
                P.dma('pool', o_pshift[0:1, :], hf[127:128, :])

            pb = norm_tile(ld, None, want_f32_row=f32row if last else None)
            cp(rot(['act', 'dve']), HTCUR[it % 2][:, :, 128 * st:128 * (st + 1)], pb.rearrange("p (k t) -> p k t", t=128))
        hcur = HTCUR[it % 2]
        P.dma('pool', hts_s[:, 512 * it:512 * (it + 1)].rearrange("(k p) t -> p k t", p=128), hcur)
        for lo in (range(0, 512, RT) if 'R' in stages else []):
            tb = 512 * it + lo
            if lo == 0:
                if it == 0:
                    memset('pool', HTP[:, :, 0:1], 0.0)
                else:
                    cp('dve', HTP[:, :, 0:1], HLAST[:, :, 0:1])
                cp('dve', HTP[:, :, 1:RT], hcur[:, :, 0:RT - 1])
            else:
                cp('dve', HTP, hcur[:, :, lo - 1:lo - 1 + RT])
            if lo + RT == 512:
                cp('dve', HLAST[:, :, 0:1], hcur[:, :, 511:512])
            rwkv_tile(RT,
                      lambda kc, lo=lo, hcur=hcur: hcur[:, kc, lo:lo + RT],
                      lambda kc: HTP[:, kc, :],
                      lambda fc, st, q: (SST[:, 0, fc, :], True), 64,
                      lambda fc, rob, tb=tb: P.dma('pool', mixp_s[128 * fc:128 * (fc + 1), tb:tb + RT], rob))
    if 'v' in stages:
        memset('pool', SNAT, 1.0)
    for fc in (range(4) if ('R' in stages and 'z' not in stages and 'v' not in stages) else []):
        for pi in range(2):
            tr(bank(4 + pi)[0:64, 0:64], SST[64 * pi:64 * (pi + 1), 0, fc, :], IDF[64 * pi:64 * (pi + 1), 64 * pi:64 * (pi + 1)])
            cp('act', SNAT[0:64, 2 * fc + pi, :], bank(4 + pi)[0:64, 0:64])
    if 'R' in stages and 'z' not in stages and 'u' not in stages:
        P.dma('sp', o_pwkv.rearrange("h i j -> i h j"), SNAT[0:64, :, :])


    HTC = XT[0][:, 0:512].bitcast(BF16).rearrange("p (k t) -> p k t", t=128)
    HPF = TOKB.rearrange("p a q c -> p (a q c)")[:, 0:D]
    if 'S' in stages:
        memset('pool', HTS, 0.0)
        for tile in range(2):
            memset('pool', HPF, 0.0)
            for q in range(2):
                P.dma('sp', HPF[64 * q + 63:64 * q + 64, :], st_shift[2 * tile + q:2 * tile + q + 1, :])

            def ld(xt, tile=tile):
                memset('pool', xt, 0.0)
                for q in range(2):
                    P.dma('sp', xt[64 * q:64 * q + 8, :], x_s[2 * tile + q, :, :])

            def f32row(hf, tile=tile):
                for q in range(2):
                    P.dma('pool', o_sshift[2 * tile + q:2 * tile + q + 1, :], hf[64 * q + 7:64 * q + 8, :])

            def hprev(hb):
                tt('dve', hb, hb, HPF, ALU.add)

            pb = norm_tile(ld, None, want_f32_row=f32row, hprev_fn=hprev)
            pb4 = pb.rearrange("p (k q j) -> p k q j", q=2, j=64)
            cp('act', HTS[:, :, 2 * tile:2 * tile + 2, 2:65], pb4[:, :, :, 0:63])
            cp('dve', HTS[:, :, 2 * tile:2 * tile + 2, 1:2], pb4[:, :, :, 63:64])
        for tile in (range(2) if 'R' in stages else []):
            for q in range(2):
                seq = 2 * tile + q
                P.dma('sp', SNAT[0:64, :, :], st_wkv[seq].rearrange("h i j -> i h j"))
                for h_ in range(8):
                    pi, fc = h_ % 2, h_ // 2
                    tr(bank(4)[0:64, 64 * (h_ % 8):64 * (h_ % 8) + 64], SNAT[0:64, h_, :], IDF[0:64, 0:64])
                for h_ in range(8):
                    pi, fc = h_ % 2, h_ // 2
                    cp('act' if pi else 'dve', SST[64 * pi:64 * (pi + 1), 1 + seq, fc, :], bank(4)[0:64, 64 * h_:64 * h_ + 64])
            cp('dve', HTC.rearrange("p k (q j) -> p k q j", j=64), HTS[:, :, 2 * tile:2 * tile + 2, 2:66])
            cp('dve', HTP[:, :, 0:128].rearrange("p k (q j) -> p k q j", j=64), HTS[:, :, 2 * tile:2 * tile + 2, 1:65])
            rwkv_tile(128,
                      lambda kc: HTC[:, kc, :],
                      lambda kc: HTP[:, kc, 0:128],
                      lambda fc, st, q, tile=tile: (SST[:, 1 + 2 * tile + q, fc, :], False), 8,
                      lambda fc, rob, tile=tile: P.dma('pool', mixs_s[128 * fc:128 * (fc + 1), 128 * tile:128 * (tile + 1)], rob))
            for q in range(2):
                seq = 2 * tile + q
                for fc in range(4):
                    for pi in range(2):
                        tr(bank(4 + pi)[0:64, 0:64], SST[64 * pi:64 * (pi + 1), 1 + seq, fc, :],
                           IDF[64 * pi:64 * (pi + 1), 64 * pi:64 * (pi + 1)])
                        cp('act', SNAT[0:64, 2 * fc + pi, :], bank(4 + pi)[0:64, 0:64])
                P.dma('sp', o_swkv[seq].rearrange("h i j -> i h j"), SNAT[0:64, :, :])


    SB.release(mA)

    mB = SB.mark()
    HT = SB.alloc((8, T), BF16)
    for kc in range(8):
        P.dma('sp', HT[:, kc, :], hts_s[128 * kc:128 * (kc + 1), :])
    WQKV = SB.alloc((8, 1536), BF16)
    mW = SB.mark()
    alloc_wst()
    for kc in range(8):
        rows = slice(128 * kc, 128 * (kc + 1))
        load_w(WQKV[:, kc, 0:1024], w_in[rows, NCOL_R:NCOL_R + 1024], 1024)
        load_w(WQKV[:, kc, 1024:1536], w_in[rows, NCOL_R + 1024:D_IN], 512)
    SB.release(mW)

    mS0 = SB.mark()
    if 'S' in stages:
        QSF = SB.alloc((4, 256), F32)
        KSF = SB.alloc((4, 256), F32)
        ATTS = SB.alloc((4, 256), BF16)
        KNEW = SB.alloc((2, 512), F32)
        VNEW = SB.alloc((2, 512), F32)
        CSS = SB.alloc((2, 128), F32)
        XBs = SB.alloc((128,), F32)
        SM = SB.alloc((8,), F32)
        ONESF = SB.alloc((64,), F32)
        RD = SB.alloc((8,), F32)
        HTC2 = SB.alloc((8, 128), BF16)
        RT1s = SB.alloc((128,), F32)
        RT2s = SB.alloc((128,), F32)
    mK = SB.mark()
    VB = [SB.alloc((512,), BF16) for _ in range(2)]
    VF = [SB.alloc((512,), F32) for _ in range(2)]
    KF = [SB.alloc((512,), F32) for _ in range(2)]
    KO = [SB.alloc((512,), F32) for _ in range(2)]
    KTMP = [SB.alloc((256,), F32) for _ in range(4)]
    CST = [SB.alloc((2, 32), F32) for _ in range(2)]

    def tokmajor_kv(lhs_ap, n_rows, i, v_bf16_dst, v_f32_dst, k_dst, cos_src, sin_src):
        pv = bank(i % 2)
        for kc in range(8):
            mm(pv, lhs_ap(kc), WQKV[:, kc, 1024:1536], start=(kc == 0), stop=(kc == 7))
        if v_bf16_dst is not None:
            cp('act', VB[i % 2], pv)
            P.dma('pool', v_bf16_dst, VB[i % 2][0:n_rows, :])
        if v_f32_dst is not None:
            cp('dve', VF[i % 2], pv)
            v_f32_dst(VF[i % 2])
        if k_dst is not None:
            pk = bank(2 + i % 2)
            for kc in range(8):
                mm(pk, lhs_ap(kc), WQKV[:, kc, 512:1024], start=(kc == 0), stop=(kc == 7))
            kf, ko, cs_ = KF[i % 2], KO[i % 2], CST[i % 2]
            cp('act', kf, pk)
            P.dma('sp', cs_[:, 0, :], cos_src)
            P.dma('sp', cs_[:, 1, :], sin_src)
            k4 = kf.rearrange("p (h a d) -> p h a d", a=2, d=32)
            o4 = ko.rearrange("p (h a d) -> p h a d", a=2, d=32)
            cb = cs_[:, 0, :].unsqueeze(1).to_broadcast([128, 8, 32])
            sb_ = cs_[:, 1, :].unsqueeze(1).to_broadcast([128, 8, 32])
            t = [x_.rearrange("p (h d) -> p h d", d=32) for x_ in KTMP]
            tt('dve', t[0], k4[:, :, 0, :], cb, ALU.mult)
            tt('pool', t[1], k4[:, :, 1, :], sb_, ALU.mult)
            tt('dve', o4[:, :, 0, :], t[0], t[1], ALU.subtract)
            tt('pool', t[2], k4[:, :, 0, :], sb_, ALU.mult)
            tt('dve', t[3], k4[:, :, 1, :], cb, ALU.mult)
            tt('pool', o4[:, :, 1, :], t[2], t[3], ALU.add)
            k_dst(ko)

    if 'B' in stages:
        for it in range(T // 128):
            t0 = 128 * it
            keep = t0 >= T - KEEP
            r0 = t0 - (T - KEEP)
            tokmajor_kv(lambda kc, t0=t0: HT[:, kc, t0:t0 + 128], 128, it, vs_s[t0:t0 + 128, :],
                        (lambda vf, r0=r0: P.dma('pool', o_pv[r0:r0 + 128, :], vf)) if keep else None,
                        (lambda ko, r0=r0: P.dma('pool', o_pk[r0:r0 + 128, :], ko)) if keep else None,
                        cost_p[r0:r0 + 128, :] if keep else None, sint_p[r0:r0 + 128, :] if keep else None)

    if 'S' in stages:
        memset('pool', ONESF, 1.0)
        memset('pool', ATTS, 0.0)
        for q in range(2):
            P.dma('sp', CSS[:, 0, 64 * q:64 * (q + 1)], cos_s[:, :])
            P.dma('sp', CSS[:, 1, 64 * q:64 * (q + 1)], sin_s[:, :])
        for tile in range(2):
            cp('dve', HTC2.rearrange("p k (q j) -> p k q j", j=64), HTS[:, :, 2 * tile:2 * tile + 2, 2:66])
            tokmajor_kv(lambda kc: HTC2[:, kc, :], 128, tile, None,
                        lambda vf, tile=tile: cp('pool', VNEW[:, tile, :], vf),
                        lambda ko, tile=tile: cp('pool', KNEW[:, tile, :], ko),
                        cost_s[:, :], sint_s[:, :])
            for q in range(2):
                seq = 2 * tile + q
                P.dma('pool', o_sk[seq, 2040:2048, :], KNEW[64 * q:64 * q + 8, tile, :])
                P.dma('pool', o_sv[seq, 2040:2048, :], VNEW[64 * q:64 * q + 8, tile, :])
            for hp in range(4):
                for which, c0_, dstF in ((0, 128 * hp, QSF), (1, 512 + 128 * hp, KSF)):
                    ps = bank(which)
                    for kc in range(8):
                        mm(ps[:, 0:128], WQKV[:, kc, c0_:c0_ + 128], HTC2[:, kc, :], start=(kc == 0), stop=(kc == 7))
                    cp('act', XBs, ps[:, 0:128])
                    pw_ = bank(7)
                    mm(pw_[:, 0:128], PERMF, XBs)
                    tt('pool', RT1s, XBs, CSS[:, 0, :], ALU.mult)
                    tt('dve', RT2s, pw_[:, 0:128], CSS[:, 1, :], ALU.mult)
                    tt('pool', dstF[:, hp, 128 * tile:128 * (tile + 1)], RT1s, RT2s, ALU.add)
    SB.release(mK)
    if 'S' in stages:
        KC = SB.alloc((17, 512), F32)
        KTC = SB.alloc((17 * 128,), F32)
        SSB = SB.alloc((17 * 128,), F32)
        LB = SB.alloc((17 * 128,), F32)
        PTs = SB.alloc((8, 17, 8), F32)
        P.dma('sp', LB[0:8, :], lb_s[:, :])
        for b_ in range(4):
            tile, q = b_ // 2, b_ % 2
            for t8 in range(0, 16, 4):
                P.dma('sp', KC[:, t8:t8 + 4, :], ck[b_, 128 * t8:128 * (t8 + 4), :].rearrange("(t p) c -> p t c", p=128))
            memset('pool', KC[:, 16, :], 0.0)
            cp('dve', KC[0:8, 16, :], KNEW[64 * q:64 * q + 8, tile, :])
            for h_ in range(8):
                pi, hp = h_ % 2, h_ // 2
                hs = slice(64 * pi, 64 * (pi + 1))
                hcol = slice(64 * h_, 64 * (h_ + 1))
                for g in range(5):
                    nt_ = 4 if g < 4 else 1
                    pk = bank(2 * (g % 2))
                    for j in range(nt_):
                        tr(pk[0:64, 128 * j:128 * (j + 1)], KC[:, 4 * g + j, hcol], IDF)
                    cp('act' if g % 2 else 'dve', KTC[hs, 512 * g:512 * g + 128 * nt_], pk[0:64, 0:128 * nt_])
                qv = QSF[hs, hp, 64 * b_:64 * b_ + 8]
                for g in range(5):
                    nn = 512 if g < 4 else 128
                    psc = bank(4 + pi + 2 * (g % 2))
                    mm(psc[0:8, 0:nn], qv, KTC[hs, 512 * g:512 * g + nn])
                    stt(SSB[0:8, 512 * g:512 * g + nn], psc[0:8, 0:nn], 0.125, LB[0:8, 512 * g:512 * g + nn], ALU.mult, ALU.add)
                P.op('dve', lambda e: e.tensor_reduce(out=SM[0:8, 0:1], in_=SSB[0:8, :], axis=AX.X, op=ALU.max),
                     [SM[0:8, 0:1]], [SSB[0:8, :]])
                ts('dve', SM[0:8, 1:2], SM[0:8, 0:1], -1.0, ALU.mult)
                act(SSB[0:8, :], SSB[0:8, :], AF.Exp, bias=SM[0:8, 1:2])
                pt_ = bank(0)
                for c_ in range(17):
                    tr(pt_[:, 8 * c_:8 * (c_ + 1)], SSB[0:8, 128 * c_:128 * (c_ + 1)], IDF[0:8, 0:8])
                cp('act', PTs[:, h_, :, :], pt_[:, 0:136].rearrange("p (c t) -> p c t", t=8))
            for t8 in range(0, 16, 4):
                P.dma('sp', KC[:, t8:t8 + 4, :], cv[b_, 128 * t8:128 * (t8 + 4), :].rearrange("(t p) c -> p t c", p=128))
            memset('pool', KC[:, 16, :], 0.0)
            cp('dve', KC[0:8, 16, :], VNEW[64 * q:64 * q + 8, tile, :])
            for h_ in range(8):
                pi, hp = h_ % 2, h_ // 2
                hs = slice(64 * pi, 64 * (pi + 1))
                hcol = slice(64 * h_, 64 * (h_ + 1))
                pn = bank(1)
                for c_ in range(17):
                    mm(pn[hs, 0:8], KC[:, c_, hcol], PTs[:, h_, c_, :], start=(c_ == 0), stop=(c_ == 16), tp=(0, 64 * pi))
                pd = bank(3)
                for c_ in range(17):
                    mm(pd[hs, 0:8], ONESF, PTs[:, h_, c_, :], start=(c_ == 0), stop=(c_ == 16), tp=(0, 64 * pi))
                cp('act', RD[hs, :], pd[hs, 0:8])
                P.op('dve', lambda e, hs=hs: e.reciprocal(out=RD[hs, :], in_=RD[hs, :]), [RD[hs, :]], [RD[hs, :]])
                tt('dve', ATTS[hs, hp, 64 * b_:64 * b_ + 8], pn[hs, 0:8], RD[hs, :], ALU.mult)
        for hp in range(4):
            P.dma('pool', mixs_s[512 + 128 * hp:512 + 128 * (hp + 1), :], ATTS[:, hp, :])
    SB.release(mS0)

    QT2 = SB.alloc((T, 2), BF16)
    KT2 = SB.alloc((T, 2), BF16)
    NV = {1: T // 128, 4: T // 128, 16: T // 128}
    VD = {d: SB.alloc((T // 128, 128), BF16) for d in (1, 4, 16)}
    NUMA = SB.alloc((2048,), F32)
    DENA = SB.alloc((2048,), F32)
    ATT = SB.alloc((2048,), BF16)
    XB = [SB.alloc((512,), BF16) for _ in range(2)]
    RT1 = [SB.alloc((512,), F32) for _ in range(2)]
    RT2 = [SB.alloc((512,), F32) for _ in range(2)]
    CSF = [SB.alloc((2, 512), F32) for _ in range(2)]
    EB = [SB.alloc((256,), BF16) for _ in range(8)]
    EM = [SB.alloc((256,), BF16) for _ in range(8)]
    PTS = [SB.alloc((2, 128), BF16) for _ in range(8)]
    BIAS = SB.alloc((8,), F32)
    DJ = [SB.alloc((128,), F32) for _ in range(2)]
    uidx = [0]

    def rope_fm(ps, xb, t1, t2, cs_, dst, n):
        cp('act', xb[:, 0:n], ps)
        pw_ = bank(7)
        mm(pw_[:, 0:n], PERM, xb[:, 0:n])
        tt('pool', t1[:, 0:n], xb[:, 0:n], cs_[:, 0, 0:n], ALU.mult)
        tt('dve', t2[:, 0:n], pw_[:, 0:n], cs_[:, 1, 0:n], ALU.mult)
        tt('pool', dst, t1[:, 0:n], t2[:, 0:n], ALU.add)

    for hp in (range(4) if ('B' in stages and BCUT >= 1) else []):
        qc = slice(128 * hp, 128 * (hp + 1))
        kcs = slice(512 + 128 * hp, 512 + 128 * (hp + 1))
        for it in range(NT):
            tb = 512 * it
            cs_ = CSF[it % 2]
            P.dma('sp', cs_[:, 0, :], cos_p[:, tb:tb + 512])
            P.dma('sp', cs_[:, 1, :], sin_p[:, tb:tb + 512])
            for which, cols, dstT in ((0, qc, QT2), (1, kcs, KT2)):
                ps = bank(which)
                for kc in range(8):
                    mm(ps, WQKV[:, kc, cols], HT[:, kc, tb:tb + 512], start=(kc == 0), stop=(kc == 7))
                rope_fm(ps, XB[which], RT1[which], RT2[which], cs_, dstT[:, tb:tb + 512, 0], 512)
        if BCUT < 2:
            continue
        for d in (1, 4, 16):
            npc = T // d // 128
            for r in range(d):
                src = vs_s[r:T:d, 128 * hp:128 * (hp + 1)].rearrange("(mt mi) c -> mi mt c", mi=128)
                for m0_ in range(0, npc, 8):
                    m1_ = min(npc, m0_ + 8)
                    P.dma('sp', VD[d][:, r * npc + m0_:r * npc + m1_, :], src[:, m0_:m1_, :])
        if BCUT < 3:
            continue
        for H in range(T // 2048):
            units = []
            for d in (1, 4, 16):
                npc = T // d // 128
                tph = 16 // d
                for r in range(d):
                    for mt in range(H * tph, (H + 1) * tph):
                        nkb = 1 if mt == 0 else 2
                        q0 = r + d * 128 * mt
                        k0 = r + d * 128 * (mt - 1) if nkb == 2 else q0
                        units.append((d, r, mt, nkb, q0, k0, q0 - 2048 * H, npc))
            GU = 4
            for g0 in range(0, len(units), GU):
                grp = units[g0:g0 + GU]
                hu = [(g, pi) for g in range(len(grp)) for pi in range(2)]
                for g, pi in hu:
                    d, r, mt, nkb, q0, k0, lo, npc = grp[g]
                    hs = slice(64 * pi, 64 * (pi + 1))
                    sb_ = bank(pi + 2 * (g % 2))[:, 256 * (g // 2):256 * (g // 2 + 1)]
                    c0 = 0 if nkb == 2 else 128
                    mm(sb_[:, c0:256], QT2[hs, sl(q0, 128, d), 0], KT2[hs, sl(k0, 128 * nkb, d), 0])
                for g, pi in hu:
                    sb_ = bank(pi + 2 * (g % 2))[:, 256 * (g // 2):256 * (g // 2 + 1)]
                    u = 2 * g + pi
                    stt(DJ[pi], sb_[:, 128:256], -0.125, IDF, ALU.mult, ALU.mult, accum=BIAS[:, u:u + 1])
                for g, pi in hu:
                    d, r, mt, nkb, q0, k0, lo, npc = grp[g]
                    sb_ = bank(pi + 2 * (g % 2))[:, 256 * (g // 2):256 * (g // 2 + 1)]
                    u = 2 * g + pi
                    c0 = 0 if nkb == 2 else 128
                    act(EB[u][:, c0:256], sb_[:, c0:256], AF.Exp, bias=BIAS[:, u:u + 1], scale=0.125)
                for g, pi in hu:
                    d, r, mt, nkb, q0, k0, lo, npc = grp[g]
                    u = 2 * g + pi
                    c0 = 0 if nkb == 2 else 128
                    tt('pool', EM[u][:, c0:256], EB[u][:, c0:256], MASK2[:, c0:256], ALU.mult)
                for g, pi in hu:
                    d, r, mt, nkb, q0, k0, lo, npc = grp[g]
                    u = 2 * g + pi
                    pt_ = bank(4 + pi).bitcast(BF16)[:, 256 * g:256 * (g + 1)]
                    for kb in range(2 - nkb, 2):
                        tr(pt_[:, 128 * kb:128 * (kb + 1)], EM[u][:, 128 * kb:128 * (kb + 1)], IDB)
                for g, pi in hu:
                    d, r, mt, nkb, q0, k0, lo, npc = grp[g]
                    u = 2 * g + pi
                    pt_ = bank(4 + pi).bitcast(BF16)[:, 256 * g:256 * (g + 1)]
                    cp('act' if pi == 0 else 'dve', PTS[u][:, 2 - nkb:2, :],
                       pt_[:, 128 * (2 - nkb):256].rearrange("p (k q) -> p k q", q=128))
                for g, pi in hu:
                    d, r, mt, nkb, q0, k0, lo, npc = grp[g]
                    u = 2 * g + pi
                    hs = slice(64 * pi, 64 * (pi + 1))
                    pn, pd = bank(6)[:, 128 * g:128 * (g + 1)], bank(7)[:, 128 * g:128 * (g + 1)]
                    for kb in range(2 - nkb, 2):
                        vt = r * npc + mt - 1 + kb
                        mm(pn[hs, :], VD[d][:, vt, hs], PTS[u][:, kb, :], start=(kb == 2 - nkb), stop=(kb == 1), tp=(0, 64 * pi))
                    for kb in range(2 - nkb, 2):
                        mm(pd[hs, :], ONESB, PTS[u][:, kb, :], start=(kb == 2 - nkb), stop=(kb == 1), tp=(0, 64 * pi))
                for g in range(len(grp)):
                    d, r, mt, nkb, q0, k0, lo, npc = grp[g]
                    pn, pd = bank(6)[:, 128 * g:128 * (g + 1)], bank(7)[:, 128 * g:128 * (g + 1)]
                    dstn = NUMA[:, sl(lo, 128, d)]
                    dstd = DENA[:, sl(lo, 128, d)]
                    if d == 1:
                        cp('dve', dstn, pn)
                        cp('act', dstd, pd)
                    else:
                        tt('dve', dstn, pn, dstn, ALU.add)
                        tt('dve', dstd, pd, dstd, ALU.add)
            P.op('dve', lambda e: e.reciprocal(out=DENA, in_=DENA), [DENA], [DENA])
            tt('pool', ATT, NUMA, DENA, ALU.mult)
            P.dma('pool', mixp_s[512 + 128 * hp:512 + 128 * (hp + 1), 2048 * H:2048 * (H + 1)], ATT)
    if debug:
        dbg['mix'] = dout("dbg_mix", [D, T], BF16)
        P.dma('pool', dbg['mix'], mixp_s)
    SB.release(m0)

    mC = SB.mark()
    WU = SB.alloc((8, 2 * DFF), BF16)
    WD = SB.alloc((22, D), BF16)
    GV = SB.alloc((3, D), F32)
    CW = SB.alloc((44, 4), F32)
    CAR = SB.alloc((5, 44, 2), F32)
    WOS = [SB.alloc((2, D), BF16) for _ in range(2)]
    mW = SB.mark()
    if 'C' in stages:
        alloc_wst()
        for i, gsrc in enumerate((n_mix_post, n_ffn_pre, n_ffn_post)):
            P.dma('sp', GV[:, i, :], gsrc[0:1, :].partition_broadcast(128))
        for j in range(3):
            dma_fc('sp', lambda c0, c1, j=j: CW[:, c0:c1, j], cw_d[j, :], False)
        dma_fc('sp', lambda c0, c1: CW[:, c0:c1, 3], cb_d.rearrange("f o -> (f o)"), False)
        for kc in range(8):
            rows = slice(128 * kc, 128 * (kc + 1))
            for j in range(0, 2 * DFF, 1024):
                n_ = min(1024, 2 * DFF - j)
                load_w(WU[:, kc, j:j + n_], w_up[rows, j:j + n_], n_)
        for fcn in range(22):
            load_w(WD[:, fcn, :], w_dn[128 * fcn:128 * (fcn + 1), :], 1024)
        for kc in range(8):
            load_w(WOS[kc % 2][:, 0, :], w_out[128 * kc:128 * (kc + 1), :], 1024)
            P.dma('pool', wout_s[128 * kc:128 * (kc + 1), :], WOS[kc % 2][:, 0, :])
    SB.release(mW)
    MIXT = [SB.alloc((8, 128), BF16) for _ in range(2)]
    WOF = [SB.alloc((512,), F32) for _ in range(2)]
    XC = [SB.alloc((D,), F32) for _ in range(2)]
    H2B = SB.alloc((D,), BF16)
    H2T = SB.alloc((8, 128), BF16)
    UE = [[SB.alloc((4, 130), F32)] * 2 for _ in range(2)]
    CGU = [SB.alloc((4, 128), F32) for _ in range(2)]
    CT = [SB.alloc((128,), F32) for _ in range(4)]
    FTB = [SB.alloc((22, 128), BF16) for _ in range(2)]
    JK = SB.alloc((D,), BF16)
    ST2 = SB.alloc((8, 4), F32)
    EPS2 = SB.alloc((1,), F32)
    ci = [0]
    pend = [None]

    def ffn_tile(*a_, **kw_):
        g_ = ffn_tile_gen(*a_, **kw_)
        next(g_)
        ffn_flush()
        next(g_)
        next(g_)
        pend[0] = g_

    def ffn_flush():
        if pend[0] is not None:
            for _ in pend[0]:
                pass
            pend[0] = None

    def rms_from(src_halves, si, col):
        act(JK[:, 0:512], src_halves[0], AF.Square, accum=ST2[:, si, 0:1])
        act(JK[:, 512:1024], src_halves[1], AF.Square, accum=ST2[:, si, 1:2])
        tt('dve', ST2[:, si, 0:1], ST2[:, si, 0:1], ST2[:, si, 1:2], ALU.add)
        act(ST2[:, si, 1:2], ST2[:, si, 0:1], AF.Sqrt, bias=EPS2[:, 0:1], scale=1.0 / D)
        P.op('dve', lambda e: e.reciprocal(out=ST2[:, si, col:col + 1], in_=ST2[:, si, 1:2]),
             [ST2[:, si, col:col + 1]], [ST2[:, si, 1:2]])

    def ffn_tile_gen(mix_src, x_src, y_dst, stream, segs, conv_out=None):
        i = ci[0] % 2
        si = ci[0] % 8
        ci[0] += 1
        mixt, xc = MIXT[i], XC[i]
        FT = FTB[i]
        P.dma('sp', mixt, mix_src.rearrange("(k p) t -> p k t", p=128))
        x_src(xc)
        for kq in range(4):
            wos = WOS[kq % 2]
            P.dma('sp', wos, wout_s[256 * kq:256 * (kq + 1), :].rearrange("(k p) c -> p k c", p=128))
            for k2 in range(2):
                kc = 2 * kq + k2
                for half in range(2):
                    mm(bank(half), mixt[:, kc, :], wos[:, k2, 512 * half:512 * (half + 1)], start=(kc == 0), stop=(kc == 7))
        yield 0
        rms_from((bank(0), bank(1)), si, 2)
        for half in range(2):
            hs_ = slice(512 * half, 512 * (half + 1))
            stt(WOF[half][:, 0:512], bank(half), ST2[:, si, 2:3], GV[:, 0, hs_], ALU.mult, ALU.mult)
            tt('pool', xc[:, hs_], xc[:, hs_], WOF[half][:, 0:512], ALU.add)
        rms_from((xc[:, 0:512], xc[:, 512:1024]), si, 3)
        stt(H2B, xc, ST2[:, si, 3:4], GV[:, 1, :], ALU.mult, ALU.mult)
        pb = bank(2).bitcast(BF16)
        for kc in range(8):
            tr(pb[:, 128 * kc:128 * (kc + 1)], H2B[:, 128 * kc:128 * (kc + 1)], IDB)
        cp('act', H2T, pb.rearrange("p (k t) -> p k t", t=128))
        g0 = 0
        gi = 0
        while g0 < 22:
            ng = min(4, 22 - g0)
            for side in range(2):
                ps = bank(3 + 2 * (gi % 2) + side)
                for c in range(ng):
                    ch = 22 * side + g0 + c
                    for kc in range(8):
                        mm(ps[:, 128 * c:128 * (c + 1)], WU[:, kc, 128 * ch:128 * (ch + 1)], H2T[:, kc, :],
                           start=(kc == 0), stop=(kc == 7))
            for side in range(2):
                ps = bank(3 + 2 * (gi % 2) + side)
                ue = UE[side][gi % 2]
                chs = slice(22 * side + g0, 22 * side + g0 + ng)
                cp('act', ue[:, 0:ng, 2:130], ps[:, 0:128 * ng].rearrange("p (c t) -> p c t", t=128))
                for (col0, ncols, cinit) in segs:
                    if cinit is None:
                        cp('pool', ue[:, 0:ng, col0:col0 + 2], CAR[:, stream, chs, :])
                    else:
                        cinit(ue[:, 0:ng, col0:col0 + 2], chs)
                for c in range(ng):
                    ch = 22 * side + g0 + c
                    t_ = CT[(2 * c + side) % 4]
                    act(t_, ue[:, c, 0:128], AF.Identity, bias=CW[:, ch, 3:4], scale=CW[:, ch, 0:1])
                    stt(t_, ue[:, c, 1:129], CW[:, ch, 1:2], t_, ALU.mult, ALU.add)
                    stt(CGU[side][:, c, :], ue[:, c, 2:130], CW[:, ch, 2:3], t_, ALU.mult, ALU.add)
                last_col = segs[-1][0] + segs[-1][1]
                cp('pool', CAR[:, stream, chs, :], ue[:, 0:ng, last_col:last_col + 2])
                if conv_out is not None:
                    conv_out(ue, chs, ng)
            act(CGU[0][:, 0:ng, :], CGU[0][:, 0:ng, :], AF.Silu)
            tt('pool', FT[:, g0:g0 + ng, :], CGU[0][:, 0:ng, :], CGU[1][:, 0:ng, :], ALU.mult)
            g0 += ng
            gi += 1
            if gi == 2:
                yield 1
        yield 2
        dbk = (bank(7), bank(2))
        for half in range(2):
            ps = dbk[half]
            for fcn in range(22):
                mm(ps, FT[:, fcn, :], WD[:, fcn, 512 * half:512 * (half + 1)], start=(fcn == 0), stop=(fcn == 21))
        rms_from(dbk, si, 2)
        for half in range(2):
            hs_ = slice(512 * half, 512 * (half + 1))
            stt(WOF[half][:, 0:512], dbk[half], ST2[:, si, 2:3], GV[:, 2, hs_], ALU.mult, ALU.mult)
            tt('pool', xc[:, hs_], xc[:, hs_], WOF[half][:, 0:512], ALU.add)
        y_dst(xc)

    if 'C' in stages:
        memset('pool', EPS2, NORM_EPS)
        memset('pool', CAR, 0.0)
        for it in range(T // 128):
            t0 = 128 * it
            ffn_tile(mixp_s[:, t0:t0 + 128],
                     lambda xc, t0=t0: P.dma('sp', xc, x_p[t0:t0 + 128, :]),
                     lambda xc, t0=t0: P.dma('pool', y_p[t0:t0 + 128, :], xc),
                     0, [(0, 128, None)])
        ffn_flush()
        if 'S' in stages:
            CONVS = SB.alloc((4, 44, 2), F32)
            SCONV = SB.alloc((4, 44, 2), F32)
            for seq in range(4):
                for r_ in range(2):
                    dma_fc('sp', lambda c0, c1, seq=seq, r_=r_: CONVS[:, seq, c0:c1, r_], st_conv[seq, r_, :], False)
            for tile in range(2):
                def xs_(xc, tile=tile):
                    memset('pool', xc, 0.0)
                    for q in range(2):
                        P.dma('sp', xc[64 * q:64 * q + 8, :], x_s[2 * tile + q, :, :])

                def ys_(xc, tile=tile):
                    for q in range(2):
                        P.dma('pool', y_s[2 * tile + q, :, :], xc[64 * q:64 * q + 8, :])

                def mk_init(seq):
                    return lambda dst, chs, seq=seq: cp('pool', dst, CONVS[:, seq, chs, :])

                def cout(ue, chs, ng, tile=tile):
                    for q in range(2):
                        cp('pool', SCONV[:, 2 * tile + q, chs, :], ue[:, 0:ng, 2 + 64 * q + 6:2 + 64 * q + 8])

                ffn_tile(mixs_s[:, 128 * tile:128 * (tile + 1)], xs_, ys_, 1,
                         [(0, 64, mk_init(2 * tile)), (64, 64, mk_init(2 * tile + 1))], conv_out=cout)
            ffn_flush()
            for seq in range(4):
                for r_ in range(2):
                    dma_fc('pool', lambda c0, c1, seq=seq, r_=r_: SCONV[:, seq, c0:c1, r_], o_sconv[seq, r_, :], True)
        for r_ in range(2):
            dma_fc('pool', lambda c0, c1, r_=r_: CAR[:, 0, c0:c1, r_], o_pconv[r_, :], True)
    SB.release(mC)

    P.final_wait_all('sp')
    P.build()
    return nc


def _rope_tables(T):
    half = 32
    inv = (np.float32(10000.0) ** (-np.arange(half, dtype=np.float32) / np.float32(half))).astype(np.float32)

    def tab(pos):
        ang = pos.astype(np.float32)[:, None] * inv[None, :]
        return np.cos(ang).astype(np.float32), np.sin(ang).astype(np.float32)
    cp_, sp_ = tab(np.arange(T))
    cs_, ss_ = tab(PAST_LEN + np.arange(64))
    def fm(c, s):
        cf = np.concatenate([c.T, c.T], 0)
        sf = np.concatenate([-s.T, s.T], 0)
        return (np.ascontiguousarray(np.concatenate([cf, cf], 0)), np.ascontiguousarray(np.concatenate([sf, sf], 0)))
    return fm(cp_, sp_), fm(cs_, ss_), (cp_, sp_), (cs_, ss_)


def _lb_table():
    cnt = np.zeros((8, 17 * 128), np.float64)
    for t in range(8):
        for d in (1, 4, 16):
            for j in range(129):
                idx = WBUF + t - j * d
                if idx >= 0:
                    cnt[t, idx] += 1
    lb = np.where(cnt > 0, np.log(np.maximum(cnt, 1)), -30000.0)
    return lb.astype(np.float32)


def _core_inputs(inp, c, T):
    (cfp, sfp), (cfs, sfs), (ctp, stp), (cts, sts) = _rope_tables(T)
    KEEP = min(WBUF, T)
    f = lambda a: np.ascontiguousarray(np.asarray(a, dtype=np.float32))
    d = {
        'x_prompt': f(inp['x_prompt'][c, :T]),
        'x_sample': f(inp['x_sample'][4 * c:4 * c + 4]),
        'state_rwkv_shift': f(inp['state_rwkv_shift'][0, 4 * c:4 * c + 4]),
        'state_rwkv_wkv': f(inp['state_rwkv_wkv'][0, 4 * c:4 * c + 4]),
        'cache_att_k': f(np.asarray(inp['cache_att_k'][0, 4 * c:4 * c + 4]).reshape(4, WBUF, 512)),
        'cache_att_v': f(np.asarray(inp['cache_att_v'][0, 4 * c:4 * c + 4]).reshape(4, WBUF, 512)),
        'state_ffn_conv': f(inp['state_ffn_conv'][0, 4 * c:4 * c + 4]),
        'norm_mix_pre': f(inp['norm_mix_pre']), 'norm_mix_post': f(inp['norm_mix_post']),
        'norm_ffn_pre': f(inp['norm_ffn_pre']), 'norm_ffn_post': f(inp['norm_ffn_post']),
        'w_in': f(inp['w_in'][0]), 'mu_shift': f(inp['mu_shift']),
        'w0': f(inp['w0']).reshape(512, 1), 'w_decay_up': f(inp['w_decay_up'][0]), 'a0': f(inp['a0']).reshape(512, 1),
        'w_iclr_up': f(inp['w_iclr_up'][0]), 'w_gate_up': f(inp['w_gate_up'][0]),
        'k_k': f(inp['k_k']).reshape(512, 1), 'k_a': f(inp['k_a']).reshape(512, 1), 'r_k': f(inp['r_k']).reshape(512, 1),
        'lnx_w': f(inp['lnx_w']).reshape(512, 1), 'lnx_b': f(inp['lnx_b']).reshape(512, 1),
        'w_out': f(inp['w_out'][0]), 'w_ffn_up': f(inp['w_ffn_up'][0]), 'ffn_conv_w': f(inp['ffn_conv_w'][0]),
        'ffn_conv_b': f(inp['ffn_conv_b']).reshape(2 * DFF, 1), 'w_ffn_down': f(inp['w_ffn_down'][0]),
        'cos_p': cfp, 'sin_p': sfp, 'cos_s': cfs, 'sin_s': sfs,
        'cost_p': np.ascontiguousarray(ctp[T - KEEP:]), 'sint_p': np.ascontiguousarray(stp[T - KEEP:]),
        'cost_s': np.ascontiguousarray(np.tile(cts[:64], (2, 1))), 'sint_s': np.ascontiguousarray(np.tile(sts[:64], (2, 1))),
        'lb_s': _lb_table(),
    }
    return d


_NC_CACHE = {}


def kernel(**inputs):
    T = 4096
    n = 8
    if T not in _NC_CACHE:
        _NC_CACHE[T] = build_program(T=T, debug=False, stages='NRBCS')
    nc = _NC_CACHE[T]
    in_maps = [_core_inputs(inputs, c, T) for c in range(n)]
    res = run_bass_kernel_spmd(nc, in_maps, core_ids=list(range(n)))
    rs = res.results
    g = lambda k: [np.asarray(r[k], dtype=np.float32) for r in rs]
    y_p = np.stack(g('y_prompt'), 0)
    y_s = np.concatenate(g('y_sample'), 0)
    p_shift = np.stack([a.reshape(D) for a in g('p_shift')], 0)[None]
    p_wkv = np.stack(g('p_wkv'), 0)[None]
    p_k = np.stack([a.reshape(WBUF, 8, 64) for a in g('p_att_k')], 0)[None]
    p_v = np.stack([a.reshape(WBUF, 8, 64) for a in g('p_att_v')], 0)[None]
    p_conv = np.stack(g('p_conv'), 0)[None]
    s_shift = np.concatenate(g('s_shift'), 0)[None]
    s_wkv = np.concatenate(g('s_wkv'), 0)[None]
    s_k = np.concatenate([a.reshape(4, WBUF, 8, 64) for a in g('s_att_k')], 0)[None]
    s_v = np.concatenate([a.reshape(4, WBUF, 8, 64) for a in g('s_att_v')], 0)[None]
    s_conv = np.concatenate(g('s_conv'), 0)[None]
    return (y_p, y_s, p_shift, p_wkv, p_k, p_v, p_conv, s_shift, s_wkv, s_k, s_v, s_conv)
```

```python
import contextlib
import os
RCUT = int(os.environ.get('RCUT', '9'))
RPRE = int(os.environ.get('RPRE', '99'))
RSUB = int(os.environ.get('RSUB', '99'))
BCUT = int(os.environ.get('BCUT', '99'))
import numpy as np
import ml_dtypes
import concourse.bass as bass
import concourse.mybir as mybir
from concourse.bass_utils import run_bass_kernel_spmd

F32 = mybir.dt.float32
BF16 = mybir.dt.bfloat16
AF = mybir.ActivationFunctionType
ALU = mybir.AluOpType
AX = mybir.AxisListType

D = 1024
HD = 64
RW = 512
NCOL_R = 1792
D_IN = 3328
DFF = 2816
PAST_LEN = 16384
WBUF = 2048
NORM_EPS = 1e-6
LNX_EPS = 64e-5
C0 = float(np.exp(-0.5))

_ESZ = {F32: 4, BF16: 2}


def _box(ap):
    name = ap.tensor.name
    dims = ap.ap
    off = int(ap.offset)
    esz = _ESZ.get(ap.dtype, 4)
    if 'DRAM' in str(ap.space).upper():
        ext = sum((c - 1) * abs(s) for s, c in dims)
        return (name, 0, 1, off * esz, (off + ext + 1) * esz)
    pstride = dims[0][0]
    pcount = dims[0][1]
    if pstride == 0:
        p0, f0 = 0, off
    else:
        p0 = off // pstride
        f0 = off - p0 * pstride
    ext = sum((c - 1) * abs(s) for s, c in dims[1:])
    b0, b1 = f0 * esz, (f0 + ext + 1) * esz
    if 'PSUM' in str(ap.space).upper():
        return (name, (p0 // 32) * 32, ((p0 + pcount + 31) // 32) * 32, (b0 // 2048) * 2048, ((b1 + 2047) // 2048) * 2048)
    return (name, p0, p0 + pcount, b0, b1)


class Prog:
    COMPUTE = ('pe', 'act', 'dve', 'pool')

    def __init__(self, nc, n_dma_sems=(20, 12, 8)):
        self.nc = nc
        self.ops = {e: [] for e in ('pe', 'act', 'dve', 'pool', 'sp')}
        self.nseq = {e: 0 for e in self.COMPUTE}
        self.waited = {}
        self.recs = {}
        self.dmaq = {'sp': [n_dma_sems[0], 0], 'pool': [n_dma_sems[1], 0], 'act': [n_dma_sems[2], 0]}
        self.sems = {}

    def _deps(self, eng, outs, ins, waiter=None):
        waiter = waiter or eng
        deps = {}
        for ap, is_w in [(a, True) for a in outs] + [(a, False) for a in ins]:
            name, p0, p1, f0, f1 = _box(ap)
            for r in self.recs.get(name, ()):
                if r[0] < p1 and p0 < r[1] and r[2] < f1 and f0 < r[3]:
                    if is_w or r[6] or (name == 'PS' and r[7] != eng):
                        if r[7] == eng and eng in self.COMPUTE:
                            if eng == 'pe':
                                continue
                            if not (r[6] and not is_w):
                                continue
                        if deps.get(r[4], 0) < r[5]:
                            deps[r[4]] = r[5]
        out = []
        for key, val in deps.items():
            if self.waited.get((waiter, key), 0) < val:
                self.waited[(waiter, key)] = val
                out.append((key, val))
        return out

    def _record(self, eng, outs, ins, key, val):
        for ap, is_w in [(a, True) for a in outs] + [(a, False) for a in ins]:
            name, p0, p1, f0, f1 = _box(ap)
            lst = self.recs.setdefault(name, [])
            if is_w:
                lst[:] = [r for r in lst if not (p0 <= r[0] and r[1] <= p1 and f0 <= r[2] and r[3] <= f1)]
            else:
                lst[:] = [r for r in lst if not ((not r[6]) and r[4] == key and p0 <= r[0] and r[1] <= p1
                                                 and f0 <= r[2] and r[3] <= f1)]
            lst.append([p0, p1, f0, f1, key, val, is_w, eng])

    def op(self, eng, fn, outs, ins):
        waits = self._deps(eng, outs, ins)
        self.nseq[eng] += 1
        key = 'c_' + eng
        self._record(eng, outs, ins, key, self.nseq[eng])
        self.ops[eng].append(('c', waits, fn, key))

    def dma(self, q, out, in_, **kw):
        ns, cnt = self.dmaq[q]
        slot = cnt % ns
        rnd = cnt // ns
        self.dmaq[q][1] += 1
        key = 'd_%s_%d' % (q, slot)
        waits = self._deps('dma_' + q, [out], [in_], waiter=q)
        if rnd > 0 and self.waited.get((q, key), 0) < 16 * rnd:
            self.waited[(q, key)] = 16 * rnd
            waits.append((key, 16 * rnd))
        self._record('dma_' + q, [out], [in_], key, 16 * (rnd + 1))
        self.ops[q].append(('d', waits, (out, in_, kw), key))

    def final_wait_all(self, eng='sp'):
        waits = []
        for q, (ns, cnt) in self.dmaq.items():
            for slot in range(min(ns, cnt)):
                nuse = (cnt - slot + ns - 1) // ns
                waits.append(('d_%s_%d' % (q, slot), 16 * nuse))
        for e in self.COMPUTE:
            if self.nseq[e] > 0:
                waits.append(('c_' + e, self.nseq[e]))
        self.ops[eng].append(('w', waits, None, None))

    def build(self):
        nc = self.nc
        keys = set()
        for e, lst in self.ops.items():
            for kind, waits, payload, key in lst:
                if key:
                    keys.add(key)
                for k, v in waits:
                    keys.add(k)
        with contextlib.ExitStack() as st:
            for k in sorted(keys):
                self.sems[k] = st.enter_context(nc.semaphore(k))
            block = st.enter_context(nc.Block())
            sems = self.sems

            def runner(lst):
                def run(e):
                    for kind, waits, payload, key in lst:
                        for k, v in waits:
                            e.wait_ge(sems[k], v)
                        if kind == 'c':
                            payload(e).then_inc(sems[key], 1)
                        elif kind == 'd':
                            out, in_, kw = payload
                            e.dma_start(out=out, in_=in_, **kw).then_inc(sems[key], 16)
                return run

            block.tensor(runner(self.ops['pe']))
            block.scalar(runner(self.ops['act']))
            block.vector(runner(self.ops['dve']))
            block.gpsimd(runner(self.ops['pool']))
            block.sync(runner(self.ops['sp']))


class Arena:
    def __init__(self, nc, name, nbytes):
        self.t = nc.alloc_sbuf_tensor(name, [128, nbytes // 4], F32)
        self.cap = nbytes
        self.off = 0
        self.peak = 0

    def alloc(self, shape, dt):
        shape = tuple(int(s) for s in shape)
        n = int(np.prod(shape))
        nb = (n * _ESZ[dt] + 63) // 64 * 64
        off = self.off
        self.off += nb
        self.peak = max(self.peak, self.off)
        assert self.off <= self.cap, ("SBUF arena overflow", self.off, self.cap)
        ap = self.t[:, off // 4:(off + nb) // 4]
        if dt != F32:
            ap = ap.bitcast(dt)
        ap = ap[:, 0:n]
        if len(shape) > 1:
            names = ['a%d' % i for i in range(len(shape))]
            pat = "p (%s) -> p %s" % (' '.join(names), ' '.join(names))
            ap = ap.rearrange(pat, **{nm: s for nm, s in zip(names, shape)})
        return ap

    def mark(self):
        return self.off

    def release(self, m):
        self.off = m


def build_program(T=4096, debug=False, stages='ABC'):
    RT = 512
    NT = T // 512
    nc = bass.Bass("TRN2", target_bir_lowering=False)
    P = Prog(nc)
    KEEP = min(WBUF, T)

    def din(name, shape, dt=F32):
        return nc.dram_tensor(name, list(shape), dt, kind="ExternalInput").ap()

    def dout(name, shape, dt=F32):
        return nc.dram_tensor(name, list(shape), dt, kind="ExternalOutput").ap()

    def dscr(name, shape, dt=BF16):
        return nc.dram_tensor(name, list(shape), dt, kind="Internal").ap()

    x_p = din("x_prompt", [T, D])
    x_s = din("x_sample", [4, 8, D])
    st_shift = din("state_rwkv_shift", [4, D])
    st_wkv = din("state_rwkv_wkv", [4, 8, 64, 64])
    ck = din("cache_att_k", [4, WBUF, 512])
    cv = din("cache_att_v", [4, WBUF, 512])
    st_conv = din("state_ffn_conv", [4, 2, 2 * DFF])
    n_mix_pre = din("norm_mix_pre", [1, D])
    n_mix_post = din("norm_mix_post", [1, D])
    n_ffn_pre = din("norm_ffn_pre", [1, D])
    n_ffn_post = din("norm_ffn_post", [1, D])
    w_in = din("w_in", [D, D_IN])
    mu_shift = din("mu_shift", [1, NCOL_R])
    w0_d = din("w0", [RW, 1])
    w_decay_up = din("w_decay_up", [64, RW])
    a0_d = din("a0", [RW, 1])
    w_iclr_up = din("w_iclr_up", [64, RW])
    w_gate_up = din("w_gate_up", [128, RW])
    k_k_d = din("k_k", [RW, 1])
    k_a_d = din("k_a", [RW, 1])
    r_k_d = din("r_k", [RW, 1])
    lnx_w_d = din("lnx_w", [RW, 1])
    lnx_b_d = din("lnx_b", [RW, 1])
    w_out = din("w_out", [D, D])
    w_up = din("w_ffn_up", [D, 2 * DFF])
    cw_d = din("ffn_conv_w", [3, 2 * DFF])
    cb_d = din("ffn_conv_b", [2 * DFF, 1])
    w_dn = din("w_ffn_down", [DFF, D])
    cos_p = din("cos_p", [128, T])
    sin_p = din("sin_p", [128, T])
    cos_s = din("cos_s", [128, 64])
    sin_s = din("sin_s", [128, 64])
    cost_p = din("cost_p", [KEEP, 32])
    sint_p = din("sint_p", [KEEP, 32])
    cost_s = din("cost_s", [128, 32])
    sint_s = din("sint_s", [128, 32])

    lb_s = din("lb_s", [8, 17 * 128])

    y_p = dout("y_prompt", [T, D])
    y_s = dout("y_sample", [4, 8, D])
    o_pshift = dout("p_shift", [1, D])
    o_pwkv = dout("p_wkv", [8, 64, 64])
    o_pk = dout("p_att_k", [KEEP, 512])
    o_pv = dout("p_att_v", [KEEP, 512])
    o_pconv = dout("p_conv", [2, 2 * DFF])
    o_sshift = dout("s_shift", [4, D])
    o_swkv = dout("s_wkv", [4, 8, 64, 64])
    o_sk = dout("s_att_k", [4, WBUF, 512])
    o_sv = dout("s_att_v", [4, WBUF, 512])
    o_sconv = dout("s_conv", [4, 2, 2 * DFF])
    dbg = {}

    n_mixp, n_mixs, n_vs = D * T, D * 256, T * 512
    SCR = dscr("SCR", [n_mixp + n_mixs + n_vs + D * D + D * T])
    mixp_s = SCR[0:n_mixp].rearrange("(f t) -> f t", t=T)
    mixs_s = SCR[n_mixp:n_mixp + n_mixs].rearrange("(f t) -> f t", t=256)
    vs_s = SCR[n_mixp + n_mixs:n_mixp + n_mixs + n_vs].rearrange("(t c) -> t c", c=512)
    wout_s = SCR[n_mixp + n_mixs + n_vs:n_mixp + n_mixs + n_vs + D * D].rearrange("(k c) -> k c", c=D)
    o_h_ = n_mixp + n_mixs + n_vs + D * D
    hts_s = SCR[o_h_:o_h_ + D * T].rearrange("(f t) -> f t", t=T)

    SB = Arena(nc, "SB", 212480)
    PS = nc.alloc_psum_tensor("PS", [128, 4096], F32)

    def bank(k, a=0, b=512):
        return PS[:, 512 * k + a:512 * k + b]

    def dma_fc(q, sb_fn, dr_row, write):
        for c0 in range(0, 44, 8):
            c1 = min(44, c0 + 8)
            d_ = dr_row[128 * c0:128 * c1].rearrange("(c p) -> p c", p=128)
            if write:
                P.dma(q, d_, sb_fn(c0, c1), allow_slow_non_contiguous=True)
            else:
                P.dma(q, sb_fn(c0, c1), d_, allow_slow_non_contiguous=True)

    def sl(start, n, step):
        return slice(start, start + step * (n - 1) + 1, step)

    def tt(eng, out, in0, in1, op):
        P.op(eng, lambda e: e.tensor_tensor(out=out, in0=in0, in1=in1, op=op), [out], [in0, in1])

    def ts(eng, out, in0, s1, op0, s2=None, op1=None):
        ins = [in0] + [s for s in (s1, s2) if not isinstance(s, (int, float, type(None)))]
        if op1 is None:
            P.op(eng, lambda e: e.tensor_scalar(out=out, in0=in0, scalar1=s1, scalar2=None, op0=op0), [out], ins)
        else:
            P.op(eng, lambda e: e.tensor_scalar(out=out, in0=in0, scalar1=s1, scalar2=s2, op0=op0, op1=op1), [out], ins)

    def stt(out, in0, scalar, in1, op0, op1, accum=None):
        ins = [in0, in1] + ([scalar] if not isinstance(scalar, (int, float)) else [])
        outs = [out] + ([accum] if accum is not None else [])
        if accum is None:
            P.op('dve', lambda e: e.scalar_tensor_tensor(out=out, in0=in0, scalar=scalar, in1=in1, op0=op0, op1=op1), outs, ins)
        else:
            P.op('dve', lambda e: e.scalar_tensor_tensor(out=out, in0=in0, scalar=scalar, in1=in1, op0=op0, op1=op1,
                                                         accum_out=accum), outs, ins)

    def act(out, in_, func, bias=None, scale=None, accum=None):
        ins = [in_] + [s for s in (bias, scale) if not isinstance(s, (int, float, type(None)))]
        outs = [out] + ([accum] if accum is not None else [])
        kw = {}
        if bias is not None:
            kw['bias'] = bias
        if scale is not None:
            kw['scale'] = scale
        if accum is not None:
            kw['accum_out'] = accum
        P.op('act', lambda e: e.activation(out=out, in_=in_, func=func, **kw), outs, ins)

    def cp(eng, out, in_):
        if eng == 'act':
            P.op('act', lambda e: e.copy(out=out, in_=in_), [out], [in_])
        else:
            P.op(eng, lambda e: e.tensor_copy(out=out, in_=in_), [out], [in_])

    pe_hist = []

    def pe_check(out, lhsT):
        import traceback
        _, p0, p1, f0, f1 = _box(lhsT)
        rg = set(range(p0 // 32, (p1 + 31) // 32))
        _, _, _, b0, b1 = _box(out)
        bk = set(range(b0 // 2048, (b1 + 2047) // 2048))
        for (rg2, bk2, where) in pe_hist[-24:]:
            if not (rg & rg2) and (bk & bk2):
                print("PE ROW-GROUP/BANK HAZARD:", sorted(rg), sorted(bk), "vs", sorted(rg2), sorted(bk2), where,
                      "|", traceback.extract_stack(limit=4)[0].lineno)
        pe_hist.append((rg, bk, traceback.extract_stack(limit=4)[0].lineno))

    def mm(out, lhsT, rhs, start=True, stop=True, tp=None):
        pe_check(out, lhsT)
        if tp is None:
            P.op('pe', lambda e: e.matmul(out, lhsT=lhsT, rhs=rhs, start=start, stop=stop), [out], [lhsT, rhs])
        else:
            P.op('pe', lambda e: e.matmul(out, lhsT=lhsT, rhs=rhs, start=start, stop=stop, tile_position=tp), [out], [lhsT, rhs])

    def tr(out, in_, ident):
        pe_check(out, in_)
        P.op('pe', lambda e: e.transpose(out, in_, ident), [out], [in_, ident])

    def memset(eng, out, val):
        P.op(eng, lambda e: e.memset(out, val), [out], [])

    def asel(out, pattern, base, cm, cmp, fill=0.0):
        P.op('pool', lambda e: e.affine_select(out=out, in_=out, pattern=pattern, compare_op=cmp, fill=fill,
                                               base=base, channel_multiplier=cm), [out], [out])

    rr = {'n': 0}

    def rot(engs):
        rr['n'] += 1
        return engs[rr['n'] % len(engs)]

    if 'S' in stages:
        for b_ in range(4):
            for (src_, dst_) in ((ck, o_sk), (cv, o_sv)):
                for c_ in range(4):
                    P.dma('act', dst_[b_, 510 * c_:510 * (c_ + 1), :], src_[b_, 8 + 510 * c_:8 + 510 * (c_ + 1), :])

    IDF = SB.alloc((128,), F32)
    IDB = SB.alloc((128,), BF16)
    BDO = SB.alloc((128,), F32)
    BDM = SB.alloc((128,), F32)
    MSU = SB.alloc((128,), F32)
    MIU = SB.alloc((128,), F32)
    MSL = SB.alloc((128,), F32)
    MAB = SB.alloc((2, 128), F32)
    PERM = SB.alloc((128,), BF16)
    PERMF = SB.alloc((128,), F32)
    MASK2 = SB.alloc((256,), BF16)
    ONESB = SB.alloc((64,), BF16)
    PAR = SB.alloc((4, 8), F32)
    m0 = SB.mark()
    HTS = SB.alloc((8, 4, 66), BF16)
    mA = SB.mark()
    RESET = SB.alloc((512,), F32)
    tmpc = SB.alloc((256,), F32)

    memset('pool', IDF, 1.0)
    asel(IDF, [[-1, 128]], 0, 1, ALU.is_equal)
    cp('dve', IDB, IDF)
    memset('pool', BDO, 1.0)
    memset('pool', BDO[0:64, 64:128], 0.0)
    memset('pool', BDO[64:128, 0:64], 0.0)
    ts('dve', BDM, BDO, 1.0 / 64, ALU.mult)
    memset('pool', MSU, 1.0)
    asel(MSU, [[1, 128]], 0, -1, ALU.is_gt)
    tt('dve', MSU, MSU, BDO, ALU.mult)
    memset('pool', MIU, 1.0)
    asel(MIU, [[1, 128]], 0, -1, ALU.is_ge)
    tt('dve', MIU, MIU, BDO, ALU.mult)
    memset('pool', MSL, 1.0)
    asel(MSL, [[-1, 128]], 0, 1, ALU.is_gt)
    tt('dve', MSL, MSL, BDO, ALU.mult)
    cp('dve', MAB[:, 0, :], MSU)
    cp('dve', MAB[:, 1, :], MIU)
    memset('pool', tmpc[:, 0:128], 1.0)
    asel(tmpc[:, 0:128], [[-1, 128]], -32, 1, ALU.is_equal)
    memset('pool', tmpc[:, 32:64], 0.0)
    memset('pool', tmpc[:, 96:128], 0.0)
    memset('pool', tmpc[:, 128:256], 1.0)
    asel(tmpc[:, 128:256], [[-1, 128]], 32, 1, ALU.is_equal)
    memset('pool', tmpc[:, 128:160], 0.0)
    memset('pool', tmpc[:, 192:224], 0.0)
    tt('dve', PERM, tmpc[:, 0:128], tmpc[:, 128:256], ALU.add)
    tt('dve', PERMF, tmpc[:, 0:128], tmpc[:, 128:256], ALU.add)
    memset('pool', tmpc[:, 0:256], 1.0)
    asel(tmpc[:, 0:128], [[1, 128]], 0, -1, ALU.is_ge)
    asel(tmpc[:, 128:256], [[-1, 128]], 0, 1, ALU.is_ge)
    cp('dve', MASK2, tmpc[:, 0:256])
    memset('pool', RESET, 1.0)
    for c in range(8):
        memset('pool', RESET[:, 64 * c:64 * c + 1], 0.0)
    memset('pool', ONESB, 1.0)

    P_W0, P_A0, P_KK, P_KA, P_RK, P_LW, P_LB, P_OMKA = range(8)
    for i, d_ in enumerate((w0_d, a0_d, k_k_d, k_a_d, r_k_d, lnx_w_d, lnx_b_d)):
        for fc in range(4):
            P.dma('sp', PAR[:, fc, i:i + 1], d_[128 * fc:128 * (fc + 1), :])
    ts('dve', PAR[:, :, P_OMKA], PAR[:, :, P_KA], -1.0, ALU.mult, 1.0, ALU.add)
    WDU = SB.alloc((RW,), F32)
    WIU = SB.alloc((RW,), F32)
    WGU = SB.alloc((RW,), F32)
    P.dma('sp', WDU[0:64, :], w_decay_up[:, :])
    P.dma('sp', WIU[64:128, :], w_iclr_up[:, :])
    P.dma('sp', WGU[:, :], w_gate_up[:, :])
    GPRE = SB.alloc((D,), F32)
    P.dma('sp', GPRE, n_mix_pre[0:1, :].partition_broadcast(128))

    WSTh = {'b': None}
    wi = [0]

    def alloc_wst():
        WSTh['b'] = [SB.alloc((1024,), F32) for _ in range(3)]

    def load_w(dst, src, ncol, scale_pairs=None):
        i = wi[0] % 3
        wi[0] += 1
        WST = WSTh['b']
        P.dma('sp', WST[i][:, 0:ncol], src)
        if scale_pairs is None:
            cp(rot(['act', 'dve', 'pool']), dst, WST[i][:, 0:ncol])
        else:
            for (d_, sc), eng in zip(scale_pairs, ('dve', 'pool')):
                tt(eng, d_, WST[i][:, 0:ncol], sc, ALU.mult)

    W12 = SB.alloc((8, 2, NCOL_R), BF16)
    mW = SB.mark()
    alloc_wst()
    MU = SB.alloc((NCOL_R,), F32)
    OMU = SB.alloc((NCOL_R,), F32)
    P.dma('sp', MU, mu_shift[0:1, :].partition_broadcast(128))
    ts('dve', OMU, MU, -1.0, ALU.mult, 1.0, ALU.add)
    for kc in (range(8) if 'x' not in stages else []):
        rows = slice(128 * kc, 128 * (kc + 1))
        for c0, c1 in ((0, 1024), (1024, NCOL_R)):
            load_w(None, w_in[rows, c0:c1], c1 - c0,
                   scale_pairs=[(W12[:, kc, 0, c0:c1], OMU[:, c0:c1]), (W12[:, kc, 1, c0:c1], MU[:, c0:c1])])
    SB.release(mW)
    XT = [SB.alloc((D,), F32) for _ in range(2)]
    HB = [SB.alloc((D,), BF16) for _ in range(2)]
    STAT = SB.alloc((8, 4), F32)
    EPSC = SB.alloc((2,), F32)
    memset('pool', EPSC[:, 0:1], NORM_EPS)
    memset('pool', EPSC[:, 1:2], LNX_EPS)
    sti = [0]

    def norm_tile(x_rows_fn, dst_fn, want_f32_row=None, hprev_fn=None):
        i = sti[0] % 2
        si = sti[0] % 8
        sti[0] += 1
        xt, hb = XT[i], HB[i]
        x_rows_fn(xt)
        act(hb, xt, AF.Square, accum=STAT[:, si, 0:1])
        act(STAT[:, si, 1:2], STAT[:, si, 0:1], AF.Sqrt, bias=EPSC[:, 0:1], scale=1.0 / D)
        P.op('dve', lambda e: e.reciprocal(out=STAT[:, si, 2:3], in_=STAT[:, si, 1:2]), [STAT[:, si, 2:3]], [STAT[:, si, 1:2]])
        stt(hb, xt, STAT[:, si, 2:3], GPRE, ALU.mult, ALU.mult)
        if want_f32_row is not None:
            stt(xt, xt, STAT[:, si, 2:3], GPRE, ALU.mult, ALU.mult)
            want_f32_row(xt)
        if hprev_fn is not None:
            hprev_fn(hb)
        pb = bank(2 + i).bitcast(BF16)
        for kc in range(8):
            tr(pb[:, 128 * kc:128 * (kc + 1)], hb[:, 128 * kc:128 * (kc + 1)], IDB)
        return pb

    RB = {}
    for nm in ('r', 'k', 'v', 'sgw', 'a', 'g', 'cs', 'csx', 'crem', 'ein', 'eneg', 'kk', 't1', 'kkn',
               'kmod', 'bb', 'bonv', 'YT'):
        RB[nm] = SB.alloc((RT,), F32)
    BTB = SB.alloc((RT,), BF16)
    KTB = SB.alloc((RT,), BF16)
    RB['eex'] = RB['csx']
    RB['erem'] = RB['crem']
    RB['BP'] = RB['bb']
    RB['KP'] = RB['kmod']
    AR = SB.alloc((RT // 128, 2, 128), F32)
    TW = SB.alloc((RT,), F32)
    AL = SB.alloc((RT,), F32)
    SG = SB.alloc((RT,), F32)
    NSTM = RT // 128
    TOKB = SB.alloc((NSTM, 4, 128), F32)
    TOK = [TOKB[:, i_, :, :] for i_ in range(NSTM)]
    TOKH = [RB[('eneg', 'kk')[(i_ * 512) // (2 * RT)]].bitcast(BF16)[:, (i_ * 512) % (2 * RT):(i_ * 512) % (2 * RT) + 512]
            .rearrange("p (q c) -> p q c", c=128) for i_ in range(NSTM)]
    ARB = RB['kkn'].bitcast(BF16).rearrange("p (s a t) -> p s a t", a=2, t=128)
    NM = [SB.alloc((2, 2, 128), BF16) for _ in range(NSTM)]
    AK = [SB.alloc((2, 2, 128), BF16) for _ in range(NSTM)]
    XP = [[SB.alloc((2, 2, 128), BF16) for _ in range(2)] for _ in range(NSTM)]
    YY = [[SB.alloc((2, 128), BF16) for _ in range(2)] for _ in range(NSTM)]
    Z1 = [SB.alloc((2, 64), BF16) for _ in range(NSTM)]
    UW = [SB.alloc((2, 2, 64), BF16) for _ in range(NSTM)]
    WTs = [SB.alloc((128,), F32) for _ in range(NSTM)]
    GT = [SB.alloc((128,), F32) for _ in range(NSTM)]
    HTt = [SB.alloc((128,), F32) for _ in range(NSTM)]
    US = SB.alloc((2, 64), F32)
    SST = SB.alloc((5, 4, 64), F32)
    ROB = RB['sgw'].bitcast(BF16)[:, 0:RT]
    HTP = SB.alloc((8, RT), BF16)
    HTCUR = [SB.alloc((8, 512), BF16) for _ in range(2)]
    HLAST = SB.alloc((8, 2), BF16)
    SNAT = TOK[0].rearrange("p q c -> p (q c)").rearrange("p (h j) -> p h j", j=64)

    def rwkv_fc(fc, ntok, ps_r, ps_k, ps_v, state_of_chunk, CV, out_dst):
        nst = ntok // 128
        n = ntok
        par = lambda i: PAR[:, fc, i:i + 1]
        B = {k_: v_[:, 0:n] for k_, v_ in RB.items()}
        cp('act', B['r'], ps_r)
        cp('act', B['k'], ps_k)
        cp('act', B['v'], ps_v)
        if RPRE < 1:
            return
        cols = slice(128 * fc, 128 * (fc + 1))
        pz = bank(1)
        mm(bank(4)[:, 0:n], WDU[0:64, cols], TW[0:64, 0:n])
        act(B['sgw'], bank(4)[:, 0:n], AF.Sigmoid, bias=par(P_W0))
        mm(bank(5)[:, 0:n], WIU[64:128, cols], AL[64:128, 0:n])
        act(B['a'], bank(5)[:, 0:n], AF.Sigmoid, bias=par(P_A0))
        mm(pz[:, 0:n], WGU[:, cols], SG[:, 0:n])
        cp('act', B['g'], pz[:, 0:n])
        if RPRE < 2:
            return
        P.op('dve', lambda e: e.tensor_tensor_scan(out=B['cs'], data0=RESET[:, 0:n], data1=B['sgw'], initial=0.0,
                                                   op0=ALU.mult, op1=ALU.add), [B['cs']], [RESET[:, 0:n], B['sgw']])
        if RPRE < 3:
            return
        tt('dve', B['csx'], B['cs'], B['sgw'], ALU.subtract)
        nch = n // 64
        cs3 = B['cs'].rearrange("p (c t) -> p c t", t=64)
        tt('dve', B['crem'].rearrange("p (c t) -> p c t", t=64), cs3[:, :, CV - 1:CV].to_broadcast([128, nch, 64]), cs3,
           ALU.subtract)
        if RPRE < 4:
            return
        act(B['ein'], B['cs'], AF.Exp, scale=-C0)
        act(B['eex'], B['csx'], AF.Exp, scale=-C0)
        act(B['eneg'], B['cs'], AF.Exp, scale=C0)
        act(B['erem'], B['crem'], AF.Exp, scale=-C0)
        if RPRE < 5:
            return
        ts('dve', B['kk'], B['k'], par(P_KK), ALU.mult)
        act(B['t1'], B['kk'], AF.Square)
        mm(pz[:, 0:n], BDO, B['t1'])
        act(B['t1'], pz[:, 0:n], AF.Sqrt)
        ts('dve', B['t1'], B['t1'], 1e-12, ALU.max)
        P.op('dve', lambda e: e.reciprocal(out=B['t1'], in_=B['t1']), [B['t1']], [B['t1']])
        tt('dve', B['kkn'], B['kk'], B['t1'], ALU.mult)
        if RPRE < 6:
            return
        ts('dve', B['t1'], B['a'], par(P_KA), ALU.mult, par(P_OMKA), ALU.add)
        tt('dve', B['kmod'], B['k'], B['t1'], ALU.mult)
        tt('dve', B['bb'], B['kkn'], B['a'], ALU.mult)
        if RPRE < 7:
            return
        ar = AR[:, 0:nst, :, :]
        stt(ar[:, :, 0, :], B['kkn'].rearrange("p (s t) -> p s t", t=128), -1.0,
            B['eex'].rearrange("p (s t) -> p s t", t=128), ALU.mult, ALU.mult)
        tt('pool', ar[:, :, 1, :], B['r'].rearrange("p (s t) -> p s t", t=128),
           B['ein'].rearrange("p (s t) -> p s t", t=128), ALU.mult)
        if RPRE < 8:
            return
        stt(B['t1'], B['r'], par(P_RK), B['kmod'], ALU.mult, ALU.mult)
        mm(pz[:, 0:n], BDO, B['t1'])
        tt('dve', B['bonv'], pz[:, 0:n], B['v'], ALU.mult)
        tt('dve', BTB[:, 0:n], B['bb'], B['eneg'], ALU.mult)
        tt('dve', KTB[:, 0:n], B['kmod'], B['eneg'], ALU.mult)
        tt('dve', B['BP'], B['bb'], B['erem'], ALU.mult)
        tt('pool', B['KP'], B['kmod'], B['erem'], ALU.mult)

        cp('act', ARB[:, 0:nst, :, :], AR[:, 0:nst, :, :])
        for st in range(nst):
            tk = slice(128 * st, 128 * (st + 1))
            tok, tokb = TOK[st], TOKH[st]
            pt = bank(3)
            for qi, src in enumerate((AR[:, st, 0, :], B['BP'][:, tk], B['KP'][:, tk], B['v'][:, tk])):
                tr(pt[:, 128 * qi:128 * (qi + 1)], src, IDF)
            cp('act', tok, pt.rearrange("p (q c) -> p q c", c=128))
            cp('dve', tokb, pt.rearrange("p (q c) -> p q c", c=128))
            for pi in range(2):
                hp = slice(64 * pi, 64 * (pi + 1))
                rhs_ar = ARB[hp, st, :, :]
                mm(bank(4 + pi)[:, 0:256], BTB[hp, tk], rhs_ar)
                mm(bank(4 + pi)[:, 256:512], KTB[hp, tk], rhs_ar)
                mm(bank(6 + pi)[:, 0:128], ARB[hp, st, 0, :], BTB[hp, tk])
            for pi in range(2):
                tt('dve', NM[st][:, pi, :, :], bank(4 + pi)[:, 0:256].rearrange("p (a t) -> p a t", t=128), MAB, ALU.mult)
                tt('dve', AK[st][:, pi, :, :], bank(4 + pi)[:, 256:512].rearrange("p (a t) -> p a t", t=128), MAB, ALU.mult)
                tt('dve', YY[st][0][:, pi, :], bank(6 + pi)[:, 0:128], MSL, ALU.mult)
                cp('act', XP[st][0][:, pi, 0, :], NM[st][:, pi, 0, :])
                tt('pool', XP[st][0][:, pi, 1, :], NM[st][:, pi, 0, :], IDF, ALU.add)
        for kstep in range(6):
            for st in range(nst):
                cur = kstep % 2
                xp, yy = XP[st][cur], YY[st][cur]
                xpn, yyn = XP[st][1 - cur], YY[st][1 - cur]
                p1, p2 = (bank(1), bank(2)) if st % 2 == 0 else (bank(3), bank(0))
                if kstep == 0:
                    for pi in range(2):
                        mm(p1[:, 256 * pi:256 * pi + 128], yy[:, pi, :], xp[:, pi, 0, :])
                        mm(p2[:, 128 * pi:128 * (pi + 1)], xp[:, pi, 0, :], yy[:, pi, :])
                    cp('act', xpn[:, :, 0, :], p1.rearrange("p (h a t) -> p h a t", a=2, t=128)[:, :, 0, :])
                    cp('pool', xpn[:, :, 1, :], xp[:, :, 1, :])
                    cp('dve', yyn, p2[:, 0:256].rearrange("p (h t) -> p h t", t=128))
                elif kstep < 5:
                    for pi in range(2):
                        mm(p1[:, 256 * pi:256 * (pi + 1)], yy[:, pi, :], xp[:, pi, :, :])
                        mm(p2[:, 128 * pi:128 * (pi + 1)], xp[:, pi, 0, :], yy[:, pi, :])
                    p1v = p1.rearrange("p (h a t) -> p h a t", a=2, t=128)
                    cp('act', xpn[:, :, 0, :], p1v[:, :, 0, :])
                    tt('dve', xpn[:, :, 1, :], p1v[:, :, 1, :], xp[:, :, 1, :], ALU.add)
                    cp('act', yyn, p2[:, 0:256].rearrange("p (h t) -> p h t", t=128))
                else:
                    for pi in range(2):
                        mm(p1[:, 256 * pi + 128:256 * (pi + 1)], yy[:, pi, :], xp[:, pi, 1, :])
                    p1v = p1.rearrange("p (h a t) -> p h a t", a=2, t=128)
                    tt('dve', xpn[:, :, 1, :], p1v[:, :, 1, :], xp[:, :, 1, :], ALU.add)
        def pmb(st):
            return bank(7) if st % 2 == 0 else bank(2)

        def pgb(st):
            return bank(3) if st % 2 == 0 else bank(1)
        for st in range(nst):
            tokb, pm = TOKH[st], pmb(st)
            for pi in range(2):
                hc = slice(64 * pi, 64 * (pi + 1))
                mm(pm[:, 64 * pi:64 * (pi + 1)], AK[st][:, pi, 0, :], tokb[:, 3, hc])
            cp('act', Z1[st], pm[:, 0:128].rearrange("p (h i) -> p h i", i=64))
        for st in range(nst):
            tokb, pm, PT = TOKH[st], pmb(st), XP[st][0]
            for pi in range(2):
                hc = slice(64 * pi, 64 * (pi + 1))
                mm(pm[:, 128 + 128 * pi:128 + 128 * pi + 64], PT[:, pi, 1, :], Z1[st][:, pi, :])
                mm(pm[:, 128 + 128 * pi + 64:128 + 128 * (pi + 1)], PT[:, pi, 1, :], tokb[:, 0, hc])
            for pi in range(2):
                hc = slice(64 * pi, 64 * (pi + 1))
                mm(pm[hc, 384:512], tokb[:, 0, hc], PT[:, pi, 1, :], tp=(0, 64 * pi))
            cp('dve', UW[st], pm[:, 128:384].rearrange("p (h a i) -> p h a i", a=2, i=64))
            cp('act', WTs[st], pm[:, 384:512])
        for st in range(nst):
            tokb, pg = TOKH[st], pgb(st)
            for pi in range(2):
                hc = slice(64 * pi, 64 * (pi + 1))
                mm(pg[hc, 0:128], UW[st][:, pi, 1, :], NM[st][:, pi, 1, :], tp=(0, 64 * pi))
            for pi in range(2):
                hc = slice(64 * pi, 64 * (pi + 1))
                mm(pg[hc, 128:256], UW[st][:, pi, 0, :], NM[st][:, pi, 1, :], start=True, stop=False, tp=(0, 64 * pi))
                mm(pg[hc, 128:256], tokb[:, 3, hc], AK[st][:, pi, 1, :], start=False, stop=True, tp=(0, 64 * pi))
            tt('dve', GT[st], pg[:, 0:128], AR[:, st, 1, :], ALU.add)
            cp('act', HTt[st], pg[:, 128:256])
        for st in range(nst):
            tok = TOK[st]
            for q in range(2):
                S, chain = state_of_chunk(st, q)
                tq = slice(64 * q, 64 * (q + 1))
                tv = slice(64 * q, 64 * q + CV)
                for pi in range(2):
                    hc = slice(64 * pi, 64 * (pi + 1))
                    mm(bank(6 + pi)[tq, 128:192], WTs[st][hc, tq], S[hc, :], tp=(64 * pi, 64 * q))
                for pi in range(2):
                    tt('dve', US[tq, pi, :], bank(6 + pi)[tq, 128:192], UW[st][tq, pi, 0, :], ALU.add)
                for pi in range(2):
                    hc = slice(64 * pi, 64 * (pi + 1))
                    mm(bank(6 + pi)[hc, 192:256], S[hc, :], GT[st][hc, tq], tp=(64 * pi, 64 * pi))
                for pi in range(2):
                    hc = slice(64 * pi, 64 * (pi + 1))
                    tt('dve', B['YT'][hc, 128 * st + 64 * q:128 * st + 64 * (q + 1)], bank(6 + pi)[hc, 192:256], HTt[st][hc, tq],
                       ALU.add)
                psq = bank(6 + q)[:, 256:320]
                for pi in range(2):
                    hc = slice(64 * pi, 64 * (pi + 1))
                    mm(psq[hc, :], tok[tv, 1, hc], US[tv, pi, :], start=True, stop=False, tp=(64 * q, 64 * pi))
                    mm(psq[hc, :], tok[tv, 2, hc], tok[tv, 3, hc], start=False, stop=True, tp=(64 * q, 64 * pi))
                pcol = B['ein'][:, 128 * st + 64 * q + CV - 1:128 * st + 64 * q + CV]
                stt(S, S, pcol, psq, ALU.mult, ALU.add)
        pz = bank(1)
        mm(pz[:, 0:n], BDM, B['YT'])
        tt('dve', B['t1'], B['YT'], pz[:, 0:n], ALU.subtract)
        act(B['kk'], B['t1'], AF.Square)
        mm(pz[:, 0:n], BDM, B['kk'])
        act(B['kk'], pz[:, 0:n], AF.Sqrt, bias=EPSC[:, 1:2])
        P.op('dve', lambda e: e.reciprocal(out=B['kk'], in_=B['kk']), [B['kk']], [B['kk']])
        tt('pool', B['t1'], B['t1'], B['kk'], ALU.mult)
        ts('dve', B['t1'], B['t1'], par(P_LW), ALU.mult, par(P_LB), ALU.add)
        tt('pool', B['t1'], B['t1'], B['bonv'], ALU.add)
        tt('dve', ROB[:, 0:n], B['t1'], B['g'], ALU.mult)
        out_dst(ROB[:, 0:n])

    def rwkv_tile(ntok, cur_ap, prv_ap, state_of_chunk, CV, out_dst_fc):
        def proj(ps, col0):
            for kc in range(8):
                mm(ps, W12[:, kc, 0, col0:col0 + 128], cur_ap(kc), start=(kc == 0), stop=False)
                mm(ps, W12[:, kc, 1, col0:col0 + 128], prv_ap(kc), start=False, stop=(kc == 7))
        n = ntok
        if RPRE < -1:
            return
        pw = bank(0)
        proj(pw[:, 0:n], 1536)
        if RPRE < 0:
            return
        act(TW[0:64, 0:n], pw[0:64, 0:n], AF.Tanh)
        cp('dve', AL[64:128, 0:n], pw[64:128, 0:n])
        proj(pw[:, 0:n], 1664)
        act(SG[:, 0:n], pw[:, 0:n], AF.Sigmoid)
        for fc in range(4):
            pr, pk, pv = bank(0), bank(1), bank(2)
            proj(pr[:, 0:n], 128 * fc)
            proj(pk[:, 0:n], 512 + 128 * fc)
            proj(pv[:, 0:n], 1024 + 128 * fc)
            rwkv_fc(fc, n, pr[:, 0:n], pk[:, 0:n], pv[:, 0:n], lambda st, q, fc=fc: state_of_chunk(fc, st, q), CV,
                    lambda rob, fc=fc: out_dst_fc(fc, rob))

    for fc in range(4):
        memset('pool', SST[:, 0, fc, :], 0.0)

    for it in range(NT):
        for st in range(4):
            t0 = 512 * it + 128 * st
            last = (t0 + 128 == T)

            def ld(xt, t0=t0):
                P.dma('sp', xt, x_p[t0:t0 + 128, :])

            def f32row(hf):
                P.dma('pool', o_pshift[0:1, :], hf[127:128, :])

            pb = norm_tile(ld, None, want_f32_row=f32row if last else None)
            cp(rot(['act', 'dve']), HTCUR[it % 2][:, :, 128 * st:128 * (st + 1)], pb.rearrange("p (k t) -> p k t", t=128))
        hcur = HTCUR[it % 2]
        P.dma('pool', hts_s[:, 512 * it:512 * (it + 1)].rearrange("(k p) t -> p k t", p=128), hcur)
        for lo in (range(0, 512, RT) if 'R' in stages else []):
            tb = 512 * it + lo
            if lo == 0:
                if it == 0:
                    memset('pool', HTP[:, :, 0:1], 0.0)
                else:
                    cp('dve', HTP[:, :, 0:1], HLAST[:, :, 0:1])
                cp('dve', HTP[:, :, 1:RT], hcur[:, :, 0:RT - 1])
            else:
                cp('dve', HTP, hcur[:, :, lo - 1:lo - 1 + RT])
            if lo + RT == 512:
                cp('dve', HLAST[:, :, 0:1], hcur[:, :, 511:512])
            rwkv_tile(RT,
                      lambda kc, lo=lo, hcur=hcur: hcur[:, kc, lo:lo + RT],
                      lambda kc: HTP[:, kc, :],
                      lambda fc, st, q: (SST[:, 0, fc, :], True), 64,
                      lambda fc, rob, tb=tb: P.dma('pool', mixp_s[128 * fc:128 * (fc + 1), tb:tb + RT], rob))
    if 'v' in stages:
        memset('pool', SNAT, 1.0)
    for fc in (range(4) if ('R' in stages and 'z' not in stages and 'v' not in stages) else []):
        for pi in range(2):
            tr(bank(4 + pi)[0:64, 0:64], SST[64 * pi:64 * (pi + 1), 0, fc, :], IDF[64 * pi:64 * (pi + 1), 64 * pi:64 * (pi + 1)])
            cp('act', SNAT[0:64, 2 * fc + pi, :], bank(4 + pi)[0:64, 0:64])
    if 'R' in stages and 'z' not in stages and 'u' not in stages:
        P.dma('sp', o_pwkv.rearrange("h i j -> i h j"), SNAT[0:64, :, :])


    HTC = XT[0][:, 0:512].bitcast(BF16).rearrange("p (k t) -> p k t", t=128)
    HPF = TOKB.rearrange("p a q c -> p (a q c)")[:, 0:D]
    if 'S' in stages:
        memset('pool', HTS, 0.0)
        for tile in range(2):
            memset('pool', HPF, 0.0)
            for q in range(2):
                P.dma('sp', HPF[64 * q + 63:64 * q + 64, :], st_shift[2 * tile + q:2 * tile + q + 1, :])

            def ld(xt, tile=tile):
                memset('pool', xt, 0.0)
                for q in range(2):
                    P.dma('sp', xt[64 * q:64 * q + 8, :], x_s[2 * tile + q, :, :])

            def f32row(hf, tile=tile):
                for q in range(2):
                    P.dma('pool', o_sshift[2 * tile + q:2 * tile + q + 1, :], hf[64 * q + 7:64 * q + 8, :])

            def hprev(hb):
                tt('dve', hb, hb, HPF, ALU.add)

            pb = norm_tile(ld, None, want_f32_row=f32row, hprev_fn=hprev)
            pb4 = pb.rearrange("p (k q j) -> p k q j", q=2, j=64)
            cp('act', HTS[:, :, 2 * tile:2 * tile + 2, 2:65], pb4[:, :, :, 0:63])
            cp('dve', HTS[:, :, 2 * tile:2 * tile + 2, 1:2], pb4[:, :, :, 63:64])
        for tile in (range(2) if 'R' in stages else []):
            for q in range(2):
                seq = 2 * tile + q
                P.dma('sp', SNAT[0:64, :, :], st_wkv[seq].rearrange("h i j -> i h j"))
                for h_ in range(8):
                    pi, fc = h_ % 2, h_ // 2
                    tr(bank(4)[0:64, 64 * (h_ % 8):64 * (h_ % 8) + 64], SNAT[0:64, h_, :], IDF[0:64, 0:64])
                for h_ in range(8):
                    pi, fc = h_ % 2, h_ // 2
                    cp('act' if pi else 'dve', SST[64 * pi:64 * (pi + 1), 1 + seq, fc, :], bank(4)[0:64, 64 * h_:64 * h_ + 64])
            cp('dve', HTC.rearrange("p k (q j) -> p k q j", j=64), HTS[:, :, 2 * tile:2 * tile + 2, 2:66])
            cp('dve', HTP[:, :, 0:128].rearrange("p k (q j) -> p k q j", j=64), HTS[:, :, 2 * tile:2 * tile + 2, 1:65])
            rwkv_tile(128,
                      lambda kc: HTC[:, kc, :],
                      lambda kc: HTP[:, kc, 0:128],
                      lambda fc, st, q, tile=tile: (SST[:, 1 + 2 * tile + q, fc, :], False), 8,
                      lambda fc, rob, tile=tile: P.dma('pool', mixs_s[128 * fc:128 * (fc + 1), 128 * tile:128 * (tile + 1)], rob))
            for q in range(2):
                seq = 2 * tile + q
                for fc in range(4):
                    for pi in range(2):
                        tr(bank(4 + pi)[0:64, 0:64], SST[64 * pi:64 * (pi + 1), 1 + seq, fc, :],
                           IDF[64 * pi:64 * (pi + 1), 64 * pi:64 * (pi + 1)])
                        cp('act', SNAT[0:64, 2 * fc + pi, :], bank(4 + pi)[0:64, 0:64])
                P.dma('sp', o_swkv[seq].rearrange("h i j -> i h j"), SNAT[0:64, :, :])


    SB.release(mA)

    mB = SB.mark()
    HT = SB.alloc((8, T), BF16)
    for kc in range(8):
        P.dma('sp', HT[:, kc, :], hts_s[128 * kc:128 * (kc + 1), :])
    WQKV = SB.alloc((8, 1536), BF16)
    mW = SB.mark()
    alloc_wst()
    for kc in range(8):
        rows = slice(128 * kc, 128 * (kc + 1))
        load_w(WQKV[:, kc, 0:1024], w_in[rows, NCOL_R:NCOL_R + 1024], 1024)
        load_w(WQKV[:, kc, 1024:1536], w_in[rows, NCOL_R + 1024:D_IN], 512)
    SB.release(mW)

    mS0 = SB.mark()
    if 'S' in stages:
        QSF = SB.alloc((4, 256), F32)
        KSF = SB.alloc((4, 256), F32)
        ATTS = SB.alloc((4, 256), BF16)
        KNEW = SB.alloc((2, 512), F32)
        VNEW = SB.alloc((2, 512), F32)
        CSS = SB.alloc((2, 128), F32)
        XBs = SB.alloc((128,), F32)
        SM = SB.alloc((8,), F32)
        ONESF = SB.alloc((64,), F32)
        RD = SB.alloc((8,), F32)
        HTC2 = SB.alloc((8, 128), BF16)
        RT1s = SB.alloc((128,), F32)
        RT2s = SB.alloc((128,), F32)
    mK = SB.mark()
    VB = [SB.alloc((512,), BF16) for _ in range(2)]
    VF = [SB.alloc((512,), F32) for _ in range(2)]
    KF = [SB.alloc((512,), F32) for _ in range(2)]
    KO = [SB.alloc((512,), F32) for _ in range(2)]
    KTMP = [SB.alloc((256,), F32) for _ in range(4)]
    CST = [SB.alloc((2, 32), F32) for _ in range(2)]

    def tokmajor_kv(lhs_ap, n_rows, i, v_bf16_dst, v_f32_dst, k_dst, cos_src, sin_src):
        pv = bank(i % 2)
        for kc in range(8):
            mm(pv, lhs_ap(kc), WQKV[:, kc, 1024:1536], start=(kc == 0), stop=(kc == 7))
        if v_bf16_dst is not None:
            cp('act', VB[i % 2], pv)
            P.dma('pool', v_bf16_dst, VB[i % 2][0:n_rows, :])
        if v_f32_dst is not None:
            cp('dve', VF[i % 2], pv)
            v_f32_dst(VF[i % 2])
        if k_dst is not None:
            pk = bank(2 + i % 2)
            for kc in range(8):
                mm(pk, lhs_ap(kc), WQKV[:, kc, 512:1024], start=(kc == 0), stop=(kc == 7))
            kf, ko, cs_ = KF[i % 2], KO[i % 2], CST[i % 2]
            cp('act', kf, pk)
            P.dma('sp', cs_[:, 0, :], cos_src)
            P.dma('sp', cs_[:, 1, :], sin_src)
            k4 = kf.rearrange("p (h a d) -> p h a d", a=2, d=32)
            o4 = ko.rearrange("p (h a d) -> p h a d", a=2, d=32)
            cb = cs_[:, 0, :].unsqueeze(1).to_broadcast([128, 8, 32])
            sb_ = cs_[:, 1, :].unsqueeze(1).to_broadcast([128, 8, 32])
            t = [x_.rearrange("p (h d) -> p h d", d=32) for x_ in KTMP]
            tt('dve', t[0], k4[:, :, 0, :], cb, ALU.mult)
            tt('pool', t[1], k4[:, :, 1, :], sb_, ALU.mult)
            tt('dve', o4[:, :, 0, :], t[0], t[1], ALU.subtract)
            tt('pool', t[2], k4[:, :, 0, :], sb_, ALU.mult)
            tt('dve', t[3], k4[:, :, 1, :], cb, ALU.mult)
            tt('pool', o4[:, :, 1, :], t[2], t[3], ALU.add)
            k_dst(ko)

    if 'B' in stages:
        for it in range(T // 128):
            t0 = 128 * it
            keep = t0 >= T - KEEP
            r0 = t0 - (T - KEEP)
            tokmajor_kv(lambda kc, t0=t0: HT[:, kc, t0:t0 + 128], 128, it, vs_s[t0:t0 + 128, :],
                        (lambda vf, r0=r0: P.dma('pool', o_pv[r0:r0 + 128, :], vf)) if keep else None,
                        (lambda ko, r0=r0: P.dma('pool', o_pk[r0:r0 + 128, :], ko)) if keep else None,
                        cost_p[r0:r0 + 128, :] if keep else None, sint_p[r0:r0 + 128, :] if keep else None)

    if 'S' in stages:
        memset('pool', ONESF, 1.0)
        memset('pool', ATTS, 0.0)
        for q in range(2):
            P.dma('sp', CSS[:, 0, 64 * q:64 * (q + 1)], cos_s[:, :])
            P.dma('sp', CSS[:, 1, 64 * q:64 * (q + 1)], sin_s[:, :])
        for tile in range(2):
            cp('dve', HTC2.rearrange("p k (q j) -> p k q j", j=64), HTS[:, :, 2 * tile:2 * tile + 2, 2:66])
            tokmajor_kv(lambda kc: HTC2[:, kc, :], 128, tile, None,
                        lambda vf, tile=tile: cp('pool', VNEW[:, tile, :], vf),
                        lambda ko, tile=tile: cp('pool', KNEW[:, tile, :], ko),
                        cost_s[:, :], sint_s[:, :])
            for q in range(2):
                seq = 2 * tile + q
                P.dma('pool', o_sk[seq, 2040:2048, :], KNEW[64 * q:64 * q + 8, tile, :])
                P.dma('pool', o_sv[seq, 2040:2048, :], VNEW[64 * q:64 * q + 8, tile, :])
            for hp in range(4):
                for which, c0_, dstF in ((0, 128 * hp, QSF), (1, 512 + 128 * hp, KSF)):
                    ps = bank(which)
                    for kc in range(8):
                        mm(ps[:, 0:128], WQKV[:, kc, c0_:c0_ + 128], HTC2[:, kc, :], start=(kc == 0), stop=(kc == 7))
                    cp('act', XBs, ps[:, 0:128])
                    pw_ = bank(7)
                    mm(pw_[:, 0:128], PERMF, XBs)
                    tt('pool', RT1s, XBs, CSS[:, 0, :], ALU.mult)
                    tt('dve', RT2s, pw_[:, 0:128], CSS[:, 1, :], ALU.mult)
                    tt('pool', dstF[:, hp, 128 * tile:128 * (tile + 1)], RT1s, RT2s, ALU.add)
    SB.release(mK)
    if 'S' in stages:
        KC = SB.alloc((17, 512), F32)
        KTC = SB.alloc((17 * 128,), F32)
        SSB = SB.alloc((17 * 128,), F32)
        LB = SB.alloc((17 * 128,), F32)
        PTs = SB.alloc((8, 17, 8), F32)
        P.dma('sp', LB[0:8, :], lb_s[:, :])
        for b_ in range(4):
            tile, q = b_ // 2, b_ % 2
            for t8 in range(0, 16, 4):
                P.dma('sp', KC[:, t8:t8 + 4, :], ck[b_, 128 * t8:128 * (t8 + 4), :].rearrange("(t p) c -> p t c", p=128))
            memset('pool', KC[:, 16, :], 0.0)
            cp('dve', KC[0:8, 16, :], KNEW[64 * q:64 * q + 8, tile, :])
            for h_ in range(8):
                pi, hp = h_ % 2, h_ // 2
                hs = slice(64 * pi, 64 * (pi + 1))
                hcol = slice(64 * h_, 64 * (h_ + 1))
                for g in range(5):
                    nt_ = 4 if g < 4 else 1
                    pk = bank(2 * (g % 2))
                    for j in range(nt_):
                        tr(pk[0:64, 128 * j:128 * (j + 1)], KC[:, 4 * g + j, hcol], IDF)
                    cp('act' if g % 2 else 'dve', KTC[hs, 512 * g:512 * g + 128 * nt_], pk[0:64, 0:128 * nt_])
                qv = QSF[hs, hp, 64 * b_:64 * b_ + 8]
                for g in range(5):
                    nn = 512 if g < 4 else 128
                    psc = bank(4 + pi + 2 * (g % 2))
                    mm(psc[0:8, 0:nn], qv, KTC[hs, 512 * g:512 * g + nn])
                    stt(SSB[0:8, 512 * g:512 * g + nn], psc[0:8, 0:nn], 0.125, LB[0:8, 512 * g:512 * g + nn], ALU.mult, ALU.add)
                P.op('dve', lambda e: e.tensor_reduce(out=SM[0:8, 0:1], in_=SSB[0:8, :], axis=AX.X, op=ALU.max),
                     [SM[0:8, 0:1]], [SSB[0:8, :]])
                ts('dve', SM[0:8, 1:2], SM[0:8, 0:1], -1.0, ALU.mult)
                act(SSB[0:8, :], SSB[0:8, :], AF.Exp, bias=SM[0:8, 1:2])
                pt_ = bank(0)
                for c_ in range(17):
                    tr(pt_[:, 8 * c_:8 * (c_ + 1)], SSB[0:8, 128 * c_:128 * (c_ + 1)], IDF[0:8, 0:8])
                cp('act', PTs[:, h_, :, :], pt_[:, 0:136].rearrange("p (c t) -> p c t", t=8))
            for t8 in range(0, 16, 4):
                P.dma('sp', KC[:, t8:t8 + 4, :], cv[b_, 128 * t8:128 * (t8 + 4), :].rearrange("(t p) c -> p t c", p=128))
            memset('pool', KC[:, 16, :], 0.0)
            cp('dve', KC[0:8, 16, :], VNEW[64 * q:64 * q + 8, tile, :])
            for h_ in range(8):
                pi, hp = h_ % 2, h_ // 2
                hs = slice(64 * pi, 64 * (pi + 1))
                hcol = slice(64 * h_, 64 * (h_ + 1))
                pn = bank(1)
                for c_ in range(17):
                    mm(pn[hs, 0:8], KC[:, c_, hcol], PTs[:, h_, c_, :], start=(c_ == 0), stop=(c_ == 16), tp=(0, 64 * pi))
                pd = bank(3)
                for c_ in range(17):
                    mm(pd[hs, 0:8], ONESF, PTs[:, h_, c_, :], start=(c_ == 0), stop=(c_ == 16), tp=(0, 64 * pi))
                cp('act', RD[hs, :], pd[hs, 0:8])
                P.op('dve', lambda e, hs=hs: e.reciprocal(out=RD[hs, :], in_=RD[hs, :]), [RD[hs, :]], [RD[hs, :]])
                tt('dve', ATTS[hs, hp, 64 * b_:64 * b_ + 8], pn[hs, 0:8], RD[hs, :], ALU.mult)
        for hp in range(4):
            P.dma('pool', mixs_s[512 + 128 * hp:512 + 128 * (hp + 1), :], ATTS[:, hp, :])
    SB.release(mS0)

    QT2 = SB.alloc((T, 2), BF16)
    KT2 = SB.alloc((T, 2), BF16)
    NV = {1: T // 128, 4: T // 128, 16: T // 128}
    VD = {d: SB.alloc((T // 128, 128), BF16) for d in (1, 4, 16)}
    NUMA = SB.alloc((2048,), F32)
    DENA = SB.alloc((2048,), F32)
    ATT = SB.alloc((2048,), BF16)
    XB = [SB.alloc((512,), BF16) for _ in range(2)]
    RT1 = [SB.alloc((512,), F32) for _ in range(2)]
    RT2 = [SB.alloc((512,), F32) for _ in range(2)]
    CSF = [SB.alloc((2, 512), F32)] * 2
    EB = [SB.alloc((256,), BF16) for _ in range(8)]
    EM = [SB.alloc((256,), BF16) for _ in range(16)]
    PTS = [SB.alloc((2, 128), BF16) for _ in range(8)]
    BIAS = SB.alloc((8,), F32)
    DJ = [SB.alloc((128,), F32)] * 2
    uidx = [0]

    def rope_fm(ps, xb, t1, t2, cs_, dst, n):
        cp('act', xb[:, 0:n], ps)
        pw_ = bank(7)
        mm(pw_[:, 0:n], PERM, xb[:, 0:n])
        tt('pool', t1[:, 0:n], xb[:, 0:n], cs_[:, 0, 0:n], ALU.mult)
        tt('dve', t2[:, 0:n], pw_[:, 0:n], cs_[:, 1, 0:n], ALU.mult)
        tt('pool', dst, t1[:, 0:n], t2[:, 0:n], ALU.add)

    for hp in (range(4) if ('B' in stages and BCUT >= 1) else []):
        qc = slice(128 * hp, 128 * (hp + 1))
        kcs = slice(512 + 128 * hp, 512 + 128 * (hp + 1))
        for it in range(NT):
            tb = 512 * it
            cs_ = CSF[it % 2]
            P.dma('sp', cs_[:, 0, :], cos_p[:, tb:tb + 512])
            P.dma('sp', cs_[:, 1, :], sin_p[:, tb:tb + 512])
            for which, cols, dstT in ((0, qc, QT2), (1, kcs, KT2)):
                ps = bank(which)
                for kc in range(8):
                    mm(ps, WQKV[:, kc, cols], HT[:, kc, tb:tb + 512], start=(kc == 0), stop=(kc == 7))
                rope_fm(ps, XB[which], RT1[which], RT2[which], cs_, dstT[:, tb:tb + 512, 0], 512)
        if BCUT < 2:
            continue
        for d in (1, 4, 16):
            npc = T // d // 128
            for r in range(d):
                src = vs_s[r:T:d, 128 * hp:128 * (hp + 1)].rearrange("(mt mi) c -> mi mt c", mi=128)
                for m0_ in range(0, npc, 8):
                    m1_ = min(npc, m0_ + 8)
                    P.dma('sp', VD[d][:, r * npc + m0_:r * npc + m1_, :], src[:, m0_:m1_, :])
        if BCUT < 3:
            continue
        for H in range(T // 2048):
            units = []
            for d in (1, 4, 16):
                npc = T // d // 128
                tph = 16 // d
                for r in range(d):
                    for mt in range(H * tph, (H + 1) * tph):
                        nkb = 1 if mt == 0 else 2
                        q0 = r + d * 128 * mt
                        k0 = r + d * 128 * (mt - 1) if nkb == 2 else q0
                        units.append((d, r, mt, nkb, q0, k0, q0 - 2048 * H, npc))
            GU = 4

            def front(grp, par):
                hu = [(g, pi) for g in range(len(grp)) for pi in range(2)]
                for g, pi in hu:
                    d, r, mt, nkb, q0, k0, lo, npc = grp[g]
                    hs = slice(64 * pi, 64 * (pi + 1))
                    sb_ = bank(pi + 2 * (g % 2))[:, 256 * (g // 2):256 * (g // 2 + 1)]
                    c0 = 0 if nkb == 2 else 128
                    mm(sb_[:, c0:256], QT2[hs, sl(q0, 128, d), 0], KT2[hs, sl(k0, 128 * nkb, d), 0])
                for g, pi in hu:
                    sb_ = bank(pi + 2 * (g % 2))[:, 256 * (g // 2):256 * (g // 2 + 1)]
                    u = 2 * g + pi
                    stt(DJ[pi], sb_[:, 128:256], -0.125, IDF, ALU.mult, ALU.mult, accum=BIAS[:, u:u + 1])
                for g, pi in hu:
                    d, r, mt, nkb, q0, k0, lo, npc = grp[g]
                    sb_ = bank(pi + 2 * (g % 2))[:, 256 * (g // 2):256 * (g // 2 + 1)]
                    u = 2 * g + pi
                    c0 = 0 if nkb == 2 else 128
                    act(EB[u][:, c0:256], sb_[:, c0:256], AF.Exp, bias=BIAS[:, u:u + 1], scale=0.125)
                for g, pi in hu:
                    d, r, mt, nkb, q0, k0, lo, npc = grp[g]
                    u = 2 * g + pi
                    c0 = 0 if nkb == 2 else 128
                    tt('pool', EM[8 * par + u][:, c0:256], EB[u][:, c0:256], MASK2[:, c0:256], ALU.mult)

            def back(grp, par):
                hu = [(g, pi) for g in range(len(grp)) for pi in range(2)]
                for g, pi in hu:
                    d, r, mt, nkb, q0, k0, lo, npc = grp[g]
                    u = 2 * g + pi
                    pt_ = bank(4 + pi).bitcast(BF16)[:, 256 * g:256 * (g + 1)]
                    for kb in range(2 - nkb, 2):
                        tr(pt_[:, 128 * kb:128 * (kb + 1)], EM[8 * par + u][:, 128 * kb:128 * (kb + 1)], IDB)
                for g, pi in hu:
                    d, r, mt, nkb, q0, k0, lo, npc = grp[g]
                    u = 2 * g + pi
                    pt_ = bank(4 + pi).bitcast(BF16)[:, 256 * g:256 * (g + 1)]
                    cp('act' if pi == 0 else 'dve', PTS[u][:, 2 - nkb:2, :],
                       pt_[:, 128 * (2 - nkb):256].rearrange("p (k q) -> p k q", q=128))
                for g, pi in hu:
                    d, r, mt, nkb, q0, k0, lo, npc = grp[g]
                    u = 2 * g + pi
                    hs = slice(64 * pi, 64 * (pi + 1))
                    pn, pd = bank(6)[:, 128 * g:128 * (g + 1)], bank(7)[:, 128 * g:128 * (g + 1)]
                    for kb in range(2 - nkb, 2):
                        vt = r * npc + mt - 1 + kb
                        mm(pn[hs, :], VD[d][:, vt, hs], PTS[u][:, kb, :], start=(kb == 2 - nkb), stop=(kb == 1), tp=(0, 64 * pi))
                    for kb in range(2 - nkb, 2):
                        mm(pd[hs, :], ONESB, PTS[u][:, kb, :], start=(kb == 2 - nkb), stop=(kb == 1), tp=(0, 64 * pi))
                for g in range(len(grp)):
                    d, r, mt, nkb, q0, k0, lo, npc = grp[g]
                    pn, pd = bank(6)[:, 128 * g:128 * (g + 1)], bank(7)[:, 128 * g:128 * (g + 1)]
                    dstn = NUMA[:, sl(lo, 128, d)]
                    dstd = DENA[:, sl(lo, 128, d)]
                    if d == 1:
                        cp('dve', dstn, pn)
                        cp('act', dstd, pd)
                    else:
                        tt('dve', dstn, pn, dstn, ALU.add)
                        tt('dve', dstd, pd, dstd, ALU.add)

            prev_ = None
            for gi_, g0 in enumerate(range(0, len(units), GU)):
                grp = units[g0:g0 + GU]
                front(grp, gi_ % 2)
                if prev_ is not None:
                    back(*prev_)
                prev_ = (grp, gi_ % 2)
            back(*prev_)
            P.op('dve', lambda e: e.reciprocal(out=DENA, in_=DENA), [DENA], [DENA])
            tt('pool', ATT, NUMA, DENA, ALU.mult)
            P.dma('pool', mixp_s[512 + 128 * hp:512 + 128 * (hp + 1), 2048 * H:2048 * (H + 1)], ATT)
    if debug:
        dbg['mix'] = dout("dbg_mix", [D, T], BF16)
        P.dma('pool', dbg['mix'], mixp_s)
    SB.release(m0)

    mC = SB.mark()
    WU = SB.alloc((8, 2 * DFF), BF16)
    WD = SB.alloc((22, D), BF16)
    GV = SB.alloc((3, D), F32)
    CW = SB.alloc((44, 4), F32)
    CAR = SB.alloc((5, 44, 2), F32)
    WOS = [SB.alloc((2, D), BF16) for _ in range(2)]
    mW = SB.mark()
    if 'C' in stages:
        alloc_wst()
        for i, gsrc in enumerate((n_mix_post, n_ffn_pre, n_ffn_post)):
            P.dma('sp', GV[:, i, :], gsrc[0:1, :].partition_broadcast(128))
        for j in range(3):
            dma_fc('sp', lambda c0, c1, j=j: CW[:, c0:c1, j], cw_d[j, :], False)
        dma_fc('sp', lambda c0, c1: CW[:, c0:c1, 3], cb_d.rearrange("f o -> (f o)"), False)
        for kc in range(8):
            rows = slice(128 * kc, 128 * (kc + 1))
            for j in range(0, 2 * DFF, 1024):
                n_ = min(1024, 2 * DFF - j)
                load_w(WU[:, kc, j:j + n_], w_up[rows, j:j + n_], n_)
        for fcn in range(22):
            load_w(WD[:, fcn, :], w_dn[128 * fcn:128 * (fcn + 1), :], 1024)
        for kc in range(8):
            load_w(WOS[kc % 2][:, 0, :], w_out[128 * kc:128 * (kc + 1), :], 1024)
            P.dma('pool', wout_s[128 * kc:128 * (kc + 1), :], WOS[kc % 2][:, 0, :])
    SB.release(mW)
    MIXT = [SB.alloc((8, 128), BF16) for _ in range(2)]
    WOF = [SB.alloc((512,), F32) for _ in range(2)]
    XC = [SB.alloc((D,), F32) for _ in range(2)]
    H2B = SB.alloc((D,), BF16)
    H2T = SB.alloc((8, 128), BF16)
    UE = [[SB.alloc((4, 130), F32)] * 2 for _ in range(2)]
    CGU = [SB.alloc((4, 128), F32) for _ in range(2)]
    CT = [SB.alloc((128,), F32) for _ in range(4)]
    FTB = [SB.alloc((22, 128), BF16) for _ in range(2)]
    JK = SB.alloc((D,), BF16)
    ST2 = SB.alloc((8, 4), F32)
    EPS2 = SB.alloc((1,), F32)
    ci = [0]
    pend = [None]

    def ffn_tile(*a_, **kw_):
        g_ = ffn_tile_gen(*a_, **kw_)
        next(g_)
        ffn_flush()
        next(g_)
        next(g_)
        pend[0] = g_

    def ffn_flush():
        if pend[0] is not None:
            for _ in pend[0]:
                pass
            pend[0] = None

    def rms_from(src_halves, si, col):
        act(JK[:, 0:512], src_halves[0], AF.Square, accum=ST2[:, si, 0:1])
        act(JK[:, 512:1024], src_halves[1], AF.Square, accum=ST2[:, si, 1:2])
        tt('dve', ST2[:, si, 0:1], ST2[:, si, 0:1], ST2[:, si, 1:2], ALU.add)
        act(ST2[:, si, 1:2], ST2[:, si, 0:1], AF.Sqrt, bias=EPS2[:, 0:1], scale=1.0 / D)
        P.op('dve', lambda e: e.reciprocal(out=ST2[:, si, col:col + 1], in_=ST2[:, si, 1:2]),
             [ST2[:, si, col:col + 1]], [ST2[:, si, 1:2]])

    def ffn_tile_gen(mix_src, x_src, y_dst, stream, segs, conv_out=None):
        i = ci[0] % 2
        si = ci[0] % 8
        ci[0] += 1
        mixt, xc = MIXT[i], XC[i]
        FT = FTB[i]
        P.dma('sp', mixt, mix_src.rearrange("(k p) t -> p k t", p=128))
        x_src(xc)
        for kq in range(4):
            wos = WOS[kq % 2]
            P.dma('sp', wos, wout_s[256 * kq:256 * (kq + 1), :].rearrange("(k p) c -> p k c", p=128))
            for k2 in range(2):
                kc = 2 * kq + k2
                for half in range(2):
                    mm(bank(half), mixt[:, kc, :], wos[:, k2, 512 * half:512 * (half + 1)], start=(kc == 0), stop=(kc == 7))
        yield 0
        rms_from((bank(0), bank(1)), si, 2)
        for half in range(2):
            hs_ = slice(512 * half, 512 * (half + 1))
            stt(WOF[half][:, 0:512], bank(half), ST2[:, si, 2:3], GV[:, 0, hs_], ALU.mult, ALU.mult)
            tt('pool', xc[:, hs_], xc[:, hs_], WOF[half][:, 0:512], ALU.add)
        rms_from((xc[:, 0:512], xc[:, 512:1024]), si, 3)
        stt(H2B, xc, ST2[:, si, 3:4], GV[:, 1, :], ALU.mult, ALU.mult)
        pb = bank(2).bitcast(BF16)
        for kc in range(8):
            tr(pb[:, 128 * kc:128 * (kc + 1)], H2B[:, 128 * kc:128 * (kc + 1)], IDB)
        cp('act', H2T, pb.rearrange("p (k t) -> p k t", t=128))
        g0 = 0
        gi = 0
        while g0 < 22:
            ng = min(4, 22 - g0)
            for side in range(2):
                ps = bank(3 + 2 * (gi % 2) + side)
                for c in range(ng):
                    ch = 22 * side + g0 + c
                    for kc in range(8):
                        mm(ps[:, 128 * c:128 * (c + 1)], WU[:, kc, 128 * ch:128 * (ch + 1)], H2T[:, kc, :],
                           start=(kc == 0), stop=(kc == 7))
            for side in range(2):
                ps = bank(3 + 2 * (gi % 2) + side)
                ue = UE[side][gi % 2]
                chs = slice(22 * side + g0, 22 * side + g0 + ng)
                cp('act', ue[:, 0:ng, 2:130], ps[:, 0:128 * ng].rearrange("p (c t) -> p c t", t=128))
                for (col0, ncols, cinit) in segs:
                    if cinit is None:
                        cp('pool', ue[:, 0:ng, col0:col0 + 2], CAR[:, stream, chs, :])
                    else:
                        cinit(ue[:, 0:ng, col0:col0 + 2], chs)
                for c in range(ng):
                    ch = 22 * side + g0 + c
                    t_ = CT[(2 * c + side) % 4]
                    act(t_, ue[:, c, 0:128], AF.Identity, bias=CW[:, ch, 3:4], scale=CW[:, ch, 0:1])
                    stt(t_, ue[:, c, 1:129], CW[:, ch, 1:2], t_, ALU.mult, ALU.add)
                    stt(CGU[side][:, c, :], ue[:, c, 2:130], CW[:, ch, 2:3], t_, ALU.mult, ALU.add)
                last_col = segs[-1][0] + segs[-1][1]
                cp('pool', CAR[:, stream, chs, :], ue[:, 0:ng, last_col:last_col + 2])
                if conv_out is not None:
                    conv_out(ue, chs, ng)
            act(CGU[0][:, 0:ng, :], CGU[0][:, 0:ng, :], AF.Silu)
            tt('pool', FT[:, g0:g0 + ng, :], CGU[0][:, 0:ng, :], CGU[1][:, 0:ng, :], ALU.mult)
            g0 += ng
            gi += 1
            if gi == 2:
                yield 1
        yield 2
        dbk = (bank(7), bank(2))
        for half in range(2):
            ps = dbk[half]
            for fcn in range(22):
                mm(ps, FT[:, fcn, :], WD[:, fcn, 512 * half:512 * (half + 1)], start=(fcn == 0), stop=(fcn == 21))
        rms_from(dbk, si, 2)
        for half in range(2):
            hs_ = slice(512 * half, 512 * (half + 1))
            stt(WOF[half][:, 0:512], dbk[half], ST2[:, si, 2:3], GV[:, 2, hs_], ALU.mult, ALU.mult)
            tt('pool', xc[:, hs_], xc[:, hs_], WOF[half][:, 0:512], ALU.add)
        y_dst(xc)

    if 'C' in stages:
        memset('pool', EPS2, NORM_EPS)
        memset('pool', CAR, 0.0)
        for it in range(T // 128):
            t0 = 128 * it
            ffn_tile(mixp_s[:, t0:t0 + 128],
                     lambda xc, t0=t0: P.dma('sp', xc, x_p[t0:t0 + 128, :]),
                     lambda xc, t0=t0: P.dma('pool', y_p[t0:t0 + 128, :], xc),
                     0, [(0, 128, None)])
        ffn_flush()
        if 'S' in stages:
            CONVS = SB.alloc((4, 44, 2), F32)
            SCONV = SB.alloc((4, 44, 2), F32)
            for seq in range(4):
                for r_ in range(2):
                    dma_fc('sp', lambda c0, c1, seq=seq, r_=r_: CONVS[:, seq, c0:c1, r_], st_conv[seq, r_, :], False)
            for tile in range(2):
                def xs_(xc, tile=tile):
                    memset('pool', xc, 0.0)
                    for q in range(2):
                        P.dma('sp', xc[64 * q:64 * q + 8, :], x_s[2 * tile + q, :, :])

                def ys_(xc, tile=tile):
                    for q in range(2):
                        P.dma('pool', y_s[2 * tile + q, :, :], xc[64 * q:64 * q + 8, :])

                def mk_init(seq):
                    return lambda dst, chs, seq=seq: cp('pool', dst, CONVS[:, seq, chs, :])

                def cout(ue, chs, ng, tile=tile):
                    for q in range(2):
                        cp('pool', SCONV[:, 2 * tile + q, chs, :], ue[:, 0:ng, 2 + 64 * q + 6:2 + 64 * q + 8])

                ffn_tile(mixs_s[:, 128 * tile:128 * (tile + 1)], xs_, ys_, 1,
                         [(0, 64, mk_init(2 * tile)), (64, 64, mk_init(2 * tile + 1))], conv_out=cout)
            ffn_flush()
            for seq in range(4):
                for r_ in range(2):
                    dma_fc('pool', lambda c0, c1, seq=seq, r_=r_: SCONV[:, seq, c0:c1, r_], o_sconv[seq, r_, :], True)
        for r_ in range(2):
            dma_fc('pool', lambda c0, c1, r_=r_: CAR[:, 0, c0:c1, r_], o_pconv[r_, :], True)
    SB.release(mC)

    P.final_wait_all('sp')
    P.build()
    return nc


def _rope_tables(T):
    half = 32
    inv = (np.float32(10000.0) ** (-np.arange(half, dtype=np.float32) / np.float32(half))).astype(np.float32)

    def tab(pos):
        ang = pos.astype(np.float32)[:, None] * inv[None, :]
        return np.cos(ang).astype(np.float32), np.sin(ang).astype(np.float32)
    cp_, sp_ = tab(np.arange(T))
    cs_, ss_ = tab(PAST_LEN + np.arange(64))
    def fm(c, s):
        cf = np.concatenate([c.T, c.T], 0)
        sf = np.concatenate([-s.T, s.T], 0)
        return (np.ascontiguousarray(np.concatenate([cf, cf], 0)), np.ascontiguousarray(np.concatenate([sf, sf], 0)))
    return fm(cp_, sp_), fm(cs_, ss_), (cp_, sp_), (cs_, ss_)


def _lb_table():
    cnt = np.zeros((8, 17 * 128), np.float64)
    for t in range(8):
        for d in (1, 4, 16):
            for j in range(129):
                idx = WBUF + t - j * d
                if idx >= 0:
                    cnt[t, idx] += 1
    lb = np.where(cnt > 0, np.log(np.maximum(cnt, 1)), -30000.0)
    return lb.astype(np.float32)


def _core_inputs(inp, c, T):
    (cfp, sfp), (cfs, sfs), (ctp, stp), (cts, sts) = _rope_tables(T)
    KEEP = min(WBUF, T)
    f = lambda a: np.ascontiguousarray(np.asarray(a, dtype=np.float32))
    d = {
        'x_prompt': f(inp['x_prompt'][c, :T]),
        'x_sample': f(inp['x_sample'][4 * c:4 * c + 4]),
        'state_rwkv_shift': f(inp['state_rwkv_shift'][0, 4 * c:4 * c + 4]),
        'state_rwkv_wkv': f(inp['state_rwkv_wkv'][0, 4 * c:4 * c + 4]),
        'cache_att_k': f(np.asarray(inp['cache_att_k'][0, 4 * c:4 * c + 4]).reshape(4, WBUF, 512)),
        'cache_att_v': f(np.asarray(inp['cache_att_v'][0, 4 * c:4 * c + 4]).reshape(4, WBUF, 512)),
        'state_ffn_conv': f(inp['state_ffn_conv'][0, 4 * c:4 * c + 4]),
        'norm_mix_pre': f(inp['norm_mix_pre']), 'norm_mix_post': f(inp['norm_mix_post']),
        'norm_ffn_pre': f(inp['norm_ffn_pre']), 'norm_ffn_post': f(inp['norm_ffn_post']),
        'w_in': f(inp['w_in'][0]), 'mu_shift': f(inp['mu_shift']),
        'w0': f(inp['w0']).reshape(512, 1), 'w_decay_up': f(inp['w_decay_up'][0]), 'a0': f(inp['a0']).reshape(512, 1),
        'w_iclr_up': f(inp['w_iclr_up'][0]), 'w_gate_up': f(inp['w_gate_up'][0]),
        'k_k': f(inp['k_k']).reshape(512, 1), 'k_a': f(inp['k_a']).reshape(512, 1), 'r_k': f(inp['r_k']).reshape(512, 1),
        'lnx_w': f(inp['lnx_w']).reshape(512, 1), 'lnx_b': f(inp['lnx_b']).reshape(512, 1),
        'w_out': f(inp['w_out'][0]), 'w_ffn_up': f(inp['w_ffn_up'][0]), 'ffn_conv_w': f(inp['ffn_conv_w'][0]),
        'ffn_conv_b': f(inp['ffn_conv_b']).reshape(2 * DFF, 1), 'w_ffn_down': f(inp['w_ffn_down'][0]),
        'cos_p': cfp, 'sin_p': sfp, 'cos_s': cfs, 'sin_s': sfs,
        'cost_p': np.ascontiguousarray(ctp[T - KEEP:]), 'sint_p': np.ascontiguousarray(stp[T - KEEP:]),
        'cost_s': np.ascontiguousarray(np.tile(cts[:64], (2, 1))), 'sint_s': np.ascontiguousarray(np.tile(sts[:64], (2, 1))),
        'lb_s': _lb_table(),
    }
    return d


_NC_CACHE = {}


def kernel(**inputs):
    T = 4096
    n = 8
    if T not in _NC_CACHE:
        _NC_CACHE[T] = build_program(T=T, debug=False, stages='NRBCS')
    nc = _NC_CACHE[T]
    in_maps = [_core_inputs(inputs, c, T) for c in range(n)]
    res = run_bass_kernel_spmd(nc, in_maps, core_ids=list(range(n)))
    rs = res.results
    g = lambda k: [np.asarray(r[k], dtype=np.float32) for r in rs]
    y_p = np.stack(g('y_prompt'), 0)
    y_s = np.concatenate(g('y_sample'), 0)
    p_shift = np.stack([a.reshape(D) for a in g('p_shift')], 0)[None]
    p_wkv = np.stack(g('p_wkv'), 0)[None]
    p_k = np.stack([a.reshape(WBUF, 8, 64) for a in g('p_att_k')], 0)[None]
    p_v = np.stack([a.reshape(WBUF, 8, 64) for a in g('p_att_v')], 0)[None]
    p_conv = np.stack(g('p_conv'), 0)[None]
    s_shift = np.concatenate(g('s_shift'), 0)[None]
    s_wkv = np.concatenate(g('s_wkv'), 0)[None]
    s_k = np.concatenate([a.reshape(4, WBUF, 8, 64) for a in g('s_att_k')], 0)[None]
    s_v = np.concatenate([a.reshape(4, WBUF, 8, 64) for a in g('s_att_v')], 0)[None]
    s_conv = np.concatenate(g('s_conv'), 0)[None]
    return (y_p, y_s, p_shift, p_wkv, p_k, p_v, p_conv, s_shift, s_wkv, s_k, s_v, s_conv)
```
